# Optimizing a Trainium2 kernel written in Bass

```python
import math, functools
import jax, jax.numpy as jnp
from jax import lax
import numpy as np

D_MODEL = 1024
BATCH = 2
SEQ = 16384
DEPTH = 2

N_EVEN = (DEPTH + 1) // 2
N_ODD = DEPTH // 2

A_HEADS = 8
A_HEAD_DIM = 64
A_WIDTH = A_HEADS * A_HEAD_DIM
MOBA_BLOCK = 256
MOBA_TOPK = 3
Q_CHUNK = 64
ROPE_THETA = 10000.0

B_HEADS = 8
B_HEAD_K = 32
B_HEAD_V = 64
B_KEY_WIDTH = B_HEADS * B_HEAD_K
B_VAL_WIDTH = B_HEADS * B_HEAD_V
GLA_GATE_RANK = 16
GLA_GATE_NORM = 16.0
GLA_CHUNK = 64

MIX_WIDTH = A_WIDTH + B_VAL_WIDTH
EVEN_PROJ = 3 * A_WIDTH + 2 * B_KEY_WIDTH + B_VAL_WIDTH + GLA_GATE_RANK + B_VAL_WIDTH
EVEN_SPLITS = (
    A_WIDTH,
    2 * A_WIDTH,
    3 * A_WIDTH,
    3 * A_WIDTH + B_KEY_WIDTH,
    3 * A_WIDTH + 2 * B_KEY_WIDTH,
    3 * A_WIDTH + 2 * B_KEY_WIDTH + B_VAL_WIDTH,
    3 * A_WIDTH + 2 * B_KEY_WIDTH + B_VAL_WIDTH + GLA_GATE_RANK,
)

D_RNN = D_MODEL
RG_BLOCKS = 4
RG_BLOCK_W = D_RNN // RG_BLOCKS
RG_CONV = 4
LRU_C = 8.0

D_FF = 3 * D_MODEL
FFN_CONV = 3

NORM_EPS = 1e-6
NEG_INF = -1e30

kernel_name = "hybrid_moba_gla_rglru_convffn"


def rms_norm(x, g):
    xf = x.astype(jnp.float32)
    y = xf * lax.rsqrt(jnp.mean(xf * xf, axis=-1, keepdims=True) + NORM_EPS)
    return (y * g.astype(jnp.float32)).astype(x.dtype)


def causal_dwconv(x, w, b):
    width = w.shape[0]
    seq = x.shape[1]
    xp = jnp.pad(x, ((0, 0), (width - 1, 0), (0, 0)))
    y = b
    for i in range(width):
        y = y + xp[:, i:i + seq] * w[i]
    return y


def rotary(x, pos):
    hd = x.shape[-1]
    half = hd // 2
    inv = ROPE_THETA ** (-jnp.arange(half, dtype=jnp.float32) / half)
    ang = pos.astype(jnp.float32)[:, None] * inv[None, :]
    cos, sin = jnp.cos(ang), jnp.sin(ang)
    xf = x.astype(jnp.float32)
    x1, x2 = xf[..., :half], xf[..., half:]
    return jnp.concatenate([x1 * cos - x2 * sin, x2 * cos + x1 * sin], axis=-1).astype(x.dtype)


def moba_attention(q, k, v):
    bsz, nh, seq, hd = q.shape
    nb = -(-seq // MOBA_BLOCK)
    s_pad = nb * MOBA_BLOCK
    pad = ((0, 0), (0, 0), (0, s_pad - seq), (0, 0))
    kp = jnp.pad(k, pad)
    vp = jnp.pad(v, pad)
    kb = kp.reshape(bsz, nh, nb, MOBA_BLOCK, hd)
    vb = vp.reshape(bsz, nh, nb, MOBA_BLOCK, hd)
    kmean = jnp.mean(kb.astype(jnp.float32), axis=3)
    n_sel = min(MOBA_TOPK, nb)
    scale = hd ** -0.5
    bi = jnp.arange(bsz)[:, None, None, None]
    hi = jnp.arange(nh)[None, :, None, None]
    blk_ids = jnp.arange(nb)
    n_chunks = seq // Q_CHUNK

    def one_chunk(c):
        start = c * Q_CHUNK
        qc = lax.dynamic_slice_in_dim(q, start, Q_CHUNK, axis=2).astype(jnp.float32)
        qpos = start + jnp.arange(Q_CHUNK)
        blk = start // MOBA_BLOCK
        gate = jnp.einsum('bhqd,bhnd->bhqn', qc, kmean)
        gate = jnp.where(blk_ids < blk, gate, -jnp.inf)
        _, idx = lax.top_k(gate, n_sel)
        valid = idx < blk
        kg = kb[bi, hi, idx].astype(jnp.float32)
        vg = vb[bi, hi, idx].astype(jnp.float32)
        s_sel = jnp.einsum('bhqd,bhqnkd->bhqnk', qc, kg) * scale
        s_sel = jnp.where(valid[..., None], s_sel, NEG_INF)
        own_start = blk * MOBA_BLOCK
        k_own = lax.dynamic_slice_in_dim(kp, own_start, MOBA_BLOCK, axis=2).astype(jnp.float32)
        v_own = lax.dynamic_slice_in_dim(vp, own_start, MOBA_BLOCK, axis=2).astype(jnp.float32)
        kpos = own_start + jnp.arange(MOBA_BLOCK)
        s_own = jnp.einsum('bhqd,bhkd->bhqk', qc, k_own) * scale
        s_own = jnp.where(kpos[None, :] <= qpos[:, None], s_own, NEG_INF)
        s = jnp.concatenate([s_sel.reshape(bsz, nh, Q_CHUNK, n_sel * MOBA_BLOCK), s_own], axis=-1)
        p = jax.nn.softmax(s, axis=-1)
        p_sel = p[..., :n_sel * MOBA_BLOCK].reshape(bsz, nh, Q_CHUNK, n_sel, MOBA_BLOCK)
        p_own = p[..., n_sel * MOBA_BLOCK:]
        o = (jnp.einsum('bhqnk,bhqnkd->bhqd', p_sel, vg)
             + jnp.einsum('bhqk,bhkd->bhqd', p_own, v_own))
        return o.astype(q.dtype)

    out = lax.map(one_chunk, jnp.arange(n_chunks))
    return out.transpose(1, 2, 0, 3, 4).reshape(bsz, nh, seq, hd)


def gla_attention(q, k, v, log_a):
    bsz, nh, seq, dk = q.shape
    dv = v.shape[-1]
    nc = seq // GLA_CHUNK

    def to_chunks(t):
        return t.astype(jnp.float32).reshape(bsz, nh, nc, GLA_CHUNK, t.shape[-1]).transpose(2, 0, 1, 3, 4)

    qc, kc, vc, gc = to_chunks(q), to_chunks(k), to_chunks(v), to_chunks(log_a)
    causal = jnp.tril(jnp.ones((GLA_CHUNK, GLA_CHUNK), dtype=bool))

    def step(state, inp):
        qi, ki, vi, gi = inp
        b = jnp.cumsum(gi, axis=2)
        diff = b[:, :, :, None, :] - b[:, :, None, :, :]
        decay = jnp.exp(jnp.where(causal[:, :, None], diff, -jnp.inf))
        attn = jnp.einsum('bhtd,bhsd,bhtsd->bhts', qi, ki, decay)
        o = (jnp.einsum('bhts,bhsv->bhtv', attn, vi)
             + jnp.einsum('bhtd,bhdv->bhtv', qi * jnp.exp(b), state))
        b_last = b[:, :, -1:, :]
        state = (jnp.exp(b_last[:, :, 0, :, None]) * state
                 + jnp.einsum('bhsd,bhsv->bhdv', ki * jnp.exp(b_last - b), vi))
        return state, o

    state0 = jnp.zeros((bsz, nh, dk, dv), jnp.float32)
    _, o = lax.scan(step, state0, (qc, kc, vc, gc))
    return o.transpose(1, 2, 0, 3, 4).reshape(bsz, nh, seq, dv).astype(v.dtype)


def even_mixer(x, w_in, w_gk2, b_gk2, gla_norm_g, w_out, pos):
    bsz, seq, _ = x.shape
    proj = x @ w_in
    aq, ak, av, bq, bk, bv, bgk, bog = jnp.split(proj, EVEN_SPLITS, axis=-1)

    def heads(t, n):
        return t.reshape(bsz, seq, n, -1).transpose(0, 2, 1, 3)

    o_a = moba_attention(rotary(heads(aq, A_HEADS), pos), rotary(heads(ak, A_HEADS), pos),
                         heads(av, A_HEADS))
    o_a = o_a.transpose(0, 2, 1, 3).reshape(bsz, seq, A_WIDTH)
    log_a = jax.nn.log_sigmoid((bgk @ w_gk2 + b_gk2).astype(jnp.float32)) / GLA_GATE_NORM
    o_b = gla_attention(heads(bq, B_HEADS) * (B_HEAD_K ** -0.5), heads(bk, B_HEADS),
                        heads(bv, B_HEADS), heads(log_a, B_HEADS))
    o_b = rms_norm(o_b, gla_norm_g[:, None, :])
    o_b = o_b.transpose(0, 2, 1, 3).reshape(bsz, seq, B_VAL_WIDTH) * jax.nn.silu(bog)
    return jnp.concatenate([o_a, o_b], axis=-1) @ w_out


def lru_combine(left, right):
    a1, b1 = left
    a2, b2 = right
    return a1 * a2, a2 * b1 + b2


def odd_mixer(x, w_in, conv_w, conv_b, w_a, b_a, w_x, b_x, lam, w_out):
    bsz, seq, _ = x.shape
    gate_branch, xr = jnp.split(x @ w_in, 2, axis=-1)
    xr = causal_dwconv(xr, conv_w, conv_b)
    xb = xr.reshape(bsz, seq, RG_BLOCKS, RG_BLOCK_W)
    r = jax.nn.sigmoid(jnp.einsum('bsnc,ncd->bsnd', xb, w_a).reshape(bsz, seq, D_RNN) + b_a)
    i = jax.nn.sigmoid(jnp.einsum('bsnc,ncd->bsnd', xb, w_x).reshape(bsz, seq, D_RNN) + b_x)
    log_a = (-LRU_C * r.astype(jnp.float32)) * jax.nn.softplus(-lam.astype(jnp.float32))
    a = jnp.exp(log_a)
    u = jnp.sqrt(-jnp.expm1(2.0 * log_a)) * (i * xr).astype(jnp.float32)
    _, h = lax.associative_scan(lru_combine, (a, u), axis=1)
    y = h.astype(x.dtype) * jax.nn.gelu(gate_branch)
    return y @ w_out


def conv_ffn(x, w_up, conv_w, conv_b, w_down):
    h = causal_dwconv(x @ w_up, conv_w, conv_b)
    u, g = jnp.split(h, 2, axis=-1)
    return (u * jax.nn.gelu(g)) @ w_down


def setup_inputs(seed: int = 0) -> dict:
    key = jax.random.key(seed)
    ks = jax.random.split(key, 24)

    def nrm(k, shape, scale):
        return jax.random.normal(k, shape, jnp.float32) * scale

    u = jax.random.uniform(ks[14], (N_ODD, D_RNN), jnp.float32, 0.9, 0.999)
    sig = u ** (1.0 / LRU_C)
    lam = jnp.log(sig) - jnp.log1p(-sig)
    return {
        "x": nrm(ks[0], (BATCH, SEQ, D_MODEL), 1.0),
        "mix_norm_g": 1.0 + nrm(ks[1], (DEPTH, D_MODEL), 0.02),
        "ffn_norm_g": 1.0 + nrm(ks[2], (DEPTH, D_MODEL), 0.02),
        "final_norm_g": 1.0 + nrm(ks[3], (D_MODEL,), 0.02),
        "ev_w_in": nrm(ks[4], (N_EVEN, D_MODEL, EVEN_PROJ), D_MODEL ** -0.5),
        "ev_w_gk2": nrm(ks[5], (N_EVEN, GLA_GATE_RANK, B_KEY_WIDTH), GLA_GATE_RANK ** -0.5),
        "ev_b_gk2": nrm(ks[6], (N_EVEN, B_KEY_WIDTH), 0.1),
        "ev_gla_norm_g": 1.0 + nrm(ks[7], (N_EVEN, B_HEADS, B_HEAD_V), 0.02),
        "ev_w_out": nrm(ks[8], (N_EVEN, MIX_WIDTH, D_MODEL), MIX_WIDTH ** -0.5),
        "od_w_in": nrm(ks[9], (N_ODD, D_MODEL, 2 * D_RNN), D_MODEL ** -0.5),
        "od_conv_w": nrm(ks[10], (N_ODD, RG_CONV, D_RNN), RG_CONV ** -0.5),
        "od_conv_b": nrm(ks[11], (N_ODD, D_RNN), 0.01),
        "od_w_a": nrm(ks[12], (N_ODD, RG_BLOCKS, RG_BLOCK_W, RG_BLOCK_W), RG_BLOCK_W ** -0.5),
        "od_b_a": nrm(ks[13], (N_ODD, D_RNN), 0.01),
        "od_w_x": nrm(ks[15], (N_ODD, RG_BLOCKS, RG_BLOCK_W, RG_BLOCK_W), RG_BLOCK_W ** -0.5),
        "od_b_x": nrm(ks[16], (N_ODD, D_RNN), 0.01),
        "od_lambda": lam,
        "od_w_out": nrm(ks[17], (N_ODD, D_RNN, D_MODEL), D_RNN ** -0.5),
        "ffn_w_up": nrm(ks[18], (DEPTH, D_MODEL, 2 * D_FF), D_MODEL ** -0.5),
        "ffn_conv_w": nrm(ks[19], (DEPTH, FFN_CONV, 2 * D_FF), FFN_CONV ** -0.5),
        "ffn_conv_b": nrm(ks[20], (DEPTH, 2 * D_FF), 0.01),
        "ffn_w_down": nrm(ks[21], (DEPTH, D_FF, D_MODEL), D_FF ** -0.5),
    }


def reference(x, mix_norm_g, ffn_norm_g, final_norm_g,
              ev_w_in, ev_w_gk2, ev_b_gk2, ev_gla_norm_g, ev_w_out,
              od_w_in, od_conv_w, od_conv_b, od_w_a, od_b_a, od_w_x, od_b_x, od_lambda, od_w_out,
              ffn_w_up, ffn_conv_w, ffn_conv_b, ffn_w_down):
    seq = x.shape[1]
    pos = jnp.arange(seq, dtype=jnp.int32)
    h = x
    for l in range(DEPTH):
        hn = rms_norm(h, mix_norm_g[l])
        j = l // 2
        if l % 2 == 0:
            mix = even_mixer(hn, ev_w_in[j], ev_w_gk2[j], ev_b_gk2[j], ev_gla_norm_g[j], ev_w_out[j], pos)
        else:
            mix = odd_mixer(hn, od_w_in[j], od_conv_w[j], od_conv_b[j], od_w_a[j], od_b_a[j],
                            od_w_x[j], od_b_x[j], od_lambda[j], od_w_out[j])
        h = h + mix
        h = h + conv_ffn(rms_norm(h, ffn_norm_g[l]), ffn_w_up[l], ffn_conv_w[l], ffn_conv_b[l], ffn_w_down[l])
    return rms_norm(h, final_norm_g)
```

```python
import contextlib
import os
import numpy as np
import ml_dtypes
import concourse.bass as bass
import concourse.mybir as mybir
from concourse.bass_utils import run_bass_kernel_spmd

F32 = mybir.dt.float32
BF16 = mybir.dt.bfloat16
AF = mybir.ActivationFunctionType
ALU = mybir.AluOpType
AX = mybir.AxisListType

SAME_SYNC = True
N_DMA_SEMS = 24
SEM_EPOCH = 2000
NEG = -240000.0
EPS = 1e-6


class Buf:
    __slots__ = ("name", "lw", "rd")

    def __init__(self, name=""):
        self.name = name
        self.lw = None
        self.rd = []


class Prog:
    ENGS = ("pe", "act", "dve", "pool", "sp")

    def __init__(self, nc):
        self.nc = nc
        self.h = {"pe": nc.tensor, "act": nc.scalar, "dve": nc.vector, "pool": nc.gpsimd, "sp": nc.sync}
        self.items = {e: [] for e in self.ENGS}
        self.known = {}
        self.sem = {e: nc.alloc_semaphore("s_" + e) for e in self.ENGS}
        self.dsem = [nc.alloc_semaphore("d%d" % i) for i in range(N_DMA_SEMS)]
        self.duse = [0] * N_DMA_SEMS
        self.dval = [0] * N_DMA_SEMS
        self.pending = {e: [] for e in self.ENGS}
        self.rank = {e: [] for e in self.ENGS}
        self.emitted = {e: 0 for e in self.ENGS}
        self.esem = {}
        self.dnext = 0

    def _need(self, eng, tok, waits):
        if tok is None:
            return
        if tok[0] == "c":
            _, e2, seq = tok
            if e2 == eng and (eng == "pe" or not SAME_SYNC):
                return
            key = (eng, "c", e2)
            if self.known.get(key, -1) >= seq:
                return
            self.known[key] = seq
            self.items[e2][seq]["flag"] = True
            waits.append(tok)
        else:
            _, idx, val = tok
            key = (eng, "d", idx)
            if self.known.get(key, -1) >= val:
                return
            self.known[key] = val
            waits.append(tok)

    def _deps(self, eng, reads, writes):
        waits = []
        for b in reads:
            self._need(eng, b.lw, waits)
        for b in writes:
            self._need(eng, b.lw, waits)
            for t in b.rd:
                self._need(eng, t, waits)
        return waits

    def op(self, eng, fn, reads=(), writes=()):
        waits = self.pending[eng] + self._deps(eng, reads, writes)
        self.pending[eng] = []
        seq = len(self.items[eng])
        self.items[eng].append({"waits": waits, "fn": fn, "flag": False, "dma": None})
        tok = ("c", eng, seq)
        for b in reads:
            b.rd.append(tok)
        for b in writes:
            b.lw = tok
            b.rd = []
        return tok

    def dma(self, fn, reads=(), writes=(), eng="sp", inc=16):
        idx = self.dnext
        self.dnext = (self.dnext + 1) % N_DMA_SEMS
        pv = self.dval[idx]
        waits = self.pending[eng] + self._deps(eng, reads, writes)
        self.pending[eng] = []
        if pv > 0:
            self._need(eng, ("d", idx, pv), waits)
        self.duse[idx] += 1
        self.dval[idx] = pv + inc
        tok = ("d", idx, pv + inc)
        self.items[eng].append({"waits": waits, "fn": fn, "flag": False, "dma": idx, "inc": inc})
        for b in reads:
            b.rd.append(tok)
        for b in writes:
            b.lw = tok
            b.rd = []
        return tok

    def _all_tokens(self):
        toks = []
        for e in ("pe", "act", "dve", "pool"):
            items = self.items[e]
            for i in range(len(items) - 1, -1, -1):
                if items[i]["dma"] is None and items[i]["fn"] is not None or (items[i]["dma"] is None and i < self.emitted[e]):
                    toks.append(("c", e, i))
                    break
        for idx in range(N_DMA_SEMS):
            if self.dval[idx]:
                toks.append(("d", idx, self.dval[idx]))
        return toks

    def barrier(self, engines=None):
        toks = self._all_tokens()
        for e in (engines or self.ENGS):
            for t in toks:
                if t[0] == "c" and t[1] == e:
                    continue
                self._need(e, t, self.pending[e])

    def _sem_of(self, e, r):
        ep = (r - 1) // SEM_EPOCH
        if (e, ep) not in self.esem:
            self.esem[(e, ep)] = self.sem[e] if ep == 0 else self.nc.alloc_semaphore("s_%s_%d" % (e, ep))
        return self.esem[(e, ep)], (r - 1) % SEM_EPOCH + 1

    def flush(self):
        for e in self.ENGS:
            items = self.items[e]
            rk = self.rank[e]
            c = rk[-1] if rk else 0
            for i in range(len(rk), len(items)):
                if items[i]["flag"]:
                    c += 1
                rk.append(c)
        for e in self.ENGS:
            h = self.h[e]
            items = self.items[e]
            for i in range(self.emitted[e], len(items)):
                it = items[i]
                for t in it["waits"]:
                    if t[0] == "c":
                        sm, v = self._sem_of(t[1], self.rank[t[1]][t[2]])
                        h.wait_ge(sm, v)
                    else:
                        h.wait_ge(self.dsem[t[1]], t[2])
                if it["fn"] is None:
                    continue
                ins = it["fn"]()
                if it["dma"] is not None:
                    ins.then_inc(self.dsem[it["dma"]], it["inc"])
                elif it["flag"]:
                    sm, _ = self._sem_of(e, self.rank[e][i])
                    ins.then_inc(sm, 1)
                it["fn"] = None
            self.emitted[e] = len(items)

    def finish(self):
        self.barrier(engines=("sp",))
        self.items["sp"].append({"waits": self.pending["sp"], "fn": None, "flag": False, "dma": None})
        self.pending["sp"] = []
        self.flush()
        return {e: (len(self.items[e]), self.rank[e][-1] if self.rank[e] else 0) for e in self.ENGS}


class K:
    def __init__(self, nc):
        self.nc = nc
        self.P = Prog(nc)
        self._n = 0
        self.stack = contextlib.ExitStack()

    def end_phase(self):
        self.P.barrier()
        self.P.flush()
        self.stack.close()
        self.stack = contextlib.ExitStack()

    def sb(self, shape, dt, name=None):
        self._n += 1
        return self.stack.enter_context(self.nc.sbuf_tensor("sb%d_" % self._n + (name or "t"), list(shape), dt)), Buf(name or "")

    def ps(self, shape, name=None):
        self._n += 1
        return self.stack.enter_context(self.nc.psum_tensor("ps%d_" % self._n + (name or "p"), list(shape), F32)), Buf(name or "")

    def din(self, name, shape, dt=F32):
        return self.nc.dram_tensor(name, list(shape), dt, kind="ExternalInput").ap()

    def dint(self, name, shape, dt=F32, **kw):
        return self.nc.dram_tensor(name, list(shape), dt, kind="Internal", **kw).ap()

    def dout(self, name, shape, dt=F32):
        return self.nc.dram_tensor(name, list(shape), dt, kind="ExternalOutput").ap()

    def mm(self, out, lhsT, rhs, st, sp, r, w):
        nc = self.nc
        return self.P.op("pe", lambda: nc.tensor.matmul(out, lhsT=lhsT, rhs=rhs, start=st, stop=sp), r, w)

    def tr(self, out, in_, ident, r, w):
        nc = self.nc
        return self.P.op("pe", lambda: nc.tensor.transpose(out, in_, ident), r, w)

    def act(self, out, in_, func, r, w, bias=None, scale=None):
        nc = self.nc
        kw = {}
        if bias is not None:
            kw["bias"] = bias
        if scale is not None:
            kw["scale"] = scale
        return self.P.op("act", lambda: nc.scalar.activation(out=out, in_=in_, func=func, **kw), r, w)

    def tt(self, out, in0, in1, op, r, w, eng="dve"):
        h = self.P.h[eng]
        return self.P.op(eng, lambda: h.tensor_tensor(out=out, in0=in0, in1=in1, op=op), r, w)

    def ts(self, out, in0, s1, op0, r, w, s2=None, op1=None, eng="dve"):
        h = self.P.h[eng]
        if op1 is None:
            return self.P.op(eng, lambda: h.tensor_scalar(out=out, in0=in0, scalar1=s1, scalar2=None, op0=op0), r, w)
        return self.P.op(eng, lambda: h.tensor_scalar(out=out, in0=in0, scalar1=s1, scalar2=s2, op0=op0, op1=op1), r, w)

    def stt(self, out, in0, scalar, in1, op0, op1, r, w):
        nc = self.nc
        return self.P.op("dve", lambda: nc.vector.scalar_tensor_tensor(out=out, in0=in0, scalar=scalar, in1=in1, op0=op0, op1=op1), r, w)

    def cp(self, out, in_, r, w, eng="dve"):
        if eng == "act":
            nc = self.nc
            return self.P.op("act", lambda: nc.scalar.copy(out=out, in_=in_), r, w)
        h = self.P.h[eng]
        return self.P.op(eng, lambda: h.tensor_copy(out=out, in_=in_), r, w)

    def memset(self, ap, val, w, eng="pool"):
        h = self.P.h[eng]
        return self.P.op(eng, lambda: h.memset(ap, val), (), w)

    def scan(self, out, d0, d1, init, r, w):
        nc = self.nc
        return self.P.op("dve", lambda: nc.vector.tensor_tensor_scan(out=out, data0=d0, data1=d1, initial=init, op0=ALU.mult, op1=ALU.add), r, w)

    def dma(self, out, in_, r, w, eng="sp"):
        h = self.P.h[eng]
        return self.P.dma(lambda: h.dma_start(out=out, in_=in_), r, w, eng=eng)


def l1_io(k, S):
    d = {}
    for name, shape in (("xT", [1024, S]), ("gn", [128, 8]), ("w1", [1024, 784]), ("wgk2", [16, 64]), ("bgk2", [64, 1]),
                        ("glag", [64, 2]), ("cosd", [64, S]), ("sind", [64, S]), ("cmd", [128, 2048]), ("ed", [64, S]),
                        ("identd", [128, 128]), ("rmd", [64, 512]), ("amd", [64, 512]), ("hm2d", [64, 128]), ("hmd", [64, 2])):
        d[name] = k.din(name, shape)
    return d


def emit_l1(k, S, d, oT):
    NT = S // 512
    NKT = S // 128
    nc = k.nc
    P = k.P
    xT, gn_d, w1_d, wgk2_d, bgk2_d, glag_d = d["xT"], d["gn"], d["w1"], d["wgk2"], d["bgk2"], d["glag"]
    cos_d, sin_d, cm_d, e_d, id_d, rm_d, am_d, hm2_d, hm_d = (d["cosd"], d["sind"], d["cmd"], d["ed"], d["identd"], d["rmd"],
                                                              d["amd"], d["hm2d"], d["hmd"])

    xt, b_xt = k.sb([128, 8, 512], F32, "xt")
    rstd, b_rstd = k.sb([128, 512], F32, "rstd")
    lnv, b_lnv = k.sb([128, 512], F32, "lnv")
    hn, b_hn = k.sb([128, 8, 512], BF16, "hn")
    sq, b_sq = hn, b_hn
    wb, b_wb = k.sb([128, 8, 784], BF16, "wb")
    gn, b_gn = k.sb([128, 8], F32, "gn")
    Kaug = [k.sb([128, S], BF16, "kaug%d" % h) for h in range(2)]
    KB = [[Buf() for _ in range(NT)] for h in range(2)]
    b_kE = [Buf(), Buf()]
    Vaug = [k.sb([128, NKT, 66], BF16, "vaug%d" % h) for h in range(2)]
    VB = [[Buf() for _ in range(NT)] for h in range(2)]
    b_vones = [Buf(), Buf()]
    Qaug = [k.sb([128, 512], BF16, "qaug%d" % h) for h in range(2)]
    kmT = [k.sb([64, 64], BF16, "kmT%d" % h) for h in range(2)]
    km32, b_km32 = k.sb([64, 2], F32, "km32")
    cosT, b_cos = k.sb([64, 512], F32, "cos")
    sinT, b_sin = k.sb([64, 512], F32, "sin")
    t1, b_t1 = k.sb([64, 512], F32, "t1")
    t2, b_t2 = k.sb([64, 512], F32, "t2")
    pTs = [k.sb([128, 512], BF16, "pT%d" % i) for i in range(3)]
    cm, b_cm = k.sb([128, 4, 512], BF16, "cm")
    id_f, b_idf = k.sb([128, 128], F32, "idf")
    id_b, b_idb = k.sb([128, 128], BF16, "idb")
    ones_b, b_onesb = k.sb([128, 128], BF16, "onesb")
    ones_f, b_onesf = k.sb([128, 64], F32, "onesf")
    bq, b_bq = k.sb([128, 128], F32, "bq")
    gsb, b_gsb = k.sb([128, 64], F32, "gsb")
    m8, b_m8 = k.sb([128, 8], F32, "m8")
    rden, b_rden = lnv, b_lnv
    osb, b_osb = t1, b_t1
    otile, b_ot = k.sb([64, 4, 512], BF16, "otile")
    QG32, b_qg = k.sb([64, 512], F32, "qg32")
    KG32, b_kg = k.sb([64, 512], F32, "kg32")
    spl, b_spl = k.sb([64, 512], F32, "spl")
    bpos, b_bpos = k.sb([64, 512], F32, "bpos")
    eb, b_eb = k.sb([64, 512], F32, "eb")
    enb, b_enb = k.sb([64, 512], F32, "enb")
    Ac, b_ac = k.sb([64, 8], F32, "Ac")
    ke32, b_ke = k.sb([64, 512], F32, "ke32")
    qt, b_qt = k.sb([64, 512], BF16, "qt")
    kpad, b_kpad = k.sb([64, 2, 512], BF16, "kpad")
    khat, b_khat = k.sb([64, 512], BF16, "khat")
    rmk, b_rmk = k.sb([64, 512], F32, "rmk")
    amk, b_amk = k.sb([64, 512], BF16, "amk")
    hm2, b_hm2 = k.sb([64, 128], F32, "hm2")
    hm, b_hm = k.sb([64, 2], F32, "hm")
    attm, b_attm = k.sb([64, 2, 512], BF16, "attm")
    gvt, b_gvt = k.sb([64, 8, 128], BF16, "gvt")
    KTt, b_ktt = k.sb([64, 8, 64], BF16, "KTt")
    gk16, b_gk16 = k.sb([16, 512], BF16, "gk16")
    wgk2f, b_wgk2f = k.sb([16, 64], F32, "wgk2f")
    wgk2b, b_wgk2b = k.sb([16, 64], BF16, "wgk2b")
    nbg, b_nbg = k.sb([64, 1], F32, "nbg")
    glag, b_glag = k.sb([64, 2], F32, "glag")
    sbog, b_sbog = k.sb([64, 2, 512], BF16, "sbog")
    st32, b_st32 = k.sb([64, 128], F32, "st32")
    stb, b_stb = k.sb([64, 128], BF16, "stb")
    stmp, b_stmp = k.sb([64, 128], F32, "stmp")
    o32, b_o32 = t1, b_t1
    osq, b_osq = k.sb([64, 512], BF16, "osq")
    on32, b_on32 = t2, b_t2
    B = [k.ps([128, 512], "bank%d" % i) for i in range(8)]

    stg = xt
    k.dma(id_f[:], id_d[:, :], (), [b_idf])
    k.cp(id_b[:], id_f[:], [b_idf], [b_idb])
    k.memset(ones_b[:], 1.0, [b_onesb])
    k.memset(ones_f[:], 1.0, [b_onesf])
    k.memset(bq[:], 0.0, [b_bq])
    k.memset(st32[:], 0.0, [b_st32])
    k.memset(stb[:], 0.0, [b_stb])
    k.dma(gn[:], gn_d[:, :], (), [b_gn])
    k.dma(glag[:], glag_d[:, :], (), [b_glag])
    k.dma(hm2[:], hm2_d[:, :], (), [b_hm2])
    k.dma(hm[:], hm_d[:, :], (), [b_hm])
    k.dma(rmk[:], rm_d[:, :], (), [b_rmk])
    k.dma(wgk2f[:], wgk2_d[:, :], (), [b_wgk2f])
    k.cp(wgk2b[:], wgk2f[:], [b_wgk2f], [b_wgk2b])
    k.dma(nbg[:], bgk2_d[:, :], (), [b_nbg])
    k.ts(nbg[:], nbg[:], -1.0, ALU.mult, [b_nbg], [b_nbg])
    k.dma(t1[:], am_d[:, :], (), [b_t1])
    k.cp(amk[:], t1[:], [b_t1], [b_amk])
    sflat = stg[:].rearrange("p a b -> p (a b)")
    k.dma(sflat[:, 0:2048], cm_d[:, :], (), [b_xt])
    k.cp(cm[:].rearrange("p a b -> p (a b)"), sflat[:, 0:2048], [b_xt], [b_cm])
    w1v = w1_d.rearrange("(kc p) f -> p kc f", p=128)
    for half in range(2):
        sv = sflat[:, 0:4 * 784].rearrange("p (a b) -> p a b", b=784)
        k.dma(sv, w1v[:, half * 4:(half + 1) * 4, :], (), [b_xt])
        k.cp(wb[:, half * 4:(half + 1) * 4, :], sv, [b_xt], [b_wb], eng="dve" if half == 0 else "pool")
    for pc in range(S // 2048):
        k.dma(sflat[64:128, 0:2048], e_d[:, pc * 2048:(pc + 1) * 2048], (), [b_xt])
        for h in range(2):
            k.cp(Kaug[h][0][64:128, pc * 2048:(pc + 1) * 2048], sflat[64:128, 0:2048], [b_xt], [b_kE[h]],
                 eng="dve" if h == 0 else "pool")
    for h in range(2):
        k.memset(Vaug[h][0][:, :, 64:65], 1.0, [b_vones[h]])
        k.memset(kmT[h][0][:], 0.0, [kmT[h][1]])

    xTv = xT.rearrange("(kc p) s -> p kc s", p=128)
    def proj(bank, M, col0, ncols=None):
        pt, pb = B[bank]
        for kc in range(8):
            k.mm(pt[0:M, 0:512], wb[:, kc, col0:col0 + M], hn[:, kc, :], kc == 0, kc == 7, [b_wb, b_hn], [pb])
        return pt, pb

    for g in range(NT):
        c0 = g * 512
        k.dma(xt[:, 0:4, :], xTv[:, 0:4, c0:c0 + 512], (), [b_xt])
        k.dma(xt[:, 4:8, :], xTv[:, 4:8, c0:c0 + 512], (), [b_xt], eng="pool")
        k.dma(cosT[:], cos_d[:, c0:c0 + 512], (), [b_cos])
        k.dma(sinT[:], sin_d[:, c0:c0 + 512], (), [b_sin])
        k.act(sq[:], xt[:], AF.Square, [b_xt], [b_sq])
        pt, pb = B[0]
        for kc in range(8):
            k.mm(pt[:, :], ones_b[:], sq[:, kc, :], kc == 0, kc == 7, [b_onesb, b_sq], [pb])
        k.act(lnv[:], pt[:, :], AF.Ln, [pb], [b_lnv], bias=EPS, scale=1.0 / 1024)
        k.act(rstd[:], lnv[:], AF.Exp, [b_lnv], [b_rstd], scale=-0.5)
        for kc in range(8):
            k.stt(hn[:, kc, :], xt[:, kc, :], gn[:, kc:kc + 1], rstd[:], ALU.mult, ALU.mult, [b_xt, b_gn, b_rstd], [b_hn])
        for idx in range(4):
            h = idx % 2
            isk = idx >= 2
            pt, pb = proj(1 + (idx % 2), 64, idx * 64)
            if isk:
                dest = Kaug[h][0][0:64, c0:c0 + 512]
                dbuf = KB[h][g]
            else:
                dest = Qaug[h][0][0:64, :]
                dbuf = Qaug[h][1]
            k.tt(t1[:], pt[0:64, 0:512], cosT[:], ALU.mult, [pb, b_cos], [b_t1])
            k.tt(t2[0:32, :], pt[32:64, 0:512], sinT[32:64, :], ALU.mult, [pb, b_sin], [b_t2])
            k.tt(t2[32:64, :], pt[0:32, 0:512], sinT[0:32, :], ALU.mult, [pb, b_sin], [b_t2])
            k.tt(dest, t1[:], t2[:], ALU.add, [b_t1, b_t2], [dbuf], eng="pool")
        pt, pb = B[1]
        for st in range(4):
            for kc in range(8):
                k.mm(pt[:, st * 128:(st + 1) * 128], hn[:, kc, st * 128:(st + 1) * 128], wb[:, kc, 256:384],
                     kc == 0, kc == 7, [b_hn, b_wb], [pb])
        pv = pt[:, 0:512].rearrange("p (a b) -> p a b", b=128)
        for h in range(2):
            k.cp(Vaug[h][0][:, 4 * g:4 * g + 4, 0:64], pv[:, :, h * 64:(h + 1) * 64], [pb], [VB[h][g]], eng="act")
        pt, pb = proj(2, 128, 384)
        k.cp(QG32[:], pt[0:64, 0:512], [pb], [b_qg], eng="act")
        k.cp(KG32[:], pt[64:128, 0:512], [pb], [b_kg], eng="act")
        for c in range(8):
            pt, pb = B[3 + c // 4]
            for kc in range(8):
                k.mm(pt[0:64, (c % 4) * 128:(c % 4 + 1) * 128], hn[:, kc, c * 64:(c + 1) * 64], wb[:, kc, 512:640],
                     kc == 0, kc == 7, [b_hn, b_wb], [pb])
        for hf in range(2):
            pt, pb = B[3 + hf]
            k.cp(gvt[:, hf * 4:(hf + 1) * 4, :].rearrange("p a b -> p (a b)"), pt[0:64, 0:512], [pb], [b_gvt], eng="act")
        pt, pb = proj(0, 16, 640)
        k.cp(gk16[:], pt[0:16, 0:512], [pb], [b_gk16], eng="act")
        k.mm(pt[0:64, 0:512], wgk2b[:], gk16[:], True, True, [b_wgk2b, b_gk16], [pb])
        k.act(spl[:], pt[0:64, 0:512], AF.Exp, [pb, b_nbg], [b_spl], bias=nbg[:, 0:1], scale=-1.0)
        k.act(spl[:], spl[:], AF.Ln, [b_spl], [b_spl], bias=1.0, scale=1.0)
        pt, pb = proj(1, 128, 656)
        k.act(sbog[:, 0, :], pt[0:64, 0:512], AF.Silu, [pb], [b_sbog])
        k.act(sbog[:, 1, :], pt[64:128, 0:512], AF.Silu, [pb], [b_sbog])

        k.scan(bpos[:], rmk[:], spl[:], 0.0, [b_rmk, b_spl], [b_bpos])
        k.act(eb[:], bpos[:], AF.Exp, [b_bpos], [b_eb], scale=-1.0 / 16)
        k.act(enb[:], bpos[:], AF.Exp, [b_bpos], [b_enb], scale=1.0 / 16)
        blast = bpos[:].rearrange("p (c t) -> p c t", t=64)[:, :, 63:64].rearrange("p c o -> p (c o)")
        k.act(Ac[:], blast, AF.Exp, [b_bpos], [b_ac], scale=-1.0 / 16)
        k.stt(qt[:], QG32[:], 32.0 ** -0.5, eb[:], ALU.mult, ALU.mult, [b_qg, b_eb], [b_qt])
        k.tt(ke32[:], KG32[:], enb[:], ALU.mult, [b_kg, b_enb], [b_ke])
        for h in range(2):
            k.ts(kpad[:, h, :], ke32[:], hm[:, h:h + 1], ALU.mult, [b_ke, b_hm], [b_kpad], eng="pool")
        for c in range(8):
            k.ts(khat[:, c * 64:(c + 1) * 64], ke32[:, c * 64:(c + 1) * 64], Ac[:, c:c + 1], ALU.mult, [b_ke, b_ac], [b_khat], eng="pool")
        pt, pb = B[0]
        for c in range(8):
            k.mm(pt[0:64, c * 64:(c + 1) * 64], khat[:, c * 64:(c + 1) * 64], id_b[0:64, 0:64], True, True, [b_khat, b_idb], [pb])
        k.cp(KTt[:].rearrange("p a b -> p (a b)"), pt[0:64, 0:512], [pb], [b_ktt], eng="act")
        for h in range(2):
            pt, pb = B[1 + h]
            for c in range(8):
                k.mm(pt[0:64, c * 64:(c + 1) * 64], kpad[:, h, c * 64:(c + 1) * 64], qt[:, c * 64:(c + 1) * 64], True, True,
                     [b_kpad, b_qt], [pb])
            k.tt(attm[:, h, :], pt[0:64, 0:512], amk[:], ALU.mult, [pb, b_amk], [b_attm])
        pso = [B[5], B[6]]
        psd, b_psd = B[7]
        for c in range(8):
            for h in range(2):
                po, pbo = pso[h]
                k.mm(po[0:64, c * 64:(c + 1) * 64], gvt[:, c, h * 64:(h + 1) * 64], attm[:, h, c * 64:(c + 1) * 64], True, False,
                     [b_gvt, b_attm], [pbo])
                k.mm(po[0:64, c * 64:(c + 1) * 64], stb[:, h * 64:(h + 1) * 64], qt[:, c * 64:(c + 1) * 64], False, True,
                     [b_stb, b_qt], [pbo])
            for h in range(2):
                k.mm(psd[0:64, h * 64:(h + 1) * 64], KTt[:, c, :], gvt[:, c, h * 64:(h + 1) * 64], True, True, [b_ktt, b_gvt], [b_psd])
            k.tt(stmp[:], psd[0:64, 0:128], hm2[:], ALU.mult, [b_psd, b_hm2], [b_stmp])
            k.stt(st32[:], st32[:], Ac[:, c:c + 1], stmp[:], ALU.mult, ALU.add, [b_st32, b_ac, b_stmp], [b_st32])
            k.cp(stb[:], st32[:], [b_st32], [b_stb], eng="act")
        for h in range(2):
            po, pbo = pso[h]
            k.cp(o32[:], po[0:64, 0:512], [pbo], [b_o32], eng="act")
            k.act(osq[:], po[0:64, 0:512], AF.Square, [pbo], [b_osq])
            pt, pb = B[0]
            k.mm(pt[0:64, 0:512], ones_b[0:64, 0:64], osq[:], True, True, [b_onesb, b_osq], [pb])
            k.act(lnv[0:64, :], pt[0:64, 0:512], AF.Ln, [pb], [b_lnv], bias=EPS, scale=1.0 / 64)
            k.act(lnv[0:64, :], lnv[0:64, :], AF.Exp, [b_lnv], [b_lnv], scale=-0.5)
            k.tt(on32[:], o32[:], lnv[0:64, :], ALU.mult, [b_o32, b_lnv], [b_on32])
            k.stt(otile[:, 2 + h, :], on32[:], glag[:, h:h + 1], sbog[:, h, :], ALU.mult, ALU.mult, [b_on32, b_glag, b_sbog], [b_ot])

        for h in range(2):
            KA, _ = Kaug[h]
            VA, _ = Vaug[h]
            QA, b_QA = Qaug[h]
            kmt, b_kmt = kmT[h]
            k.P.op("dve", (lambda o=km32[:], i=KA[0:64, c0:c0 + 512].rearrange("p (a b) -> p a b", b=256):
                           nc.vector.tensor_reduce(out=o, in_=i, axis=AX.X, op=ALU.add)), [KB[h][g]], [b_km32])
            k.cp(kmt[:, 2 * g:2 * g + 2], km32[:], [b_km32], [b_kmt])
            pg, b_pg = B[7]
            for st in range(4):
                blk = 2 * g + st // 2
                k.mm(pg[:, 0:64], QA[0:64, st * 128:(st + 1) * 128], kmt[:, :], True, True, [b_QA, b_kmt], [b_pg])
                k.memset(gsb[:], -1e30, [b_gsb])
                if blk > 0:
                    k.cp(gsb[:, 0:blk], pg[:, 0:blk], [b_pg], [b_gsb])
                k.P.op("dve", (lambda: nc.vector.max(out=m8[:], in_=gsb[:])), [b_gsb], [b_m8])
                k.ts(bq[:, 64:128], gsb[:], m8[:, 2:3], ALU.is_ge, [b_gsb, b_m8], [b_bq], s2=-NEG, op1=ALU.mult)
                k.ts(bq[:, 64:128], bq[:, 64:128], NEG, ALU.add, [b_bq], [b_bq])
                k.memset(bq[:, 64 + blk:65 + blk], 0.0, [b_bq], eng="dve")
                k.tr(pg[:, 128:256], bq[:], id_f[:], [b_bq, b_idf], [b_pg])
                k.cp(QA[64:128, st * 128:(st + 1) * 128], pg[64:128, 128:256], [b_pg], [b_QA], eng="act")
            pO, b_pO = B[5 + h]
            nkt = 4 * g + 4
            for kt in range(nkt):
                j = kt - 4 * g
                q0 = 256 if j >= 2 else 0
                pS, b_pS = B[3 + (kt % 2)]
                off = 0
                pTt, b_pT = pTs[kt % 3]
                k.mm(pS[:, off + q0:off + 512], KA[:, kt * 128:(kt + 1) * 128], QA[:, q0:512], True, j < 0,
                     [KB[h][kt // 4], b_kE[h], b_QA], [b_pS])
                if j >= 0:
                    k.mm(pS[:, off + q0:off + 512], id_b[:], cm[:, j, q0:512], False, True, [b_idb, b_cm], [b_pS])
                k.act(pTt[:, q0:512], pS[:, off + q0:off + 512], AF.Exp, [b_pS], [b_pT], scale=0.125)
                k.mm(pO[0:65, q0:512], VA[:, kt, 0:65], pTt[:, q0:512], kt == 0, kt == nkt - 1,
                     [VB[h][kt // 4], b_vones[h], b_pT], [b_pO])
            k.P.op("dve", (lambda o=rden[64:65, :], i=pO[64:65, 0:512]: nc.vector.reciprocal(out=o, in_=i)), [b_pO], [b_rden])
            pt, pb = B[0]
            k.mm(pt[0:64, 0:512], ones_f[64:65, 0:64], rden[64:65, :], True, True, [b_onesf, b_rden], [pb])
            k.cp(osb[:], pO[0:64, 0:512], [b_pO], [b_osb], eng="act")
            k.tt(otile[:, h, :], osb[:], pt[0:64, 0:512], ALU.mult, [b_osb, pb], [b_ot])
        oT(g, otile, b_ot)


def l1_consts(S):
    half = 32
    inv = (10000.0 ** (-np.arange(half, dtype=np.float32) / half)).astype(np.float32)
    ang = np.arange(S, dtype=np.float32)[None, :] * inv[:, None]
    cos = np.cos(ang).astype(np.float32)
    sin = np.sin(ang).astype(np.float32)
    cosd = np.concatenate([cos, cos], 0)
    sind = np.concatenate([sin, -sin], 0)
    kk = np.arange(128)[:, None]
    qq = np.arange(512)[None, :]
    cm = np.concatenate([np.where(qq < j * 128 + kk, NEG, 0.0) for j in range(4)], 1).astype(np.float32)
    ed = (np.arange(S)[None, :] // 256 == np.arange(64)[:, None]).astype(np.float32)
    ident = np.eye(128, dtype=np.float32)
    rm = np.tile((np.arange(512) % 64 != 0).astype(np.float32)[None, :], (64, 1))
    s_ = np.arange(64)[:, None]
    t_ = np.arange(64)[None, :]
    am = np.tile((s_ <= t_).astype(np.float32), (1, 8))
    hm = np.zeros((64, 2), np.float32)
    hm[0:32, 0] = 1
    hm[32:64, 1] = 1
    hm2 = np.repeat(hm, 64, axis=1)
    return dict(cosd=cosd, sind=sind, cmd=cm, ed=ed, identd=ident, rmd=rm, amd=am, hm2d=hm2, hmd=hm)


HALO = 8
OCS = 2048


def choose_tiles(W):
    if W == 4104:
        return 4, 3, 342
    if W == 520:
        return 2, 1, 260
    raise ValueError(W)


class Trunk:
    def __init__(self, k, W):
        self.k = k
        self.nc = k.nc
        self.W = W
        self.NB, self.NTB, self.TW = choose_tiles(W)
        self.TB = self.NTB * self.TW
        TB, TW = self.TB, self.TW
        self.h, self.b_h = k.sb([128, 8, TB], F32, "h")
        self.hn, self.b_hn = k.sb([128, 8, TB], BF16, "hn")
        self.act, self.b_act = k.sb([128, 24, TB], BF16, "act")
        self.wst = [k.sb([128, 24, 128], F32, "wst0")] * 2
        self.wbf = [k.sb([128, 24, 128], BF16, "wbf%d" % i) for i in range(2)]
        self.wi = 0
        self.pc = [k.sb([128, 4 + TB], F32, "pc%d" % i) for i in range(2)]
        self.pci = 0
        self.yt = [k.sb([128, TB], F32, "yt%d" % i) for i in range(2)]
        self.gel, self.b_gel = k.sb([128, TB], F32, "gel")
        self.sqt, self.b_sqt = k.sb([128, 8, TW], BF16, "sqt")
        self.lnv, self.b_lnv = k.sb([128, TW], F32, "lnvt")
        self.rstd, self.b_rstd = k.sb([128, TW], F32, "rstdt")
        self.ones_b, self.b_ones = k.sb([128, 128], BF16, "onesb")
        self.m, self.b_m = k.sb([128, 1], F32, "hmask")
        self.carry, self.b_carry = k.sb([128, 48, 2], F32, "carry")
        self.B = [k.ps([128, 512], "bank%d" % i) for i in range(8)]
        self.bi = 0
        k.memset(self.ones_b[:], 1.0, [self.b_ones])
        k.memset(self.carry[:], 0.0, [self.b_carry])

    def bank(self):
        b = self.B[self.bi]
        self.bi = (self.bi + 1) % 8
        return b

    def small(self, name, dram_ap, shape):
        t, b = self.k.sb(shape, F32, name)
        self.k.dma(t[:], dram_ap, (), [b])
        return t, b

    def load_w(self, wd, KC, col0):
        k = self.k
        i = self.wi
        self.wi = (self.wi + 1) % 2
        st, b_st = self.wst[i]
        wb, b_wb = self.wbf[i]
        wv = wd.rearrange("(kc p) f -> p kc f", p=128)
        k.dma(st[:, 0:KC, :], wv[:, :, col0:col0 + 128], (), [b_st])
        k.cp(wb[:, 0:KC, :], st[:, 0:KC, :], [b_st], [b_wb], eng="pool")
        return wb, b_wb

    def rmsnorm(self, g_t, b_g, out_f32=None):
        k = self.k
        TW = self.TW
        for nt in range(self.NTB):
            cs = slice(nt * TW, (nt + 1) * TW)
            k.act(self.sqt[:], self.h[:, :, cs], AF.Square, [self.b_h], [self.b_sqt])
            pt, pb = self.bank()
            for kc in range(8):
                k.mm(pt[:, 0:TW], self.ones_b[:], self.sqt[:, kc, :], kc == 0, kc == 7, [self.b_ones, self.b_sqt], [pb])
            k.act(self.lnv[:], pt[:, 0:TW], AF.Ln, [pb], [self.b_lnv], bias=EPS, scale=1.0 / 1024)
            k.act(self.rstd[:], self.lnv[:], AF.Exp, [self.b_lnv], [self.b_rstd], scale=-0.5)
            for kc in range(8):
                if out_f32 is None:
                    k.stt(self.hn[:, kc, cs], self.h[:, kc, cs], g_t[:, kc:kc + 1], self.rstd[:], ALU.mult, ALU.mult,
                          [self.b_h, b_g, self.b_rstd], [self.b_hn])
                else:
                    k.stt(out_f32[0][:, kc, cs], self.h[:, kc, cs], g_t[:, kc:kc + 1], self.rstd[:], ALU.mult, ALU.mult,
                          [self.b_h, b_g, self.b_rstd], [out_f32[1]])

    def linear(self, src, b_src, KC, wd, col0, evac):
        k = self.k
        TW = self.TW
        wb, b_wb = self.load_w(wd, KC, col0)
        for nt in range(self.NTB):
            cs = slice(nt * TW, (nt + 1) * TW)
            pt, pb = self.bank()
            for kc in range(KC):
                k.mm(pt[:, 0:TW], wb[:, kc, :], src[:, kc, cs], kc == 0, kc == KC - 1, [b_wb, b_src], [pb])
            evac(nt, cs, pt[:, 0:TW], pb)

    def conv(self, pc, b_pc, ntap, cw_t, b_cw, cb_t, b_cb, idx, yt, b_yt):
        k = self.k
        TB = self.TB
        last = ntap - 1
        k.act(yt[:], pc[:, last:last + TB], AF.Identity, [b_pc, b_cw, b_cb], [b_yt],
              bias=cb_t[:, idx:idx + 1], scale=cw_t[:, idx, last:last + 1])
        for i in range(last - 1, -1, -1):
            k.stt(yt[:], pc[:, i:i + TB], cw_t[:, idx, i:i + 1], yt[:], ALU.mult, ALU.add, [b_pc, b_cw, b_yt], [b_yt])

    def ffn(self, first, gn_t, b_gn, w_up, cw_t, b_cw, cb_t, b_cb, w_down, mask_halo):
        k = self.k
        TB, TW = self.TB, self.TW
        self.rmsnorm(gn_t, b_gn)
        for j in range(24):
            ys = []
            for part in range(2):
                idx = part * 24 + j
                pc, b_pc = self.pc[self.pci]
                self.pci = (self.pci + 1) % 2
                yt, b_yt = self.yt[part]
                k.cp(pc[:, 0:2], self.carry[:, idx, :], [self.b_carry], [b_pc], eng="pool")

                def evac(nt, cs, ps, pb, pc=pc, b_pc=b_pc):
                    k.cp(pc[:, 2 + cs.start:2 + cs.stop], ps, [pb], [b_pc], eng="act")

                self.linear(self.hn, self.b_hn, 8, w_up, idx * 128, evac)
                if first and mask_halo:
                    k.ts(pc[:, 2:2 + HALO], pc[:, 2:2 + HALO], self.m[:, 0:1], ALU.mult, [b_pc, self.b_m], [b_pc])
                k.cp(self.carry[:, idx, :], pc[:, TB:TB + 2], [b_pc], [self.b_carry], eng="pool")
                self.conv(pc, b_pc, 3, cw_t, b_cw, cb_t, b_cb, idx, yt, b_yt)
                ys.append((yt, b_yt))
            (yu, b_yu), (yg, b_yg) = ys
            k.act(self.gel[:], yg[:], AF.Gelu_apprx_tanh, [b_yg], [self.b_gel])
            k.tt(self.act[:, j, :], yu[:], self.gel[:], ALU.mult, [b_yu, self.b_gel], [self.b_act])
        for fc in range(8):
            def evac(nt, cs, ps, pb, fc=fc):
                k.tt(self.h[:, fc, cs], self.h[:, fc, cs], ps, ALU.add, [self.b_h, pb], [self.b_h])

            self.linear(self.act, self.b_act, 24, w_down, fc * 128, evac)


def l2_io(k, W):
    d = {}
    for name, shape in (("xTw", [1024, W]), ("m", [128, 1]), ("sel", [128, 4]), ("w_out0", [1024, 1024]), ("fg0", [128, 8]),
                        ("w_up0", [1024, 6144]), ("cw0", [128, 48, 3]), ("cb0", [128, 48]), ("w_down0", [3072, 1024]),
                        ("mg", [128, 8]), ("w_in", [1024, 2048]), ("rcw", [128, 8, 4]), ("rcb", [128, 8]), ("wa", [1024, 256]),
                        ("ba", [128, 8]), ("wx", [1024, 256]), ("bx", [128, 8]), ("lam", [128, 8])):
        d[name] = k.din(name, shape)
    return d


def emit_l2(k, W, S, d, og, h1_o, gg_o, hl_o, pl_o, ex_o):
    nc = k.nc
    T = Trunk(k, W)
    NB, TB, TW = T.NB, T.TB, T.TW
    TOK = W - HALO
    xTw, m_d, sel_d, w_out, fg_d, w_up, cw_d, cb_d, w_down = (d["xTw"], d["m"], d["sel"], d["w_out0"], d["fg0"], d["w_up0"],
                                                               d["cw0"], d["cb0"], d["w_down0"])
    mg_d, w_in, rcw_d, rcb_d, wa_d, ba_d, wx_d, bx_d, lam_d = (d["mg"], d["w_in"], d["rcw"], d["rcb"], d["wa"], d["ba"],
                                                               d["wx"], d["bx"], d["lam"])
    sel, b_sel = T.small("sel", sel_d[:, :], [128, 4])

    k.dma(T.m[:], m_d[:, :], (), [T.b_m])
    fg, b_fg = T.small("fg", fg_d[:, :], [128, 8])
    cw, b_cw = T.small("cw", cw_d[:, :, :], [128, 48, 3])
    cb, b_cb = T.small("cb", cb_d[:, :], [128, 48])
    mg, b_mg = T.small("mg", mg_d[:, :], [128, 8])
    rcw, b_rcw = T.small("rcw", rcw_d[:, :, :], [128, 8, 4])
    rcb, b_rcb = T.small("rcb", rcb_d[:, :], [128, 8])
    ba, b_ba = T.small("ba", ba_d[:, :], [128, 8])
    bx, b_bx = T.small("bx", bx_d[:, :], [128, 8])
    lam, b_lam = T.small("lam", lam_d[:, :], [128, 8])
    c1, b_c1 = k.sb([128, 8], F32, "c1")
    c2, b_c2 = k.sb([128, 8], F32, "c2")
    k.act(c1[:], lam[:], AF.Exp, [b_lam], [b_c1], scale=-1.0)
    k.act(c1[:], c1[:], AF.Ln, [b_c1], [b_c1], bias=1.0, scale=1.0)
    k.ts(c2[:], c1[:], -16.0, ALU.mult, [b_c1], [b_c2])
    k.ts(c1[:], c1[:], -8.0, ALU.mult, [b_c1, b_c2], [b_c1])
    ob, b_ob = T.hn, T.b_hn
    gg, b_gg = k.sb([128, TB], BF16, "gg")
    rcar, b_rcar = k.sb([128, 8, 3], F32, "rcar")
    hcar, b_hcar = k.sb([128, 8], F32, "hcar")
    pcar, b_pcar = k.sb([128, 8], F32, "pcar")
    zer, b_zer = k.sb([128, TB], F32, "zer")
    xrc, b_xrc = k.sb([128, 2, TB], F32, "xrc")
    xrb, b_xrb = k.sb([128, 2, TB], BF16, "xrb")
    rr, b_rr = k.sb([128, TB], F32, "rr")
    ii, b_ii = k.sb([128, TB], F32, "ii")
    aa, b_aa = k.sb([128, TB], F32, "aa")
    uu, b_uu = k.sb([128, TB], F32, "uu")
    hl, b_hl = k.sb([128, TB], F32, "hl")
    pl, b_pl = k.sb([128, TB], F32, "pl")
    k.memset(rcar[:], 0.0, [b_rcar])
    k.memset(zer[:], 0.0, [b_zer])
    ext, b_ext = k.sb([128, 8, 2], F32, "ext")

    oTv = [o_.rearrange("(kc p) s -> p kc s", p=128) for o_ in og]
    xTv = xTw.rearrange("(kc p) s -> p kc s", p=128)
    h1v = h1_o.rearrange("(kc p) s -> p kc s", p=128)
    ggv = gg_o.rearrange("(kc p) s -> p kc s", p=128)
    hlv = hl_o.rearrange("(kc p) s -> p kc s", p=128)
    plv = pl_o.rearrange("(kc p) s -> p kc s", p=128)
    exv = ex_o.rearrange("(kc p) s -> p kc s", p=128)

    for blk in range(NB):
        first = blk == 0
        g0 = blk * TB
        for c in range(4):
            cand = T.act[:, 8 * (c % 3):8 * (c % 3) + 8, :]
            lo = c * TOK - HALO + g0
            skip = max(0, -lo)
            if skip:
                k.memset(cand[:, :, 0:skip], 0.0, [T.b_act])
            a = lo + skip
            while a < lo + TB:
                j = a // OCS
                e = min(lo + TB, (j + 1) * OCS)
                k.dma(cand[:, :, a - lo:e - lo], oTv[j][:, :, a - j * OCS:e - j * OCS], (), [T.b_act])
                a = e
            if c == 0:
                k.ts(ob[:], cand, sel[:, 0:1], ALU.mult, [T.b_act, b_sel], [b_ob])
            else:
                k.stt(ob[:], cand, sel[:, c:c + 1], ob[:], ALU.mult, ALU.add, [T.b_act, b_sel, b_ob], [b_ob])
        k.dma(T.h[:, 0:4, :], xTv[:, 0:4, g0:g0 + TB], (), [T.b_h])
        k.dma(T.h[:, 4:8, :], xTv[:, 4:8, g0:g0 + TB], (), [T.b_h])
        for fc in range(8):
            def evac(nt, cs, ps, pb, fc=fc):
                k.tt(T.h[:, fc, cs], T.h[:, fc, cs], ps, ALU.add, [T.b_h, pb], [T.b_h])

            T.linear(ob, b_ob, 8, w_out, fc * 128, evac)
        T.ffn(first, fg, b_fg, w_up, cw, b_cw, cb, b_cb, w_down, False)
        k.dma(h1v[:, :, g0:g0 + TB], T.h[:], [T.b_h], ())
        T.rmsnorm(mg, b_mg)
        for fc in range(8):
            def evac(nt, cs, ps, pb, fc=fc):
                k.act(gg[:, cs], ps, AF.Gelu_apprx_tanh, [pb], [b_gg])

            T.linear(T.hn, T.b_hn, 8, w_in, fc * 128, evac)
            k.dma(ggv[:, fc, g0:g0 + TB], gg[:], [b_gg], ())
        for n in range(4):
            for c2i in range(2):
                c8 = 2 * n + c2i
                pc, b_pc = T.pc[T.pci]
                T.pci = (T.pci + 1) % 2
                k.cp(pc[:, 0:3], rcar[:, c8, :], [b_rcar], [b_pc], eng="pool")

                def evac(nt, cs, ps, pb, pc=pc, b_pc=b_pc):
                    k.cp(pc[:, 3 + cs.start:3 + cs.stop], ps, [pb], [b_pc], eng="act")

                T.linear(T.hn, T.b_hn, 8, w_in, 1024 + c8 * 128, evac)
                if first:
                    k.ts(pc[:, 3:3 + HALO], pc[:, 3:3 + HALO], T.m[:, 0:1], ALU.mult, [b_pc, T.b_m], [b_pc])
                k.cp(rcar[:, c8, :], pc[:, TB:TB + 3], [b_pc], [b_rcar], eng="pool")
                yt, b_yt = T.yt[c2i]
                T.conv(pc, b_pc, 4, rcw, b_rcw, rcb, b_rcb, c8, yt, b_yt)
                k.cp(xrc[:, c2i, :], yt[:], [b_yt], [b_xrc], eng="pool")
                k.cp(xrb[:, c2i, :], yt[:], [b_yt], [b_xrb], eng="act")
            for c2i in range(2):
                fc = 2 * n + c2i
                for (wd, bias_t, b_bias, dst, b_dst) in ((wa_d, ba, b_ba, rr, b_rr), (wx_d, bx, b_bx, ii, b_ii)):
                    def evac(nt, cs, ps, pb, dst=dst, b_dst=b_dst, bias_t=bias_t, b_bias=b_bias, fc=fc):
                        k.act(dst[:, cs], ps, AF.Sigmoid, [pb, b_bias], [b_dst], bias=bias_t[:, fc:fc + 1], scale=1.0)

                    T.linear(xrb, b_xrb, 2, wd[n * 256:(n + 1) * 256, :], c2i * 128, evac)
                k.act(aa[:], rr[:], AF.Exp, [b_rr, b_c1], [b_aa], scale=c1[:, fc:fc + 1])
                k.act(rr[:], rr[:], AF.Exp, [b_rr, b_c2], [b_rr], scale=c2[:, fc:fc + 1])
                k.act(rr[:], rr[:], AF.Sqrt, [b_rr], [b_rr], bias=1.0, scale=-1.0)
                k.tt(uu[:], xrc[:, c2i, :], ii[:], ALU.mult, [b_xrc, b_ii], [b_uu])
                k.tt(uu[:], uu[:], rr[:], ALU.mult, [b_uu, b_rr], [b_uu])
                if first:
                    k.ts(uu[:, 0:HALO], uu[:, 0:HALO], T.m[:, 0:1], ALU.mult, [b_uu, T.b_m], [b_uu])
                    k.memset(hl[:, 0:5], 0.0, [b_hl])
                    k.memset(pl[:, 0:5], 0.0, [b_pl])
                    s0, hi, pi = 5, 0.0, 1.0
                else:
                    s0, hi, pi = 0, hcar[:, fc:fc + 1], pcar[:, fc:fc + 1]
                k.scan(hl[:, s0:TB], aa[:, s0:TB], uu[:, s0:TB], hi, [b_aa, b_uu, b_hcar], [b_hl])
                k.scan(pl[:, s0:TB], aa[:, s0:TB], zer[:, s0:TB], pi, [b_aa, b_zer, b_pcar], [b_pl])
                k.cp(hcar[:, fc:fc + 1], hl[:, TB - 1:TB], [b_hl], [b_hcar], eng="pool")
                k.cp(pcar[:, fc:fc + 1], pl[:, TB - 1:TB], [b_pl], [b_pcar], eng="pool")
                k.dma(hlv[:, fc, g0:g0 + TB], hl[:], [b_hl], ())
                k.dma(plv[:, fc, g0:g0 + TB], pl[:], [b_pl], ())
                if blk == NB - 1:
                    k.cp(ext[:, fc, 0:1], hl[:, TB - 4:TB - 3], [b_hl], [b_ext], eng="pool")
                    k.cp(ext[:, fc, 1:2], pl[:, TB - 4:TB - 3], [b_pl], [b_ext], eng="pool")
    k.dma(exv[:, :, :], ext[:], [b_ext], ())


def l3_io(k, W):
    d = {}
    for name, shape in (("srank", [128, 4]), ("oms", [128, 4]), ("w_out1", [1024, 1024]), ("fg1", [128, 8]),
                        ("w_up1", [1024, 6144]), ("cw1", [128, 48, 3]), ("cb1", [128, 48]), ("w_down1", [3072, 1024]),
                        ("fin", [128, 8])):
        d[name] = k.din(name, shape)
    return d


def emit_l3(k, W, d, m_d, h1w, ggw, hlw, plw, exg, out_o):
    nc = k.nc
    T = Trunk(k, W)
    NB, TB, TW = T.NB, T.TB, T.TW
    TOK = W - HALO
    w_out, fg_d, w_up, cw_d, cb_d, w_down, fin_d = (d["w_out1"], d["fg1"], d["w_up1"], d["cw1"], d["cb1"], d["w_down1"], d["fin"])
    k.dma(T.m[:], m_d[:, :], (), [T.b_m])
    fg, b_fg = T.small("fg", fg_d[:, :], [128, 8])
    cw, b_cw = T.small("cw", cw_d[:, :, :], [128, 48, 3])
    cb, b_cb = T.small("cb", cb_d[:, :], [128, 48])
    fin, b_fin = T.small("fin", fin_d[:, :], [128, 8])
    sr, b_sr = T.small("sr", d["srank"][:, :], [128, 4])
    oms, b_oms = T.small("oms", d["oms"][:, :], [128, 4])
    pe, b_pe = k.sb([128, 4, 8, 2], F32, "pe")
    exv = exg.rearrange("(r kc p) t -> p r kc t", p=128, kc=8)
    for r in range(4):
        k.dma(pe[:, r, :, :], exv[:, r, :, :], (), [b_pe])
    Hc, b_Hc = k.sb([128, 8], F32, "Hc")
    Pm, b_Pm = k.sb([128, 8], F32, "Pm")
    Em, b_Em = k.sb([128, 8], F32, "Em")
    k.memset(Hc[:], 0.0, [b_Hc])
    for r in range(4):
        k.ts(Pm[:], pe[:, r, :, 1], sr[:, r:r + 1], ALU.mult, [b_pe, b_sr], [b_Pm], s2=oms[:, r:r + 1], op1=ALU.add)
        k.ts(Em[:], pe[:, r, :, 0], sr[:, r:r + 1], ALU.mult, [b_pe, b_sr], [b_Em])
        k.tt(Hc[:], Hc[:], Pm[:], ALU.mult, [b_Hc, b_Pm], [b_Hc])
        k.tt(Hc[:], Hc[:], Em[:], ALU.add, [b_Hc, b_Em], [b_Hc])
    yb, b_yb = T.hn, T.b_hn
    hl, b_hl = k.sb([128, TB], F32, "hl")
    pl, b_pl = k.sb([128, TB], F32, "pl")
    ggt, b_ggt = k.sb([128, TB], BF16, "ggt")
    outt, b_outt = k.sb([128, 8, TB], F32, "outt")

    h1v = h1w.rearrange("(kc p) s -> p kc s", p=128)
    ggv = ggw.rearrange("(kc p) s -> p kc s", p=128)
    hlv = hlw.rearrange("(kc p) s -> p kc s", p=128)
    plv = plw.rearrange("(kc p) s -> p kc s", p=128)
    outv = out_o.rearrange("(kc p) s -> p kc s", p=128)

    for blk in range(NB):
        first = blk == 0
        g0 = blk * TB
        k.dma(T.h[:, 0:4, :], h1v[:, 0:4, g0:g0 + TB], (), [T.b_h])
        k.dma(T.h[:, 4:8, :], h1v[:, 4:8, g0:g0 + TB], (), [T.b_h])
        for fc in range(8):
            k.dma(hl[:], hlv[:, fc, g0:g0 + TB], (), [b_hl])
            k.dma(pl[:], plv[:, fc, g0:g0 + TB], (), [b_pl])
            k.dma(ggt[:], ggv[:, fc, g0:g0 + TB], (), [b_ggt])
            k.stt(hl[:], pl[:], Hc[:, fc:fc + 1], hl[:], ALU.mult, ALU.add, [b_pl, b_Hc, b_hl], [b_hl])
            k.tt(yb[:, fc, :], hl[:], ggt[:], ALU.mult, [b_hl, b_ggt], [b_yb])
        for fc in range(8):
            def evac(nt, cs, ps, pb, fc=fc):
                k.tt(T.h[:, fc, cs], T.h[:, fc, cs], ps, ALU.add, [T.b_h, pb], [T.b_h])

            T.linear(yb, b_yb, 8, w_out, fc * 128, evac)
        T.ffn(first, fg, b_fg, w_up, cw, b_cw, cb, b_cb, w_down, True)
        T.rmsnorm(fin, b_fin, out_f32=(outt, b_outt))
        lo = HALO if first else 0
        k.dma(outv[:, :, g0 + lo - HALO:g0 + TB - HALO], outt[:, :, lo:TB], [b_outt], ())


_F = {}


def build_fused(S):
    TOK = S // 4
    W = TOK + HALO
    nc = bass.Bass("TRN2", target_bir_lowering=False)
    k = K(nc)
    io1 = l1_io(k, S)
    io2 = l2_io(k, W)
    io3 = l3_io(k, W)
    NCH = max(1, S // OCS)
    osrc = [k.dint("osrc%d" % j, [256, min(S, OCS)], BF16) for j in range(NCH)]
    og = [k.dint("og%d" % j, [1024, min(S, OCS)], BF16) for j in range(NCH)]
    h1 = k.dint("h1s", [1024, W])
    gg = k.dint("ggs", [1024, W], BF16)
    hl = k.dint("hls", [1024, W])
    pl = k.dint("pls", [1024, W])
    exs = k.dint("exs", [1024, 2])
    exg = k.dint("exg", [4096, 2])
    out = k.dout("outT", [1024, TOK])
    groups = [[0, 1, 2, 3], [4, 5, 6, 7]]
    upto = int(os.environ.get("FUSE_UPTO", "3"))
    tile_bufs = {}

    def o_out(g, otile, b_ot):
        j, off = (g * 512) // OCS, (g * 512) % OCS
        ov = osrc[j].rearrange("(r p) s -> p r s", p=64)
        bb = Buf()
        tile_bufs.setdefault(j, []).append(bb)
        k.dma(ov[:, :, off:off + 512], otile[:], [b_ot], [bb])
        if off + 512 == min(S, OCS):
            k.P.dma((lambda j=j: nc.gpsimd.collective_compute("AllGather", ALU.bypass, replica_groups=groups,
                                                               ins=[osrc[j][:, :]], outs=[og[j][:, :]])),
                    tile_bufs[j], (), eng="pool", inc=1)

    emit_l1(k, S, io1, o_out)
    k.end_phase()
    emit_l2(k, W, S, io2, og, h1, gg, hl, pl, exs)
    k.end_phase()
    if upto == 2:
        return nc, k.P.finish()
    if os.environ.get("NOCC2"):
        k.dma(exg[0:1024, :], exs[:, :], (), ())
    else:
        k.P.dma(lambda: nc.gpsimd.collective_compute("AllGather", ALU.bypass, replica_groups=groups, ins=[exs[:, :]], outs=[exg[:, :]]),
                (), (), eng="pool", inc=1)
    k.P.barrier()
    emit_l3(k, W, io3, io2["m"], h1, gg, hl, pl, exg, out)
    cnt = k.P.finish()
    return nc, cnt


def _pk(v):
    return np.ascontiguousarray(v.reshape(-1, 128).T)


def _pkw(w):
    t, C = w.shape
    return np.ascontiguousarray(w.T.reshape(C // 128, 128, t).transpose(1, 0, 2))


def kernel(x, mix_norm_g, ffn_norm_g, final_norm_g,
           ev_w_in, ev_w_gk2, ev_b_gk2, ev_gla_norm_g, ev_w_out,
           od_w_in, od_conv_w, od_conv_b, od_w_a, od_b_a, od_w_x, od_b_x, od_lambda, od_w_out,
           ffn_w_up, ffn_conv_w, ffn_conv_b, ffn_w_down):
    f = lambda a: np.asarray(a, dtype=np.float32)
    x = f(x)
    Bsz, S, D = x.shape
    TOK = S // 4
    W = TOK + HALO
    if S not in _F:
        _F[S] = build_fused(S)
    nc, _ = _F[S]
    consts = l1_consts(S)
    w = f(ev_w_in)[0]
    gn = _pk(f(mix_norm_g)[0])
    perm = []
    for g in range(4):
        for j in range(4):
            head = 2 * g + (j % 2)
            base = head * 64 if j < 2 else 512 + head * 64
            perm += list(range(base, base + 64))
    w_out0 = np.ascontiguousarray(f(ev_w_out)[0][np.asarray(perm)])
    shared = {
        "gn": gn, "w_out0": w_out0, "fg0": _pk(f(ffn_norm_g)[0]), "w_up0": f(ffn_w_up)[0],
        "cw0": _pkw(f(ffn_conv_w)[0]), "cb0": _pk(f(ffn_conv_b)[0]), "w_down0": f(ffn_w_down)[0],
        "mg": _pk(f(mix_norm_g)[1]), "w_in": f(od_w_in)[0], "rcw": _pkw(f(od_conv_w)[0]), "rcb": _pk(f(od_conv_b)[0]),
        "wa": np.ascontiguousarray(f(od_w_a)[0].reshape(1024, 256)), "ba": _pk(f(od_b_a)[0]),
        "wx": np.ascontiguousarray(f(od_w_x)[0].reshape(1024, 256)), "bx": _pk(f(od_b_x)[0]),
        "lam": _pk(f(od_lambda)[0]),
        "w_out1": f(od_w_out)[0], "fg1": _pk(f(ffn_norm_g)[1]), "w_up1": f(ffn_w_up)[1],
        "cw1": _pkw(f(ffn_conv_w)[1]), "cb1": _pk(f(ffn_conv_b)[1]), "w_down1": f(ffn_w_down)[1],
        "fin": _pk(f(final_norm_g)),
    }
    shared.update(consts)
    in_maps = []
    for b in range(Bsz):
        xTb = np.ascontiguousarray(x[b].T)
        for c in range(4):
            h0, h1 = 2 * c, 2 * c + 1
            cols = []
            for base in (0, 512, 1024):
                cols += [np.arange(base + h0 * 64, base + h0 * 64 + 64), np.arange(base + h1 * 64, base + h1 * 64 + 64)]
            for base in (1536, 1792):
                cols += [np.arange(base + h0 * 32, base + h0 * 32 + 32), np.arange(base + h1 * 32, base + h1 * 32 + 32)]
            cols += [np.arange(2048 + h0 * 64, 2048 + h0 * 64 + 64), np.arange(2048 + h1 * 64, 2048 + h1 * 64 + 64)]
            cols += [np.arange(2560, 2576)]
            cols += [np.arange(2576 + h0 * 64, 2576 + h0 * 64 + 64), np.arange(2576 + h1 * 64, 2576 + h1 * 64 + 64)]
            cols = np.concatenate(cols)
            t0 = c * TOK
            xw = np.zeros((1024, W), np.float32)
            if c == 0:
                xw[:, HALO:] = xTb[:, 0:TOK]
            else:
                xw[:] = xTb[:, t0 - HALO:t0 + TOK]
            sel = np.zeros((128, 4), np.float32)
            sel[:, c] = 1.0
            sr = np.zeros((128, 4), np.float32)
            sr[:, :c] = 1.0
            m = dict(shared)
            m.update({
                "xT": xTb, "w1": np.ascontiguousarray(w[:, cols]),
                "wgk2": np.ascontiguousarray(f(ev_w_gk2)[0][:, h0 * 32:h0 * 32 + 64]),
                "bgk2": np.ascontiguousarray(f(ev_b_gk2)[0][h0 * 32:h0 * 32 + 64].reshape(64, 1)),
                "glag": np.ascontiguousarray(f(ev_gla_norm_g)[0][h0:h0 + 2].T),
                "xTw": xw, "m": np.full((128, 1), 0.0 if c == 0 else 1.0, np.float32),
                "sel": sel, "srank": sr, "oms": 1.0 - sr,
            })
            in_maps.append(m)
    res = run_bass_kernel_spmd(nc, in_maps, core_ids=list(range(len(in_maps)))).results
    out = np.zeros((Bsz, S, D), np.float32)
    for b in range(Bsz):
        for c in range(4):
            out[b, c * TOK:(c + 1) * TOK, :] = np.asarray(res[b * 4 + c]["outT"]).T
    return out
```

```python
import contextlib
import os
import numpy as np
import ml_dtypes
import concourse.bass as bass
import concourse.mybir as mybir
from concourse.bass_utils import run_bass_kernel_spmd

F32 = mybir.dt.float32
BF16 = mybir.dt.bfloat16
AF = mybir.ActivationFunctionType
ALU = mybir.AluOpType
AX = mybir.AxisListType

SAME_SYNC = True
N_DMA_SEMS = 24
SEM_EPOCH = 2000
NEG = -240000.0
EPS = 1e-6


class Buf:
    __slots__ = ("name", "lw", "rd")

    def __init__(self, name=""):
        self.name = name
        self.lw = None
        self.rd = []


class Prog:
    ENGS = ("pe", "act", "dve", "pool", "sp")

    def __init__(self, nc):
        self.nc = nc
        self.h = {"pe": nc.tensor, "act": nc.scalar, "dve": nc.vector, "pool": nc.gpsimd, "sp": nc.sync}
        self.items = {e: [] for e in self.ENGS}
        self.known = {}
        self.sem = {e: nc.alloc_semaphore("s_" + e) for e in self.ENGS}
        self.dsem = [nc.alloc_semaphore("d%d" % i) for i in range(N_DMA_SEMS)]
        self.duse = [0] * N_DMA_SEMS
        self.dval = [0] * N_DMA_SEMS
        self.pending = {e: [] for e in self.ENGS}
        self.rank = {e: [] for e in self.ENGS}
        self.emitted = {e: 0 for e in self.ENGS}
        self.esem = {}
        self.dnext = 0

    def _need(self, eng, tok, waits):
        if tok is None:
            return
        if tok[0] == "c":
            _, e2, seq = tok
            if e2 == eng and (eng == "pe" or not SAME_SYNC):
                return
            key = (eng, "c", e2)
            if self.known.get(key, -1) >= seq:
                return
            self.known[key] = seq
            self.items[e2][seq]["flag"] = True
            waits.append(tok)
        else:
            _, idx, val = tok
            key = (eng, "d", idx)
            if self.known.get(key, -1) >= val:
                return
            self.known[key] = val
            waits.append(tok)

    def _deps(self, eng, reads, writes):
        waits = []
        for b in reads:
            self._need(eng, b.lw, waits)
        for b in writes:
            self._need(eng, b.lw, waits)
            for t in b.rd:
                self._need(eng, t, waits)
        return waits

    def op(self, eng, fn, reads=(), writes=()):
        waits = self.pending[eng] + self._deps(eng, reads, writes)
        self.pending[eng] = []
        seq = len(self.items[eng])
        self.items[eng].append({"waits": waits, "fn": fn, "flag": False, "dma": None})
        tok = ("c", eng, seq)
        for b in reads:
            b.rd.append(tok)
        for b in writes:
            b.lw = tok
            b.rd = []
        return tok

    def dma(self, fn, reads=(), writes=(), eng="sp", inc=16):
        idx = self.dnext
        self.dnext = (self.dnext + 1) % N_DMA_SEMS
        pv = self.dval[idx]
        waits = self.pending[eng] + self._deps(eng, reads, writes)
        self.pending[eng] = []
        if pv > 0:
            self._need(eng, ("d", idx, pv), waits)
        self.duse[idx] += 1
        self.dval[idx] = pv + inc
        tok = ("d", idx, pv + inc)
        self.items[eng].append({"waits": waits, "fn": fn, "flag": False, "dma": idx, "inc": inc})
        for b in reads:
            b.rd.append(tok)
        for b in writes:
            b.lw = tok
            b.rd = []
        return tok

    def _all_tokens(self):
        toks = []
        for e in ("pe", "act", "dve", "pool"):
            items = self.items[e]
            for i in range(len(items) - 1, -1, -1):
                if items[i]["dma"] is None and items[i]["fn"] is not None or (items[i]["dma"] is None and i < self.emitted[e]):
                    toks.append(("c", e, i))
                    break
        for idx in range(N_DMA_SEMS):
            if self.dval[idx]:
                toks.append(("d", idx, self.dval[idx]))
        return toks

    def barrier(self, engines=None):
        toks = self._all_tokens()
        for e in (engines or self.ENGS):
            for t in toks:
                if t[0] == "c" and t[1] == e:
                    continue
                self._need(e, t, self.pending[e])

    def _sem_of(self, e, r):
        ep = (r - 1) // SEM_EPOCH
        if (e, ep) not in self.esem:
            self.esem[(e, ep)] = self.sem[e] if ep == 0 else self.nc.alloc_semaphore("s_%s_%d" % (e, ep))
        return self.esem[(e, ep)], (r - 1) % SEM_EPOCH + 1

    def flush(self):
        for e in self.ENGS:
            items = self.items[e]
            rk = self.rank[e]
            c = rk[-1] if rk else 0
            for i in range(len(rk), len(items)):
                if items[i]["flag"]:
                    c += 1
                rk.append(c)
        for e in self.ENGS:
            h = self.h[e]
            items = self.items[e]
            for i in range(self.emitted[e], len(items)):
                it = items[i]
                for t in it["waits"]:
                    if t[0] == "c":
                        sm, v = self._sem_of(t[1], self.rank[t[1]][t[2]])
                        h.wait_ge(sm, v)
                    else:
                        h.wait_ge(self.dsem[t[1]], t[2])
                if it["fn"] is None:
                    continue
                ins = it["fn"]()
                if it["dma"] is not None:
                    ins.then_inc(self.dsem[it["dma"]], it["inc"])
                elif it["flag"]:
                    sm, _ = self._sem_of(e, self.rank[e][i])
                    ins.then_inc(sm, 1)
                it["fn"] = None
            self.emitted[e] = len(items)

    def finish(self):
        self.barrier(engines=("sp",))
        self.items["sp"].append({"waits": self.pending["sp"], "fn": None, "flag": False, "dma": None})
        self.pending["sp"] = []
        self.flush()
        return {e: (len(self.items[e]), self.rank[e][-1] if self.rank[e] else 0) for e in self.ENGS}


class K:
    def __init__(self, nc):
        self.nc = nc
        self.P = Prog(nc)
        self._n = 0
        self.stack = contextlib.ExitStack()

    def end_phase(self):
        self.P.barrier()
        self.P.flush()
        self.stack.close()
        self.stack = contextlib.ExitStack()

    def sb(self, shape, dt, name=None):
        self._n += 1
        return self.stack.enter_context(self.nc.sbuf_tensor("sb%d_" % self._n + (name or "t"), list(shape), dt)), Buf(name or "")

    def ps(self, shape, name=None):
        self._n += 1
        return self.stack.enter_context(self.nc.psum_tensor("ps%d_" % self._n + (name or "p"), list(shape), F32)), Buf(name or "")

    def din(self, name, shape, dt=F32):
        return self.nc.dram_tensor(name, list(shape), dt, kind="ExternalInput").ap()

    def dint(self, name, shape, dt=F32, **kw):
        return self.nc.dram_tensor(name, list(shape), dt, kind="Internal", **kw).ap()

    def dout(self, name, shape, dt=F32):
        return self.nc.dram_tensor(name, list(shape), dt, kind="ExternalOutput").ap()

    def mm(self, out, lhsT, rhs, st, sp, r, w):
        nc = self.nc
        return self.P.op("pe", lambda: nc.tensor.matmul(out, lhsT=lhsT, rhs=rhs, start=st, stop=sp), r, w)

    def tr(self, out, in_, ident, r, w):
        nc = self.nc
        return self.P.op("pe", lambda: nc.tensor.transpose(out, in_, ident), r, w)

    def act(self, out, in_, func, r, w, bias=None, scale=None):
        nc = self.nc
        kw = {}
        if bias is not None:
            kw["bias"] = bias
        if scale is not None:
            kw["scale"] = scale
        return self.P.op("act", lambda: nc.scalar.activation(out=out, in_=in_, func=func, **kw), r, w)

    def tt(self, out, in0, in1, op, r, w, eng="dve"):
        h = self.P.h[eng]
        return self.P.op(eng, lambda: h.tensor_tensor(out=out, in0=in0, in1=in1, op=op), r, w)

    def ts(self, out, in0, s1, op0, r, w, s2=None, op1=None, eng="dve"):
        h = self.P.h[eng]
        if op1 is None:
            return self.P.op(eng, lambda: h.tensor_scalar(out=out, in0=in0, scalar1=s1, scalar2=None, op0=op0), r, w)
        return self.P.op(eng, lambda: h.tensor_scalar(out=out, in0=in0, scalar1=s1, scalar2=s2, op0=op0, op1=op1), r, w)

    def stt(self, out, in0, scalar, in1, op0, op1, r, w):
        nc = self.nc
        return self.P.op("dve", lambda: nc.vector.scalar_tensor_tensor(out=out, in0=in0, scalar=scalar, in1=in1, op0=op0, op1=op1), r, w)

    def cp(self, out, in_, r, w, eng="dve"):
        if eng == "act":
            nc = self.nc
            return self.P.op("act", lambda: nc.scalar.copy(out=out, in_=in_), r, w)
        h = self.P.h[eng]
        return self.P.op(eng, lambda: h.tensor_copy(out=out, in_=in_), r, w)

    def memset(self, ap, val, w, eng="pool"):
        h = self.P.h[eng]
        return self.P.op(eng, lambda: h.memset(ap, val), (), w)

    def scan(self, out, d0, d1, init, r, w):
        nc = self.nc
        return self.P.op("dve", lambda: nc.vector.tensor_tensor_scan(out=out, data0=d0, data1=d1, initial=init, op0=ALU.mult, op1=ALU.add), r, w)

    def dma(self, out, in_, r, w, eng="sp"):
        h = self.P.h[eng]
        return self.P.dma(lambda: h.dma_start(out=out, in_=in_), r, w, eng=eng)


def l1_io(k, S):
    d = {}
    for name, shape in (("xT", [1024, S]), ("gn", [128, 8]), ("w1", [1024, 784]), ("wgk2", [16, 64]), ("bgk2", [64, 1]),
                        ("glag", [64, 2]), ("cosd", [64, S]), ("sind", [64, S]), ("cmd", [128, 2048]), ("ed", [64, S]),
                        ("identd", [128, 128]), ("rmd", [64, 512]), ("amd", [64, 512]), ("hm2d", [64, 128]), ("hmd", [64, 2])):
        d[name] = k.din(name, shape)
    return d


def emit_l1(k, S, d, oT):
    NT = S // 512
    NKT = S // 128
    nc = k.nc
    P = k.P
    xT, gn_d, w1_d, wgk2_d, bgk2_d, glag_d = d["xT"], d["gn"], d["w1"], d["wgk2"], d["bgk2"], d["glag"]
    cos_d, sin_d, cm_d, e_d, id_d, rm_d, am_d, hm2_d, hm_d = (d["cosd"], d["sind"], d["cmd"], d["ed"], d["identd"], d["rmd"],
                                                              d["amd"], d["hm2d"], d["hmd"])

    xt, b_xt = k.sb([128, 8, 512], F32, "xt")
    rstd, b_rstd = k.sb([128, 512], F32, "rstd")
    lnv, b_lnv = k.sb([128, 512], F32, "lnv")
    hn, b_hn = k.sb([128, 8, 512], BF16, "hn")
    sq, b_sq = hn, b_hn
    wb, b_wb = k.sb([128, 8, 784], BF16, "wb")
    gn, b_gn = k.sb([128, 8], F32, "gn")
    Kaug = [k.sb([128, S], BF16, "kaug%d" % h) for h in range(2)]
    KB = [[Buf() for _ in range(NT)] for h in range(2)]
    b_kE = [Buf(), Buf()]
    Vaug = [k.sb([128, NKT, 66], BF16, "vaug%d" % h) for h in range(2)]
    VB = [[Buf() for _ in range(NT)] for h in range(2)]
    b_vones = [Buf(), Buf()]
    Qaug = [k.sb([128, 512], BF16, "qaug%d" % h) for h in range(2)]
    kmT = [k.sb([64, 64], BF16, "kmT%d" % h) for h in range(2)]
    km32, b_km32 = k.sb([64, 2], F32, "km32")
    cosT, b_cos = k.sb([64, 512], F32, "cos")
    sinT, b_sin = k.sb([64, 512], F32, "sin")
    t1, b_t1 = k.sb([64, 512], F32, "t1")
    t2, b_t2 = k.sb([64, 512], F32, "t2")
    pTs = [k.sb([128, 512], BF16, "pT%d" % i) for i in range(4)]
    cm, b_cm = k.sb([128, 4, 512], BF16, "cm")
    id_f, b_idf = k.sb([128, 128], F32, "idf")
    id_b, b_idb = k.sb([128, 128], BF16, "idb")
    ones_b, b_onesb = k.sb([128, 128], BF16, "onesb")
    ones_f, b_onesf = k.sb([128, 64], F32, "onesf")
    bq, b_bq = k.sb([128, 128], F32, "bq")
    gsb, b_gsb = k.sb([128, 64], F32, "gsb")
    m8, b_m8 = k.sb([128, 8], F32, "m8")
    rden, b_rden = lnv, b_lnv
    osb, b_osb = t1, b_t1
    otile, b_ot = k.sb([64, 4, 512], BF16, "otile")
    QG32, b_qg = k.sb([64, 512], F32, "qg32")
    KG32, b_kg = k.sb([64, 512], F32, "kg32")
    spl, b_spl = k.sb([64, 512], F32, "spl")
    bpos, b_bpos = k.sb([64, 512], F32, "bpos")
    eb, b_eb = k.sb([64, 512], F32, "eb")
    enb, b_enb = k.sb([64, 512], F32, "enb")
    Ac, b_ac = k.sb([64, 8], F32, "Ac")
    ke32, b_ke = k.sb([64, 512], F32, "ke32")
    qt, b_qt = k.sb([64, 512], BF16, "qt")
    kpad, b_kpad = k.sb([64, 2, 512], BF16, "kpad")
    khat, b_khat = k.sb([64, 512], BF16, "khat")
    rmk, b_rmk = k.sb([64, 512], F32, "rmk")
    amk, b_amk = k.sb([64, 512], BF16, "amk")
    hm2, b_hm2 = k.sb([64, 128], F32, "hm2")
    hm, b_hm = k.sb([64, 2], F32, "hm")
    attm, b_attm = k.sb([64, 2, 512], BF16, "attm")
    gvt, b_gvt = k.sb([64, 8, 128], BF16, "gvt")
    KTt, b_ktt = k.sb([64, 8, 64], BF16, "KTt")
    gk16, b_gk16 = k.sb([16, 512], BF16, "gk16")
    wgk2f, b_wgk2f = k.sb([16, 64], F32, "wgk2f")
    wgk2b, b_wgk2b = k.sb([16, 64], BF16, "wgk2b")
    nbg, b_nbg = k.sb([64, 1], F32, "nbg")
    glag, b_glag = k.sb([64, 2], F32, "glag")
    sbog, b_sbog = k.sb([64, 2, 512], BF16, "sbog")
    st32, b_st32 = k.sb([64, 128], F32, "st32")
    stb, b_stb = k.sb([64, 128], BF16, "stb")
    stmp, b_stmp = k.sb([64, 128], F32, "stmp")
    o32, b_o32 = t1, b_t1
    osq, b_osq = k.sb([64, 512], BF16, "osq")
    on32, b_on32 = t2, b_t2
    B = [k.ps([128, 512], "bank%d" % i) for i in range(8)]

    stg = xt
    k.dma(id_f[:], id_d[:, :], (), [b_idf])
    k.cp(id_b[:], id_f[:], [b_idf], [b_idb])
    k.memset(ones_b[:], 1.0, [b_onesb])
    k.memset(ones_f[:], 1.0, [b_onesf])
    k.memset(bq[:], 0.0, [b_bq])
    k.memset(st32[:], 0.0, [b_st32])
    k.memset(stb[:], 0.0, [b_stb])
    k.dma(gn[:], gn_d[:, :], (), [b_gn])
    k.dma(glag[:], glag_d[:, :], (), [b_glag])
    k.dma(hm2[:], hm2_d[:, :], (), [b_hm2])
    k.dma(hm[:], hm_d[:, :], (), [b_hm])
    k.dma(rmk[:], rm_d[:, :], (), [b_rmk])
    k.dma(wgk2f[:], wgk2_d[:, :], (), [b_wgk2f])
    k.cp(wgk2b[:], wgk2f[:], [b_wgk2f], [b_wgk2b])
    k.dma(nbg[:], bgk2_d[:, :], (), [b_nbg])
    k.ts(nbg[:], nbg[:], -1.0, ALU.mult, [b_nbg], [b_nbg])
    k.dma(t1[:], am_d[:, :], (), [b_t1])
    k.cp(amk[:], t1[:], [b_t1], [b_amk])
    sflat = stg[:].rearrange("p a b -> p (a b)")
    k.dma(sflat[:, 0:2048], cm_d[:, :], (), [b_xt])
    k.cp(cm[:].rearrange("p a b -> p (a b)"), sflat[:, 0:2048], [b_xt], [b_cm])
    w1v = w1_d.rearrange("(kc p) f -> p kc f", p=128)
    for half in range(2):
        sv = sflat[:, 0:4 * 784].rearrange("p (a b) -> p a b", b=784)
        k.dma(sv, w1v[:, half * 4:(half + 1) * 4, :], (), [b_xt])
        k.cp(wb[:, half * 4:(half + 1) * 4, :], sv, [b_xt], [b_wb], eng="dve" if half == 0 else "pool")
    for pc in range(S // 2048):
        k.dma(sflat[64:128, 0:2048], e_d[:, pc * 2048:(pc + 1) * 2048], (), [b_xt])
        for h in range(2):
            k.cp(Kaug[h][0][64:128, pc * 2048:(pc + 1) * 2048], sflat[64:128, 0:2048], [b_xt], [b_kE[h]],
                 eng="dve" if h == 0 else "pool")
    for h in range(2):
        k.memset(Vaug[h][0][:, :, 64:65], 1.0, [b_vones[h]])
        k.memset(kmT[h][0][:], 0.0, [kmT[h][1]])

    xTv = xT.rearrange("(kc p) s -> p kc s", p=128)
    def proj(bank, M, col0, ncols=None):
        pt, pb = B[bank]
        for kc in range(8):
            k.mm(pt[0:M, 0:512], wb[:, kc, col0:col0 + M], hn[:, kc, :], kc == 0, kc == 7, [b_wb, b_hn], [pb])
        return pt, pb

    for g in range(NT):
        c0 = g * 512
        k.dma(xt[:, 0:4, :], xTv[:, 0:4, c0:c0 + 512], (), [b_xt])
        k.dma(xt[:, 4:8, :], xTv[:, 4:8, c0:c0 + 512], (), [b_xt], eng="pool")
        k.dma(cosT[:], cos_d[:, c0:c0 + 512], (), [b_cos])
        k.dma(sinT[:], sin_d[:, c0:c0 + 512], (), [b_sin])
        k.act(sq[:], xt[:], AF.Square, [b_xt], [b_sq])
        pt, pb = B[0]
        for kc in range(8):
            k.mm(pt[:, :], ones_b[:], sq[:, kc, :], kc == 0, kc == 7, [b_onesb, b_sq], [pb])
        k.act(lnv[:], pt[:, :], AF.Ln, [pb], [b_lnv], bias=EPS, scale=1.0 / 1024)
        k.act(rstd[:], lnv[:], AF.Exp, [b_lnv], [b_rstd], scale=-0.5)
        for kc in range(8):
            k.stt(hn[:, kc, :], xt[:, kc, :], gn[:, kc:kc + 1], rstd[:], ALU.mult, ALU.mult, [b_xt, b_gn, b_rstd], [b_hn])
        for idx in range(4):
            h = idx % 2
            isk = idx >= 2
            pt, pb = proj(1 + (idx % 2), 64, idx * 64)
            if isk:
                dest = Kaug[h][0][0:64, c0:c0 + 512]
                dbuf = KB[h][g]
            else:
                dest = Qaug[h][0][0:64, :]
                dbuf = Qaug[h][1]
            k.tt(t1[:], pt[0:64, 0:512], cosT[:], ALU.mult, [pb, b_cos], [b_t1])
            k.tt(t2[0:32, :], pt[32:64, 0:512], sinT[32:64, :], ALU.mult, [pb, b_sin], [b_t2])
            k.tt(t2[32:64, :], pt[0:32, 0:512], sinT[0:32, :], ALU.mult, [pb, b_sin], [b_t2])
            k.tt(dest, t1[:], t2[:], ALU.add, [b_t1, b_t2], [dbuf], eng="pool")
        pt, pb = B[1]
        for st in range(4):
            for kc in range(8):
                k.mm(pt[:, st * 128:(st + 1) * 128], hn[:, kc, st * 128:(st + 1) * 128], wb[:, kc, 256:384],
                     kc == 0, kc == 7, [b_hn, b_wb], [pb])
        pv = pt[:, 0:512].rearrange("p (a b) -> p a b", b=128)
        for h in range(2):
            k.cp(Vaug[h][0][:, 4 * g:4 * g + 4, 0:64], pv[:, :, h * 64:(h + 1) * 64], [pb], [VB[h][g]], eng="act")
        pt, pb = proj(2, 128, 384)
        k.cp(QG32[:], pt[0:64, 0:512], [pb], [b_qg], eng="act")
        k.cp(KG32[:], pt[64:128, 0:512], [pb], [b_kg], eng="act")
        for c in range(8):
            pt, pb = B[3 + c // 4]
            for kc in range(8):
                k.mm(pt[0:64, (c % 4) * 128:(c % 4 + 1) * 128], hn[:, kc, c * 64:(c + 1) * 64], wb[:, kc, 512:640],
                     kc == 0, kc == 7, [b_hn, b_wb], [pb])
        for hf in range(2):
            pt, pb = B[3 + hf]
            k.cp(gvt[:, hf * 4:(hf + 1) * 4, :].rearrange("p a b -> p (a b)"), pt[0:64, 0:512], [pb], [b_gvt], eng="act")
        pt, pb = proj(0, 16, 640)
        k.cp(gk16[:], pt[0:16, 0:512], [pb], [b_gk16], eng="act")
        k.mm(pt[0:64, 0:512], wgk2b[:], gk16[:], True, True, [b_wgk2b, b_gk16], [pb])
        k.act(spl[:], pt[0:64, 0:512], AF.Exp, [pb, b_nbg], [b_spl], bias=nbg[:, 0:1], scale=-1.0)
        k.act(spl[:], spl[:], AF.Ln, [b_spl], [b_spl], bias=1.0, scale=1.0)
        pt, pb = proj(1, 128, 656)
        k.act(sbog[:, 0, :], pt[0:64, 0:512], AF.Silu, [pb], [b_sbog])
        k.act(sbog[:, 1, :], pt[64:128, 0:512], AF.Silu, [pb], [b_sbog])

        k.scan(bpos[:], rmk[:], spl[:], 0.0, [b_rmk, b_spl], [b_bpos])
        k.act(eb[:], bpos[:], AF.Exp, [b_bpos], [b_eb], scale=-1.0 / 16)
        k.act(enb[:], bpos[:], AF.Exp, [b_bpos], [b_enb], scale=1.0 / 16)
        blast = bpos[:].rearrange("p (c t) -> p c t", t=64)[:, :, 63:64].rearrange("p c o -> p (c o)")
        k.act(Ac[:], blast, AF.Exp, [b_bpos], [b_ac], scale=-1.0 / 16)
        k.stt(qt[:], QG32[:], 32.0 ** -0.5, eb[:], ALU.mult, ALU.mult, [b_qg, b_eb], [b_qt])
        k.tt(ke32[:], KG32[:], enb[:], ALU.mult, [b_kg, b_enb], [b_ke])
        for h in range(2):
            k.ts(kpad[:, h, :], ke32[:], hm[:, h:h + 1], ALU.mult, [b_ke, b_hm], [b_kpad], eng="pool")
        for c in range(8):
            k.ts(khat[:, c * 64:(c + 1) * 64], ke32[:, c * 64:(c + 1) * 64], Ac[:, c:c + 1], ALU.mult, [b_ke, b_ac], [b_khat], eng="pool")
        pt, pb = B[0]
        for c in range(8):
            k.mm(pt[0:64, c * 64:(c + 1) * 64], khat[:, c * 64:(c + 1) * 64], id_b[0:64, 0:64], True, True, [b_khat, b_idb], [pb])
        k.cp(KTt[:].rearrange("p a b -> p (a b)"), pt[0:64, 0:512], [pb], [b_ktt], eng="act")
        for h in range(2):
            pt, pb = B[1 + h]
            for c in range(8):
                k.mm(pt[0:64, c * 64:(c + 1) * 64], kpad[:, h, c * 64:(c + 1) * 64], qt[:, c * 64:(c + 1) * 64], True, True,
                     [b_kpad, b_qt], [pb])
            k.tt(attm[:, h, :], pt[0:64, 0:512], amk[:], ALU.mult, [pb, b_amk], [b_attm])
        pso = [B[5], B[6]]
        psd, b_psd = B[7]
        for c in range(8):
            for h in range(2):
                po, pbo = pso[h]
                k.mm(po[0:64, c * 64:(c + 1) * 64], gvt[:, c, h * 64:(h + 1) * 64], attm[:, h, c * 64:(c + 1) * 64], True, False,
                     [b_gvt, b_attm], [pbo])
                k.mm(po[0:64, c * 64:(c + 1) * 64], stb[:, h * 64:(h + 1) * 64], qt[:, c * 64:(c + 1) * 64], False, True,
                     [b_stb, b_qt], [pbo])
            for h in range(2):
                k.mm(psd[0:64, h * 64:(h + 1) * 64], KTt[:, c, :], gvt[:, c, h * 64:(h + 1) * 64], True, True, [b_ktt, b_gvt], [b_psd])
            k.tt(stmp[:], psd[0:64, 0:128], hm2[:], ALU.mult, [b_psd, b_hm2], [b_stmp])
            k.stt(st32[:], st32[:], Ac[:, c:c + 1], stmp[:], ALU.mult, ALU.add, [b_st32, b_ac, b_stmp], [b_st32])
            k.cp(stb[:], st32[:], [b_st32], [b_stb], eng="act")
        for h in range(2):
            po, pbo = pso[h]
            k.cp(o32[:], po[0:64, 0:512], [pbo], [b_o32], eng="act")
            k.act(osq[:], po[0:64, 0:512], AF.Square, [pbo], [b_osq])
            pt, pb = B[0]
            k.mm(pt[0:64, 0:512], ones_b[0:64, 0:64], osq[:], True, True, [b_onesb, b_osq], [pb])
            k.act(lnv[0:64, :], pt[0:64, 0:512], AF.Ln, [pb], [b_lnv], bias=EPS, scale=1.0 / 64)
            k.act(lnv[0:64, :], lnv[0:64, :], AF.Exp, [b_lnv], [b_lnv], scale=-0.5)
            k.tt(on32[:], o32[:], lnv[0:64, :], ALU.mult, [b_o32, b_lnv], [b_on32])
            k.stt(otile[:, 2 + h, :], on32[:], glag[:, h:h + 1], sbog[:, h, :], ALU.mult, ALU.mult, [b_on32, b_glag, b_sbog], [b_ot])

        for h in range(2):
            KA, _ = Kaug[h]
            VA, _ = Vaug[h]
            QA, b_QA = Qaug[h]
            kmt, b_kmt = kmT[h]
            k.P.op("dve", (lambda o=km32[:], i=KA[0:64, c0:c0 + 512].rearrange("p (a b) -> p a b", b=256):
                           nc.vector.tensor_reduce(out=o, in_=i, axis=AX.X, op=ALU.add)), [KB[h][g]], [b_km32])
            k.cp(kmt[:, 2 * g:2 * g + 2], km32[:], [b_km32], [b_kmt])
            pg, b_pg = B[7]
            for st in range(4):
                blk = 2 * g + st // 2
                k.mm(pg[:, 0:64], QA[0:64, st * 128:(st + 1) * 128], kmt[:, :], True, True, [b_QA, b_kmt], [b_pg])
                k.memset(gsb[:], -1e30, [b_gsb])
                if blk > 0:
                    k.cp(gsb[:, 0:blk], pg[:, 0:blk], [b_pg], [b_gsb])
                k.P.op("dve", (lambda: nc.vector.max(out=m8[:], in_=gsb[:])), [b_gsb], [b_m8])
                k.ts(bq[:, 64:128], gsb[:], m8[:, 2:3], ALU.is_ge, [b_gsb, b_m8], [b_bq], s2=-NEG, op1=ALU.mult)
                k.ts(bq[:, 64:128], bq[:, 64:128], NEG, ALU.add, [b_bq], [b_bq])
                k.memset(bq[:, 64 + blk:65 + blk], 0.0, [b_bq], eng="dve")
                k.tr(pg[:, 128:256], bq[:], id_f[:], [b_bq, b_idf], [b_pg])
                k.cp(QA[64:128, st * 128:(st + 1) * 128], pg[64:128, 128:256], [b_pg], [b_QA], eng="act")
            pO, b_pO = B[5 + h]
            nkt = 4 * g + 4
            LA = 3

            def qk(kt):
                j = kt - 4 * g
                q0 = 256 if j >= 2 else 0
                pS, b_pS = B[1 + (kt % 4)]
                k.mm(pS[:, q0:512], KA[:, kt * 128:(kt + 1) * 128], QA[:, q0:512], True, j < 0,
                     [KB[h][kt // 4], b_kE[h], b_QA], [b_pS])
                if j >= 0:
                    k.mm(pS[:, q0:512], id_b[:], cm[:, j, q0:512], False, True, [b_idb, b_cm], [b_pS])

            def pv(kt):
                j = kt - 4 * g
                q0 = 256 if j >= 2 else 0
                pS, b_pS = B[1 + (kt % 4)]
                pTt, b_pT = pTs[kt % 4]
                k.act(pTt[:, q0:512], pS[:, q0:512], AF.Exp, [b_pS], [b_pT], scale=0.125)
                k.mm(pO[0:65, q0:512], VA[:, kt, 0:65], pTt[:, q0:512], kt == 0, kt == nkt - 1,
                     [VB[h][kt // 4], b_vones[h], b_pT], [b_pO])

            for i in range(nkt + LA):
                if i < nkt:
                    qk(i)
                if i >= LA:
                    pv(i - LA)
            k.P.op("dve", (lambda o=rden[64:65, :], i=pO[64:65, 0:512]: nc.vector.reciprocal(out=o, in_=i)), [b_pO], [b_rden])
            pt, pb = B[0]
            k.mm(pt[0:64, 0:512], ones_f[64:65, 0:64], rden[64:65, :], True, True, [b_onesf, b_rden], [pb])
            k.cp(osb[:], pO[0:64, 0:512], [b_pO], [b_osb], eng="act")
            k.tt(otile[:, h, :], osb[:], pt[0:64, 0:512], ALU.mult, [b_osb, pb], [b_ot])
        oT(g, otile, b_ot)


def l1_consts(S):
    half = 32
    inv = (10000.0 ** (-np.arange(half, dtype=np.float32) / half)).astype(np.float32)
    ang = np.arange(S, dtype=np.float32)[None, :] * inv[:, None]
    cos = np.cos(ang).astype(np.float32)
    sin = np.sin(ang).astype(np.float32)
    cosd = np.concatenate([cos, cos], 0)
    sind = np.concatenate([sin, -sin], 0)
    kk = np.arange(128)[:, None]
    qq = np.arange(512)[None, :]
    cm = np.concatenate([np.where(qq < j * 128 + kk, NEG, 0.0) for j in range(4)], 1).astype(np.float32)
    ed = (np.arange(S)[None, :] // 256 == np.arange(64)[:, None]).astype(np.float32)
    ident = np.eye(128, dtype=np.float32)
    rm = np.tile((np.arange(512) % 64 != 0).astype(np.float32)[None, :], (64, 1))
    s_ = np.arange(64)[:, None]
    t_ = np.arange(64)[None, :]
    am = np.tile((s_ <= t_).astype(np.float32), (1, 8))
    hm = np.zeros((64, 2), np.float32)
    hm[0:32, 0] = 1
    hm[32:64, 1] = 1
    hm2 = np.repeat(hm, 64, axis=1)
    return dict(cosd=cosd, sind=sind, cmd=cm, ed=ed, identd=ident, rmd=rm, amd=am, hm2d=hm2, hmd=hm)


HALO = 8
OCS = 2048


def choose_tiles(W):
    if W == 4104:
        return 4, 3, 342
    if W == 520:
        return 2, 1, 260
    raise ValueError(W)


class Trunk:
    def __init__(self, k, W):
        self.k = k
        self.nc = k.nc
        self.W = W
        self.NB, self.NTB, self.TW = choose_tiles(W)
        self.TB = self.NTB * self.TW
        TB, TW = self.TB, self.TW
        self.h, self.b_h = k.sb([128, 8, TB], F32, "h")
        self.hn, self.b_hn = k.sb([128, 8, TB], BF16, "hn")
        self.act, self.b_act = k.sb([128, 24, TB], BF16, "act")
        self.wst = [k.sb([128, 24, 128], F32, "wst0")] * 2
        self.wbf = [k.sb([128, 24, 128], BF16, "wbf%d" % i) for i in range(2)]
        self.wi = 0
        self.pc = [k.sb([128, 4 + TB], F32, "pc%d" % i) for i in range(2)]
        self.pci = 0
        self.yt = [k.sb([128, TB], F32, "yt%d" % i) for i in range(2)]
        self.gel, self.b_gel = k.sb([128, TB], F32, "gel")
        self.sqt, self.b_sqt = k.sb([128, 8, TW], BF16, "sqt")
        self.lnv, self.b_lnv = k.sb([128, TW], F32, "lnvt")
        self.rstd, self.b_rstd = k.sb([128, TW], F32, "rstdt")
        self.ones_b, self.b_ones = k.sb([128, 128], BF16, "onesb")
        self.m, self.b_m = k.sb([128, 1], F32, "hmask")
        self.carry, self.b_carry = k.sb([128, 48, 2], F32, "carry")
        self.B = [k.ps([128, 512], "bank%d" % i) for i in range(8)]
        self.bi = 0
        k.memset(self.ones_b[:], 1.0, [self.b_ones])
        k.memset(self.carry[:], 0.0, [self.b_carry])

    def bank(self):
        b = self.B[self.bi]
        self.bi = (self.bi + 1) % 8
        return b

    def small(self, name, dram_ap, shape):
        t, b = self.k.sb(shape, F32, name)
        self.k.dma(t[:], dram_ap, (), [b])
        return t, b

    def load_w(self, wd, KC, col0):
        k = self.k
        i = self.wi
        self.wi = (self.wi + 1) % 2
        st, b_st = self.wst[i]
        wb, b_wb = self.wbf[i]
        wv = wd.rearrange("(kc p) f -> p kc f", p=128)
        k.dma(st[:, 0:KC, :], wv[:, :, col0:col0 + 128], (), [b_st])
        k.cp(wb[:, 0:KC, :], st[:, 0:KC, :], [b_st], [b_wb], eng="pool")
        return wb, b_wb

    def rmsnorm(self, g_t, b_g, out_f32=None):
        k = self.k
        TW = self.TW
        for nt in range(self.NTB):
            cs = slice(nt * TW, (nt + 1) * TW)
            k.act(self.sqt[:], self.h[:, :, cs], AF.Square, [self.b_h], [self.b_sqt])
            pt, pb = self.bank()
            for kc in range(8):
                k.mm(pt[:, 0:TW], self.ones_b[:], self.sqt[:, kc, :], kc == 0, kc == 7, [self.b_ones, self.b_sqt], [pb])
            k.act(self.lnv[:], pt[:, 0:TW], AF.Ln, [pb], [self.b_lnv], bias=EPS, scale=1.0 / 1024)
            k.act(self.rstd[:], self.lnv[:], AF.Exp, [self.b_lnv], [self.b_rstd], scale=-0.5)
            for kc in range(8):
                if out_f32 is None:
                    k.stt(self.hn[:, kc, cs], self.h[:, kc, cs], g_t[:, kc:kc + 1], self.rstd[:], ALU.mult, ALU.mult,
                          [self.b_h, b_g, self.b_rstd], [self.b_hn])
                else:
                    k.stt(out_f32[0][:, kc, cs], self.h[:, kc, cs], g_t[:, kc:kc + 1], self.rstd[:], ALU.mult, ALU.mult,
                          [self.b_h, b_g, self.b_rstd], [out_f32[1]])

    def linear(self, src, b_src, KC, wd, col0, evac):
        k = self.k
        TW = self.TW
        wb, b_wb = self.load_w(wd, KC, col0)
        for nt in range(self.NTB):
            cs = slice(nt * TW, (nt + 1) * TW)
            pt, pb = self.bank()
            for kc in range(KC):
                k.mm(pt[:, 0:TW], wb[:, kc, :], src[:, kc, cs], kc == 0, kc == KC - 1, [b_wb, b_src], [pb])
            evac(nt, cs, pt[:, 0:TW], pb)

    def conv(self, pc, b_pc, ntap, cw_t, b_cw, cb_t, b_cb, idx, yt, b_yt):
        k = self.k
        TB = self.TB
        last = ntap - 1
        k.act(yt[:], pc[:, last:last + TB], AF.Identity, [b_pc, b_cw, b_cb], [b_yt],
              bias=cb_t[:, idx:idx + 1], scale=cw_t[:, idx, last:last + 1])
        for i in range(last - 1, -1, -1):
            k.stt(yt[:], pc[:, i:i + TB], cw_t[:, idx, i:i + 1], yt[:], ALU.mult, ALU.add, [b_pc, b_cw, b_yt], [b_yt])

    def ffn(self, first, gn_t, b_gn, w_up, cw_t, b_cw, cb_t, b_cb, w_down, mask_halo):
        k = self.k
        TB, TW = self.TB, self.TW
        self.rmsnorm(gn_t, b_gn)
        for j in range(24):
            ys = []
            for part in range(2):
                idx = part * 24 + j
                pc, b_pc = self.pc[self.pci]
                self.pci = (self.pci + 1) % 2
                yt, b_yt = self.yt[part]
                k.cp(pc[:, 0:2], self.carry[:, idx, :], [self.b_carry], [b_pc], eng="pool")

                def evac(nt, cs, ps, pb, pc=pc, b_pc=b_pc):
                    k.cp(pc[:, 2 + cs.start:2 + cs.stop], ps, [pb], [b_pc], eng="act")

                self.linear(self.hn, self.b_hn, 8, w_up, idx * 128, evac)
                if first and mask_halo:
                    k.ts(pc[:, 2:2 + HALO], pc[:, 2:2 + HALO], self.m[:, 0:1], ALU.mult, [b_pc, self.b_m], [b_pc])
                k.cp(self.carry[:, idx, :], pc[:, TB:TB + 2], [b_pc], [self.b_carry], eng="pool")
                self.conv(pc, b_pc, 3, cw_t, b_cw, cb_t, b_cb, idx, yt, b_yt)
                ys.append((yt, b_yt))
            (yu, b_yu), (yg, b_yg) = ys
            k.act(self.gel[:], yg[:], AF.Gelu_apprx_tanh, [b_yg], [self.b_gel])
            k.tt(self.act[:, j, :], yu[:], self.gel[:], ALU.mult, [b_yu, self.b_gel], [self.b_act])
        for fc in range(8):
            def evac(nt, cs, ps, pb, fc=fc):
                k.tt(self.h[:, fc, cs], self.h[:, fc, cs], ps, ALU.add, [self.b_h, pb], [self.b_h])

            self.linear(self.act, self.b_act, 24, w_down, fc * 128, evac)


def l2_io(k, W):
    d = {}
    for name, shape in (("xTw", [1024, W]), ("m", [128, 1]), ("sel", [128, 4]), ("w_out0", [1024, 1024]), ("fg0", [128, 8]),
                        ("w_up0", [1024, 6144]), ("cw0", [128, 48, 3]), ("cb0", [128, 48]), ("w_down0", [3072, 1024]),
                        ("mg", [128, 8]), ("w_in", [1024, 2048]), ("rcw", [128, 8, 4]), ("rcb", [128, 8]), ("wa", [1024, 256]),
                        ("ba", [128, 8]), ("wx", [1024, 256]), ("bx", [128, 8]), ("lam", [128, 8])):
        d[name] = k.din(name, shape)
    return d


def emit_l2(k, W, S, d, og, h1_o, gg_o, hl_o, pl_o, ex_o):
    nc = k.nc
    T = Trunk(k, W)
    NB, TB, TW = T.NB, T.TB, T.TW
    TOK = W - HALO
    xTw, m_d, sel_d, w_out, fg_d, w_up, cw_d, cb_d, w_down = (d["xTw"], d["m"], d["sel"], d["w_out0"], d["fg0"], d["w_up0"],
                                                               d["cw0"], d["cb0"], d["w_down0"])
    mg_d, w_in, rcw_d, rcb_d, wa_d, ba_d, wx_d, bx_d, lam_d = (d["mg"], d["w_in"], d["rcw"], d["rcb"], d["wa"], d["ba"],
                                                               d["wx"], d["bx"], d["lam"])
    sel, b_sel = T.small("sel", sel_d[:, :], [128, 4])

    k.dma(T.m[:], m_d[:, :], (), [T.b_m])
    fg, b_fg = T.small("fg", fg_d[:, :], [128, 8])
    cw, b_cw = T.small("cw", cw_d[:, :, :], [128, 48, 3])
    cb, b_cb = T.small("cb", cb_d[:, :], [128, 48])
    mg, b_mg = T.small("mg", mg_d[:, :], [128, 8])
    rcw, b_rcw = T.small("rcw", rcw_d[:, :, :], [128, 8, 4])
    rcb, b_rcb = T.small("rcb", rcb_d[:, :], [128, 8])
    ba, b_ba = T.small("ba", ba_d[:, :], [128, 8])
    bx, b_bx = T.small("bx", bx_d[:, :], [128, 8])
    lam, b_lam = T.small("lam", lam_d[:, :], [128, 8])
    c1, b_c1 = k.sb([128, 8], F32, "c1")
    c2, b_c2 = k.sb([128, 8], F32, "c2")
    k.act(c1[:], lam[:], AF.Exp, [b_lam], [b_c1], scale=-1.0)
    k.act(c1[:], c1[:], AF.Ln, [b_c1], [b_c1], bias=1.0, scale=1.0)
    k.ts(c2[:], c1[:], -16.0, ALU.mult, [b_c1], [b_c2])
    k.ts(c1[:], c1[:], -8.0, ALU.mult, [b_c1, b_c2], [b_c1])
    ob, b_ob = T.hn, T.b_hn
    gg, b_gg = k.sb([128, TB], BF16, "gg")
    rcar, b_rcar = k.sb([128, 8, 3], F32, "rcar")
    hcar, b_hcar = k.sb([128, 8], F32, "hcar")
    pcar, b_pcar = k.sb([128, 8], F32, "pcar")
    zer, b_zer = k.sb([128, TB], F32, "zer")
    xrc, b_xrc = k.sb([128, 2, TB], F32, "xrc")
    xrb, b_xrb = k.sb([128, 2, TB], BF16, "xrb")
    rr, b_rr = k.sb([128, TB], F32, "rr")
    ii, b_ii = k.sb([128, TB], F32, "ii")
    aa, b_aa = k.sb([128, TB], F32, "aa")
    uu, b_uu = k.sb([128, TB], F32, "uu")
    hl, b_hl = k.sb([128, TB], F32, "hl")
    pl, b_pl = k.sb([128, TB], F32, "pl")
    k.memset(rcar[:], 0.0, [b_rcar])
    k.memset(zer[:], 0.0, [b_zer])
    ext, b_ext = k.sb([128, 8, 2], F32, "ext")

    oTv = [o_.rearrange("(kc p) s -> p kc s", p=128) for o_ in og]
    xTv = xTw.rearrange("(kc p) s -> p kc s", p=128)
    h1v = h1_o.rearrange("(kc p) s -> p kc s", p=128)
    ggv = gg_o.rearrange("(kc p) s -> p kc s", p=128)
    hlv = hl_o.rearrange("(kc p) s -> p kc s", p=128)
    plv = pl_o.rearrange("(kc p) s -> p kc s", p=128)
    exv = ex_o.rearrange("(kc p) s -> p kc s", p=128)

    for blk in range(NB):
        first = blk == 0
        g0 = blk * TB
        for c in range(4):
            cand = T.act[:, 8 * (c % 3):8 * (c % 3) + 8, :]
            lo = c * TOK - HALO + g0
            skip = max(0, -lo)
            if skip:
                k.memset(cand[:, :, 0:skip], 0.0, [T.b_act])
            a = lo + skip
            while a < lo + TB:
                j = a // OCS
                e = min(lo + TB, (j + 1) * OCS)
                k.dma(cand[:, :, a - lo:e - lo], oTv[j][:, :, a - j * OCS:e - j * OCS], (), [T.b_act])
                a = e
            if c == 0:
                k.ts(ob[:], cand, sel[:, 0:1], ALU.mult, [T.b_act, b_sel], [b_ob])
            else:
                k.stt(ob[:], cand, sel[:, c:c + 1], ob[:], ALU.mult, ALU.add, [T.b_act, b_sel, b_ob], [b_ob])
        k.dma(T.h[:, 0:4, :], xTv[:, 0:4, g0:g0 + TB], (), [T.b_h])
        k.dma(T.h[:, 4:8, :], xTv[:, 4:8, g0:g0 + TB], (), [T.b_h])
        for fc in range(8):
            def evac(nt, cs, ps, pb, fc=fc):
                k.tt(T.h[:, fc, cs], T.h[:, fc, cs], ps, ALU.add, [T.b_h, pb], [T.b_h])

            T.linear(ob, b_ob, 8, w_out, fc * 128, evac)
        T.ffn(first, fg, b_fg, w_up, cw, b_cw, cb, b_cb, w_down, False)
        k.dma(h1v[:, :, g0:g0 + TB], T.h[:], [T.b_h], ())
        T.rmsnorm(mg, b_mg)
        for fc in range(8):
            def evac(nt, cs, ps, pb, fc=fc):
                k.act(gg[:, cs], ps, AF.Gelu_apprx_tanh, [pb], [b_gg])

            T.linear(T.hn, T.b_hn, 8, w_in, fc * 128, evac)
            k.dma(ggv[:, fc, g0:g0 + TB], gg[:], [b_gg], ())
        for n in range(4):
            for c2i in range(2):
                c8 = 2 * n + c2i
                pc, b_pc = T.pc[T.pci]
                T.pci = (T.pci + 1) % 2
                k.cp(pc[:, 0:3], rcar[:, c8, :], [b_rcar], [b_pc], eng="pool")

                def evac(nt, cs, ps, pb, pc=pc, b_pc=b_pc):
                    k.cp(pc[:, 3 + cs.start:3 + cs.stop], ps, [pb], [b_pc], eng="act")

                T.linear(T.hn, T.b_hn, 8, w_in, 1024 + c8 * 128, evac)
                if first:
                    k.ts(pc[:, 3:3 + HALO], pc[:, 3:3 + HALO], T.m[:, 0:1], ALU.mult, [b_pc, T.b_m], [b_pc])
                k.cp(rcar[:, c8, :], pc[:, TB:TB + 3], [b_pc], [b_rcar], eng="pool")
                yt, b_yt = T.yt[c2i]
                T.conv(pc, b_pc, 4, rcw, b_rcw, rcb, b_rcb, c8, yt, b_yt)
                k.cp(xrc[:, c2i, :], yt[:], [b_yt], [b_xrc], eng="pool")
                k.cp(xrb[:, c2i, :], yt[:], [b_yt], [b_xrb], eng="act")
            for c2i in range(2):
                fc = 2 * n + c2i
                for (wd, bias_t, b_bias, dst, b_dst) in ((wa_d, ba, b_ba, rr, b_rr), (wx_d, bx, b_bx, ii, b_ii)):
                    def evac(nt, cs, ps, pb, dst=dst, b_dst=b_dst, bias_t=bias_t, b_bias=b_bias, fc=fc):
                        k.act(dst[:, cs], ps, AF.Sigmoid, [pb, b_bias], [b_dst], bias=bias_t[:, fc:fc + 1], scale=1.0)

                    T.linear(xrb, b_xrb, 2, wd[n * 256:(n + 1) * 256, :], c2i * 128, evac)
                k.act(aa[:], rr[:], AF.Exp, [b_rr, b_c1], [b_aa], scale=c1[:, fc:fc + 1])
                k.act(rr[:], rr[:], AF.Exp, [b_rr, b_c2], [b_rr], scale=c2[:, fc:fc + 1])
                k.act(rr[:], rr[:], AF.Sqrt, [b_rr], [b_rr], bias=1.0, scale=-1.0)
                k.tt(uu[:], xrc[:, c2i, :], ii[:], ALU.mult, [b_xrc, b_ii], [b_uu])
                k.tt(uu[:], uu[:], rr[:], ALU.mult, [b_uu, b_rr], [b_uu])
                if first:
                    k.ts(uu[:, 0:HALO], uu[:, 0:HALO], T.m[:, 0:1], ALU.mult, [b_uu, T.b_m], [b_uu])
                    k.memset(hl[:, 0:5], 0.0, [b_hl])
                    k.memset(pl[:, 0:5], 0.0, [b_pl])
                    s0, hi, pi = 5, 0.0, 1.0
                else:
                    s0, hi, pi = 0, hcar[:, fc:fc + 1], pcar[:, fc:fc + 1]
                k.scan(hl[:, s0:TB], aa[:, s0:TB], uu[:, s0:TB], hi, [b_aa, b_uu, b_hcar], [b_hl])
                k.scan(pl[:, s0:TB], aa[:, s0:TB], zer[:, s0:TB], pi, [b_aa, b_zer, b_pcar], [b_pl])
                k.cp(hcar[:, fc:fc + 1], hl[:, TB - 1:TB], [b_hl], [b_hcar], eng="pool")
                k.cp(pcar[:, fc:fc + 1], pl[:, TB - 1:TB], [b_pl], [b_pcar], eng="pool")
                k.dma(hlv[:, fc, g0:g0 + TB], hl[:], [b_hl], ())
                k.dma(plv[:, fc, g0:g0 + TB], pl[:], [b_pl], ())
                if blk == NB - 1:
                    k.cp(ext[:, fc, 0:1], hl[:, TB - 4:TB - 3], [b_hl], [b_ext], eng="pool")
                    k.cp(ext[:, fc, 1:2], pl[:, TB - 4:TB - 3], [b_pl], [b_ext], eng="pool")
    k.dma(exv[:, :, :], ext[:], [b_ext], ())


def l3_io(k, W):
    d = {}
    for name, shape in (("srank", [128, 4]), ("oms", [128, 4]), ("w_out1", [1024, 1024]), ("fg1", [128, 8]),
                        ("w_up1", [1024, 6144]), ("cw1", [128, 48, 3]), ("cb1", [128, 48]), ("w_down1", [3072, 1024]),
                        ("fin", [128, 8])):
        d[name] = k.din(name, shape)
    return d


def emit_l3(k, W, d, m_d, h1w, ggw, hlw, plw, exg, out_o):
    nc = k.nc
    T = Trunk(k, W)
    NB, TB, TW = T.NB, T.TB, T.TW
    TOK = W - HALO
    w_out, fg_d, w_up, cw_d, cb_d, w_down, fin_d = (d["w_out1"], d["fg1"], d["w_up1"], d["cw1"], d["cb1"], d["w_down1"], d["fin"])
    k.dma(T.m[:], m_d[:, :], (), [T.b_m])
    fg, b_fg = T.small("fg", fg_d[:, :], [128, 8])
    cw, b_cw = T.small("cw", cw_d[:, :, :], [128, 48, 3])
    cb, b_cb = T.small("cb", cb_d[:, :], [128, 48])
    fin, b_fin = T.small("fin", fin_d[:, :], [128, 8])
    sr, b_sr = T.small("sr", d["srank"][:, :], [128, 4])
    oms, b_oms = T.small("oms", d["oms"][:, :], [128, 4])
    pe, b_pe = k.sb([128, 4, 8, 2], F32, "pe")
    exv = exg.rearrange("(r kc p) t -> p r kc t", p=128, kc=8)
    for r in range(4):
        k.dma(pe[:, r, :, :], exv[:, r, :, :], (), [b_pe])
    Hc, b_Hc = k.sb([128, 8], F32, "Hc")
    Pm, b_Pm = k.sb([128, 8], F32, "Pm")
    Em, b_Em = k.sb([128, 8], F32, "Em")
    k.memset(Hc[:], 0.0, [b_Hc])
    for r in range(4):
        k.ts(Pm[:], pe[:, r, :, 1], sr[:, r:r + 1], ALU.mult, [b_pe, b_sr], [b_Pm], s2=oms[:, r:r + 1], op1=ALU.add)
        k.ts(Em[:], pe[:, r, :, 0], sr[:, r:r + 1], ALU.mult, [b_pe, b_sr], [b_Em])
        k.tt(Hc[:], Hc[:], Pm[:], ALU.mult, [b_Hc, b_Pm], [b_Hc])
        k.tt(Hc[:], Hc[:], Em[:], ALU.add, [b_Hc, b_Em], [b_Hc])
    yb, b_yb = T.hn, T.b_hn
    hl, b_hl = k.sb([128, TB], F32, "hl")
    pl, b_pl = k.sb([128, TB], F32, "pl")
    ggt, b_ggt = k.sb([128, TB], BF16, "ggt")
    outt, b_outt = k.sb([128, 8, TB], F32, "outt")

    h1v = h1w.rearrange("(kc p) s -> p kc s", p=128)
    ggv = ggw.rearrange("(kc p) s -> p kc s", p=128)
    hlv = hlw.rearrange("(kc p) s -> p kc s", p=128)
    plv = plw.rearrange("(kc p) s -> p kc s", p=128)
    outv = out_o.rearrange("(kc p) s -> p kc s", p=128)

    for blk in range(NB):
        first = blk == 0
        g0 = blk * TB
        k.dma(T.h[:, 0:4, :], h1v[:, 0:4, g0:g0 + TB], (), [T.b_h])
        k.dma(T.h[:, 4:8, :], h1v[:, 4:8, g0:g0 + TB], (), [T.b_h])
        for fc in range(8):
            k.dma(hl[:], hlv[:, fc, g0:g0 + TB], (), [b_hl])
            k.dma(pl[:], plv[:, fc, g0:g0 + TB], (), [b_pl])
            k.dma(ggt[:], ggv[:, fc, g0:g0 + TB], (), [b_ggt])
            k.stt(hl[:], pl[:], Hc[:, fc:fc + 1], hl[:], ALU.mult, ALU.add, [b_pl, b_Hc, b_hl], [b_hl])
            k.tt(yb[:, fc, :], hl[:], ggt[:], ALU.mult, [b_hl, b_ggt], [b_yb])
        for fc in range(8):
            def evac(nt, cs, ps, pb, fc=fc):
                k.tt(T.h[:, fc, cs], T.h[:, fc, cs], ps, ALU.add, [T.b_h, pb], [T.b_h])

            T.linear(yb, b_yb, 8, w_out, fc * 128, evac)
        T.ffn(first, fg, b_fg, w_up, cw, b_cw, cb, b_cb, w_down, True)
        T.rmsnorm(fin, b_fin, out_f32=(outt, b_outt))
        lo = HALO if first else 0
        k.dma(outv[:, :, g0 + lo - HALO:g0 + TB - HALO], outt[:, :, lo:TB], [b_outt], ())


_F = {}


def build_fused(S):
    TOK = S // 4
    W = TOK + HALO
    nc = bass.Bass("TRN2", target_bir_lowering=False)
    k = K(nc)
    io1 = l1_io(k, S)
    io2 = l2_io(k, W)
    io3 = l3_io(k, W)
    NCH = max(1, S // OCS)
    osrc = [k.dint("osrc%d" % j, [256, min(S, OCS)], BF16) for j in range(NCH)]
    og = [k.dint("og%d" % j, [1024, min(S, OCS)], BF16) for j in range(NCH)]
    h1 = k.dint("h1s", [1024, W])
    gg = k.dint("ggs", [1024, W], BF16)
    hl = k.dint("hls", [1024, W])
    pl = k.dint("pls", [1024, W])
    exs = k.dint("exs", [1024, 2])
    exg = k.dint("exg", [4096, 2])
    out = k.dout("outT", [1024, TOK])
    groups = [[0, 1, 2, 3], [4, 5, 6, 7]]
    upto = int(os.environ.get("FUSE_UPTO", "3"))
    tile_bufs = {}

    def o_out(g, otile, b_ot):
        j, off = (g * 512) // OCS, (g * 512) % OCS
        ov = osrc[j].rearrange("(r p) s -> p r s", p=64)
        bb = Buf()
        tile_bufs.setdefault(j, []).append(bb)
        k.dma(ov[:, :, off:off + 512], otile[:], [b_ot], [bb])
        if off + 512 == min(S, OCS):
            k.P.dma((lambda j=j: nc.gpsimd.collective_compute("AllGather", ALU.bypass, replica_groups=groups,
                                                               ins=[osrc[j][:, :]], outs=[og[j][:, :]])),
                    tile_bufs[j], (), eng="pool", inc=1)

    emit_l1(k, S, io1, o_out)
    k.end_phase()
    emit_l2(k, W, S, io2, og, h1, gg, hl, pl, exs)
    k.end_phase()
    if upto == 2:
        return nc, k.P.finish()
    if os.environ.get("NOCC2"):
        k.dma(exg[0:1024, :], exs[:, :], (), ())
    else:
        k.P.dma(lambda: nc.gpsimd.collective_compute("AllGather", ALU.bypass, replica_groups=groups, ins=[exs[:, :]], outs=[exg[:, :]]),
                (), (), eng="pool", inc=1)
    k.P.barrier()
    emit_l3(k, W, io3, io2["m"], h1, gg, hl, pl, exg, out)
    cnt = k.P.finish()
    return nc, cnt


def _pk(v):
    return np.ascontiguousarray(v.reshape(-1, 128).T)


def _pkw(w):
    t, C = w.shape
    return np.ascontiguousarray(w.T.reshape(C // 128, 128, t).transpose(1, 0, 2))


def kernel(x, mix_norm_g, ffn_norm_g, final_norm_g,
           ev_w_in, ev_w_gk2, ev_b_gk2, ev_gla_norm_g, ev_w_out,
           od_w_in, od_conv_w, od_conv_b, od_w_a, od_b_a, od_w_x, od_b_x, od_lambda, od_w_out,
           ffn_w_up, ffn_conv_w, ffn_conv_b, ffn_w_down):
    f = lambda a: np.asarray(a, dtype=np.float32)
    x = f(x)
    Bsz, S, D = x.shape
    TOK = S // 4
    W = TOK + HALO
    if S not in _F:
        _F[S] = build_fused(S)
    nc, _ = _F[S]
    consts = l1_consts(S)
    w = f(ev_w_in)[0]
    gn = _pk(f(mix_norm_g)[0])
    perm = []
    for g in range(4):
        for j in range(4):
            head = 2 * g + (j % 2)
            base = head * 64 if j < 2 else 512 + head * 64
            perm += list(range(base, base + 64))
    w_out0 = np.ascontiguousarray(f(ev_w_out)[0][np.asarray(perm)])
    shared = {
        "gn": gn, "w_out0": w_out0, "fg0": _pk(f(ffn_norm_g)[0]), "w_up0": f(ffn_w_up)[0],
        "cw0": _pkw(f(ffn_conv_w)[0]), "cb0": _pk(f(ffn_conv_b)[0]), "w_down0": f(ffn_w_down)[0],
        "mg": _pk(f(mix_norm_g)[1]), "w_in": f(od_w_in)[0], "rcw": _pkw(f(od_conv_w)[0]), "rcb": _pk(f(od_conv_b)[0]),
        "wa": np.ascontiguousarray(f(od_w_a)[0].reshape(1024, 256)), "ba": _pk(f(od_b_a)[0]),
        "wx": np.ascontiguousarray(f(od_w_x)[0].reshape(1024, 256)), "bx": _pk(f(od_b_x)[0]),
        "lam": _pk(f(od_lambda)[0]),
        "w_out1": f(od_w_out)[0], "fg1": _pk(f(ffn_norm_g)[1]), "w_up1": f(ffn_w_up)[1],
        "cw1": _pkw(f(ffn_conv_w)[1]), "cb1": _pk(f(ffn_conv_b)[1]), "w_down1": f(ffn_w_down)[1],
        "fin": _pk(f(final_norm_g)),
    }
    shared.update(consts)
    in_maps = []
    for b in range(Bsz):
        xTb = np.ascontiguousarray(x[b].T)
        for c in range(4):
            h0, h1 = 2 * c, 2 * c + 1
            cols = []
            for base in (0, 512, 1024):
                cols += [np.arange(base + h0 * 64, base + h0 * 64 + 64), np.arange(base + h1 * 64, base + h1 * 64 + 64)]
            for base in (1536, 1792):
                cols += [np.arange(base + h0 * 32, base + h0 * 32 + 32), np.arange(base + h1 * 32, base + h1 * 32 + 32)]
            cols += [np.arange(2048 + h0 * 64, 2048 + h0 * 64 + 64), np.arange(2048 + h1 * 64, 2048 + h1 * 64 + 64)]
            cols += [np.arange(2560, 2576)]
            cols += [np.arange(2576 + h0 * 64, 2576 + h0 * 64 + 64), np.arange(2576 + h1 * 64, 2576 + h1 * 64 + 64)]
            cols = np.concatenate(cols)
            t0 = c * TOK
            xw = np.zeros((1024, W), np.float32)
            if c == 0:
                xw[:, HALO:] = xTb[:, 0:TOK]
            else:
                xw[:] = xTb[:, t0 - HALO:t0 + TOK]
            sel = np.zeros((128, 4), np.float32)
            sel[:, c] = 1.0
            sr = np.zeros((128, 4), np.float32)
            sr[:, :c] = 1.0
            m = dict(shared)
            m.update({
                "xT": xTb, "w1": np.ascontiguousarray(w[:, cols]),
                "wgk2": np.ascontiguousarray(f(ev_w_gk2)[0][:, h0 * 32:h0 * 32 + 64]),
                "bgk2": np.ascontiguousarray(f(ev_b_gk2)[0][h0 * 32:h0 * 32 + 64].reshape(64, 1)),
                "glag": np.ascontiguousarray(f(ev_gla_norm_g)[0][h0:h0 + 2].T),
                "xTw": xw, "m": np.full((128, 1), 0.0 if c == 0 else 1.0, np.float32),
                "sel": sel, "srank": sr, "oms": 1.0 - sr,
            })
            in_maps.append(m)
    res = run_bass_kernel_spmd(nc, in_maps, core_ids=list(range(len(in_maps)))).results
    out = np.zeros((Bsz, S, D), np.float32)
    for b in range(Bsz):
        for c in range(4):
            out[b, c * TOK:(c + 1) * TOK, :] = np.asarray(res[b * 4 + c]["outT"]).T
    return out
```

```python
import contextlib
import os
import numpy as np
import ml_dtypes
import concourse.bass as bass
import concourse.mybir as mybir
from concourse.bass_utils import run_bass_kernel_spmd

F32 = mybir.dt.float32
BF16 = mybir.dt.bfloat16
AF = mybir.ActivationFunctionType
ALU = mybir.AluOpType
AX = mybir.AxisListType

SAME_SYNC = True
N_DMA_SEMS = 24
SEM_EPOCH = 2000
NEG = -240000.0
EPS = 1e-6


class Buf:
    __slots__ = ("name", "lw", "rd")

    def __init__(self, name=""):
        self.name = name
        self.lw = None
        self.rd = []


class Prog:
    ENGS = ("pe", "act", "dve", "pool", "sp")

    def __init__(self, nc):
        self.nc = nc
        self.h = {"pe": nc.tensor, "act": nc.scalar, "dve": nc.vector, "pool": nc.gpsimd, "sp": nc.sync}
        self.items = {e: [] for e in self.ENGS}
        self.known = {}
        self.sem = {e: nc.alloc_semaphore("s_" + e) for e in self.ENGS}
        self.dsem = [nc.alloc_semaphore("d%d" % i) for i in range(N_DMA_SEMS)]
        self.duse = [0] * N_DMA_SEMS
        self.dval = [0] * N_DMA_SEMS
        self.pending = {e: [] for e in self.ENGS}
        self.rank = {e: [] for e in self.ENGS}
        self.emitted = {e: 0 for e in self.ENGS}
        self.esem = {}
        self.dnext = 0

    def _need(self, eng, tok, waits):
        if tok is None:
            return
        if tok[0] == "c":
            _, e2, seq = tok
            if e2 == eng and (eng == "pe" or not SAME_SYNC):
                return
            key = (eng, "c", e2)
            if self.known.get(key, -1) >= seq:
                return
            self.known[key] = seq
            self.items[e2][seq]["flag"] = True
            waits.append(tok)
        else:
            _, idx, val = tok
            key = (eng, "d", idx)
            if self.known.get(key, -1) >= val:
                return
            self.known[key] = val
            waits.append(tok)

    def _deps(self, eng, reads, writes):
        waits = []
        for b in reads:
            self._need(eng, b.lw, waits)
        for b in writes:
            self._need(eng, b.lw, waits)
            for t in b.rd:
                self._need(eng, t, waits)
        return waits

    def op(self, eng, fn, reads=(), writes=()):
        waits = self.pending[eng] + self._deps(eng, reads, writes)
        self.pending[eng] = []
        seq = len(self.items[eng])
        self.items[eng].append({"waits": waits, "fn": fn, "flag": False, "dma": None})
        tok = ("c", eng, seq)
        for b in reads:
            b.rd.append(tok)
        for b in writes:
            b.lw = tok
            b.rd = []
        return tok

    def dma(self, fn, reads=(), writes=(), eng="sp", inc=16):
        idx = self.dnext
        self.dnext = (self.dnext + 1) % N_DMA_SEMS
        pv = self.dval[idx]
        waits = self.pending[eng] + self._deps(eng, reads, writes)
        self.pending[eng] = []
        if pv > 0:
            self._need(eng, ("d", idx, pv), waits)
        self.duse[idx] += 1
        self.dval[idx] = pv + inc
        tok = ("d", idx, pv + inc)
        self.items[eng].append({"waits": waits, "fn": fn, "flag": False, "dma": idx, "inc": inc})
        for b in reads:
            b.rd.append(tok)
        for b in writes:
            b.lw = tok
            b.rd = []
        return tok

    def _all_tokens(self):
        toks = []
        for e in ("pe", "act", "dve", "pool"):
            items = self.items[e]
            for i in range(len(items) - 1, -1, -1):
                if items[i]["dma"] is None and items[i]["fn"] is not None or (items[i]["dma"] is None and i < self.emitted[e]):
                    toks.append(("c", e, i))
                    break
        for idx in range(N_DMA_SEMS):
            if self.dval[idx]:
                toks.append(("d", idx, self.dval[idx]))
        return toks

    def barrier(self, engines=None):
        toks = self._all_tokens()
        for e in (engines or self.ENGS):
            for t in toks:
                if t[0] == "c" and t[1] == e:
                    continue
                self._need(e, t, self.pending[e])

    def _sem_of(self, e, r):
        ep = (r - 1) // SEM_EPOCH
        if (e, ep) not in self.esem:
            self.esem[(e, ep)] = self.sem[e] if ep == 0 else self.nc.alloc_semaphore("s_%s_%d" % (e, ep))
        return self.esem[(e, ep)], (r - 1) % SEM_EPOCH + 1

    def flush(self):
        for e in self.ENGS:
            items = self.items[e]
            rk = self.rank[e]
            c = rk[-1] if rk else 0
            for i in range(len(rk), len(items)):
                if items[i]["flag"]:
                    c += 1
                rk.append(c)
        for e in self.ENGS:
            h = self.h[e]
            items = self.items[e]
            for i in range(self.emitted[e], len(items)):
                it = items[i]
                for t in it["waits"]:
                    if t[0] == "c":
                        sm, v = self._sem_of(t[1], self.rank[t[1]][t[2]])
                        h.wait_ge(sm, v)
                    else:
                        h.wait_ge(self.dsem[t[1]], t[2])
                if it["fn"] is None:
                    continue
                ins = it["fn"]()
                if it["dma"] is not None:
                    ins.then_inc(self.dsem[it["dma"]], it["inc"])
                elif it["flag"]:
                    sm, _ = self._sem_of(e, self.rank[e][i])
                    ins.then_inc(sm, 1)
                it["fn"] = None
            self.emitted[e] = len(items)

    def finish(self):
        self.barrier(engines=("sp",))
        self.items["sp"].append({"waits": self.pending["sp"], "fn": None, "flag": False, "dma": None})
        self.pending["sp"] = []
        self.flush()
        return {e: (len(self.items[e]), self.rank[e][-1] if self.rank[e] else 0) for e in self.ENGS}


class K:
    def __init__(self, nc):
        self.nc = nc
        self.P = Prog(nc)
        self._n = 0
        self.stack = contextlib.ExitStack()

    def end_phase(self):
        self.P.barrier()
        self.P.flush()
        self.stack.close()
        self.stack = contextlib.ExitStack()

    def sb(self, shape, dt, name=None):
        self._n += 1
        return self.stack.enter_context(self.nc.sbuf_tensor("sb%d_" % self._n + (name or "t"), list(shape), dt)), Buf(name or "")

    def ps(self, shape, name=None):
        self._n += 1
        return self.stack.enter_context(self.nc.psum_tensor("ps%d_" % self._n + (name or "p"), list(shape), F32)), Buf(name or "")

    def din(self, name, shape, dt=F32):
        return self.nc.dram_tensor(name, list(shape), dt, kind="ExternalInput").ap()

    def dint(self, name, shape, dt=F32, **kw):
        return self.nc.dram_tensor(name, list(shape), dt, kind="Internal", **kw).ap()

    def dout(self, name, shape, dt=F32):
        return self.nc.dram_tensor(name, list(shape), dt, kind="ExternalOutput").ap()

    def mm(self, out, lhsT, rhs, st, sp, r, w):
        nc = self.nc
        return self.P.op("pe", lambda: nc.tensor.matmul(out, lhsT=lhsT, rhs=rhs, start=st, stop=sp), r, w)

    def tr(self, out, in_, ident, r, w):
        nc = self.nc
        return self.P.op("pe", lambda: nc.tensor.transpose(out, in_, ident), r, w)

    def act(self, out, in_, func, r, w, bias=None, scale=None):
        nc = self.nc
        kw = {}
        if bias is not None:
            kw["bias"] = bias
        if scale is not None:
            kw["scale"] = scale
        return self.P.op("act", lambda: nc.scalar.activation(out=out, in_=in_, func=func, **kw), r, w)

    def tt(self, out, in0, in1, op, r, w, eng="dve"):
        h = self.P.h[eng]
        return self.P.op(eng, lambda: h.tensor_tensor(out=out, in0=in0, in1=in1, op=op), r, w)

    def ts(self, out, in0, s1, op0, r, w, s2=None, op1=None, eng="dve"):
        h = self.P.h[eng]
        if op1 is None:
            return self.P.op(eng, lambda: h.tensor_scalar(out=out, in0=in0, scalar1=s1, scalar2=None, op0=op0), r, w)
        return self.P.op(eng, lambda: h.tensor_scalar(out=out, in0=in0, scalar1=s1, scalar2=s2, op0=op0, op1=op1), r, w)

    def stt(self, out, in0, scalar, in1, op0, op1, r, w):
        nc = self.nc
        return self.P.op("dve", lambda: nc.vector.scalar_tensor_tensor(out=out, in0=in0, scalar=scalar, in1=in1, op0=op0, op1=op1), r, w)

    def cp(self, out, in_, r, w, eng="dve"):
        if eng == "act":
            nc = self.nc
            return self.P.op("act", lambda: nc.scalar.copy(out=out, in_=in_), r, w)
        h = self.P.h[eng]
        return self.P.op(eng, lambda: h.tensor_copy(out=out, in_=in_), r, w)

    def memset(self, ap, val, w, eng="pool"):
        h = self.P.h[eng]
        return self.P.op(eng, lambda: h.memset(ap, val), (), w)

    def scan(self, out, d0, d1, init, r, w):
        nc = self.nc
        return self.P.op("dve", lambda: nc.vector.tensor_tensor_scan(out=out, data0=d0, data1=d1, initial=init, op0=ALU.mult, op1=ALU.add), r, w)

    def dma(self, out, in_, r, w, eng="sp"):
        h = self.P.h[eng]
        return self.P.dma(lambda: h.dma_start(out=out, in_=in_), r, w, eng=eng)


def l1_io(k, S):
    d = {}
    for name, shape in (("xT", [1024, S]), ("gn", [128, 8]), ("w1", [1024, 784]), ("wgk2", [16, 64]), ("bgk2", [64, 1]),
                        ("glag", [64, 2]), ("cosd", [64, S]), ("sind", [64, S]), ("cmd", [128, 2048]), ("ed", [64, S]),
                        ("identd", [128, 128]), ("rmd", [64, 512]), ("amd", [64, 512]), ("hm2d", [64, 128]), ("hmd", [64, 2])):
        d[name] = k.din(name, shape)
    return d


def emit_l1(k, S, d, oT):
    NT = S // 512
    NKT = S // 128
    nc = k.nc
    P = k.P
    xT, gn_d, w1_d, wgk2_d, bgk2_d, glag_d = d["xT"], d["gn"], d["w1"], d["wgk2"], d["bgk2"], d["glag"]
    cos_d, sin_d, cm_d, e_d, id_d, rm_d, am_d, hm2_d, hm_d = (d["cosd"], d["sind"], d["cmd"], d["ed"], d["identd"], d["rmd"],
                                                              d["amd"], d["hm2d"], d["hmd"])

    xt, b_xt = k.sb([128, 8, 512], F32, "xt")
    rstd, b_rstd = k.sb([128, 512], F32, "rstd")
    lnv, b_lnv = k.sb([128, 512], F32, "lnv")
    hn, b_hn = k.sb([128, 8, 512], BF16, "hn")
    sq, b_sq = hn, b_hn
    wb, b_wb = k.sb([128, 8, 784], BF16, "wb")
    gn, b_gn = k.sb([128, 8], F32, "gn")
    Kaug = [k.sb([128, S], BF16, "kaug%d" % h) for h in range(2)]
    KB = [[Buf() for _ in range(NT)] for h in range(2)]
    b_kE = [Buf(), Buf()]
    Vaug = [k.sb([128, NKT, 66], BF16, "vaug%d" % h) for h in range(2)]
    VB = [[Buf() for _ in range(NT)] for h in range(2)]
    b_vones = [Buf(), Buf()]
    Qaug = [k.sb([128, 512], BF16, "qaug%d" % h) for h in range(2)]
    kmT = [k.sb([64, 64], BF16, "kmT%d" % h) for h in range(2)]
    km32, b_km32 = k.sb([64, 2], F32, "km32")
    cosT, b_cos = k.sb([64, 512], F32, "cos")
    sinT, b_sin = k.sb([64, 512], F32, "sin")
    t1, b_t1 = k.sb([64, 512], F32, "t1")
    t2, b_t2 = k.sb([64, 512], F32, "t2")
    pTs = [k.sb([128, 512], BF16, "pT%d" % i) for i in range(4)]
    cm, b_cm = k.sb([128, 4, 512], BF16, "cm")
    id_f, b_idf = k.sb([128, 128], F32, "idf")
    id_b, b_idb = k.sb([128, 128], BF16, "idb")
    ones_b, b_onesb = k.sb([128, 128], BF16, "onesb")
    ones_f, b_onesf = k.sb([128, 64], F32, "onesf")
    bq, b_bq = k.sb([128, 128], F32, "bq")
    gsb, b_gsb = k.sb([128, 64], F32, "gsb")
    m8, b_m8 = k.sb([128, 8], F32, "m8")
    rden, b_rden = lnv, b_lnv
    osb, b_osb = t1, b_t1
    otile, b_ot = k.sb([64, 4, 512], BF16, "otile")
    QG32, b_qg = k.sb([64, 512], F32, "qg32")
    KG32, b_kg = k.sb([64, 512], F32, "kg32")
    spl, b_spl = k.sb([64, 512], F32, "spl")
    bpos, b_bpos = k.sb([64, 512], F32, "bpos")
    eb, b_eb = k.sb([64, 512], F32, "eb")
    enb, b_enb = k.sb([64, 512], F32, "enb")
    Ac, b_ac = k.sb([64, 8], F32, "Ac")
    ke32, b_ke = k.sb([64, 512], F32, "ke32")
    qt, b_qt = k.sb([64, 512], BF16, "qt")
    kpad, b_kpad = k.sb([64, 2, 512], BF16, "kpad")
    khat, b_khat = k.sb([64, 512], BF16, "khat")
    rmk, b_rmk = k.sb([64, 512], F32, "rmk")
    amk, b_amk = k.sb([64, 512], BF16, "amk")
    hm2, b_hm2 = k.sb([64, 128], F32, "hm2")
    hm, b_hm = k.sb([64, 2], F32, "hm")
    attm, b_attm = k.sb([64, 2, 512], BF16, "attm")
    gvt, b_gvt = k.sb([64, 8, 128], BF16, "gvt")
    KTt, b_ktt = k.sb([64, 8, 64], BF16, "KTt")
    gk16, b_gk16 = k.sb([16, 512], BF16, "gk16")
    wgk2f, b_wgk2f = k.sb([16, 64], F32, "wgk2f")
    wgk2b, b_wgk2b = k.sb([16, 64], BF16, "wgk2b")
    nbg, b_nbg = k.sb([64, 1], F32, "nbg")
    glag, b_glag = k.sb([64, 2], F32, "glag")
    sbog, b_sbog = k.sb([64, 2, 512], BF16, "sbog")
    st32, b_st32 = k.sb([64, 128], F32, "st32")
    stb, b_stb = k.sb([64, 128], BF16, "stb")
    stmp, b_stmp = k.sb([64, 128], F32, "stmp")
    o32, b_o32 = t1, b_t1
    osq, b_osq = k.sb([64, 512], BF16, "osq")
    on32, b_on32 = t2, b_t2
    B = [k.ps([128, 512], "bank%d" % i) for i in range(8)]

    stg = xt
    k.dma(id_f[:], id_d[:, :], (), [b_idf])
    k.cp(id_b[:], id_f[:], [b_idf], [b_idb])
    k.memset(ones_b[:], 1.0, [b_onesb])
    k.memset(ones_f[:], 1.0, [b_onesf])
    k.memset(bq[:], 0.0, [b_bq])
    k.memset(st32[:], 0.0, [b_st32])
    k.memset(stb[:], 0.0, [b_stb])
    k.dma(gn[:], gn_d[:, :], (), [b_gn])
    k.dma(glag[:], glag_d[:, :], (), [b_glag])
    k.dma(hm2[:], hm2_d[:, :], (), [b_hm2])
    k.dma(hm[:], hm_d[:, :], (), [b_hm])
    k.dma(rmk[:], rm_d[:, :], (), [b_rmk])
    k.dma(wgk2f[:], wgk2_d[:, :], (), [b_wgk2f])
    k.cp(wgk2b[:], wgk2f[:], [b_wgk2f], [b_wgk2b])
    k.dma(nbg[:], bgk2_d[:, :], (), [b_nbg])
    k.ts(nbg[:], nbg[:], -1.0, ALU.mult, [b_nbg], [b_nbg])
    k.dma(t1[:], am_d[:, :], (), [b_t1])
    k.cp(amk[:], t1[:], [b_t1], [b_amk])
    sflat = stg[:].rearrange("p a b -> p (a b)")
    k.dma(sflat[:, 0:2048], cm_d[:, :], (), [b_xt])
    k.cp(cm[:].rearrange("p a b -> p (a b)"), sflat[:, 0:2048], [b_xt], [b_cm])
    w1v = w1_d.rearrange("(kc p) f -> p kc f", p=128)
    for half in range(2):
        sv = sflat[:, 0:4 * 784].rearrange("p (a b) -> p a b", b=784)
        k.dma(sv, w1v[:, half * 4:(half + 1) * 4, :], (), [b_xt])
        k.cp(wb[:, half * 4:(half + 1) * 4, :], sv, [b_xt], [b_wb], eng="dve" if half == 0 else "pool")
    for pc in range(S // 2048):
        k.dma(sflat[64:128, 0:2048], e_d[:, pc * 2048:(pc + 1) * 2048], (), [b_xt])
        for h in range(2):
            k.cp(Kaug[h][0][64:128, pc * 2048:(pc + 1) * 2048], sflat[64:128, 0:2048], [b_xt], [b_kE[h]],
                 eng="dve" if h == 0 else "pool")
    for h in range(2):
        k.memset(Vaug[h][0][:, :, 64:65], 1.0, [b_vones[h]])
        k.memset(kmT[h][0][:], 0.0, [kmT[h][1]])

    xTv = xT.rearrange("(kc p) s -> p kc s", p=128)
    def proj(bank, M, col0, ncols=None):
        pt, pb = B[bank]
        for kc in range(8):
            k.mm(pt[0:M, 0:512], wb[:, kc, col0:col0 + M], hn[:, kc, :], kc == 0, kc == 7, [b_wb, b_hn], [pb])
        return pt, pb

    for g in range(NT):
        c0 = g * 512
        k.dma(xt[:, 0:4, :], xTv[:, 0:4, c0:c0 + 512], (), [b_xt])
        k.dma(xt[:, 4:8, :], xTv[:, 4:8, c0:c0 + 512], (), [b_xt], eng="pool")
        k.dma(cosT[:], cos_d[:, c0:c0 + 512], (), [b_cos])
        k.dma(sinT[:], sin_d[:, c0:c0 + 512], (), [b_sin])
        k.act(sq[:], xt[:], AF.Square, [b_xt], [b_sq])
        pt, pb = B[0]
        for kc in range(8):
            k.mm(pt[:, :], ones_b[:], sq[:, kc, :], kc == 0, kc == 7, [b_onesb, b_sq], [pb])
        k.act(lnv[:], pt[:, :], AF.Ln, [pb], [b_lnv], bias=EPS, scale=1.0 / 1024)
        k.act(rstd[:], lnv[:], AF.Exp, [b_lnv], [b_rstd], scale=-0.5)
        for kc in range(8):
            k.stt(hn[:, kc, :], xt[:, kc, :], gn[:, kc:kc + 1], rstd[:], ALU.mult, ALU.mult, [b_xt, b_gn, b_rstd], [b_hn])
        for idx in range(4):
            h = idx % 2
            isk = idx >= 2
            pt, pb = proj(1 + (idx % 2), 64, idx * 64)
            if isk:
                dest = Kaug[h][0][0:64, c0:c0 + 512]
                dbuf = KB[h][g]
            else:
                dest = Qaug[h][0][0:64, :]
                dbuf = Qaug[h][1]
            k.tt(t1[:], pt[0:64, 0:512], cosT[:], ALU.mult, [pb, b_cos], [b_t1])
            k.tt(t2[0:32, :], pt[32:64, 0:512], sinT[32:64, :], ALU.mult, [pb, b_sin], [b_t2])
            k.tt(t2[32:64, :], pt[0:32, 0:512], sinT[0:32, :], ALU.mult, [pb, b_sin], [b_t2])
            k.tt(dest, t1[:], t2[:], ALU.add, [b_t1, b_t2], [dbuf], eng="pool")
        pt, pb = B[1]
        for st in range(4):
            for kc in range(8):
                k.mm(pt[:, st * 128:(st + 1) * 128], hn[:, kc, st * 128:(st + 1) * 128], wb[:, kc, 256:384],
                     kc == 0, kc == 7, [b_hn, b_wb], [pb])
        pv = pt[:, 0:512].rearrange("p (a b) -> p a b", b=128)
        for h in range(2):
            k.cp(Vaug[h][0][:, 4 * g:4 * g + 4, 0:64], pv[:, :, h * 64:(h + 1) * 64], [pb], [VB[h][g]], eng="act")
        pt, pb = proj(2, 128, 384)
        k.cp(QG32[:], pt[0:64, 0:512], [pb], [b_qg], eng="act")
        k.cp(KG32[:], pt[64:128, 0:512], [pb], [b_kg], eng="act")
        for c in range(8):
            pt, pb = B[3 + c // 4]
            for kc in range(8):
                k.mm(pt[0:64, (c % 4) * 128:(c % 4 + 1) * 128], hn[:, kc, c * 64:(c + 1) * 64], wb[:, kc, 512:640],
                     kc == 0, kc == 7, [b_hn, b_wb], [pb])
        for hf in range(2):
            pt, pb = B[3 + hf]
            k.cp(gvt[:, hf * 4:(hf + 1) * 4, :].rearrange("p a b -> p (a b)"), pt[0:64, 0:512], [pb], [b_gvt], eng="act")
        pt, pb = proj(0, 16, 640)
        k.cp(gk16[:], pt[0:16, 0:512], [pb], [b_gk16], eng="act")
        k.mm(pt[0:64, 0:512], wgk2b[:], gk16[:], True, True, [b_wgk2b, b_gk16], [pb])
        k.act(spl[:], pt[0:64, 0:512], AF.Exp, [pb, b_nbg], [b_spl], bias=nbg[:, 0:1], scale=-1.0)
        k.act(spl[:], spl[:], AF.Ln, [b_spl], [b_spl], bias=1.0, scale=1.0)
        pt, pb = proj(1, 128, 656)
        k.act(sbog[:, 0, :], pt[0:64, 0:512], AF.Silu, [pb], [b_sbog])
        k.act(sbog[:, 1, :], pt[64:128, 0:512], AF.Silu, [pb], [b_sbog])

        k.scan(bpos[:], rmk[:], spl[:], 0.0, [b_rmk, b_spl], [b_bpos])
        k.act(eb[:], bpos[:], AF.Exp, [b_bpos], [b_eb], scale=-1.0 / 16)
        k.act(enb[:], bpos[:], AF.Exp, [b_bpos], [b_enb], scale=1.0 / 16)
        blast = bpos[:].rearrange("p (c t) -> p c t", t=64)[:, :, 63:64].rearrange("p c o -> p (c o)")
        k.act(Ac[:], blast, AF.Exp, [b_bpos], [b_ac], scale=-1.0 / 16)
        k.stt(qt[:], QG32[:], 32.0 ** -0.5, eb[:], ALU.mult, ALU.mult, [b_qg, b_eb], [b_qt])
        k.tt(ke32[:], KG32[:], enb[:], ALU.mult, [b_kg, b_enb], [b_ke])
        for h in range(2):
            k.ts(kpad[:, h, :], ke32[:], hm[:, h:h + 1], ALU.mult, [b_ke, b_hm], [b_kpad], eng="pool")
        for c in range(8):
            k.ts(khat[:, c * 64:(c + 1) * 64], ke32[:, c * 64:(c + 1) * 64], Ac[:, c:c + 1], ALU.mult, [b_ke, b_ac], [b_khat], eng="pool")
        pt, pb = B[0]
        for c in range(8):
            k.mm(pt[0:64, c * 64:(c + 1) * 64], khat[:, c * 64:(c + 1) * 64], id_b[0:64, 0:64], True, True, [b_khat, b_idb], [pb])
        k.cp(KTt[:].rearrange("p a b -> p (a b)"), pt[0:64, 0:512], [pb], [b_ktt], eng="act")
        for h in range(2):
            pt, pb = B[1 + h]
            for c in range(8):
                k.mm(pt[0:64, c * 64:(c + 1) * 64], kpad[:, h, c * 64:(c + 1) * 64], qt[:, c * 64:(c + 1) * 64], True, True,
                     [b_kpad, b_qt], [pb])
            k.tt(attm[:, h, :], pt[0:64, 0:512], amk[:], ALU.mult, [pb, b_amk], [b_attm])
        pso = [B[5], B[6]]
        psd, b_psd = B[7]
        for c in range(8):
            for h in range(2):
                po, pbo = pso[h]
                k.mm(po[0:64, c * 64:(c + 1) * 64], gvt[:, c, h * 64:(h + 1) * 64], attm[:, h, c * 64:(c + 1) * 64], True, False,
                     [b_gvt, b_attm], [pbo])
                k.mm(po[0:64, c * 64:(c + 1) * 64], stb[:, h * 64:(h + 1) * 64], qt[:, c * 64:(c + 1) * 64], False, True,
                     [b_stb, b_qt], [pbo])
            for h in range(2):
                k.mm(psd[0:64, h * 64:(h + 1) * 64], KTt[:, c, :], gvt[:, c, h * 64:(h + 1) * 64], True, True, [b_ktt, b_gvt], [b_psd])
            k.tt(stmp[:], psd[0:64, 0:128], hm2[:], ALU.mult, [b_psd, b_hm2], [b_stmp])
            k.stt(st32[:], st32[:], Ac[:, c:c + 1], stmp[:], ALU.mult, ALU.add, [b_st32, b_ac, b_stmp], [b_st32])
            k.cp(stb[:], st32[:], [b_st32], [b_stb], eng="act")
        for h in range(2):
            po, pbo = pso[h]
            k.cp(o32[:], po[0:64, 0:512], [pbo], [b_o32], eng="act")
            k.act(osq[:], po[0:64, 0:512], AF.Square, [pbo], [b_osq])
            pt, pb = B[0]
            k.mm(pt[0:64, 0:512], ones_b[0:64, 0:64], osq[:], True, True, [b_onesb, b_osq], [pb])
            k.act(lnv[0:64, :], pt[0:64, 0:512], AF.Ln, [pb], [b_lnv], bias=EPS, scale=1.0 / 64)
            k.act(lnv[0:64, :], lnv[0:64, :], AF.Exp, [b_lnv], [b_lnv], scale=-0.5)
            k.tt(on32[:], o32[:], lnv[0:64, :], ALU.mult, [b_o32, b_lnv], [b_on32])
            k.stt(otile[:, 2 + h, :], on32[:], glag[:, h:h + 1], sbog[:, h, :], ALU.mult, ALU.mult, [b_on32, b_glag, b_sbog], [b_ot])

        for h in range(2):
            KA, _ = Kaug[h]
            VA, _ = Vaug[h]
            QA, b_QA = Qaug[h]
            kmt, b_kmt = kmT[h]
            k.P.op("dve", (lambda o=km32[:], i=KA[0:64, c0:c0 + 512].rearrange("p (a b) -> p a b", b=256):
                           nc.vector.tensor_reduce(out=o, in_=i, axis=AX.X, op=ALU.add)), [KB[h][g]], [b_km32])
            k.cp(kmt[:, 2 * g:2 * g + 2], km32[:], [b_km32], [b_kmt])
            pg, b_pg = B[7]
            for st in range(4):
                blk = 2 * g + st // 2
                k.mm(pg[:, 0:64], QA[0:64, st * 128:(st + 1) * 128], kmt[:, :], True, True, [b_QA, b_kmt], [b_pg])
                k.memset(gsb[:], -1e30, [b_gsb])
                if blk > 0:
                    k.cp(gsb[:, 0:blk], pg[:, 0:blk], [b_pg], [b_gsb])
                k.P.op("dve", (lambda: nc.vector.max(out=m8[:], in_=gsb[:])), [b_gsb], [b_m8])
                k.ts(bq[:, 64:128], gsb[:], m8[:, 2:3], ALU.is_ge, [b_gsb, b_m8], [b_bq], s2=-NEG, op1=ALU.mult)
                k.ts(bq[:, 64:128], bq[:, 64:128], NEG, ALU.add, [b_bq], [b_bq])
                k.memset(bq[:, 64 + blk:65 + blk], 0.0, [b_bq], eng="dve")
                k.tr(pg[:, 128:256], bq[:], id_f[:], [b_bq, b_idf], [b_pg])
                k.cp(QA[64:128, st * 128:(st + 1) * 128], pg[64:128, 128:256], [b_pg], [b_QA], eng="act")
            pO, b_pO = B[5 + h]
            nkt = 4 * g + 4
            LA = 3

            def qk(kt):
                j = kt - 4 * g
                q0 = 256 if j >= 2 else 0
                pS, b_pS = B[1 + (kt % 4)]
                k.mm(pS[:, q0:512], KA[:, kt * 128:(kt + 1) * 128], QA[:, q0:512], True, j < 0,
                     [KB[h][kt // 4], b_kE[h], b_QA], [b_pS])
                if j >= 0:
                    k.mm(pS[:, q0:512], id_b[:], cm[:, j, q0:512], False, True, [b_idb, b_cm], [b_pS])

            def pv(kt):
                j = kt - 4 * g
                q0 = 256 if j >= 2 else 0
                pS, b_pS = B[1 + (kt % 4)]
                pTt, b_pT = pTs[kt % 4]
                k.act(pTt[:, q0:512], pS[:, q0:512], AF.Exp, [b_pS], [b_pT], scale=0.125)
                k.mm(pO[0:65, q0:512], VA[:, kt, 0:65], pTt[:, q0:512], kt == 0, kt == nkt - 1,
                     [VB[h][kt // 4], b_vones[h], b_pT], [b_pO])

            for i in range(nkt + LA):
                if i < nkt:
                    qk(i)
                if i >= LA:
                    pv(i - LA)
            k.P.op("dve", (lambda o=rden[64:65, :], i=pO[64:65, 0:512]: nc.vector.reciprocal(out=o, in_=i)), [b_pO], [b_rden])
            pt, pb = B[0]
            k.mm(pt[0:64, 0:512], ones_f[64:65, 0:64], rden[64:65, :], True, True, [b_onesf, b_rden], [pb])
            k.cp(osb[:], pO[0:64, 0:512], [b_pO], [b_osb], eng="act")
            k.tt(otile[:, h, :], osb[:], pt[0:64, 0:512], ALU.mult, [b_osb, pb], [b_ot])
        oT(g, otile, b_ot)


def l1_consts(S):
    half = 32
    inv = (10000.0 ** (-np.arange(half, dtype=np.float32) / half)).astype(np.float32)
    ang = np.arange(S, dtype=np.float32)[None, :] * inv[:, None]
    cos = np.cos(ang).astype(np.float32)
    sin = np.sin(ang).astype(np.float32)
    cosd = np.concatenate([cos, cos], 0)
    sind = np.concatenate([sin, -sin], 0)
    kk = np.arange(128)[:, None]
    qq = np.arange(512)[None, :]
    cm = np.concatenate([np.where(qq < j * 128 + kk, NEG, 0.0) for j in range(4)], 1).astype(np.float32)
    ed = (np.arange(S)[None, :] // 256 == np.arange(64)[:, None]).astype(np.float32)
    ident = np.eye(128, dtype=np.float32)
    rm = np.tile((np.arange(512) % 64 != 0).astype(np.float32)[None, :], (64, 1))
    s_ = np.arange(64)[:, None]
    t_ = np.arange(64)[None, :]
    am = np.tile((s_ <= t_).astype(np.float32), (1, 8))
    hm = np.zeros((64, 2), np.float32)
    hm[0:32, 0] = 1
    hm[32:64, 1] = 1
    hm2 = np.repeat(hm, 64, axis=1)
    return dict(cosd=cosd, sind=sind, cmd=cm, ed=ed, identd=ident, rmd=rm, amd=am, hm2d=hm2, hmd=hm)


HALO = 8
OCS = 2048


def choose_tiles(W):
    if W == 4104:
        return 4, 3, 342
    if W == 520:
        return 2, 1, 260
    raise ValueError(W)


class Trunk:
    def __init__(self, k, W):
        self.k = k
        self.nc = k.nc
        self.W = W
        self.NB, self.NTB, self.TW = choose_tiles(W)
        self.TB = self.NTB * self.TW
        TB, TW = self.TB, self.TW
        self.h, self.b_h = k.sb([128, 8, TB], F32, "h")
        self.hn, self.b_hn = k.sb([128, 8, TB], BF16, "hn")
        self.act, self.b_act = k.sb([128, 24, TB], BF16, "act")
        self.wst = [k.sb([128, 8, 128], F32, "wst%d" % i) for i in range(3)]
        self.wbf = [k.sb([128, 24, 128], BF16, "wbf%d" % i) for i in range(3)]
        self.wi = 0
        self.si = 0
        self.wq = []
        self.pc = [k.sb([128, 4 + TB], F32, "pc%d" % i) for i in range(2)]
        self.pci = 0
        self.yt = [k.sb([128, TB], F32, "yt%d" % i) for i in range(2)]
        self.gel, self.b_gel = k.sb([128, TB], F32, "gel")
        self.sqt, self.b_sqt = k.sb([128, 8, TW], BF16, "sqt")
        self.lnv, self.b_lnv = k.sb([128, TW], F32, "lnvt")
        self.rstd, self.b_rstd = k.sb([128, TW], F32, "rstdt")
        self.ones_b, self.b_ones = k.sb([128, 128], BF16, "onesb")
        self.m, self.b_m = k.sb([128, 1], F32, "hmask")
        self.carry, self.b_carry = k.sb([128, 48, 2], F32, "carry")
        self.B = [k.ps([128, 512], "bank%d" % i) for i in range(8)]
        self.bi = 0
        k.memset(self.ones_b[:], 1.0, [self.b_ones])
        k.memset(self.carry[:], 0.0, [self.b_carry])

    def bank(self):
        b = self.B[self.bi]
        self.bi = (self.bi + 1) % 8
        return b

    def small(self, name, dram_ap, shape):
        t, b = self.k.sb(shape, F32, name)
        self.k.dma(t[:], dram_ap, (), [b])
        return t, b

    def request(self, wd, KC, col0):
        k = self.k
        wb, b_wb = self.wbf[self.wi]
        self.wi = (self.wi + 1) % 3
        wv = wd.rearrange("(kc p) f -> p kc f", p=128)
        for k0 in range(0, KC, 8):
            k1 = min(KC, k0 + 8)
            st, b_st = self.wst[self.si]
            self.si = (self.si + 1) % 3
            k.dma(st[:, 0:k1 - k0, :], wv[:, k0:k1, col0:col0 + 128], (), [b_st])
            k.cp(wb[:, k0:k1, :], st[:, 0:k1 - k0, :], [b_st], [b_wb], eng="act")
        self.wq.append((wb, b_wb))

    def run_stage(self, reqs, body):
        LA = 2
        for r in reqs[:LA]:
            self.request(*r)
        for i in range(len(reqs)):
            if i + LA < len(reqs):
                self.request(*reqs[i + LA])
            body(i)

    def rmsnorm(self, g_t, b_g, out_f32=None):
        k = self.k
        TW = self.TW
        for nt in range(self.NTB):
            cs = slice(nt * TW, (nt + 1) * TW)
            k.act(self.sqt[:], self.h[:, :, cs], AF.Square, [self.b_h], [self.b_sqt])
            pt, pb = self.bank()
            for kc in range(8):
                k.mm(pt[:, 0:TW], self.ones_b[:], self.sqt[:, kc, :], kc == 0, kc == 7, [self.b_ones, self.b_sqt], [pb])
            k.act(self.lnv[:], pt[:, 0:TW], AF.Ln, [pb], [self.b_lnv], bias=EPS, scale=1.0 / 1024)
            k.act(self.rstd[:], self.lnv[:], AF.Exp, [self.b_lnv], [self.b_rstd], scale=-0.5)
            for kc in range(8):
                if out_f32 is None:
                    k.stt(self.hn[:, kc, cs], self.h[:, kc, cs], g_t[:, kc:kc + 1], self.rstd[:], ALU.mult, ALU.mult,
                          [self.b_h, b_g, self.b_rstd], [self.b_hn])
                else:
                    k.stt(out_f32[0][:, kc, cs], self.h[:, kc, cs], g_t[:, kc:kc + 1], self.rstd[:], ALU.mult, ALU.mult,
                          [self.b_h, b_g, self.b_rstd], [out_f32[1]])

    def linear(self, src, b_src, KC, wd, col0, evac):
        k = self.k
        TW = self.TW
        wb, b_wb = self.wq.pop(0)
        for nt in range(self.NTB):
            cs = slice(nt * TW, (nt + 1) * TW)
            pt, pb = self.bank()
            for kc in range(KC):
                k.mm(pt[:, 0:TW], wb[:, kc, :], src[:, kc, cs], kc == 0, kc == KC - 1, [b_wb, b_src], [pb])
            evac(nt, cs, pt[:, 0:TW], pb)

    def conv(self, pc, b_pc, ntap, cw_t, b_cw, cb_t, b_cb, idx, yt, b_yt):
        k = self.k
        TB = self.TB
        last = ntap - 1
        k.act(yt[:], pc[:, last:last + TB], AF.Identity, [b_pc, b_cw, b_cb], [b_yt],
              bias=cb_t[:, idx:idx + 1], scale=cw_t[:, idx, last:last + 1])
        for i in range(last - 1, -1, -1):
            k.stt(yt[:], pc[:, i:i + TB], cw_t[:, idx, i:i + 1], yt[:], ALU.mult, ALU.add, [b_pc, b_cw, b_yt], [b_yt])

    def ffn(self, first, gn_t, b_gn, w_up, cw_t, b_cw, cb_t, b_cb, w_down, mask_halo):
        k = self.k
        TB, TW = self.TB, self.TW
        self.rmsnorm(gn_t, b_gn)
        ys = [None, None]

        def up_body(i):
            j, part = i // 2, i % 2
            idx = part * 24 + j
            pc, b_pc = self.pc[self.pci]
            self.pci = (self.pci + 1) % 2
            yt, b_yt = self.yt[part]
            k.cp(pc[:, 0:2], self.carry[:, idx, :], [self.b_carry], [b_pc], eng="act")

            def evac(nt, cs, ps, pb, pc=pc, b_pc=b_pc):
                k.cp(pc[:, 2 + cs.start:2 + cs.stop], ps, [pb], [b_pc], eng="act")

            self.linear(self.hn, self.b_hn, 8, w_up, idx * 128, evac)
            if first and mask_halo:
                k.ts(pc[:, 2:2 + HALO], pc[:, 2:2 + HALO], self.m[:, 0:1], ALU.mult, [b_pc, self.b_m], [b_pc])
            k.cp(self.carry[:, idx, :], pc[:, TB:TB + 2], [b_pc], [self.b_carry], eng="act")
            self.conv(pc, b_pc, 3, cw_t, b_cw, cb_t, b_cb, idx, yt, b_yt)
            ys[part] = (yt, b_yt)
            if part == 1:
                (yu, b_yu), (yg, b_yg) = ys
                k.act(self.gel[:], yg[:], AF.Gelu_apprx_tanh, [b_yg], [self.b_gel])
                k.tt(self.act[:, j, :], yu[:], self.gel[:], ALU.mult, [b_yu, self.b_gel], [self.b_act])

        self.run_stage([(w_up, 8, (part * 24 + j) * 128) for j in range(24) for part in range(2)], up_body)

        def down_body(fc):
            def evac(nt, cs, ps, pb, fc=fc):
                k.tt(self.h[:, fc, cs], self.h[:, fc, cs], ps, ALU.add, [self.b_h, pb], [self.b_h])

            self.linear(self.act, self.b_act, 24, w_down, fc * 128, evac)

        self.run_stage([(w_down, 24, fc * 128) for fc in range(8)], down_body)


def l2_io(k, W):
    d = {}
    for name, shape in (("xTw", [1024, W]), ("m", [128, 1]), ("sel", [128, 4]), ("w_out0", [1024, 1024]), ("fg0", [128, 8]),
                        ("w_up0", [1024, 6144]), ("cw0", [128, 48, 3]), ("cb0", [128, 48]), ("w_down0", [3072, 1024]),
                        ("mg", [128, 8]), ("w_in", [1024, 2048]), ("rcw", [128, 8, 4]), ("rcb", [128, 8]), ("wa", [1024, 256]),
                        ("ba", [128, 8]), ("wx", [1024, 256]), ("bx", [128, 8]), ("lam", [128, 8])):
        d[name] = k.din(name, shape)
    return d


def emit_l2(k, W, S, d, og, h1_o, gg_o, hl_o, pl_o, ex_o):
    nc = k.nc
    T = Trunk(k, W)
    NB, TB, TW = T.NB, T.TB, T.TW
    TOK = W - HALO
    xTw, m_d, sel_d, w_out, fg_d, w_up, cw_d, cb_d, w_down = (d["xTw"], d["m"], d["sel"], d["w_out0"], d["fg0"], d["w_up0"],
                                                               d["cw0"], d["cb0"], d["w_down0"])
    mg_d, w_in, rcw_d, rcb_d, wa_d, ba_d, wx_d, bx_d, lam_d = (d["mg"], d["w_in"], d["rcw"], d["rcb"], d["wa"], d["ba"],
                                                               d["wx"], d["bx"], d["lam"])
    sel, b_sel = T.small("sel", sel_d[:, :], [128, 4])

    k.dma(T.m[:], m_d[:, :], (), [T.b_m])
    fg, b_fg = T.small("fg", fg_d[:, :], [128, 8])
    cw, b_cw = T.small("cw", cw_d[:, :, :], [128, 48, 3])
    cb, b_cb = T.small("cb", cb_d[:, :], [128, 48])
    mg, b_mg = T.small("mg", mg_d[:, :], [128, 8])
    rcw, b_rcw = T.small("rcw", rcw_d[:, :, :], [128, 8, 4])
    rcb, b_rcb = T.small("rcb", rcb_d[:, :], [128, 8])
    ba, b_ba = T.small("ba", ba_d[:, :], [128, 8])
    bx, b_bx = T.small("bx", bx_d[:, :], [128, 8])
    lam, b_lam = T.small("lam", lam_d[:, :], [128, 8])
    c1, b_c1 = k.sb([128, 8], F32, "c1")
    c2, b_c2 = k.sb([128, 8], F32, "c2")
    k.act(c1[:], lam[:], AF.Exp, [b_lam], [b_c1], scale=-1.0)
    k.act(c1[:], c1[:], AF.Ln, [b_c1], [b_c1], bias=1.0, scale=1.0)
    k.ts(c2[:], c1[:], -16.0, ALU.mult, [b_c1], [b_c2])
    k.ts(c1[:], c1[:], -8.0, ALU.mult, [b_c1, b_c2], [b_c1])
    ob, b_ob = T.hn, T.b_hn
    gg, b_gg = k.sb([128, TB], BF16, "gg")
    rcar, b_rcar = k.sb([128, 8, 3], F32, "rcar")
    hcar, b_hcar = k.sb([128, 8], F32, "hcar")
    pcar, b_pcar = k.sb([128, 8], F32, "pcar")
    zer, b_zer = k.sb([128, TB], F32, "zer")
    xrc, b_xrc = k.sb([128, 2, TB], F32, "xrc")
    xrb, b_xrb = k.sb([128, 2, TB], BF16, "xrb")
    rr, b_rr = k.sb([128, TB], F32, "rr")
    ii, b_ii = k.sb([128, TB], F32, "ii")
    aa, b_aa = k.sb([128, TB], F32, "aa")
    uu, b_uu = k.sb([128, TB], F32, "uu")
    hl, b_hl = k.sb([128, TB], F32, "hl")
    pl, b_pl = k.sb([128, TB], F32, "pl")
    k.memset(rcar[:], 0.0, [b_rcar])
    k.memset(zer[:], 0.0, [b_zer])
    ext, b_ext = k.sb([128, 8, 2], F32, "ext")

    oTv = [o_.rearrange("(kc p) s -> p kc s", p=128) for o_ in og]
    xTv = xTw.rearrange("(kc p) s -> p kc s", p=128)
    h1v = h1_o.rearrange("(kc p) s -> p kc s", p=128)
    ggv = gg_o.rearrange("(kc p) s -> p kc s", p=128)
    hlv = hl_o.rearrange("(kc p) s -> p kc s", p=128)
    plv = pl_o.rearrange("(kc p) s -> p kc s", p=128)
    exv = ex_o.rearrange("(kc p) s -> p kc s", p=128)

    for blk in range(NB):
        first = blk == 0
        g0 = blk * TB
        for c in range(4):
            cand = T.act[:, 8 * (c % 3):8 * (c % 3) + 8, :]
            lo = c * TOK - HALO + g0
            skip = max(0, -lo)
            if skip:
                k.memset(cand[:, :, 0:skip], 0.0, [T.b_act])
            a = lo + skip
            while a < lo + TB:
                j = a // OCS
                e = min(lo + TB, (j + 1) * OCS)
                k.dma(cand[:, :, a - lo:e - lo], oTv[j][:, :, a - j * OCS:e - j * OCS], (), [T.b_act])
                a = e
            if c == 0:
                k.ts(ob[:], cand, sel[:, 0:1], ALU.mult, [T.b_act, b_sel], [b_ob])
            else:
                k.stt(ob[:], cand, sel[:, c:c + 1], ob[:], ALU.mult, ALU.add, [T.b_act, b_sel, b_ob], [b_ob])
        k.dma(T.h[:, 0:4, :], xTv[:, 0:4, g0:g0 + TB], (), [T.b_h])
        k.dma(T.h[:, 4:8, :], xTv[:, 4:8, g0:g0 + TB], (), [T.b_h])
        def wo_body(fc):
            def evac(nt, cs, ps, pb, fc=fc):
                k.tt(T.h[:, fc, cs], T.h[:, fc, cs], ps, ALU.add, [T.b_h, pb], [T.b_h])

            T.linear(ob, b_ob, 8, w_out, fc * 128, evac)

        T.run_stage([(w_out, 8, fc * 128) for fc in range(8)], wo_body)
        T.ffn(first, fg, b_fg, w_up, cw, b_cw, cb, b_cb, w_down, False)
        k.dma(h1v[:, :, g0:g0 + TB], T.h[:], [T.b_h], ())
        T.rmsnorm(mg, b_mg)
        def gb_body(fc):
            def evac(nt, cs, ps, pb, fc=fc):
                k.act(gg[:, cs], ps, AF.Gelu_apprx_tanh, [pb], [b_gg])

            T.linear(T.hn, T.b_hn, 8, w_in, fc * 128, evac)
            k.dma(ggv[:, fc, g0:g0 + TB], gg[:], [b_gg], ())

        T.run_stage([(w_in, 8, fc * 128) for fc in range(8)], gb_body)
        reqs = []
        for n in range(4):
            reqs += [(w_in, 8, 1024 + (2 * n + c2i) * 128) for c2i in range(2)]
            for c2i in range(2):
                reqs += [(wa_d[n * 256:(n + 1) * 256, :], 2, c2i * 128), (wx_d[n * 256:(n + 1) * 256, :], 2, c2i * 128)]

        def rg_body(i, blk=blk, first=first, g0=g0):
            n, r = i // 6, i % 6
            if r < 2:
                c2i = r
                c8 = 2 * n + c2i
                pc, b_pc = T.pc[T.pci]
                T.pci = (T.pci + 1) % 2
                k.cp(pc[:, 0:3], rcar[:, c8, :], [b_rcar], [b_pc], eng="act")

                def evac(nt, cs, ps, pb, pc=pc, b_pc=b_pc):
                    k.cp(pc[:, 3 + cs.start:3 + cs.stop], ps, [pb], [b_pc], eng="act")

                T.linear(T.hn, T.b_hn, 8, w_in, 1024 + c8 * 128, evac)
                if first:
                    k.ts(pc[:, 3:3 + HALO], pc[:, 3:3 + HALO], T.m[:, 0:1], ALU.mult, [b_pc, T.b_m], [b_pc])
                k.cp(rcar[:, c8, :], pc[:, TB:TB + 3], [b_pc], [b_rcar], eng="act")
                yt, b_yt = T.yt[c2i]
                T.conv(pc, b_pc, 4, rcw, b_rcw, rcb, b_rcb, c8, yt, b_yt)
                k.cp(xrc[:, c2i, :], yt[:], [b_yt], [b_xrc], eng="pool")
                k.cp(xrb[:, c2i, :], yt[:], [b_yt], [b_xrb], eng="act")
                return
            c2i, which = (r - 2) // 2, (r - 2) % 2
            fc = 2 * n + c2i
            wd, bias_t, b_bias, dst, b_dst = ((wa_d, ba, b_ba, rr, b_rr), (wx_d, bx, b_bx, ii, b_ii))[which]

            def evac(nt, cs, ps, pb, dst=dst, b_dst=b_dst, bias_t=bias_t, b_bias=b_bias, fc=fc):
                k.act(dst[:, cs], ps, AF.Sigmoid, [pb, b_bias], [b_dst], bias=bias_t[:, fc:fc + 1], scale=1.0)

            T.linear(xrb, b_xrb, 2, wd[n * 256:(n + 1) * 256, :], c2i * 128, evac)
            if which == 0:
                return
            k.act(aa[:], rr[:], AF.Exp, [b_rr, b_c1], [b_aa], scale=c1[:, fc:fc + 1])
            k.act(rr[:], rr[:], AF.Exp, [b_rr, b_c2], [b_rr], scale=c2[:, fc:fc + 1])
            k.act(rr[:], rr[:], AF.Sqrt, [b_rr], [b_rr], bias=1.0, scale=-1.0)
            k.tt(uu[:], xrc[:, c2i, :], ii[:], ALU.mult, [b_xrc, b_ii], [b_uu])
            k.tt(uu[:], uu[:], rr[:], ALU.mult, [b_uu, b_rr], [b_uu])
            if first:
                k.ts(uu[:, 0:HALO], uu[:, 0:HALO], T.m[:, 0:1], ALU.mult, [b_uu, T.b_m], [b_uu])
                k.memset(hl[:, 0:5], 0.0, [b_hl])
                k.memset(pl[:, 0:5], 0.0, [b_pl])
                s0, hi, pi = 5, 0.0, 1.0
            else:
                s0, hi, pi = 0, hcar[:, fc:fc + 1], pcar[:, fc:fc + 1]
            k.scan(hl[:, s0:TB], aa[:, s0:TB], uu[:, s0:TB], hi, [b_aa, b_uu, b_hcar], [b_hl])
            k.scan(pl[:, s0:TB], aa[:, s0:TB], zer[:, s0:TB], pi, [b_aa, b_zer, b_pcar], [b_pl])
            k.cp(hcar[:, fc:fc + 1], hl[:, TB - 1:TB], [b_hl], [b_hcar], eng="pool")
            k.cp(pcar[:, fc:fc + 1], pl[:, TB - 1:TB], [b_pl], [b_pcar], eng="pool")
            k.dma(hlv[:, fc, g0:g0 + TB], hl[:], [b_hl], ())
            k.dma(plv[:, fc, g0:g0 + TB], pl[:], [b_pl], ())
            if blk == NB - 1:
                k.cp(ext[:, fc, 0:1], hl[:, TB - 4:TB - 3], [b_hl], [b_ext], eng="pool")
                k.cp(ext[:, fc, 1:2], pl[:, TB - 4:TB - 3], [b_pl], [b_ext], eng="pool")

        T.run_stage(reqs, rg_body)
    k.dma(exv[:, :, :], ext[:], [b_ext], ())


def l3_io(k, W):
    d = {}
    for name, shape in (("srank", [128, 4]), ("oms", [128, 4]), ("w_out1", [1024, 1024]), ("fg1", [128, 8]),
                        ("w_up1", [1024, 6144]), ("cw1", [128, 48, 3]), ("cb1", [128, 48]), ("w_down1", [3072, 1024]),
                        ("fin", [128, 8])):
        d[name] = k.din(name, shape)
    return d


def emit_l3(k, W, d, m_d, h1w, ggw, hlw, plw, exg, out_o):
    nc = k.nc
    T = Trunk(k, W)
    NB, TB, TW = T.NB, T.TB, T.TW
    TOK = W - HALO
    w_out, fg_d, w_up, cw_d, cb_d, w_down, fin_d = (d["w_out1"], d["fg1"], d["w_up1"], d["cw1"], d["cb1"], d["w_down1"], d["fin"])
    k.dma(T.m[:], m_d[:, :], (), [T.b_m])
    fg, b_fg = T.small("fg", fg_d[:, :], [128, 8])
    cw, b_cw = T.small("cw", cw_d[:, :, :], [128, 48, 3])
    cb, b_cb = T.small("cb", cb_d[:, :], [128, 48])
    fin, b_fin = T.small("fin", fin_d[:, :], [128, 8])
    sr, b_sr = T.small("sr", d["srank"][:, :], [128, 4])
    oms, b_oms = T.small("oms", d["oms"][:, :], [128, 4])
    pe, b_pe = k.sb([128, 4, 8, 2], F32, "pe")
    exv = exg.rearrange("(r kc p) t -> p r kc t", p=128, kc=8)
    for r in range(4):
        k.dma(pe[:, r, :, :], exv[:, r, :, :], (), [b_pe])
    Hc, b_Hc = k.sb([128, 8], F32, "Hc")
    Pm, b_Pm = k.sb([128, 8], F32, "Pm")
    Em, b_Em = k.sb([128, 8], F32, "Em")
    k.memset(Hc[:], 0.0, [b_Hc])
    for r in range(4):
        k.ts(Pm[:], pe[:, r, :, 1], sr[:, r:r + 1], ALU.mult, [b_pe, b_sr], [b_Pm], s2=oms[:, r:r + 1], op1=ALU.add)
        k.ts(Em[:], pe[:, r, :, 0], sr[:, r:r + 1], ALU.mult, [b_pe, b_sr], [b_Em])
        k.tt(Hc[:], Hc[:], Pm[:], ALU.mult, [b_Hc, b_Pm], [b_Hc])
        k.tt(Hc[:], Hc[:], Em[:], ALU.add, [b_Hc, b_Em], [b_Hc])
    yb, b_yb = T.hn, T.b_hn
    hl, b_hl = k.sb([128, TB], F32, "hl")
    pl, b_pl = k.sb([128, TB], F32, "pl")
    ggt, b_ggt = k.sb([128, TB], BF16, "ggt")
    outt, b_outt = k.sb([128, 8, TB], F32, "outt")

    h1v = h1w.rearrange("(kc p) s -> p kc s", p=128)
    ggv = ggw.rearrange("(kc p) s -> p kc s", p=128)
    hlv = hlw.rearrange("(kc p) s -> p kc s", p=128)
    plv = plw.rearrange("(kc p) s -> p kc s", p=128)
    outv = out_o.rearrange("(kc p) s -> p kc s", p=128)

    for blk in range(NB):
        first = blk == 0
        g0 = blk * TB
        k.dma(T.h[:, 0:4, :], h1v[:, 0:4, g0:g0 + TB], (), [T.b_h])
        k.dma(T.h[:, 4:8, :], h1v[:, 4:8, g0:g0 + TB], (), [T.b_h])
        for fc in range(8):
            k.dma(hl[:], hlv[:, fc, g0:g0 + TB], (), [b_hl])
            k.dma(pl[:], plv[:, fc, g0:g0 + TB], (), [b_pl])
            k.dma(ggt[:], ggv[:, fc, g0:g0 + TB], (), [b_ggt])
            k.stt(hl[:], pl[:], Hc[:, fc:fc + 1], hl[:], ALU.mult, ALU.add, [b_pl, b_Hc, b_hl], [b_hl])
            k.tt(yb[:, fc, :], hl[:], ggt[:], ALU.mult, [b_hl, b_ggt], [b_yb])
        def wo_body(fc):
            def evac(nt, cs, ps, pb, fc=fc):
                k.tt(T.h[:, fc, cs], T.h[:, fc, cs], ps, ALU.add, [T.b_h, pb], [T.b_h])

            T.linear(yb, b_yb, 8, w_out, fc * 128, evac)

        T.run_stage([(w_out, 8, fc * 128) for fc in range(8)], wo_body)
        T.ffn(first, fg, b_fg, w_up, cw, b_cw, cb, b_cb, w_down, True)
        T.rmsnorm(fin, b_fin, out_f32=(outt, b_outt))
        lo = HALO if first else 0
        k.dma(outv[:, :, g0 + lo - HALO:g0 + TB - HALO], outt[:, :, lo:TB], [b_outt], ())


_F = {}


def build_fused(S):
    TOK = S // 4
    W = TOK + HALO
    nc = bass.Bass("TRN2", target_bir_lowering=False)
    k = K(nc)
    io1 = l1_io(k, S)
    io2 = l2_io(k, W)
    io3 = l3_io(k, W)
    NCH = max(1, S // OCS)
    osrc = [k.dint("osrc%d" % j, [256, min(S, OCS)], BF16) for j in range(NCH)]
    og = [k.dint("og%d" % j, [1024, min(S, OCS)], BF16) for j in range(NCH)]
    h1 = k.dint("h1s", [1024, W])
    gg = k.dint("ggs", [1024, W], BF16)
    hl = k.dint("hls", [1024, W])
    pl = k.dint("pls", [1024, W])
    exs = k.dint("exs", [1024, 2])
    exg = k.dint("exg", [4096, 2])
    out = k.dout("outT", [1024, TOK])
    groups = [[0, 1, 2, 3], [4, 5, 6, 7]]
    upto = int(os.environ.get("FUSE_UPTO", "3"))
    tile_bufs = {}

    def o_out(g, otile, b_ot):
        j, off = (g * 512) // OCS, (g * 512) % OCS
        ov = osrc[j].rearrange("(r p) s -> p r s", p=64)
        bb = Buf()
        tile_bufs.setdefault(j, []).append(bb)
        k.dma(ov[:, :, off:off + 512], otile[:], [b_ot], [bb])
        if off + 512 == min(S, OCS):
            k.P.dma((lambda j=j: nc.gpsimd.collective_compute("AllGather", ALU.bypass, replica_groups=groups,
                                                               ins=[osrc[j][:, :]], outs=[og[j][:, :]])),
                    tile_bufs[j], (), eng="pool", inc=1)

    emit_l1(k, S, io1, o_out)
    k.end_phase()
    emit_l2(k, W, S, io2, og, h1, gg, hl, pl, exs)
    k.end_phase()
    if upto == 2:
        return nc, k.P.finish()
    if os.environ.get("NOCC2"):
        k.dma(exg[0:1024, :], exs[:, :], (), ())
    else:
        k.P.dma(lambda: nc.gpsimd.collective_compute("AllGather", ALU.bypass, replica_groups=groups, ins=[exs[:, :]], outs=[exg[:, :]]),
                (), (), eng="pool", inc=1)
    k.P.barrier()
    emit_l3(k, W, io3, io2["m"], h1, gg, hl, pl, exg, out)
    cnt = k.P.finish()
    return nc, cnt


def _pk(v):
    return np.ascontiguousarray(v.reshape(-1, 128).T)


def _pkw(w):
    t, C = w.shape
    return np.ascontiguousarray(w.T.reshape(C // 128, 128, t).transpose(1, 0, 2))


def kernel(x, mix_norm_g, ffn_norm_g, final_norm_g,
           ev_w_in, ev_w_gk2, ev_b_gk2, ev_gla_norm_g, ev_w_out,
           od_w_in, od_conv_w, od_conv_b, od_w_a, od_b_a, od_w_x, od_b_x, od_lambda, od_w_out,
           ffn_w_up, ffn_conv_w, ffn_conv_b, ffn_w_down):
    f = lambda a: np.asarray(a, dtype=np.float32)
    x = f(x)
    Bsz, S, D = x.shape
    TOK = S // 4
    W = TOK + HALO
    if S not in _F:
        _F[S] = build_fused(S)
    nc, _ = _F[S]
    consts = l1_consts(S)
    w = f(ev_w_in)[0]
    gn = _pk(f(mix_norm_g)[0])
    perm = []
    for g in range(4):
        for j in range(4):
            head = 2 * g + (j % 2)
            base = head * 64 if j < 2 else 512 + head * 64
            perm += list(range(base, base + 64))
    w_out0 = np.ascontiguousarray(f(ev_w_out)[0][np.asarray(perm)])
    shared = {
        "gn": gn, "w_out0": w_out0, "fg0": _pk(f(ffn_norm_g)[0]), "w_up0": f(ffn_w_up)[0],
        "cw0": _pkw(f(ffn_conv_w)[0]), "cb0": _pk(f(ffn_conv_b)[0]), "w_down0": f(ffn_w_down)[0],
        "mg": _pk(f(mix_norm_g)[1]), "w_in": f(od_w_in)[0], "rcw": _pkw(f(od_conv_w)[0]), "rcb": _pk(f(od_conv_b)[0]),
        "wa": np.ascontiguousarray(f(od_w_a)[0].reshape(1024, 256)), "ba": _pk(f(od_b_a)[0]),
        "wx": np.ascontiguousarray(f(od_w_x)[0].reshape(1024, 256)), "bx": _pk(f(od_b_x)[0]),
        "lam": _pk(f(od_lambda)[0]),
        "w_out1": f(od_w_out)[0], "fg1": _pk(f(ffn_norm_g)[1]), "w_up1": f(ffn_w_up)[1],
        "cw1": _pkw(f(ffn_conv_w)[1]), "cb1": _pk(f(ffn_conv_b)[1]), "w_down1": f(ffn_w_down)[1],
        "fin": _pk(f(final_norm_g)),
    }
    shared.update(consts)
    in_maps = []
    for b in range(Bsz):
        xTb = np.ascontiguousarray(x[b].T)
        for c in range(4):
            h0, h1 = 2 * c, 2 * c + 1
            cols = []
            for base in (0, 512, 1024):
                cols += [np.arange(base + h0 * 64, base + h0 * 64 + 64), np.arange(base + h1 * 64, base + h1 * 64 + 64)]
            for base in (1536, 1792):
                cols += [np.arange(base + h0 * 32, base + h0 * 32 + 32), np.arange(base + h1 * 32, base + h1 * 32 + 32)]
            cols += [np.arange(2048 + h0 * 64, 2048 + h0 * 64 + 64), np.arange(2048 + h1 * 64, 2048 + h1 * 64 + 64)]
            cols += [np.arange(2560, 2576)]
            cols += [np.arange(2576 + h0 * 64, 2576 + h0 * 64 + 64), np.arange(2576 + h1 * 64, 2576 + h1 * 64 + 64)]
            cols = np.concatenate(cols)
            t0 = c * TOK
            xw = np.zeros((1024, W), np.float32)
            if c == 0:
                xw[:, HALO:] = xTb[:, 0:TOK]
            else:
                xw[:] = xTb[:, t0 - HALO:t0 + TOK]
            sel = np.zeros((128, 4), np.float32)
            sel[:, c] = 1.0
            sr = np.zeros((128, 4), np.float32)
            sr[:, :c] = 1.0
            m = dict(shared)
            m.update({
                "xT": xTb, "w1": np.ascontiguousarray(w[:, cols]),
                "wgk2": np.ascontiguousarray(f(ev_w_gk2)[0][:, h0 * 32:h0 * 32 + 64]),
                "bgk2": np.ascontiguousarray(f(ev_b_gk2)[0][h0 * 32:h0 * 32 + 64].reshape(64, 1)),
                "glag": np.ascontiguousarray(f(ev_gla_norm_g)[0][h0:h0 + 2].T),
                "xTw": xw, "m": np.full((128, 1), 0.0 if c == 0 else 1.0, np.float32),
                "sel": sel, "srank": sr, "oms": 1.0 - sr,
            })
            in_maps.append(m)
    res = run_bass_kernel_spmd(nc, in_maps, core_ids=list(range(len(in_maps)))).results
    out = np.zeros((Bsz, S, D), np.float32)
    for b in range(Bsz):
        for c in range(4):
            out[b, c * TOK:(c + 1) * TOK, :] = np.asarray(res[b * 4 + c]["outT"]).T
    return out
```

```python
import contextlib
import os
import numpy as np
import ml_dtypes
import concourse.bass as bass
import concourse.mybir as mybir
from concourse.bass_utils import run_bass_kernel_spmd

F32 = mybir.dt.float32
BF16 = mybir.dt.bfloat16
AF = mybir.ActivationFunctionType
ALU = mybir.AluOpType
AX = mybir.AxisListType

SAME_SYNC = True
N_DMA_SEMS = 24
SEM_EPOCH = 2000
NEG = -240000.0
EPS = 1e-6


class Buf:
    __slots__ = ("name", "lw", "rd")

    def __init__(self, name=""):
        self.name = name
        self.lw = None
        self.rd = []


class Prog:
    ENGS = ("pe", "act", "dve", "pool", "sp")

    def __init__(self, nc):
        self.nc = nc
        self.h = {"pe": nc.tensor, "act": nc.scalar, "dve": nc.vector, "pool": nc.gpsimd, "sp": nc.sync}
        self.items = {e: [] for e in self.ENGS}
        self.known = {}
        self.sem = {e: nc.alloc_semaphore("s_" + e) for e in self.ENGS}
        self.dsem = [nc.alloc_semaphore("d%d" % i) for i in range(N_DMA_SEMS)]
        self.duse = [0] * N_DMA_SEMS
        self.dval = [0] * N_DMA_SEMS
        self.pending = {e: [] for e in self.ENGS}
        self.rank = {e: [] for e in self.ENGS}
        self.emitted = {e: 0 for e in self.ENGS}
        self.esem = {}
        self.dnext = 0

    def _need(self, eng, tok, waits):
        if tok is None:
            return
        if tok[0] == "c":
            _, e2, seq = tok
            if e2 == eng and (eng == "pe" or not SAME_SYNC):
                return
            key = (eng, "c", e2)
            if self.known.get(key, -1) >= seq:
                return
            self.known[key] = seq
            self.items[e2][seq]["flag"] = True
            waits.append(tok)
        else:
            _, idx, val = tok
            key = (eng, "d", idx)
            if self.known.get(key, -1) >= val:
                return
            self.known[key] = val
            waits.append(tok)

    def _deps(self, eng, reads, writes):
        waits = []
        for b in reads:
            self._need(eng, b.lw, waits)
        for b in writes:
            self._need(eng, b.lw, waits)
            for t in b.rd:
                self._need(eng, t, waits)
        return waits

    def op(self, eng, fn, reads=(), writes=()):
        waits = self.pending[eng] + self._deps(eng, reads, writes)
        self.pending[eng] = []
        seq = len(self.items[eng])
        self.items[eng].append({"waits": waits, "fn": fn, "flag": False, "dma": None})
        tok = ("c", eng, seq)
        for b in reads:
            b.rd.append(tok)
        for b in writes:
            b.lw = tok
            b.rd = []
        return tok

    def dma(self, fn, reads=(), writes=(), eng="sp", inc=16):
        idx = self.dnext
        self.dnext = (self.dnext + 1) % N_DMA_SEMS
        pv = self.dval[idx]
        waits = self.pending[eng] + self._deps(eng, reads, writes)
        self.pending[eng] = []
        if pv > 0:
            self._need(eng, ("d", idx, pv), waits)
        self.duse[idx] += 1
        self.dval[idx] = pv + inc
        tok = ("d", idx, pv + inc)
        self.items[eng].append({"waits": waits, "fn": fn, "flag": False, "dma": idx, "inc": inc})
        for b in reads:
            b.rd.append(tok)
        for b in writes:
            b.lw = tok
            b.rd = []
        return tok

    def _all_tokens(self):
        toks = []
        for e in ("pe", "act", "dve", "pool"):
            items = self.items[e]
            for i in range(len(items) - 1, -1, -1):
                if items[i]["dma"] is None and items[i]["fn"] is not None or (items[i]["dma"] is None and i < self.emitted[e]):
                    toks.append(("c", e, i))
                    break
        for idx in range(N_DMA_SEMS):
            if self.dval[idx]:
                toks.append(("d", idx, self.dval[idx]))
        return toks

    def barrier(self, engines=None):
        toks = self._all_tokens()
        for e in (engines or self.ENGS):
            for t in toks:
                if t[0] == "c" and t[1] == e:
                    continue
                self._need(e, t, self.pending[e])

    def _sem_of(self, e, r):
        ep = (r - 1) // SEM_EPOCH
        if (e, ep) not in self.esem:
            self.esem[(e, ep)] = self.sem[e] if ep == 0 else self.nc.alloc_semaphore("s_%s_%d" % (e, ep))
        return self.esem[(e, ep)], (r - 1) % SEM_EPOCH + 1

    def flush(self):
        for e in self.ENGS:
            items = self.items[e]
            rk = self.rank[e]
            c = rk[-1] if rk else 0
            for i in range(len(rk), len(items)):
                if items[i]["flag"]:
                    c += 1
                rk.append(c)
        for e in self.ENGS:
            h = self.h[e]
            items = self.items[e]
            for i in range(self.emitted[e], len(items)):
                it = items[i]
                for t in it["waits"]:
                    if t[0] == "c":
                        sm, v = self._sem_of(t[1], self.rank[t[1]][t[2]])
                        h.wait_ge(sm, v)
                    else:
                        h.wait_ge(self.dsem[t[1]], t[2])
                if it["fn"] is None:
                    continue
                ins = it["fn"]()
                if it["dma"] is not None:
                    ins.then_inc(self.dsem[it["dma"]], it["inc"])
                elif it["flag"]:
                    sm, _ = self._sem_of(e, self.rank[e][i])
                    ins.then_inc(sm, 1)
                it["fn"] = None
            self.emitted[e] = len(items)

    def finish(self):
        self.barrier(engines=("sp",))
        self.items["sp"].append({"waits": self.pending["sp"], "fn": None, "flag": False, "dma": None})
        self.pending["sp"] = []
        self.flush()
        return {e: (len(self.items[e]), self.rank[e][-1] if self.rank[e] else 0) for e in self.ENGS}


class K:
    def __init__(self, nc):
        self.nc = nc
        self.P = Prog(nc)
        self._n = 0
        self.stack = contextlib.ExitStack()

    def end_phase(self):
        self.P.barrier()
        self.P.flush()
        self.stack.close()
        self.stack = contextlib.ExitStack()

    def sb(self, shape, dt, name=None):
        self._n += 1
        return self.stack.enter_context(self.nc.sbuf_tensor("sb%d_" % self._n + (name or "t"), list(shape), dt)), Buf(name or "")

    def ps(self, shape, name=None):
        self._n += 1
        return self.stack.enter_context(self.nc.psum_tensor("ps%d_" % self._n + (name or "p"), list(shape), F32)), Buf(name or "")

    def din(self, name, shape, dt=F32):
        return self.nc.dram_tensor(name, list(shape), dt, kind="ExternalInput").ap()

    def dint(self, name, shape, dt=F32, **kw):
        return self.nc.dram_tensor(name, list(shape), dt, kind="Internal", **kw).ap()

    def dout(self, name, shape, dt=F32):
        return self.nc.dram_tensor(name, list(shape), dt, kind="ExternalOutput").ap()

    def mm(self, out, lhsT, rhs, st, sp, r, w):
        nc = self.nc
        return self.P.op("pe", lambda: nc.tensor.matmul(out, lhsT=lhsT, rhs=rhs, start=st, stop=sp), r, w)

    def tr(self, out, in_, ident, r, w):
        nc = self.nc
        return self.P.op("pe", lambda: nc.tensor.transpose(out, in_, ident), r, w)

    def act(self, out, in_, func, r, w, bias=None, scale=None):
        nc = self.nc
        kw = {}
        if bias is not None:
            kw["bias"] = bias
        if scale is not None:
            kw["scale"] = scale
        return self.P.op("act", lambda: nc.scalar.activation(out=out, in_=in_, func=func, **kw), r, w)

    def tt(self, out, in0, in1, op, r, w, eng="dve"):
        h = self.P.h[eng]
        return self.P.op(eng, lambda: h.tensor_tensor(out=out, in0=in0, in1=in1, op=op), r, w)

    def ts(self, out, in0, s1, op0, r, w, s2=None, op1=None, eng="dve"):
        h = self.P.h[eng]
        if op1 is None:
            return self.P.op(eng, lambda: h.tensor_scalar(out=out, in0=in0, scalar1=s1, scalar2=None, op0=op0), r, w)
        return self.P.op(eng, lambda: h.tensor_scalar(out=out, in0=in0, scalar1=s1, scalar2=s2, op0=op0, op1=op1), r, w)

    def stt(self, out, in0, scalar, in1, op0, op1, r, w):
        nc = self.nc
        return self.P.op("dve", lambda: nc.vector.scalar_tensor_tensor(out=out, in0=in0, scalar=scalar, in1=in1, op0=op0, op1=op1), r, w)

    def cp(self, out, in_, r, w, eng="dve"):
        if eng == "act":
            nc = self.nc
            return self.P.op("act", lambda: nc.scalar.copy(out=out, in_=in_), r, w)
        h = self.P.h[eng]
        return self.P.op(eng, lambda: h.tensor_copy(out=out, in_=in_), r, w)

    def memset(self, ap, val, w, eng="pool"):
        h = self.P.h[eng]
        return self.P.op(eng, lambda: h.memset(ap, val), (), w)

    def scan(self, out, d0, d1, init, r, w):
        nc = self.nc
        return self.P.op("dve", lambda: nc.vector.tensor_tensor_scan(out=out, data0=d0, data1=d1, initial=init, op0=ALU.mult, op1=ALU.add), r, w)

    def dma(self, out, in_, r, w, eng="sp"):
        h = self.P.h[eng]
        return self.P.dma(lambda: h.dma_start(out=out, in_=in_), r, w, eng=eng)


def l1_io(k, S):
    d = {}
    for name, shape in (("xT", [1024, S]), ("gn", [128, 8]), ("w1", [1024, 784]), ("wgk2", [16, 64]), ("bgk2", [64, 1]),
                        ("glag", [64, 2]), ("cosd", [64, S]), ("sind", [64, S]), ("cmd", [128, 2048]), ("ed", [64, S]),
                        ("identd", [128, 128]), ("rmd", [64, 512]), ("amd", [64, 512]), ("hm2d", [64, 128]), ("hmd", [64, 2])):
        d[name] = k.din(name, shape)
    return d


def emit_l1(k, S, d, oT):
    NT = S // 512
    NKT = S // 128
    nc = k.nc
    P = k.P
    xT, gn_d, w1_d, wgk2_d, bgk2_d, glag_d = d["xT"], d["gn"], d["w1"], d["wgk2"], d["bgk2"], d["glag"]
    cos_d, sin_d, cm_d, e_d, id_d, rm_d, am_d, hm2_d, hm_d = (d["cosd"], d["sind"], d["cmd"], d["ed"], d["identd"], d["rmd"],
                                                              d["amd"], d["hm2d"], d["hmd"])

    xt, b_xt = k.sb([128, 8, 512], F32, "xt")
    rstd, b_rstd = k.sb([128, 512], F32, "rstd")
    lnv, b_lnv = k.sb([128, 512], F32, "lnv")
    hn, b_hn = k.sb([128, 8, 512], BF16, "hn")
    sq, b_sq = hn, b_hn
    wb, b_wb = k.sb([128, 8, 784], BF16, "wb")
    gn, b_gn = k.sb([128, 8], F32, "gn")
    Kaug = [k.sb([128, S], BF16, "kaug%d" % h) for h in range(2)]
    KB = [[Buf() for _ in range(NT)] for h in range(2)]
    b_kE = [Buf(), Buf()]
    Vaug = [k.sb([128, NKT, 66], BF16, "vaug%d" % h) for h in range(2)]
    VB = [[Buf() for _ in range(NT)] for h in range(2)]
    b_vones = [Buf(), Buf()]
    Qaug = [k.sb([128, 512], BF16, "qaug%d" % h) for h in range(2)]
    kmT = [k.sb([64, 64], BF16, "kmT%d" % h) for h in range(2)]
    km32, b_km32 = k.sb([64, 2], F32, "km32")
    cosT, b_cos = k.sb([64, 512], F32, "cos")
    sinT, b_sin = k.sb([64, 512], F32, "sin")
    t1, b_t1 = k.sb([64, 512], F32, "t1")
    t2, b_t2 = k.sb([64, 512], F32, "t2")
    pTs = [k.sb([128, 512], BF16, "pT%d" % i) for i in range(4)]
    cm, b_cm = k.sb([128, 4, 512], BF16, "cm")
    id_f, b_idf = k.sb([128, 128], F32, "idf")
    id_b, b_idb = k.sb([128, 128], BF16, "idb")
    ones_b, b_onesb = k.sb([128, 128], BF16, "onesb")
    ones_f, b_onesf = k.sb([128, 64], F32, "onesf")
    bq, b_bq = k.sb([128, 4, 128], F32, "bq")
    gsb, b_gsb = k.sb([128, 4, 64], F32, "gsb")
    m8, b_m8 = k.sb([128, 4, 8], F32, "m8")
    rden, b_rden = lnv, b_lnv
    osb, b_osb = t1, b_t1
    otile, b_ot = k.sb([64, 4, 512], BF16, "otile")
    QG32, b_qg = k.sb([64, 512], F32, "qg32")
    KG32, b_kg = k.sb([64, 512], F32, "kg32")
    spl, b_spl = k.sb([64, 512], F32, "spl")
    bpos, b_bpos = k.sb([64, 512], F32, "bpos")
    eb, b_eb = k.sb([64, 512], F32, "eb")
    enb, b_enb = k.sb([64, 512], F32, "enb")
    Ac, b_ac = k.sb([64, 8], F32, "Ac")
    ke32, b_ke = k.sb([64, 512], F32, "ke32")
    qt, b_qt = k.sb([64, 512], BF16, "qt")
    kpad, b_kpad = k.sb([64, 2, 512], BF16, "kpad")
    khat, b_khat = k.sb([64, 512], BF16, "khat")
    rmk, b_rmk = k.sb([64, 512], F32, "rmk")
    amk, b_amk = k.sb([64, 512], BF16, "amk")
    hm2, b_hm2 = k.sb([64, 128], F32, "hm2")
    hm, b_hm = k.sb([64, 2], F32, "hm")
    attm, b_attm = k.sb([64, 2, 512], BF16, "attm")
    gvt, b_gvt = k.sb([64, 8, 128], BF16, "gvt")
    KTt, b_ktt = k.sb([64, 8, 64], BF16, "KTt")
    gk16, b_gk16 = k.sb([16, 512], BF16, "gk16")
    wgk2f, b_wgk2f = k.sb([16, 64], F32, "wgk2f")
    wgk2b, b_wgk2b = k.sb([16, 64], BF16, "wgk2b")
    nbg, b_nbg = k.sb([64, 1], F32, "nbg")
    glag, b_glag = k.sb([64, 2], F32, "glag")
    sbog, b_sbog = k.sb([64, 2, 512], BF16, "sbog")
    st32, b_st32 = k.sb([64, 128], F32, "st32")
    stall, b_stall = k.sb([64, 9, 128], BF16, "stall")
    stmp, b_stmp = k.sb([64, 128], F32, "stmp")
    o32, b_o32 = t1, b_t1
    osq, b_osq = k.sb([64, 512], BF16, "osq")
    on32, b_on32 = t2, b_t2
    B = [k.ps([128, 512], "bank%d" % i) for i in range(8)]

    stg = xt
    k.dma(id_f[:], id_d[:, :], (), [b_idf])
    k.cp(id_b[:], id_f[:], [b_idf], [b_idb])
    k.memset(ones_b[:], 1.0, [b_onesb])
    k.memset(ones_f[:], 1.0, [b_onesf])
    k.memset(bq[:], 0.0, [b_bq])
    k.memset(st32[:], 0.0, [b_st32])
    k.memset(stall[:], 0.0, [b_stall])
    k.dma(gn[:], gn_d[:, :], (), [b_gn])
    k.dma(glag[:], glag_d[:, :], (), [b_glag])
    k.dma(hm2[:], hm2_d[:, :], (), [b_hm2])
    k.dma(hm[:], hm_d[:, :], (), [b_hm])
    k.dma(rmk[:], rm_d[:, :], (), [b_rmk])
    k.dma(wgk2f[:], wgk2_d[:, :], (), [b_wgk2f])
    k.cp(wgk2b[:], wgk2f[:], [b_wgk2f], [b_wgk2b])
    k.dma(nbg[:], bgk2_d[:, :], (), [b_nbg])
    k.ts(nbg[:], nbg[:], -1.0, ALU.mult, [b_nbg], [b_nbg])
    k.dma(t1[:], am_d[:, :], (), [b_t1])
    k.cp(amk[:], t1[:], [b_t1], [b_amk])
    sflat = stg[:].rearrange("p a b -> p (a b)")
    k.dma(sflat[:, 0:2048], cm_d[:, :], (), [b_xt])
    k.cp(cm[:].rearrange("p a b -> p (a b)"), sflat[:, 0:2048], [b_xt], [b_cm])
    w1v = w1_d.rearrange("(kc p) f -> p kc f", p=128)
    for half in range(2):
        sv = sflat[:, 0:4 * 784].rearrange("p (a b) -> p a b", b=784)
        k.dma(sv, w1v[:, half * 4:(half + 1) * 4, :], (), [b_xt])
        k.cp(wb[:, half * 4:(half + 1) * 4, :], sv, [b_xt], [b_wb], eng="dve" if half == 0 else "pool")
    for pc in range(S // 2048):
        k.dma(sflat[64:128, 0:2048], e_d[:, pc * 2048:(pc + 1) * 2048], (), [b_xt])
        for h in range(2):
            k.cp(Kaug[h][0][64:128, pc * 2048:(pc + 1) * 2048], sflat[64:128, 0:2048], [b_xt], [b_kE[h]],
                 eng="dve" if h == 0 else "pool")
    for h in range(2):
        k.memset(Vaug[h][0][:, :, 64:65], 1.0, [b_vones[h]])
        k.memset(kmT[h][0][:], 0.0, [kmT[h][1]])

    xTv = xT.rearrange("(kc p) s -> p kc s", p=128)
    def proj(bank, M, col0, ncols=None):
        pt, pb = B[bank]
        for kc in range(8):
            k.mm(pt[0:M, 0:512], wb[:, kc, col0:col0 + M], hn[:, kc, :], kc == 0, kc == 7, [b_wb, b_hn], [pb])
        return pt, pb

    for g in range(NT):
        c0 = g * 512
        k.dma(xt[:, 0:4, :], xTv[:, 0:4, c0:c0 + 512], (), [b_xt])
        k.dma(xt[:, 4:8, :], xTv[:, 4:8, c0:c0 + 512], (), [b_xt])
        k.dma(cosT[:], cos_d[:, c0:c0 + 512], (), [b_cos])
        k.dma(sinT[:], sin_d[:, c0:c0 + 512], (), [b_sin])
        k.act(sq[:], xt[:], AF.Square, [b_xt], [b_sq])
        pt, pb = B[0]
        for kc in range(8):
            k.mm(pt[:, :], ones_b[:], sq[:, kc, :], kc == 0, kc == 7, [b_onesb, b_sq], [pb])
        k.act(lnv[:], pt[:, :], AF.Ln, [pb], [b_lnv], bias=EPS, scale=1.0 / 1024)
        k.act(rstd[:], lnv[:], AF.Exp, [b_lnv], [b_rstd], scale=-0.5)
        for kc in range(8):
            k.stt(hn[:, kc, :], xt[:, kc, :], gn[:, kc:kc + 1], rstd[:], ALU.mult, ALU.mult, [b_xt, b_gn, b_rstd], [b_hn])
        for idx in range(4):
            h = idx % 2
            isk = idx >= 2
            pt, pb = proj(1 + (idx % 2), 64, idx * 64)
            if isk:
                dest = Kaug[h][0][0:64, c0:c0 + 512]
                dbuf = KB[h][g]
            else:
                dest = Qaug[h][0][0:64, :]
                dbuf = Qaug[h][1]
            k.tt(t1[:], pt[0:64, 0:512], cosT[:], ALU.mult, [pb, b_cos], [b_t1])
            k.tt(t2[0:32, :], pt[32:64, 0:512], sinT[32:64, :], ALU.mult, [pb, b_sin], [b_t2])
            k.tt(t2[32:64, :], pt[0:32, 0:512], sinT[0:32, :], ALU.mult, [pb, b_sin], [b_t2])
            k.tt(dest, t1[:], t2[:], ALU.add, [b_t1, b_t2], [dbuf])
        pt, pb = B[1]
        for st in range(4):
            for kc in range(8):
                k.mm(pt[:, st * 128:(st + 1) * 128], hn[:, kc, st * 128:(st + 1) * 128], wb[:, kc, 256:384],
                     kc == 0, kc == 7, [b_hn, b_wb], [pb])
        pv = pt[:, 0:512].rearrange("p (a b) -> p a b", b=128)
        for h in range(2):
            k.cp(Vaug[h][0][:, 4 * g:4 * g + 4, 0:64], pv[:, :, h * 64:(h + 1) * 64], [pb], [VB[h][g]], eng="act")
        pt, pb = proj(2, 128, 384)
        k.cp(QG32[:], pt[0:64, 0:512], [pb], [b_qg], eng="act")
        k.cp(KG32[:], pt[64:128, 0:512], [pb], [b_kg], eng="act")
        for c in range(8):
            pt, pb = B[3 + c // 4]
            for kc in range(8):
                k.mm(pt[0:64, (c % 4) * 128:(c % 4 + 1) * 128], hn[:, kc, c * 64:(c + 1) * 64], wb[:, kc, 512:640],
                     kc == 0, kc == 7, [b_hn, b_wb], [pb])
        for hf in range(2):
            pt, pb = B[3 + hf]
            k.cp(gvt[:, hf * 4:(hf + 1) * 4, :].rearrange("p a b -> p (a b)"), pt[0:64, 0:512], [pb], [b_gvt], eng="act")
        pt, pb = proj(0, 16, 640)
        k.cp(gk16[:], pt[0:16, 0:512], [pb], [b_gk16], eng="act")
        k.mm(pt[0:64, 0:512], wgk2b[:], gk16[:], True, True, [b_wgk2b, b_gk16], [pb])
        k.act(spl[:], pt[0:64, 0:512], AF.Exp, [pb, b_nbg], [b_spl], bias=nbg[:, 0:1], scale=-1.0)
        k.act(spl[:], spl[:], AF.Ln, [b_spl], [b_spl], bias=1.0, scale=1.0)
        pt, pb = proj(1, 128, 656)
        k.act(sbog[:, 0, :], pt[0:64, 0:512], AF.Silu, [pb], [b_sbog])
        k.act(sbog[:, 1, :], pt[64:128, 0:512], AF.Silu, [pb], [b_sbog])

        k.scan(bpos[:], rmk[:], spl[:], 0.0, [b_rmk, b_spl], [b_bpos])
        k.act(eb[:], bpos[:], AF.Exp, [b_bpos], [b_eb], scale=-1.0 / 16)
        k.act(enb[:], bpos[:], AF.Exp, [b_bpos], [b_enb], scale=1.0 / 16)
        blast = bpos[:].rearrange("p (c t) -> p c t", t=64)[:, :, 63:64].rearrange("p c o -> p (c o)")
        k.act(Ac[:], blast, AF.Exp, [b_bpos], [b_ac], scale=-1.0 / 16)
        k.stt(qt[:], QG32[:], 32.0 ** -0.5, eb[:], ALU.mult, ALU.mult, [b_qg, b_eb], [b_qt])
        k.tt(ke32[:], KG32[:], enb[:], ALU.mult, [b_kg, b_enb], [b_ke])
        for h in range(2):
            k.ts(kpad[:, h, :], ke32[:], hm[:, h:h + 1], ALU.mult, [b_ke, b_hm], [b_kpad])
        for c in range(8):
            k.act(khat[:, c * 64:(c + 1) * 64], ke32[:, c * 64:(c + 1) * 64], AF.Copy, [b_ke, b_ac], [b_khat], scale=Ac[:, c:c + 1])
        pt, pb = B[0]
        for c in range(8):
            k.mm(pt[0:64, c * 64:(c + 1) * 64], khat[:, c * 64:(c + 1) * 64], id_b[0:64, 0:64], True, True, [b_khat, b_idb], [pb])
        k.cp(KTt[:].rearrange("p a b -> p (a b)"), pt[0:64, 0:512], [pb], [b_ktt], eng="act")
        for h in range(2):
            pt, pb = B[1 + h]
            for c in range(8):
                k.mm(pt[0:64, c * 64:(c + 1) * 64], kpad[:, h, c * 64:(c + 1) * 64], qt[:, c * 64:(c + 1) * 64], True, True,
                     [b_kpad, b_qt], [pb])
            k.tt(attm[:, h, :], pt[0:64, 0:512], amk[:], ALU.mult, [pb, b_amk], [b_attm])
        pso = [B[5], B[6]]
        for c in range(8):
            psd, b_psd = B[7] if c < 4 else B[0]
            for h in range(2):
                k.mm(psd[0:64, (c % 4) * 128 + h * 64:(c % 4) * 128 + (h + 1) * 64], KTt[:, c, :], gvt[:, c, h * 64:(h + 1) * 64],
                     True, True, [b_ktt, b_gvt], [b_psd])
        k.cp(stall[:, 0, :], stall[:, 8, :], [b_stall], [b_stall])
        for c in range(8):
            psd, b_psd = B[7] if c < 4 else B[0]
            k.tt(stmp[:], psd[0:64, (c % 4) * 128:(c % 4 + 1) * 128], hm2[:], ALU.mult, [b_psd, b_hm2], [b_stmp])
            k.stt(st32[:], st32[:], Ac[:, c:c + 1], stmp[:], ALU.mult, ALU.add, [b_st32, b_ac, b_stmp], [b_st32])
            k.cp(stall[:, c + 1, :], st32[:], [b_st32], [b_stall])
        for c in range(8):
            for h in range(2):
                po, pbo = pso[h]
                k.mm(po[0:64, c * 64:(c + 1) * 64], gvt[:, c, h * 64:(h + 1) * 64], attm[:, h, c * 64:(c + 1) * 64], True, False,
                     [b_gvt, b_attm], [pbo])
                k.mm(po[0:64, c * 64:(c + 1) * 64], stall[:, c, h * 64:(h + 1) * 64], qt[:, c * 64:(c + 1) * 64], False, True,
                     [b_stall, b_qt], [pbo])
        for h in range(2):
            po, pbo = pso[h]
            k.cp(o32[:], po[0:64, 0:512], [pbo], [b_o32], eng="act")
            k.act(osq[:], po[0:64, 0:512], AF.Square, [pbo], [b_osq])
            pt, pb = B[0]
            k.mm(pt[0:64, 0:512], ones_b[0:64, 0:64], osq[:], True, True, [b_onesb, b_osq], [pb])
            k.act(lnv[0:64, :], pt[0:64, 0:512], AF.Ln, [pb], [b_lnv], bias=EPS, scale=1.0 / 64)
            k.act(lnv[0:64, :], lnv[0:64, :], AF.Exp, [b_lnv], [b_lnv], scale=-0.5)
            k.tt(on32[:], o32[:], lnv[0:64, :], ALU.mult, [b_o32, b_lnv], [b_on32])
            k.stt(otile[:, 2 + h, :], on32[:], glag[:, h:h + 1], sbog[:, h, :], ALU.mult, ALU.mult, [b_on32, b_glag, b_sbog], [b_ot])

        for h in range(2):
            KA, _ = Kaug[h]
            VA, _ = Vaug[h]
            QA, b_QA = Qaug[h]
            kmt, b_kmt = kmT[h]
            k.P.op("dve", (lambda o=km32[:], i=KA[0:64, c0:c0 + 512].rearrange("p (a b) -> p a b", b=256):
                           nc.vector.tensor_reduce(out=o, in_=i, axis=AX.X, op=ALU.add)), [KB[h][g]], [b_km32])
            k.cp(kmt[:, 2 * g:2 * g + 2], km32[:], [b_km32], [b_kmt])
            pg, b_pg = B[7]
            for st in range(4):
                k.mm(pg[:, st * 64:(st + 1) * 64], QA[0:64, st * 128:(st + 1) * 128], kmt[:, :], True, True, [b_QA, b_kmt], [b_pg])
            k.memset(gsb[:], -1e30, [b_gsb], eng="dve")
            for st in range(4):
                blk = 2 * g + st // 2
                if blk > 0:
                    k.cp(gsb[:, st, 0:blk], pg[:, st * 64:st * 64 + blk], [b_pg], [b_gsb])
            for st in range(4):
                blk = 2 * g + st // 2
                k.P.op("dve", (lambda o=m8[:, st, :], i=gsb[:, st, :]: nc.vector.max(out=o, in_=i)), [b_gsb], [b_m8])
                k.ts(bq[:, st, 64:128], gsb[:, st, :], m8[:, st, 2:3], ALU.is_ge, [b_gsb, b_m8], [b_bq], s2=-NEG, op1=ALU.mult)
            k.ts(bq[:, :, 64:128], bq[:, :, 64:128], NEG, ALU.add, [b_bq], [b_bq])
            for st in range(4):
                blk = 2 * g + st // 2
                k.memset(bq[:, st, 64 + blk:65 + blk], 0.0, [b_bq], eng="dve")
            pg2, b_pg2 = B[0]
            for st in range(4):
                k.tr(pg2[:, st * 128:(st + 1) * 128], bq[:, st, :], id_f[:], [b_bq, b_idf], [b_pg2])
            k.cp(QA[64:128, :], pg2[64:128, 0:512], [b_pg2], [b_QA], eng="act")
            pO, b_pO = B[5 + h]
            nkt = 4 * g + 4
            LA = 3

            def qk(kt):
                j = kt - 4 * g
                q0 = 256 if j >= 2 else 0
                pS, b_pS = B[1 + (kt % 4)]
                k.mm(pS[:, q0:512], KA[:, kt * 128:(kt + 1) * 128], QA[:, q0:512], True, j < 0,
                     [KB[h][kt // 4], b_kE[h], b_QA], [b_pS])
                if j >= 0:
                    k.mm(pS[:, q0:512], id_b[:], cm[:, j, q0:512], False, True, [b_idb, b_cm], [b_pS])

            def pv(kt):
                j = kt - 4 * g
                q0 = 256 if j >= 2 else 0
                pS, b_pS = B[1 + (kt % 4)]
                pTt, b_pT = pTs[kt % 4]
                k.act(pTt[:, q0:512], pS[:, q0:512], AF.Exp, [b_pS], [b_pT], scale=0.125)
                k.mm(pO[0:65, q0:512], VA[:, kt, 0:65], pTt[:, q0:512], kt == 0, kt == nkt - 1,
                     [VB[h][kt // 4], b_vones[h], b_pT], [b_pO])

            for i in range(nkt + LA):
                if i < nkt:
                    qk(i)
                if i >= LA:
                    pv(i - LA)
            k.P.op("dve", (lambda o=rden[64:65, :], i=pO[64:65, 0:512]: nc.vector.reciprocal(out=o, in_=i)), [b_pO], [b_rden])
            pt, pb = B[0]
            k.mm(pt[0:64, 0:512], ones_f[64:65, 0:64], rden[64:65, :], True, True, [b_onesf, b_rden], [pb])
            k.cp(osb[:], pO[0:64, 0:512], [b_pO], [b_osb], eng="act")
            k.tt(otile[:, h, :], osb[:], pt[0:64, 0:512], ALU.mult, [b_osb, pb], [b_ot])
        oT(g, otile, b_ot)


def l1_consts(S):
    half = 32
    inv = (10000.0 ** (-np.arange(half, dtype=np.float32) / half)).astype(np.float32)
    ang = np.arange(S, dtype=np.float32)[None, :] * inv[:, None]
    cos = np.cos(ang).astype(np.float32)
    sin = np.sin(ang).astype(np.float32)
    cosd = np.concatenate([cos, cos], 0)
    sind = np.concatenate([sin, -sin], 0)
    kk = np.arange(128)[:, None]
    qq = np.arange(512)[None, :]
    cm = np.concatenate([np.where(qq < j * 128 + kk, NEG, 0.0) for j in range(4)], 1).astype(np.float32)
    ed = (np.arange(S)[None, :] // 256 == np.arange(64)[:, None]).astype(np.float32)
    ident = np.eye(128, dtype=np.float32)
    rm = np.tile((np.arange(512) % 64 != 0).astype(np.float32)[None, :], (64, 1))
    s_ = np.arange(64)[:, None]
    t_ = np.arange(64)[None, :]
    am = np.tile((s_ <= t_).astype(np.float32), (1, 8))
    hm = np.zeros((64, 2), np.float32)
    hm[0:32, 0] = 1
    hm[32:64, 1] = 1
    hm2 = np.repeat(hm, 64, axis=1)
    return dict(cosd=cosd, sind=sind, cmd=cm, ed=ed, identd=ident, rmd=rm, amd=am, hm2d=hm2, hmd=hm)


HALO = 8
OCS = 2048


def choose_tiles(W):
    if W == 4104:
        return 4, 3, 342
    if W == 520:
        return 2, 1, 260
    raise ValueError(W)


class Trunk:
    def __init__(self, k, W):
        self.k = k
        self.nc = k.nc
        self.W = W
        self.NB, self.NTB, self.TW = choose_tiles(W)
        self.TB = self.NTB * self.TW
        TB, TW = self.TB, self.TW
        self.h, self.b_h = k.sb([128, 8, TB], F32, "h")
        self.hn, self.b_hn = k.sb([128, 8, TB], BF16, "hn")
        self.act, self.b_act = k.sb([128, 24, TB], BF16, "act")
        self.wst = [k.sb([128, 8, 128], F32, "wst%d" % i) for i in range(3)]
        self.wbf = [k.sb([128, 24, 128], BF16, "wbf%d" % i) for i in range(3)]
        self.wi = 0
        self.si = 0
        self.wq = []
        self.pc = [k.sb([128, 4 + TB], F32, "pc%d" % i) for i in range(2)]
        self.pci = 0
        self.yt = [k.sb([128, TB], F32, "yt%d" % i) for i in range(2)]
        self.gel, self.b_gel = k.sb([128, TB], F32, "gel")
        self.sqt, self.b_sqt = k.sb([128, 8, TW], BF16, "sqt")
        self.lnv, self.b_lnv = k.sb([128, TW], F32, "lnvt")
        self.rstd, self.b_rstd = k.sb([128, TW], F32, "rstdt")
        self.ones_b, self.b_ones = k.sb([128, 128], BF16, "onesb")
        self.m, self.b_m = k.sb([128, 1], F32, "hmask")
        self.carry, self.b_carry = k.sb([128, 48, 2], F32, "carry")
        self.B = [k.ps([128, 512], "bank%d" % i) for i in range(8)]
        self.bi = 0
        k.memset(self.ones_b[:], 1.0, [self.b_ones])
        k.memset(self.carry[:], 0.0, [self.b_carry])

    def bank(self):
        b = self.B[self.bi]
        self.bi = (self.bi + 1) % 8
        return b

    def small(self, name, dram_ap, shape):
        t, b = self.k.sb(shape, F32, name)
        self.k.dma(t[:], dram_ap, (), [b])
        return t, b

    def request(self, wd, KC, col0):
        k = self.k
        wb, b_wb = self.wbf[self.wi]
        self.wi = (self.wi + 1) % 3
        wv = wd.rearrange("(kc p) f -> p kc f", p=128)
        for k0 in range(0, KC, 8):
            k1 = min(KC, k0 + 8)
            st, b_st = self.wst[self.si]
            self.si = (self.si + 1) % 3
            k.dma(st[:, 0:k1 - k0, :], wv[:, k0:k1, col0:col0 + 128], (), [b_st])
            k.cp(wb[:, k0:k1, :], st[:, 0:k1 - k0, :], [b_st], [b_wb], eng="act")
        self.wq.append((wb, b_wb))

    def run_stage(self, reqs, body):
        LA = 2
        for r in reqs[:LA]:
            self.request(*r)
        for i in range(len(reqs)):
            if i + LA < len(reqs):
                self.request(*reqs[i + LA])
            body(i)

    def rmsnorm(self, g_t, b_g, out_f32=None):
        k = self.k
        TW = self.TW
        for nt in range(self.NTB):
            cs = slice(nt * TW, (nt + 1) * TW)
            k.act(self.sqt[:], self.h[:, :, cs], AF.Square, [self.b_h], [self.b_sqt])
            pt, pb = self.bank()
            for kc in range(8):
                k.mm(pt[:, 0:TW], self.ones_b[:], self.sqt[:, kc, :], kc == 0, kc == 7, [self.b_ones, self.b_sqt], [pb])
            k.act(self.lnv[:], pt[:, 0:TW], AF.Ln, [pb], [self.b_lnv], bias=EPS, scale=1.0 / 1024)
            k.act(self.rstd[:], self.lnv[:], AF.Exp, [self.b_lnv], [self.b_rstd], scale=-0.5)
            for kc in range(8):
                if out_f32 is None:
                    k.stt(self.hn[:, kc, cs], self.h[:, kc, cs], g_t[:, kc:kc + 1], self.rstd[:], ALU.mult, ALU.mult,
                          [self.b_h, b_g, self.b_rstd], [self.b_hn])
                else:
                    k.stt(out_f32[0][:, kc, cs], self.h[:, kc, cs], g_t[:, kc:kc + 1], self.rstd[:], ALU.mult, ALU.mult,
                          [self.b_h, b_g, self.b_rstd], [out_f32[1]])

    def linear(self, src, b_src, KC, wd, col0, evac):
        k = self.k
        TW = self.TW
        wb, b_wb = self.wq.pop(0)
        for nt in range(self.NTB):
            cs = slice(nt * TW, (nt + 1) * TW)
            pt, pb = self.bank()
            for kc in range(KC):
                k.mm(pt[:, 0:TW], wb[:, kc, :], src[:, kc, cs], kc == 0, kc == KC - 1, [b_wb, b_src], [pb])
            evac(nt, cs, pt[:, 0:TW], pb)

    def conv(self, pc, b_pc, ntap, cw_t, b_cw, cb_t, b_cb, idx, yt, b_yt):
        k = self.k
        TB = self.TB
        last = ntap - 1
        k.act(yt[:], pc[:, last:last + TB], AF.Identity, [b_pc, b_cw, b_cb], [b_yt],
              bias=cb_t[:, idx:idx + 1], scale=cw_t[:, idx, last:last + 1])
        for i in range(last - 1, -1, -1):
            k.stt(yt[:], pc[:, i:i + TB], cw_t[:, idx, i:i + 1], yt[:], ALU.mult, ALU.add, [b_pc, b_cw, b_yt], [b_yt])

    def ffn(self, first, gn_t, b_gn, w_up, cw_t, b_cw, cb_t, b_cb, w_down, mask_halo):
        k = self.k
        TB, TW = self.TB, self.TW
        self.rmsnorm(gn_t, b_gn)
        ys = [None, None]

        def up_body(i):
            j, part = i // 2, i % 2
            idx = part * 24 + j
            pc, b_pc = self.pc[self.pci]
            self.pci = (self.pci + 1) % 2
            yt, b_yt = self.yt[part]
            k.cp(pc[:, 0:2], self.carry[:, idx, :], [self.b_carry], [b_pc], eng="act")

            def evac(nt, cs, ps, pb, pc=pc, b_pc=b_pc):
                k.cp(pc[:, 2 + cs.start:2 + cs.stop], ps, [pb], [b_pc], eng="act")

            self.linear(self.hn, self.b_hn, 8, w_up, idx * 128, evac)
            if first and mask_halo:
                k.ts(pc[:, 2:2 + HALO], pc[:, 2:2 + HALO], self.m[:, 0:1], ALU.mult, [b_pc, self.b_m], [b_pc])
            k.cp(self.carry[:, idx, :], pc[:, TB:TB + 2], [b_pc], [self.b_carry], eng="act")
            self.conv(pc, b_pc, 3, cw_t, b_cw, cb_t, b_cb, idx, yt, b_yt)
            ys[part] = (yt, b_yt)
            if part == 1:
                (yu, b_yu), (yg, b_yg) = ys
                k.act(self.gel[:], yg[:], AF.Gelu_apprx_tanh, [b_yg], [self.b_gel])
                k.tt(self.act[:, j, :], yu[:], self.gel[:], ALU.mult, [b_yu, self.b_gel], [self.b_act])

        self.run_stage([(w_up, 8, (part * 24 + j) * 128) for j in range(24) for part in range(2)], up_body)

        def down_body(fc):
            def evac(nt, cs, ps, pb, fc=fc):
                k.tt(self.h[:, fc, cs], self.h[:, fc, cs], ps, ALU.add, [self.b_h, pb], [self.b_h])

            self.linear(self.act, self.b_act, 24, w_down, fc * 128, evac)

        self.run_stage([(w_down, 24, fc * 128) for fc in range(8)], down_body)


def l2_io(k, W):
    d = {}
    for name, shape in (("xTw", [1024, W]), ("m", [128, 1]), ("sel", [128, 4]), ("w_out0", [1024, 1024]), ("fg0", [128, 8]),
                        ("w_up0", [1024, 6144]), ("cw0", [128, 48, 3]), ("cb0", [128, 48]), ("w_down0", [3072, 1024]),
                        ("mg", [128, 8]), ("w_in", [1024, 2048]), ("rcw", [128, 8, 4]), ("rcb", [128, 8]), ("wa", [1024, 256]),
                        ("ba", [128, 8]), ("wx", [1024, 256]), ("bx", [128, 8]), ("lam", [128, 8])):
        d[name] = k.din(name, shape)
    return d


def emit_l2(k, W, S, d, og, h1_o, gg_o, hl_o, pl_o, ex_o):
    nc = k.nc
    T = Trunk(k, W)
    NB, TB, TW = T.NB, T.TB, T.TW
    TOK = W - HALO
    xTw, m_d, sel_d, w_out, fg_d, w_up, cw_d, cb_d, w_down = (d["xTw"], d["m"], d["sel"], d["w_out0"], d["fg0"], d["w_up0"],
                                                               d["cw0"], d["cb0"], d["w_down0"])
    mg_d, w_in, rcw_d, rcb_d, wa_d, ba_d, wx_d, bx_d, lam_d = (d["mg"], d["w_in"], d["rcw"], d["rcb"], d["wa"], d["ba"],
                                                               d["wx"], d["bx"], d["lam"])
    sel, b_sel = T.small("sel", sel_d[:, :], [128, 4])

    k.dma(T.m[:], m_d[:, :], (), [T.b_m])
    fg, b_fg = T.small("fg", fg_d[:, :], [128, 8])
    cw, b_cw = T.small("cw", cw_d[:, :, :], [128, 48, 3])
    cb, b_cb = T.small("cb", cb_d[:, :], [128, 48])
    mg, b_mg = T.small("mg", mg_d[:, :], [128, 8])
    rcw, b_rcw = T.small("rcw", rcw_d[:, :, :], [128, 8, 4])
    rcb, b_rcb = T.small("rcb", rcb_d[:, :], [128, 8])
    ba, b_ba = T.small("ba", ba_d[:, :], [128, 8])
    bx, b_bx = T.small("bx", bx_d[:, :], [128, 8])
    lam, b_lam = T.small("lam", lam_d[:, :], [128, 8])
    c1, b_c1 = k.sb([128, 8], F32, "c1")
    c2, b_c2 = k.sb([128, 8], F32, "c2")
    k.act(c1[:], lam[:], AF.Exp, [b_lam], [b_c1], scale=-1.0)
    k.act(c1[:], c1[:], AF.Ln, [b_c1], [b_c1], bias=1.0, scale=1.0)
    k.ts(c2[:], c1[:], -16.0, ALU.mult, [b_c1], [b_c2])
    k.ts(c1[:], c1[:], -8.0, ALU.mult, [b_c1, b_c2], [b_c1])
    ob, b_ob = T.hn, T.b_hn
    gg, b_gg = k.sb([128, TB], BF16, "gg")
    rcar, b_rcar = k.sb([128, 8, 3], F32, "rcar")
    hcar, b_hcar = k.sb([128, 8], F32, "hcar")
    pcar, b_pcar = k.sb([128, 8], F32, "pcar")
    zer, b_zer = k.sb([128, TB], F32, "zer")
    xrc, b_xrc = k.sb([128, 2, TB], F32, "xrc")
    xrb, b_xrb = k.sb([128, 2, TB], BF16, "xrb")
    rr, b_rr = k.sb([128, TB], F32, "rr")
    ii, b_ii = k.sb([128, TB], F32, "ii")
    aa, b_aa = k.sb([128, TB], F32, "aa")
    uu, b_uu = k.sb([128, TB], F32, "uu")
    hl, b_hl = k.sb([128, TB], F32, "hl")
    pl, b_pl = k.sb([128, TB], F32, "pl")
    k.memset(rcar[:], 0.0, [b_rcar])
    k.memset(zer[:], 0.0, [b_zer])
    ext, b_ext = k.sb([128, 8, 2], F32, "ext")

    oTv = [o_.rearrange("(kc p) s -> p kc s", p=128) for o_ in og]
    xTv = xTw.rearrange("(kc p) s -> p kc s", p=128)
    h1v = h1_o.rearrange("(kc p) s -> p kc s", p=128)
    ggv = gg_o.rearrange("(kc p) s -> p kc s", p=128)
    hlv = hl_o.rearrange("(kc p) s -> p kc s", p=128)
    plv = pl_o.rearrange("(kc p) s -> p kc s", p=128)
    exv = ex_o.rearrange("(kc p) s -> p kc s", p=128)

    for blk in range(NB):
        first = blk == 0
        g0 = blk * TB
        for c in range(4):
            cand = T.act[:, 8 * (c % 3):8 * (c % 3) + 8, :]
            lo = c * TOK - HALO + g0
            skip = max(0, -lo)
            if skip:
                k.memset(cand[:, :, 0:skip], 0.0, [T.b_act])
            a = lo + skip
            while a < lo + TB:
                j = a // OCS
                e = min(lo + TB, (j + 1) * OCS)
                k.dma(cand[:, :, a - lo:e - lo], oTv[j][:, :, a - j * OCS:e - j * OCS], (), [T.b_act])
                a = e
            if c == 0:
                k.ts(ob[:], cand, sel[:, 0:1], ALU.mult, [T.b_act, b_sel], [b_ob])
            else:
                k.stt(ob[:], cand, sel[:, c:c + 1], ob[:], ALU.mult, ALU.add, [T.b_act, b_sel, b_ob], [b_ob])
        k.dma(T.h[:, 0:4, :], xTv[:, 0:4, g0:g0 + TB], (), [T.b_h])
        k.dma(T.h[:, 4:8, :], xTv[:, 4:8, g0:g0 + TB], (), [T.b_h])
        def wo_body(fc):
            def evac(nt, cs, ps, pb, fc=fc):
                k.tt(T.h[:, fc, cs], T.h[:, fc, cs], ps, ALU.add, [T.b_h, pb], [T.b_h])

            T.linear(ob, b_ob, 8, w_out, fc * 128, evac)

        T.run_stage([(w_out, 8, fc * 128) for fc in range(8)], wo_body)
        T.ffn(first, fg, b_fg, w_up, cw, b_cw, cb, b_cb, w_down, False)
        k.dma(h1v[:, :, g0:g0 + TB], T.h[:], [T.b_h], (), eng="pool")
        T.rmsnorm(mg, b_mg)
        def gb_body(fc):
            def evac(nt, cs, ps, pb, fc=fc):
                k.act(gg[:, cs], ps, AF.Gelu_apprx_tanh, [pb], [b_gg])

            T.linear(T.hn, T.b_hn, 8, w_in, fc * 128, evac)
            k.dma(ggv[:, fc, g0:g0 + TB], gg[:], [b_gg], (), eng="pool")

        T.run_stage([(w_in, 8, fc * 128) for fc in range(8)], gb_body)
        reqs = []
        for n in range(4):
            reqs += [(w_in, 8, 1024 + (2 * n + c2i) * 128) for c2i in range(2)]
            for c2i in range(2):
                reqs += [(wa_d[n * 256:(n + 1) * 256, :], 2, c2i * 128), (wx_d[n * 256:(n + 1) * 256, :], 2, c2i * 128)]

        def rg_body(i, blk=blk, first=first, g0=g0):
            n, r = i // 6, i % 6
            if r < 2:
                c2i = r
                c8 = 2 * n + c2i
                pc, b_pc = T.pc[T.pci]
                T.pci = (T.pci + 1) % 2
                k.cp(pc[:, 0:3], rcar[:, c8, :], [b_rcar], [b_pc], eng="act")

                def evac(nt, cs, ps, pb, pc=pc, b_pc=b_pc):
                    k.cp(pc[:, 3 + cs.start:3 + cs.stop], ps, [pb], [b_pc], eng="act")

                T.linear(T.hn, T.b_hn, 8, w_in, 1024 + c8 * 128, evac)
                if first:
                    k.ts(pc[:, 3:3 + HALO], pc[:, 3:3 + HALO], T.m[:, 0:1], ALU.mult, [b_pc, T.b_m], [b_pc])
                k.cp(rcar[:, c8, :], pc[:, TB:TB + 3], [b_pc], [b_rcar], eng="act")
                yt, b_yt = T.yt[c2i]
                T.conv(pc, b_pc, 4, rcw, b_rcw, rcb, b_rcb, c8, yt, b_yt)
                k.cp(xrc[:, c2i, :], yt[:], [b_yt], [b_xrc], eng="pool")
                k.cp(xrb[:, c2i, :], yt[:], [b_yt], [b_xrb], eng="act")
                return
            c2i, which = (r - 2) // 2, (r - 2) % 2
            fc = 2 * n + c2i
            wd, bias_t, b_bias, dst, b_dst = ((wa_d, ba, b_ba, rr, b_rr), (wx_d, bx, b_bx, ii, b_ii))[which]

            def evac(nt, cs, ps, pb, dst=dst, b_dst=b_dst, bias_t=bias_t, b_bias=b_bias, fc=fc):
                k.act(dst[:, cs], ps, AF.Sigmoid, [pb, b_bias], [b_dst], bias=bias_t[:, fc:fc + 1], scale=1.0)

            T.linear(xrb, b_xrb, 2, wd[n * 256:(n + 1) * 256, :], c2i * 128, evac)
            if which == 0:
                return
            k.act(aa[:], rr[:], AF.Exp, [b_rr, b_c1], [b_aa], scale=c1[:, fc:fc + 1])
            k.act(rr[:], rr[:], AF.Exp, [b_rr, b_c2], [b_rr], scale=c2[:, fc:fc + 1])
            k.act(rr[:], rr[:], AF.Sqrt, [b_rr], [b_rr], bias=1.0, scale=-1.0)
            k.tt(uu[:], xrc[:, c2i, :], ii[:], ALU.mult, [b_xrc, b_ii], [b_uu])
            k.tt(uu[:], uu[:], rr[:], ALU.mult, [b_uu, b_rr], [b_uu])
            if first:
                k.ts(uu[:, 0:HALO], uu[:, 0:HALO], T.m[:, 0:1], ALU.mult, [b_uu, T.b_m], [b_uu])
                k.memset(hl[:, 0:5], 0.0, [b_hl])
                k.memset(pl[:, 0:5], 0.0, [b_pl])
                s0, hi, pi = 5, 0.0, 1.0
            else:
                s0, hi, pi = 0, hcar[:, fc:fc + 1], pcar[:, fc:fc + 1]
            k.scan(hl[:, s0:TB], aa[:, s0:TB], uu[:, s0:TB], hi, [b_aa, b_uu, b_hcar], [b_hl])
            k.scan(pl[:, s0:TB], aa[:, s0:TB], zer[:, s0:TB], pi, [b_aa, b_zer, b_pcar], [b_pl])
            k.cp(hcar[:, fc:fc + 1], hl[:, TB - 1:TB], [b_hl], [b_hcar], eng="pool")
            k.cp(pcar[:, fc:fc + 1], pl[:, TB - 1:TB], [b_pl], [b_pcar], eng="pool")
            k.dma(hlv[:, fc, g0:g0 + TB], hl[:], [b_hl], (), eng="pool")
            k.dma(plv[:, fc, g0:g0 + TB], pl[:], [b_pl], (), eng="pool")
            if blk == NB - 1:
                k.cp(ext[:, fc, 0:1], hl[:, TB - 4:TB - 3], [b_hl], [b_ext], eng="pool")
                k.cp(ext[:, fc, 1:2], pl[:, TB - 4:TB - 3], [b_pl], [b_ext], eng="pool")

        T.run_stage(reqs, rg_body)
    k.dma(exv[:, :, :], ext[:], [b_ext], (), eng="pool")


def l3_io(k, W):
    d = {}
    for name, shape in (("srank", [128, 4]), ("oms", [128, 4]), ("w_out1", [1024, 1024]), ("fg1", [128, 8]),
                        ("w_up1", [1024, 6144]), ("cw1", [128, 48, 3]), ("cb1", [128, 48]), ("w_down1", [3072, 1024]),
                        ("fin", [128, 8])):
        d[name] = k.din(name, shape)
    return d


def emit_l3(k, W, d, m_d, h1w, ggw, hlw, plw, exg, out_o):
    nc = k.nc
    T = Trunk(k, W)
    NB, TB, TW = T.NB, T.TB, T.TW
    TOK = W - HALO
    w_out, fg_d, w_up, cw_d, cb_d, w_down, fin_d = (d["w_out1"], d["fg1"], d["w_up1"], d["cw1"], d["cb1"], d["w_down1"], d["fin"])
    k.dma(T.m[:], m_d[:, :], (), [T.b_m])
    fg, b_fg = T.small("fg", fg_d[:, :], [128, 8])
    cw, b_cw = T.small("cw", cw_d[:, :, :], [128, 48, 3])
    cb, b_cb = T.small("cb", cb_d[:, :], [128, 48])
    fin, b_fin = T.small("fin", fin_d[:, :], [128, 8])
    sr, b_sr = T.small("sr", d["srank"][:, :], [128, 4])
    oms, b_oms = T.small("oms", d["oms"][:, :], [128, 4])
    pe, b_pe = k.sb([128, 4, 8, 2], F32, "pe")
    exv = exg.rearrange("(r kc p) t -> p r kc t", p=128, kc=8)
    for r in range(4):
        k.dma(pe[:, r, :, :], exv[:, r, :, :], (), [b_pe])
    Hc, b_Hc = k.sb([128, 8], F32, "Hc")
    Pm, b_Pm = k.sb([128, 8], F32, "Pm")
    Em, b_Em = k.sb([128, 8], F32, "Em")
    k.memset(Hc[:], 0.0, [b_Hc])
    for r in range(4):
        k.ts(Pm[:], pe[:, r, :, 1], sr[:, r:r + 1], ALU.mult, [b_pe, b_sr], [b_Pm], s2=oms[:, r:r + 1], op1=ALU.add)
        k.ts(Em[:], pe[:, r, :, 0], sr[:, r:r + 1], ALU.mult, [b_pe, b_sr], [b_Em])
        k.tt(Hc[:], Hc[:], Pm[:], ALU.mult, [b_Hc, b_Pm], [b_Hc])
        k.tt(Hc[:], Hc[:], Em[:], ALU.add, [b_Hc, b_Em], [b_Hc])
    yb, b_yb = T.hn, T.b_hn
    hl, b_hl = k.sb([128, TB], F32, "hl")
    pl, b_pl = k.sb([128, TB], F32, "pl")
    ggt, b_ggt = k.sb([128, TB], BF16, "ggt")
    outt, b_outt = k.sb([128, 8, TB], F32, "outt")

    h1v = h1w.rearrange("(kc p) s -> p kc s", p=128)
    ggv = ggw.rearrange("(kc p) s -> p kc s", p=128)
    hlv = hlw.rearrange("(kc p) s -> p kc s", p=128)
    plv = plw.rearrange("(kc p) s -> p kc s", p=128)
    outv = out_o.rearrange("(kc p) s -> p kc s", p=128)

    for blk in range(NB):
        first = blk == 0
        g0 = blk * TB
        k.dma(T.h[:, 0:4, :], h1v[:, 0:4, g0:g0 + TB], (), [T.b_h])
        k.dma(T.h[:, 4:8, :], h1v[:, 4:8, g0:g0 + TB], (), [T.b_h])
        for fc in range(8):
            k.dma(hl[:], hlv[:, fc, g0:g0 + TB], (), [b_hl])
            k.dma(pl[:], plv[:, fc, g0:g0 + TB], (), [b_pl])
            k.dma(ggt[:], ggv[:, fc, g0:g0 + TB], (), [b_ggt])
            k.stt(hl[:], pl[:], Hc[:, fc:fc + 1], hl[:], ALU.mult, ALU.add, [b_pl, b_Hc, b_hl], [b_hl])
            k.tt(yb[:, fc, :], hl[:], ggt[:], ALU.mult, [b_hl, b_ggt], [b_yb])
        def wo_body(fc):
            def evac(nt, cs, ps, pb, fc=fc):
                k.tt(T.h[:, fc, cs], T.h[:, fc, cs], ps, ALU.add, [T.b_h, pb], [T.b_h])

            T.linear(yb, b_yb, 8, w_out, fc * 128, evac)

        T.run_stage([(w_out, 8, fc * 128) for fc in range(8)], wo_body)
        T.ffn(first, fg, b_fg, w_up, cw, b_cw, cb, b_cb, w_down, True)
        T.rmsnorm(fin, b_fin, out_f32=(outt, b_outt))
        lo = HALO if first else 0
        k.dma(outv[:, :, g0 + lo - HALO:g0 + TB - HALO], outt[:, :, lo:TB], [b_outt], (), eng="pool")


_F = {}


def build_fused(S):
    TOK = S // 4
    W = TOK + HALO
    nc = bass.Bass("TRN2", target_bir_lowering=False)
    k = K(nc)
    io1 = l1_io(k, S)
    io2 = l2_io(k, W)
    io3 = l3_io(k, W)
    NCH = max(1, S // OCS)
    osrc = [k.dint("osrc%d" % j, [256, min(S, OCS)], BF16) for j in range(NCH)]
    og = [k.dint("og%d" % j, [1024, min(S, OCS)], BF16) for j in range(NCH)]
    h1 = k.dint("h1s", [1024, W])
    gg = k.dint("ggs", [1024, W], BF16)
    hl = k.dint("hls", [1024, W])
    pl = k.dint("pls", [1024, W])
    exs = k.dint("exs", [1024, 2])
    exg = k.dint("exg", [4096, 2])
    out = k.dout("outT", [1024, TOK])
    groups = [[0, 1, 2, 3], [4, 5, 6, 7]]
    upto = int(os.environ.get("FUSE_UPTO", "3"))
    tile_bufs = {}

    def o_out(g, otile, b_ot):
        j, off = (g * 512) // OCS, (g * 512) % OCS
        ov = osrc[j].rearrange("(r p) s -> p r s", p=64)
        bb = Buf()
        tile_bufs.setdefault(j, []).append(bb)
        k.dma(ov[:, :, off:off + 512], otile[:], [b_ot], [bb], eng="pool")
        if off + 512 == min(S, OCS):
            k.P.dma((lambda j=j: nc.gpsimd.collective_compute("AllGather", ALU.bypass, replica_groups=groups,
                                                               ins=[osrc[j][:, :]], outs=[og[j][:, :]])),
                    tile_bufs[j], (), eng="pool", inc=1)

    emit_l1(k, S, io1, o_out)
    k.end_phase()
    emit_l2(k, W, S, io2, og, h1, gg, hl, pl, exs)
    k.end_phase()
    if upto == 2:
        return nc, k.P.finish()
    if os.environ.get("NOCC2"):
        k.dma(exg[0:1024, :], exs[:, :], (), ())
    else:
        k.P.dma(lambda: nc.gpsimd.collective_compute("AllGather", ALU.bypass, replica_groups=groups, ins=[exs[:, :]], outs=[exg[:, :]]),
                (), (), eng="pool", inc=1)
    k.P.barrier()
    emit_l3(k, W, io3, io2["m"], h1, gg, hl, pl, exg, out)
    cnt = k.P.finish()
    return nc, cnt


def _pk(v):
    return np.ascontiguousarray(v.reshape(-1, 128).T)


def _pkw(w):
    t, C = w.shape
    return np.ascontiguousarray(w.T.reshape(C // 128, 128, t).transpose(1, 0, 2))


def kernel(x, mix_norm_g, ffn_norm_g, final_norm_g,
           ev_w_in, ev_w_gk2, ev_b_gk2, ev_gla_norm_g, ev_w_out,
           od_w_in, od_conv_w, od_conv_b, od_w_a, od_b_a, od_w_x, od_b_x, od_lambda, od_w_out,
           ffn_w_up, ffn_conv_w, ffn_conv_b, ffn_w_down):
    f = lambda a: np.asarray(a, dtype=np.float32)
    x = f(x)
    Bsz, S, D = x.shape
    TOK = S // 4
    W = TOK + HALO
    if S not in _F:
        _F[S] = build_fused(S)
    nc, _ = _F[S]
    consts = l1_consts(S)
    w = f(ev_w_in)[0]
    gn = _pk(f(mix_norm_g)[0])
    perm = []
    for g in range(4):
        for j in range(4):
            head = 2 * g + (j % 2)
            base = head * 64 if j < 2 else 512 + head * 64
            perm += list(range(base, base + 64))
    w_out0 = np.ascontiguousarray(f(ev_w_out)[0][np.asarray(perm)])
    shared = {
        "gn": gn, "w_out0": w_out0, "fg0": _pk(f(ffn_norm_g)[0]), "w_up0": f(ffn_w_up)[0],
        "cw0": _pkw(f(ffn_conv_w)[0]), "cb0": _pk(f(ffn_conv_b)[0]), "w_down0": f(ffn_w_down)[0],
        "mg": _pk(f(mix_norm_g)[1]), "w_in": f(od_w_in)[0], "rcw": _pkw(f(od_conv_w)[0]), "rcb": _pk(f(od_conv_b)[0]),
        "wa": np.ascontiguousarray(f(od_w_a)[0].reshape(1024, 256)), "ba": _pk(f(od_b_a)[0]),
        "wx": np.ascontiguousarray(f(od_w_x)[0].reshape(1024, 256)), "bx": _pk(f(od_b_x)[0]),
        "lam": _pk(f(od_lambda)[0]),
        "w_out1": f(od_w_out)[0], "fg1": _pk(f(ffn_norm_g)[1]), "w_up1": f(ffn_w_up)[1],
        "cw1": _pkw(f(ffn_conv_w)[1]), "cb1": _pk(f(ffn_conv_b)[1]), "w_down1": f(ffn_w_down)[1],
        "fin": _pk(f(final_norm_g)),
    }
    shared.update(consts)
    in_maps = []
    for b in range(Bsz):
        xTb = np.ascontiguousarray(x[b].T)
        for c in range(4):
            h0, h1 = 2 * c, 2 * c + 1
            cols = []
            for base in (0, 512, 1024):
                cols += [np.arange(base + h0 * 64, base + h0 * 64 + 64), np.arange(base + h1 * 64, base + h1 * 64 + 64)]
            for base in (1536, 1792):
                cols += [np.arange(base + h0 * 32, base + h0 * 32 + 32), np.arange(base + h1 * 32, base + h1 * 32 + 32)]
            cols += [np.arange(2048 + h0 * 64, 2048 + h0 * 64 + 64), np.arange(2048 + h1 * 64, 2048 + h1 * 64 + 64)]
            cols += [np.arange(2560, 2576)]
            cols += [np.arange(2576 + h0 * 64, 2576 + h0 * 64 + 64), np.arange(2576 + h1 * 64, 2576 + h1 * 64 + 64)]
            cols = np.concatenate(cols)
            t0 = c * TOK
            xw = np.zeros((1024, W), np.float32)
            if c == 0:
                xw[:, HALO:] = xTb[:, 0:TOK]
            else:
                xw[:] = xTb[:, t0 - HALO:t0 + TOK]
            sel = np.zeros((128, 4), np.float32)
            sel[:, c] = 1.0
            sr = np.zeros((128, 4), np.float32)
            sr[:, :c] = 1.0
            m = dict(shared)
            m.update({
                "xT": xTb, "w1": np.ascontiguousarray(w[:, cols]),
                "wgk2": np.ascontiguousarray(f(ev_w_gk2)[0][:, h0 * 32:h0 * 32 + 64]),
                "bgk2": np.ascontiguousarray(f(ev_b_gk2)[0][h0 * 32:h0 * 32 + 64].reshape(64, 1)),
                "glag": np.ascontiguousarray(f(ev_gla_norm_g)[0][h0:h0 + 2].T),
                "xTw": xw, "m": np.full((128, 1), 0.0 if c == 0 else 1.0, np.float32),
                "sel": sel, "srank": sr, "oms": 1.0 - sr,
            })
            in_maps.append(m)
    res = run_bass_kernel_spmd(nc, in_maps, core_ids=list(range(len(in_maps)))).results
    out = np.zeros((Bsz, S, D), np.float32)
    for b in range(Bsz):
        for c in range(4):
            out[b, c * TOK:(c + 1) * TOK, :] = np.asarray(res[b * 4 + c]["outT"]).T
    return out
```

```python
import contextlib
import os
import numpy as np
import ml_dtypes
import concourse.bass as bass
import concourse.mybir as mybir
from concourse.bass_utils import run_bass_kernel_spmd

F32 = mybir.dt.float32
BF16 = mybir.dt.bfloat16
AF = mybir.ActivationFunctionType
ALU = mybir.AluOpType
AX = mybir.AxisListType

SAME_SYNC = True
N_DMA_SEMS = 24
SEM_EPOCH = 2000
NEG = -240000.0
EPS = 1e-6


class Buf:
    __slots__ = ("name", "lw", "rd")

    def __init__(self, name=""):
        self.name = name
        self.lw = None
        self.rd = []


class Prog:
    ENGS = ("pe", "act", "dve", "pool", "sp")

    def __init__(self, nc):
        self.nc = nc
        self.h = {"pe": nc.tensor, "act": nc.scalar, "dve": nc.vector, "pool": nc.gpsimd, "sp": nc.sync}
        self.items = {e: [] for e in self.ENGS}
        self.known = {}
        self.sem = {e: nc.alloc_semaphore("s_" + e) for e in self.ENGS}
        self.dsem = [nc.alloc_semaphore("d%d" % i) for i in range(N_DMA_SEMS)]
        self.duse = [0] * N_DMA_SEMS
        self.dval = [0] * N_DMA_SEMS
        self.pending = {e: [] for e in self.ENGS}
        self.rank = {e: [] for e in self.ENGS}
        self.emitted = {e: 0 for e in self.ENGS}
        self.esem = {}
        self.dnext = 0

    def _need(self, eng, tok, waits):
        if tok is None:
            return
        if tok[0] == "c":
            _, e2, seq = tok
            if e2 == eng and (eng == "pe" or not SAME_SYNC):
                return
            key = (eng, "c", e2)
            if self.known.get(key, -1) >= seq:
                return
            self.known[key] = seq
            self.items[e2][seq]["flag"] = True
            waits.append(tok)
        else:
            _, idx, val = tok
            key = (eng, "d", idx)
            if self.known.get(key, -1) >= val:
                return
            self.known[key] = val
            waits.append(tok)

    def _deps(self, eng, reads, writes):
        waits = []
        for b in reads:
            self._need(eng, b.lw, waits)
        for b in writes:
            self._need(eng, b.lw, waits)
            for t in b.rd:
                self._need(eng, t, waits)
        return waits

    def op(self, eng, fn, reads=(), writes=()):
        waits = self.pending[eng] + self._deps(eng, reads, writes)
        self.pending[eng] = []
        seq = len(self.items[eng])
        self.items[eng].append({"waits": waits, "fn": fn, "flag": False, "dma": None})
        tok = ("c", eng, seq)
        for b in reads:
            b.rd.append(tok)
        for b in writes:
            b.lw = tok
            b.rd = []
        return tok

    def dma(self, fn, reads=(), writes=(), eng="sp", inc=16):
        idx = self.dnext
        self.dnext = (self.dnext + 1) % N_DMA_SEMS
        pv = self.dval[idx]
        waits = self.pending[eng] + self._deps(eng, reads, writes)
        self.pending[eng] = []
        if pv > 0:
            self._need(eng, ("d", idx, pv), waits)
        self.duse[idx] += 1
        self.dval[idx] = pv + inc
        tok = ("d", idx, pv + inc)
        self.items[eng].append({"waits": waits, "fn": fn, "flag": False, "dma": idx, "inc": inc})
        for b in reads:
            b.rd.append(tok)
        for b in writes:
            b.lw = tok
            b.rd = []
        return tok

    def _all_tokens(self):
        toks = []
        for e in ("pe", "act", "dve", "pool"):
            items = self.items[e]
            for i in range(len(items) - 1, -1, -1):
                if items[i]["dma"] is None and items[i]["fn"] is not None or (items[i]["dma"] is None and i < self.emitted[e]):
                    toks.append(("c", e, i))
                    break
        for idx in range(N_DMA_SEMS):
            if self.dval[idx]:
                toks.append(("d", idx, self.dval[idx]))
        return toks

    def barrier(self, engines=None):
        toks = self._all_tokens()
        for e in (engines or self.ENGS):
            for t in toks:
                if t[0] == "c" and t[1] == e:
                    continue
                self._need(e, t, self.pending[e])

    def _sem_of(self, e, r):
        ep = (r - 1) // SEM_EPOCH
        if (e, ep) not in self.esem:
            self.esem[(e, ep)] = self.sem[e] if ep == 0 else self.nc.alloc_semaphore("s_%s_%d" % (e, ep))
        return self.esem[(e, ep)], (r - 1) % SEM_EPOCH + 1

    def flush(self):
        for e in self.ENGS:
            items = self.items[e]
            rk = self.rank[e]
            c = rk[-1] if rk else 0
            for i in range(len(rk), len(items)):
                if items[i]["flag"]:
                    c += 1
                rk.append(c)
        for e in self.ENGS:
            h = self.h[e]
            items = self.items[e]
            for i in range(self.emitted[e], len(items)):
                it = items[i]
                for t in it["waits"]:
                    if t[0] == "c":
                        sm, v = self._sem_of(t[1], self.rank[t[1]][t[2]])
                        h.wait_ge(sm, v)
                    else:
                        h.wait_ge(self.dsem[t[1]], t[2])
                if it["fn"] is None:
                    continue
                ins = it["fn"]()
                if it["dma"] is not None:
                    ins.then_inc(self.dsem[it["dma"]], it["inc"])
                elif it["flag"]:
                    sm, _ = self._sem_of(e, self.rank[e][i])
                    ins.then_inc(sm, 1)
                it["fn"] = None
            self.emitted[e] = len(items)

    def finish(self):
        self.barrier(engines=("sp",))
        self.items["sp"].append({"waits": self.pending["sp"], "fn": None, "flag": False, "dma": None})
        self.pending["sp"] = []
        self.flush()
        return {e: (len(self.items[e]), self.rank[e][-1] if self.rank[e] else 0) for e in self.ENGS}


class K:
    def __init__(self, nc):
        self.nc = nc
        self.P = Prog(nc)
        self._n = 0
        self.stack = contextlib.ExitStack()

    def end_phase(self):
        self.P.barrier()
        self.P.flush()
        self.stack.close()
        self.stack = contextlib.ExitStack()

    def sb(self, shape, dt, name=None):
        self._n += 1
        return self.stack.enter_context(self.nc.sbuf_tensor("sb%d_" % self._n + (name or "t"), list(shape), dt)), Buf(name or "")

    def ps(self, shape, name=None):
        self._n += 1
        return self.stack.enter_context(self.nc.psum_tensor("ps%d_" % self._n + (name or "p"), list(shape), F32)), Buf(name or "")

    def din(self, name, shape, dt=F32):
        return self.nc.dram_tensor(name, list(shape), dt, kind="ExternalInput").ap()

    def dint(self, name, shape, dt=F32, **kw):
        return self.nc.dram_tensor(name, list(shape), dt, kind="Internal", **kw).ap()

    def dout(self, name, shape, dt=F32):
        return self.nc.dram_tensor(name, list(shape), dt, kind="ExternalOutput").ap()

    def mm(self, out, lhsT, rhs, st, sp, r, w):
        nc = self.nc
        return self.P.op("pe", lambda: nc.tensor.matmul(out, lhsT=lhsT, rhs=rhs, start=st, stop=sp), r, w)

    def tr(self, out, in_, ident, r, w):
        nc = self.nc
        return self.P.op("pe", lambda: nc.tensor.transpose(out, in_, ident), r, w)

    def act(self, out, in_, func, r, w, bias=None, scale=None):
        nc = self.nc
        kw = {}
        if bias is not None:
            kw["bias"] = bias
        if scale is not None:
            kw["scale"] = scale
        return self.P.op("act", lambda: nc.scalar.activation(out=out, in_=in_, func=func, **kw), r, w)

    def tt(self, out, in0, in1, op, r, w, eng="dve"):
        h = self.P.h[eng]
        return self.P.op(eng, lambda: h.tensor_tensor(out=out, in0=in0, in1=in1, op=op), r, w)

    def ts(self, out, in0, s1, op0, r, w, s2=None, op1=None, eng="dve"):
        h = self.P.h[eng]
        if op1 is None:
            return self.P.op(eng, lambda: h.tensor_scalar(out=out, in0=in0, scalar1=s1, scalar2=None, op0=op0), r, w)
        return self.P.op(eng, lambda: h.tensor_scalar(out=out, in0=in0, scalar1=s1, scalar2=s2, op0=op0, op1=op1), r, w)

    def stt(self, out, in0, scalar, in1, op0, op1, r, w):
        nc = self.nc
        return self.P.op("dve", lambda: nc.vector.scalar_tensor_tensor(out=out, in0=in0, scalar=scalar, in1=in1, op0=op0, op1=op1), r, w)

    def cp(self, out, in_, r, w, eng="dve"):
        if eng == "act":
            nc = self.nc
            return self.P.op("act", lambda: nc.scalar.copy(out=out, in_=in_), r, w)
        h = self.P.h[eng]
        return self.P.op(eng, lambda: h.tensor_copy(out=out, in_=in_), r, w)

    def memset(self, ap, val, w, eng="pool"):
        h = self.P.h[eng]
        return self.P.op(eng, lambda: h.memset(ap, val), (), w)

    def scan(self, out, d0, d1, init, r, w):
        nc = self.nc
        return self.P.op("dve", lambda: nc.vector.tensor_tensor_scan(out=out, data0=d0, data1=d1, initial=init, op0=ALU.mult, op1=ALU.add), r, w)

    def dma(self, out, in_, r, w, eng="sp"):
        h = self.P.h[eng]
        return self.P.dma(lambda: h.dma_start(out=out, in_=in_), r, w, eng=eng)


def l1_io(k, S):
    d = {}
    for name, shape in (("xT", [1024, S]), ("gn", [128, 8]), ("w1", [1024, 784]), ("wgk2", [16, 64]), ("bgk2", [64, 1]),
                        ("glag", [64, 2]), ("cosd", [64, S]), ("sind", [64, S]), ("cmd", [128, 2048]), ("ed", [64, S]),
                        ("identd", [128, 128]), ("rmd", [64, 512]), ("amd", [64, 512]), ("hm2d", [64, 128]), ("hmd", [64, 2])):
        d[name] = k.din(name, shape)
    return d


def emit_l1(k, S, d, oT):
    NT = S // 512
    NKT = S // 128
    nc = k.nc
    P = k.P
    xT, gn_d, w1_d, wgk2_d, bgk2_d, glag_d = d["xT"], d["gn"], d["w1"], d["wgk2"], d["bgk2"], d["glag"]
    cos_d, sin_d, cm_d, e_d, id_d, rm_d, am_d, hm2_d, hm_d = (d["cosd"], d["sind"], d["cmd"], d["ed"], d["identd"], d["rmd"],
                                                              d["amd"], d["hm2d"], d["hmd"])

    xt, b_xt = k.sb([128, 8, 512], F32, "xt")
    rstd, b_rstd = k.sb([128, 512], F32, "rstd")
    lnv, b_lnv = k.sb([128, 512], F32, "lnv")
    hn, b_hn = k.sb([128, 8, 512], BF16, "hn")
    sq, b_sq = hn, b_hn
    wb, b_wb = k.sb([128, 8, 784], BF16, "wb")
    gn, b_gn = k.sb([128, 8], F32, "gn")
    Kaug = [k.sb([128, S], BF16, "kaug%d" % h) for h in range(2)]
    KB = [[Buf() for _ in range(NT)] for h in range(2)]
    b_kE = [Buf(), Buf()]
    Vaug = [k.sb([128, NKT, 66], BF16, "vaug%d" % h) for h in range(2)]
    VB = [[Buf() for _ in range(NT)] for h in range(2)]
    b_vones = [Buf(), Buf()]
    Qaug = [k.sb([128, 512], BF16, "qaug%d" % h) for h in range(2)]
    kmT = [k.sb([64, 64], BF16, "kmT%d" % h) for h in range(2)]
    km32, b_km32 = k.sb([64, 2], F32, "km32")
    cosT, b_cos = k.sb([64, 512], F32, "cos")
    sinT, b_sin = k.sb([64, 512], F32, "sin")
    t1, b_t1 = k.sb([64, 512], F32, "t1")
    t2, b_t2 = k.sb([64, 512], F32, "t2")
    pTs = [k.sb([128, 512], BF16, "pT%d" % i) for i in range(4)]
    cm, b_cm = k.sb([128, 4, 512], BF16, "cm")
    id_f, b_idf = k.sb([128, 128], F32, "idf")
    id_b, b_idb = k.sb([128, 128], BF16, "idb")
    ones_b, b_onesb = k.sb([128, 128], BF16, "onesb")
    ones_f, b_onesf = k.sb([128, 64], F32, "onesf")
    bq, b_bq = k.sb([128, 4, 128], F32, "bq")
    gsb, b_gsb = k.sb([128, 4, 64], F32, "gsb")
    m8, b_m8 = k.sb([128, 4, 8], F32, "m8")
    rden, b_rden = lnv, b_lnv
    osb, b_osb = t1, b_t1
    otile, b_ot = k.sb([64, 4, 512], BF16, "otile")
    QG32, b_qg = k.sb([64, 512], F32, "qg32")
    KG32, b_kg = k.sb([64, 512], F32, "kg32")
    spl, b_spl = k.sb([64, 512], F32, "spl")
    bpos, b_bpos = k.sb([64, 512], F32, "bpos")
    eb, b_eb = k.sb([64, 512], F32, "eb")
    enb, b_enb = k.sb([64, 512], F32, "enb")
    Ac, b_ac = k.sb([64, 8], F32, "Ac")
    ke32, b_ke = k.sb([64, 512], F32, "ke32")
    qt, b_qt = k.sb([64, 512], BF16, "qt")
    kpad, b_kpad = k.sb([64, 2, 512], BF16, "kpad")
    khat, b_khat = k.sb([64, 512], BF16, "khat")
    rmk, b_rmk = k.sb([64, 512], F32, "rmk")
    amk, b_amk = k.sb([64, 512], BF16, "amk")
    hm2, b_hm2 = k.sb([64, 128], F32, "hm2")
    hm, b_hm = k.sb([64, 2], F32, "hm")
    attm, b_attm = k.sb([64, 2, 512], BF16, "attm")
    gvt, b_gvt = k.sb([64, 8, 128], BF16, "gvt")
    KTt, b_ktt = k.sb([64, 8, 64], BF16, "KTt")
    gk16, b_gk16 = k.sb([16, 512], BF16, "gk16")
    wgk2f, b_wgk2f = k.sb([16, 64], F32, "wgk2f")
    wgk2b, b_wgk2b = k.sb([16, 64], BF16, "wgk2b")
    nbg, b_nbg = k.sb([64, 1], F32, "nbg")
    glag, b_glag = k.sb([64, 2], F32, "glag")
    sbog, b_sbog = k.sb([64, 2, 512], BF16, "sbog")
    st32, b_st32 = k.sb([64, 128], F32, "st32")
    stall, b_stall = k.sb([64, 9, 128], BF16, "stall")
    stmp, b_stmp = k.sb([64, 128], F32, "stmp")
    o32, b_o32 = t1, b_t1
    osq, b_osq = k.sb([64, 512], BF16, "osq")
    on32, b_on32 = t2, b_t2
    B = [k.ps([128, 512], "bank%d" % i) for i in range(8)]

    stg = xt
    k.dma(id_f[:], id_d[:, :], (), [b_idf])
    k.cp(id_b[:], id_f[:], [b_idf], [b_idb])
    k.memset(ones_b[:], 1.0, [b_onesb])
    k.memset(ones_f[:], 1.0, [b_onesf])
    k.memset(bq[:], 0.0, [b_bq])
    k.memset(st32[:], 0.0, [b_st32])
    k.memset(stall[:], 0.0, [b_stall])
    k.dma(gn[:], gn_d[:, :], (), [b_gn])
    k.dma(glag[:], glag_d[:, :], (), [b_glag])
    k.dma(hm2[:], hm2_d[:, :], (), [b_hm2])
    k.dma(hm[:], hm_d[:, :], (), [b_hm])
    k.dma(rmk[:], rm_d[:, :], (), [b_rmk])
    k.dma(wgk2f[:], wgk2_d[:, :], (), [b_wgk2f])
    k.cp(wgk2b[:], wgk2f[:], [b_wgk2f], [b_wgk2b])
    k.dma(nbg[:], bgk2_d[:, :], (), [b_nbg])
    k.ts(nbg[:], nbg[:], -1.0, ALU.mult, [b_nbg], [b_nbg])
    k.dma(t1[:], am_d[:, :], (), [b_t1])
    k.cp(amk[:], t1[:], [b_t1], [b_amk])
    sflat = stg[:].rearrange("p a b -> p (a b)")
    k.dma(sflat[:, 0:2048], cm_d[:, :], (), [b_xt])
    k.cp(cm[:].rearrange("p a b -> p (a b)"), sflat[:, 0:2048], [b_xt], [b_cm])
    w1v = w1_d.rearrange("(kc p) f -> p kc f", p=128)
    for half in range(2):
        sv = sflat[:, 0:4 * 784].rearrange("p (a b) -> p a b", b=784)
        k.dma(sv, w1v[:, half * 4:(half + 1) * 4, :], (), [b_xt])
        k.cp(wb[:, half * 4:(half + 1) * 4, :], sv, [b_xt], [b_wb], eng="dve" if half == 0 else "pool")
    for pc in range(S // 2048):
        k.dma(sflat[64:128, 0:2048], e_d[:, pc * 2048:(pc + 1) * 2048], (), [b_xt])
        for h in range(2):
            k.cp(Kaug[h][0][64:128, pc * 2048:(pc + 1) * 2048], sflat[64:128, 0:2048], [b_xt], [b_kE[h]],
                 eng="dve" if h == 0 else "pool")
    for h in range(2):
        k.memset(Vaug[h][0][:, :, 64:65], 1.0, [b_vones[h]])
        k.memset(kmT[h][0][:], 0.0, [kmT[h][1]])

    xTv = xT.rearrange("(kc p) s -> p kc s", p=128)
    def proj(bank, M, col0, ncols=None):
        pt, pb = B[bank]
        for kc in range(8):
            k.mm(pt[0:M, 0:512], wb[:, kc, col0:col0 + M], hn[:, kc, :], kc == 0, kc == 7, [b_wb, b_hn], [pb])
        return pt, pb

    for g in range(NT):
        c0 = g * 512
        k.dma(xt[:, 0:4, :], xTv[:, 0:4, c0:c0 + 512], (), [b_xt])
        k.dma(xt[:, 4:8, :], xTv[:, 4:8, c0:c0 + 512], (), [b_xt])
        k.dma(cosT[:], cos_d[:, c0:c0 + 512], (), [b_cos])
        k.dma(sinT[:], sin_d[:, c0:c0 + 512], (), [b_sin])
        k.act(sq[:], xt[:], AF.Square, [b_xt], [b_sq])
        pt, pb = B[0]
        for kc in range(8):
            k.mm(pt[:, :], ones_b[:], sq[:, kc, :], kc == 0, kc == 7, [b_onesb, b_sq], [pb])
        k.act(lnv[:], pt[:, :], AF.Ln, [pb], [b_lnv], bias=EPS, scale=1.0 / 1024)
        k.act(rstd[:], lnv[:], AF.Exp, [b_lnv], [b_rstd], scale=-0.5)
        for kc in range(8):
            k.stt(hn[:, kc, :], xt[:, kc, :], gn[:, kc:kc + 1], rstd[:], ALU.mult, ALU.mult, [b_xt, b_gn, b_rstd], [b_hn])
        for idx in range(4):
            h = idx % 2
            isk = idx >= 2
            pt, pb = proj(1 + (idx % 2), 64, idx * 64)
            if isk:
                dest = Kaug[h][0][0:64, c0:c0 + 512]
                dbuf = KB[h][g]
            else:
                dest = Qaug[h][0][0:64, :]
                dbuf = Qaug[h][1]
            k.tt(t1[:], pt[0:64, 0:512], cosT[:], ALU.mult, [pb, b_cos], [b_t1])
            k.tt(t2[0:32, :], pt[32:64, 0:512], sinT[32:64, :], ALU.mult, [pb, b_sin], [b_t2])
            k.tt(t2[32:64, :], pt[0:32, 0:512], sinT[0:32, :], ALU.mult, [pb, b_sin], [b_t2])
            k.tt(dest, t1[:], t2[:], ALU.add, [b_t1, b_t2], [dbuf])
        pt, pb = B[1]
        for st in range(4):
            for kc in range(8):
                k.mm(pt[:, st * 128:(st + 1) * 128], hn[:, kc, st * 128:(st + 1) * 128], wb[:, kc, 256:384],
                     kc == 0, kc == 7, [b_hn, b_wb], [pb])
        pv = pt[:, 0:512].rearrange("p (a b) -> p a b", b=128)
        for h in range(2):
            k.cp(Vaug[h][0][:, 4 * g:4 * g + 4, 0:64], pv[:, :, h * 64:(h + 1) * 64], [pb], [VB[h][g]], eng="act")
        pt, pb = proj(2, 128, 384)
        k.cp(QG32[:], pt[0:64, 0:512], [pb], [b_qg], eng="act")
        k.cp(KG32[:], pt[64:128, 0:512], [pb], [b_kg], eng="act")
        for c in range(8):
            pt, pb = B[3 + c // 4]
            for kc in range(8):
                k.mm(pt[0:64, (c % 4) * 128:(c % 4 + 1) * 128], hn[:, kc, c * 64:(c + 1) * 64], wb[:, kc, 512:640],
                     kc == 0, kc == 7, [b_hn, b_wb], [pb])
        for hf in range(2):
            pt, pb = B[3 + hf]
            k.cp(gvt[:, hf * 4:(hf + 1) * 4, :].rearrange("p a b -> p (a b)"), pt[0:64, 0:512], [pb], [b_gvt], eng="act")
        pt, pb = proj(0, 16, 640)
        k.cp(gk16[:], pt[0:16, 0:512], [pb], [b_gk16], eng="act")
        k.mm(pt[0:64, 0:512], wgk2b[:], gk16[:], True, True, [b_wgk2b, b_gk16], [pb])
        k.act(spl[:], pt[0:64, 0:512], AF.Exp, [pb, b_nbg], [b_spl], bias=nbg[:, 0:1], scale=-1.0)
        k.act(spl[:], spl[:], AF.Ln, [b_spl], [b_spl], bias=1.0, scale=1.0)
        pt, pb = proj(1, 128, 656)
        k.act(sbog[:, 0, :], pt[0:64, 0:512], AF.Silu, [pb], [b_sbog])
        k.act(sbog[:, 1, :], pt[64:128, 0:512], AF.Silu, [pb], [b_sbog])

        k.scan(bpos[:], rmk[:], spl[:], 0.0, [b_rmk, b_spl], [b_bpos])
        k.act(eb[:], bpos[:], AF.Exp, [b_bpos], [b_eb], scale=-1.0 / 16)
        k.act(enb[:], bpos[:], AF.Exp, [b_bpos], [b_enb], scale=1.0 / 16)
        blast = bpos[:].rearrange("p (c t) -> p c t", t=64)[:, :, 63:64].rearrange("p c o -> p (c o)")
        k.act(Ac[:], blast, AF.Exp, [b_bpos], [b_ac], scale=-1.0 / 16)
        k.stt(qt[:], QG32[:], 32.0 ** -0.5, eb[:], ALU.mult, ALU.mult, [b_qg, b_eb], [b_qt])
        k.tt(ke32[:], KG32[:], enb[:], ALU.mult, [b_kg, b_enb], [b_ke])
        for h in range(2):
            k.ts(kpad[:, h, :], ke32[:], hm[:, h:h + 1], ALU.mult, [b_ke, b_hm], [b_kpad])
        for c in range(8):
            k.act(khat[:, c * 64:(c + 1) * 64], ke32[:, c * 64:(c + 1) * 64], AF.Copy, [b_ke, b_ac], [b_khat], scale=Ac[:, c:c + 1])
        pt, pb = B[0]
        for c in range(8):
            k.mm(pt[0:64, c * 64:(c + 1) * 64], khat[:, c * 64:(c + 1) * 64], id_b[0:64, 0:64], True, True, [b_khat, b_idb], [pb])
        k.cp(KTt[:].rearrange("p a b -> p (a b)"), pt[0:64, 0:512], [pb], [b_ktt], eng="act")
        for h in range(2):
            pt, pb = B[1 + h]
            for c in range(8):
                k.mm(pt[0:64, c * 64:(c + 1) * 64], kpad[:, h, c * 64:(c + 1) * 64], qt[:, c * 64:(c + 1) * 64], True, True,
                     [b_kpad, b_qt], [pb])
            k.tt(attm[:, h, :], pt[0:64, 0:512], amk[:], ALU.mult, [pb, b_amk], [b_attm])
        pso = [B[5], B[6]]
        for c in range(8):
            psd, b_psd = B[7] if c < 4 else B[0]
            for h in range(2):
                k.mm(psd[0:64, (c % 4) * 128 + h * 64:(c % 4) * 128 + (h + 1) * 64], KTt[:, c, :], gvt[:, c, h * 64:(h + 1) * 64],
                     True, True, [b_ktt, b_gvt], [b_psd])
        k.cp(stall[:, 0, :], stall[:, 8, :], [b_stall], [b_stall])
        for c in range(8):
            psd, b_psd = B[7] if c < 4 else B[0]
            k.tt(stmp[:], psd[0:64, (c % 4) * 128:(c % 4 + 1) * 128], hm2[:], ALU.mult, [b_psd, b_hm2], [b_stmp])
            k.stt(st32[:], st32[:], Ac[:, c:c + 1], stmp[:], ALU.mult, ALU.add, [b_st32, b_ac, b_stmp], [b_st32])
            k.cp(stall[:, c + 1, :], st32[:], [b_st32], [b_stall])
        for c in range(8):
            for h in range(2):
                po, pbo = pso[h]
                k.mm(po[0:64, c * 64:(c + 1) * 64], gvt[:, c, h * 64:(h + 1) * 64], attm[:, h, c * 64:(c + 1) * 64], True, False,
                     [b_gvt, b_attm], [pbo])
                k.mm(po[0:64, c * 64:(c + 1) * 64], stall[:, c, h * 64:(h + 1) * 64], qt[:, c * 64:(c + 1) * 64], False, True,
                     [b_stall, b_qt], [pbo])
        for h in range(2):
            po, pbo = pso[h]
            k.cp(o32[:], po[0:64, 0:512], [pbo], [b_o32], eng="act")
            k.act(osq[:], po[0:64, 0:512], AF.Square, [pbo], [b_osq])
            pt, pb = B[0]
            k.mm(pt[0:64, 0:512], ones_b[0:64, 0:64], osq[:], True, True, [b_onesb, b_osq], [pb])
            k.act(lnv[0:64, :], pt[0:64, 0:512], AF.Ln, [pb], [b_lnv], bias=EPS, scale=1.0 / 64)
            k.act(lnv[0:64, :], lnv[0:64, :], AF.Exp, [b_lnv], [b_lnv], scale=-0.5)
            k.tt(on32[:], o32[:], lnv[0:64, :], ALU.mult, [b_o32, b_lnv], [b_on32])
            k.stt(otile[:, 2 + h, :], on32[:], glag[:, h:h + 1], sbog[:, h, :], ALU.mult, ALU.mult, [b_on32, b_glag, b_sbog], [b_ot])

        for h in range(2):
            KA, _ = Kaug[h]
            VA, _ = Vaug[h]
            QA, b_QA = Qaug[h]
            kmt, b_kmt = kmT[h]
            k.P.op("dve", (lambda o=km32[:], i=KA[0:64, c0:c0 + 512].rearrange("p (a b) -> p a b", b=256):
                           nc.vector.tensor_reduce(out=o, in_=i, axis=AX.X, op=ALU.add)), [KB[h][g]], [b_km32])
            k.cp(kmt[:, 2 * g:2 * g + 2], km32[:], [b_km32], [b_kmt])
            pg, b_pg = B[7]
            for st in range(4):
                k.mm(pg[:, st * 64:(st + 1) * 64], QA[0:64, st * 128:(st + 1) * 128], kmt[:, :], True, True, [b_QA, b_kmt], [b_pg])
            k.memset(gsb[:], -1e30, [b_gsb], eng="dve")
            for st in range(4):
                blk = 2 * g + st // 2
                if blk > 0:
                    k.cp(gsb[:, st, 0:blk], pg[:, st * 64:st * 64 + blk], [b_pg], [b_gsb])
            for st in range(4):
                blk = 2 * g + st // 2
                k.P.op("dve", (lambda o=m8[:, st, :], i=gsb[:, st, :]: nc.vector.max(out=o, in_=i)), [b_gsb], [b_m8])
                k.ts(bq[:, st, 64:128], gsb[:, st, :], m8[:, st, 2:3], ALU.is_ge, [b_gsb, b_m8], [b_bq], s2=-NEG, op1=ALU.mult)
            k.ts(bq[:, :, 64:128], bq[:, :, 64:128], NEG, ALU.add, [b_bq], [b_bq])
            for st in range(4):
                blk = 2 * g + st // 2
                k.memset(bq[:, st, 64 + blk:65 + blk], 0.0, [b_bq], eng="dve")
            pg2, b_pg2 = B[0]
            for st in range(4):
                k.tr(pg2[:, st * 128:(st + 1) * 128], bq[:, st, :], id_f[:], [b_bq, b_idf], [b_pg2])
            k.cp(QA[64:128, :], pg2[64:128, 0:512], [b_pg2], [b_QA], eng="act")
            pO, b_pO = B[5 + h]
            nkt = 4 * g + 4
            LA = 3

            def qk(kt):
                j = kt - 4 * g
                q0 = 256 if j >= 2 else 0
                pS, b_pS = B[1 + (kt % 4)]
                k.mm(pS[:, q0:512], KA[:, kt * 128:(kt + 1) * 128], QA[:, q0:512], True, j < 0,
                     [KB[h][kt // 4], b_kE[h], b_QA], [b_pS])
                if j >= 0:
                    k.mm(pS[:, q0:512], id_b[:], cm[:, j, q0:512], False, True, [b_idb, b_cm], [b_pS])

            def pv(kt):
                j = kt - 4 * g
                q0 = 256 if j >= 2 else 0
                pS, b_pS = B[1 + (kt % 4)]
                pTt, b_pT = pTs[kt % 4]
                k.act(pTt[:, q0:512], pS[:, q0:512], AF.Exp, [b_pS], [b_pT], scale=0.125)
                k.mm(pO[0:65, q0:512], VA[:, kt, 0:65], pTt[:, q0:512], kt == 0, kt == nkt - 1,
                     [VB[h][kt // 4], b_vones[h], b_pT], [b_pO])

            for i in range(nkt + LA):
                if i < nkt:
                    qk(i)
                if i >= LA:
                    pv(i - LA)
            k.P.op("dve", (lambda o=rden[64:65, :], i=pO[64:65, 0:512]: nc.vector.reciprocal(out=o, in_=i)), [b_pO], [b_rden])
            pt, pb = B[0]
            k.mm(pt[0:64, 0:512], ones_f[64:65, 0:64], rden[64:65, :], True, True, [b_onesf, b_rden], [pb])
            k.cp(osb[:], pO[0:64, 0:512], [b_pO], [b_osb], eng="act")
            k.tt(otile[:, h, :], osb[:], pt[0:64, 0:512], ALU.mult, [b_osb, pb], [b_ot])
        oT(g, otile, b_ot)


def l1_consts(S):
    half = 32
    inv = (10000.0 ** (-np.arange(half, dtype=np.float32) / half)).astype(np.float32)
    ang = np.arange(S, dtype=np.float32)[None, :] * inv[:, None]
    cos = np.cos(ang).astype(np.float32)
    sin = np.sin(ang).astype(np.float32)
    cosd = np.concatenate([cos, cos], 0)
    sind = np.concatenate([sin, -sin], 0)
    kk = np.arange(128)[:, None]
    qq = np.arange(512)[None, :]
    cm = np.concatenate([np.where(qq < j * 128 + kk, NEG, 0.0) for j in range(4)], 1).astype(np.float32)
    ed = (np.arange(S)[None, :] // 256 == np.arange(64)[:, None]).astype(np.float32)
    ident = np.eye(128, dtype=np.float32)
    rm = np.tile((np.arange(512) % 64 != 0).astype(np.float32)[None, :], (64, 1))
    s_ = np.arange(64)[:, None]
    t_ = np.arange(64)[None, :]
    am = np.tile((s_ <= t_).astype(np.float32), (1, 8))
    hm = np.zeros((64, 2), np.float32)
    hm[0:32, 0] = 1
    hm[32:64, 1] = 1
    hm2 = np.repeat(hm, 64, axis=1)
    return dict(cosd=cosd, sind=sind, cmd=cm, ed=ed, identd=ident, rmd=rm, amd=am, hm2d=hm2, hmd=hm)


HALO = 8
OCS = 2048


def choose_tiles(W):
    if W == 4104:
        return 4, 3, 342
    if W == 520:
        return 2, 1, 260
    raise ValueError(W)


class Trunk:
    def __init__(self, k, W):
        self.k = k
        self.nc = k.nc
        self.W = W
        self.NB, self.NTB, self.TW = choose_tiles(W)
        self.TB = self.NTB * self.TW
        TB, TW = self.TB, self.TW
        self.h, self.b_h = k.sb([128, 8, TB], F32, "h")
        self.hn, self.b_hn = k.sb([128, 8, TB], BF16, "hn")
        self.act, self.b_act = k.sb([128, 24, TB], BF16, "act")
        self.wst = [k.sb([128, 8, 128], F32, "wst%d" % i) for i in range(3)]
        self.wbf = [k.sb([128, 24, 128], BF16, "wbf%d" % i) for i in range(3)]
        self.wi = 0
        self.si = 0
        self.wq = []
        self.todo = []
        self.pc = [k.sb([128, 4 + TB], F32, "pc%d" % i) for i in range(2)]
        self.pci = 0
        self.yt = [k.sb([128, TB], F32, "yt%d" % i) for i in range(2)]
        self.gel, self.b_gel = k.sb([128, TB], F32, "gel")
        self.sqt, self.b_sqt = k.sb([128, 8, TW], BF16, "sqt")
        self.lnv, self.b_lnv = k.sb([128, TW], F32, "lnvt")
        self.rstd, self.b_rstd = k.sb([128, TW], F32, "rstdt")
        self.ones_b, self.b_ones = k.sb([128, 128], BF16, "onesb")
        self.m, self.b_m = k.sb([128, 1], F32, "hmask")
        self.carry, self.b_carry = k.sb([128, 48, 2], F32, "carry")
        self.B = [k.ps([128, 512], "bank%d" % i) for i in range(8)]
        self.bi = 0
        k.memset(self.ones_b[:], 1.0, [self.b_ones])
        k.memset(self.carry[:], 0.0, [self.b_carry])

    def bank(self):
        b = self.B[self.bi]
        self.bi = (self.bi + 1) % 8
        return b

    def small(self, name, dram_ap, shape):
        t, b = self.k.sb(shape, F32, name)
        self.k.dma(t[:], dram_ap, (), [b])
        return t, b

    def request(self, wd, KC, col0):
        k = self.k
        wb, b_wb = self.wbf[self.wi]
        self.wi = (self.wi + 1) % 3
        wv = wd.rearrange("(kc p) f -> p kc f", p=128)
        for k0 in range(0, KC, 8):
            k1 = min(KC, k0 + 8)
            st, b_st = self.wst[self.si]
            self.si = (self.si + 1) % 3
            k.dma(st[:, 0:k1 - k0, :], wv[:, k0:k1, col0:col0 + 128], (), [b_st])
            k.cp(wb[:, k0:k1, :], st[:, 0:k1 - k0, :], [b_st], [b_wb], eng="act")
        self.wq.append((wb, b_wb))

    def plan(self, reqs):
        assert not self.wq and not getattr(self, "todo", None), "previous plan not fully consumed"
        self.todo = list(reqs)
        for _ in range(2):
            if self.todo:
                self.request(*self.todo.pop(0))

    def run_stage(self, reqs, body):
        for i in range(len(reqs)):
            body(i)

    def rmsnorm(self, g_t, b_g, out_f32=None):
        k = self.k
        TW = self.TW
        for nt in range(self.NTB):
            cs = slice(nt * TW, (nt + 1) * TW)
            k.act(self.sqt[:], self.h[:, :, cs], AF.Square, [self.b_h], [self.b_sqt])
            pt, pb = self.bank()
            for kc in range(8):
                k.mm(pt[:, 0:TW], self.ones_b[:], self.sqt[:, kc, :], kc == 0, kc == 7, [self.b_ones, self.b_sqt], [pb])
            k.act(self.lnv[:], pt[:, 0:TW], AF.Ln, [pb], [self.b_lnv], bias=EPS, scale=1.0 / 1024)
            k.act(self.rstd[:], self.lnv[:], AF.Exp, [self.b_lnv], [self.b_rstd], scale=-0.5)
            for kc in range(8):
                if out_f32 is None:
                    k.stt(self.hn[:, kc, cs], self.h[:, kc, cs], g_t[:, kc:kc + 1], self.rstd[:], ALU.mult, ALU.mult,
                          [self.b_h, b_g, self.b_rstd], [self.b_hn])
                else:
                    k.stt(out_f32[0][:, kc, cs], self.h[:, kc, cs], g_t[:, kc:kc + 1], self.rstd[:], ALU.mult, ALU.mult,
                          [self.b_h, b_g, self.b_rstd], [out_f32[1]])

    def linear(self, src, b_src, KC, wd, col0, evac):
        k = self.k
        TW = self.TW
        if self.todo:
            self.request(*self.todo.pop(0))
        wb, b_wb = self.wq.pop(0)
        for nt in range(self.NTB):
            cs = slice(nt * TW, (nt + 1) * TW)
            pt, pb = self.bank()
            for kc in range(KC):
                k.mm(pt[:, 0:TW], wb[:, kc, :], src[:, kc, cs], kc == 0, kc == KC - 1, [b_wb, b_src], [pb])
            evac(nt, cs, pt[:, 0:TW], pb)

    @staticmethod
    def ffn_reqs(w_up, w_down):
        return ([(w_up, 8, (part * 24 + j) * 128) for j in range(24) for part in range(2)]
                + [(w_down, 24, fc * 128) for fc in range(8)])

    def conv(self, pc, b_pc, ntap, cw_t, b_cw, cb_t, b_cb, idx, yt, b_yt):
        k = self.k
        TB = self.TB
        last = ntap - 1
        k.act(yt[:], pc[:, last:last + TB], AF.Identity, [b_pc, b_cw, b_cb], [b_yt],
              bias=cb_t[:, idx:idx + 1], scale=cw_t[:, idx, last:last + 1])
        for i in range(last - 1, -1, -1):
            k.stt(yt[:], pc[:, i:i + TB], cw_t[:, idx, i:i + 1], yt[:], ALU.mult, ALU.add, [b_pc, b_cw, b_yt], [b_yt])

    def ffn(self, first, gn_t, b_gn, w_up, cw_t, b_cw, cb_t, b_cb, w_down, mask_halo):
        k = self.k
        TB, TW = self.TB, self.TW
        self.rmsnorm(gn_t, b_gn)
        ys = [None, None]

        def up_body(i):
            j, part = i // 2, i % 2
            idx = part * 24 + j
            pc, b_pc = self.pc[self.pci]
            self.pci = (self.pci + 1) % 2
            yt, b_yt = self.yt[part]
            k.cp(pc[:, 0:2], self.carry[:, idx, :], [self.b_carry], [b_pc], eng="act")

            def evac(nt, cs, ps, pb, pc=pc, b_pc=b_pc):
                k.cp(pc[:, 2 + cs.start:2 + cs.stop], ps, [pb], [b_pc], eng="act")

            self.linear(self.hn, self.b_hn, 8, w_up, idx * 128, evac)
            if first and mask_halo:
                k.ts(pc[:, 2:2 + HALO], pc[:, 2:2 + HALO], self.m[:, 0:1], ALU.mult, [b_pc, self.b_m], [b_pc])
            k.cp(self.carry[:, idx, :], pc[:, TB:TB + 2], [b_pc], [self.b_carry], eng="act")
            self.conv(pc, b_pc, 3, cw_t, b_cw, cb_t, b_cb, idx, yt, b_yt)
            ys[part] = (yt, b_yt)
            if part == 1:
                (yu, b_yu), (yg, b_yg) = ys
                k.act(self.gel[:], yg[:], AF.Gelu_apprx_tanh, [b_yg], [self.b_gel])
                k.tt(self.act[:, j, :], yu[:], self.gel[:], ALU.mult, [b_yu, self.b_gel], [self.b_act])

        self.run_stage([(w_up, 8, (part * 24 + j) * 128) for j in range(24) for part in range(2)], up_body)

        def down_body(fc):
            def evac(nt, cs, ps, pb, fc=fc):
                k.tt(self.h[:, fc, cs], self.h[:, fc, cs], ps, ALU.add, [self.b_h, pb], [self.b_h])

            self.linear(self.act, self.b_act, 24, w_down, fc * 128, evac)

        self.run_stage([(w_down, 24, fc * 128) for fc in range(8)], down_body)


def l2_io(k, W):
    d = {}
    for name, shape in (("xTw", [1024, W]), ("m", [128, 1]), ("sel", [128, 4]), ("w_out0", [1024, 1024]), ("fg0", [128, 8]),
                        ("w_up0", [1024, 6144]), ("cw0", [128, 48, 3]), ("cb0", [128, 48]), ("w_down0", [3072, 1024]),
                        ("mg", [128, 8]), ("w_in", [1024, 2048]), ("rcw", [128, 8, 4]), ("rcb", [128, 8]), ("wa", [1024, 256]),
                        ("ba", [128, 8]), ("wx", [1024, 256]), ("bx", [128, 8]), ("lam", [128, 8])):
        d[name] = k.din(name, shape)
    return d


def emit_l2(k, W, S, d, og, h1_o, gg_o, hl_o, pl_o, ex_o):
    nc = k.nc
    T = Trunk(k, W)
    NB, TB, TW = T.NB, T.TB, T.TW
    TOK = W - HALO
    xTw, m_d, sel_d, w_out, fg_d, w_up, cw_d, cb_d, w_down = (d["xTw"], d["m"], d["sel"], d["w_out0"], d["fg0"], d["w_up0"],
                                                               d["cw0"], d["cb0"], d["w_down0"])
    mg_d, w_in, rcw_d, rcb_d, wa_d, ba_d, wx_d, bx_d, lam_d = (d["mg"], d["w_in"], d["rcw"], d["rcb"], d["wa"], d["ba"],
                                                               d["wx"], d["bx"], d["lam"])
    sel, b_sel = T.small("sel", sel_d[:, :], [128, 4])

    k.dma(T.m[:], m_d[:, :], (), [T.b_m])
    fg, b_fg = T.small("fg", fg_d[:, :], [128, 8])
    cw, b_cw = T.small("cw", cw_d[:, :, :], [128, 48, 3])
    cb, b_cb = T.small("cb", cb_d[:, :], [128, 48])
    mg, b_mg = T.small("mg", mg_d[:, :], [128, 8])
    rcw, b_rcw = T.small("rcw", rcw_d[:, :, :], [128, 8, 4])
    rcb, b_rcb = T.small("rcb", rcb_d[:, :], [128, 8])
    ba, b_ba = T.small("ba", ba_d[:, :], [128, 8])
    bx, b_bx = T.small("bx", bx_d[:, :], [128, 8])
    lam, b_lam = T.small("lam", lam_d[:, :], [128, 8])
    c1, b_c1 = k.sb([128, 8], F32, "c1")
    c2, b_c2 = k.sb([128, 8], F32, "c2")
    k.act(c1[:], lam[:], AF.Exp, [b_lam], [b_c1], scale=-1.0)
    k.act(c1[:], c1[:], AF.Ln, [b_c1], [b_c1], bias=1.0, scale=1.0)
    k.ts(c2[:], c1[:], -16.0, ALU.mult, [b_c1], [b_c2])
    k.ts(c1[:], c1[:], -8.0, ALU.mult, [b_c1, b_c2], [b_c1])
    ob, b_ob = T.hn, T.b_hn
    gg, b_gg = k.sb([128, TB], BF16, "gg")
    rcar, b_rcar = k.sb([128, 8, 3], F32, "rcar")
    hcar, b_hcar = k.sb([128, 8], F32, "hcar")
    pcar, b_pcar = k.sb([128, 8], F32, "pcar")
    zer, b_zer = k.sb([128, TB], F32, "zer")
    xrc, b_xrc = k.sb([128, 2, TB], F32, "xrc")
    xrb, b_xrb = k.sb([128, 2, TB], BF16, "xrb")
    rr, b_rr = k.sb([128, TB], F32, "rr")
    ii, b_ii = k.sb([128, TB], F32, "ii")
    aa, b_aa = k.sb([128, TB], F32, "aa")
    uu, b_uu = k.sb([128, TB], F32, "uu")
    hl, b_hl = k.sb([128, TB], F32, "hl")
    pl, b_pl = k.sb([128, TB], F32, "pl")
    k.memset(rcar[:], 0.0, [b_rcar])
    k.memset(zer[:], 0.0, [b_zer])
    ext, b_ext = k.sb([128, 8, 2], F32, "ext")

    oTv = [o_.rearrange("(kc p) s -> p kc s", p=128) for o_ in og]
    xTv = xTw.rearrange("(kc p) s -> p kc s", p=128)
    h1v = h1_o.rearrange("(kc p) s -> p kc s", p=128)
    ggv = gg_o.rearrange("(kc p) s -> p kc s", p=128)
    hlv = hl_o.rearrange("(kc p) s -> p kc s", p=128)
    plv = pl_o.rearrange("(kc p) s -> p kc s", p=128)
    exv = ex_o.rearrange("(kc p) s -> p kc s", p=128)

    for blk in range(NB):
        first = blk == 0
        g0 = blk * TB
        for c in range(4):
            cand = T.act[:, 8 * (c % 3):8 * (c % 3) + 8, :]
            lo = c * TOK - HALO + g0
            skip = max(0, -lo)
            if skip:
                k.memset(cand[:, :, 0:skip], 0.0, [T.b_act])
            a = lo + skip
            while a < lo + TB:
                j = a // OCS
                e = min(lo + TB, (j + 1) * OCS)
                k.dma(cand[:, :, a - lo:e - lo], oTv[j][:, :, a - j * OCS:e - j * OCS], (), [T.b_act])
                a = e
            if c == 0:
                k.ts(ob[:], cand, sel[:, 0:1], ALU.mult, [T.b_act, b_sel], [b_ob])
            else:
                k.stt(ob[:], cand, sel[:, c:c + 1], ob[:], ALU.mult, ALU.add, [T.b_act, b_sel, b_ob], [b_ob])
        k.dma(T.h[:, 0:4, :], xTv[:, 0:4, g0:g0 + TB], (), [T.b_h])
        k.dma(T.h[:, 4:8, :], xTv[:, 4:8, g0:g0 + TB], (), [T.b_h])
        rg_reqs = []
        for n in range(4):
            rg_reqs += [(w_in, 8, 1024 + (2 * n + c2i) * 128) for c2i in range(2)]
            for c2i in range(2):
                rg_reqs += [(wa_d[n * 256:(n + 1) * 256, :], 2, c2i * 128), (wx_d[n * 256:(n + 1) * 256, :], 2, c2i * 128)]
        T.plan([(w_out, 8, fc * 128) for fc in range(8)] + T.ffn_reqs(w_up, w_down)
               + [(w_in, 8, fc * 128) for fc in range(8)] + rg_reqs)

        def wo_body(fc):
            def evac(nt, cs, ps, pb, fc=fc):
                k.tt(T.h[:, fc, cs], T.h[:, fc, cs], ps, ALU.add, [T.b_h, pb], [T.b_h])

            T.linear(ob, b_ob, 8, w_out, fc * 128, evac)

        T.run_stage([(w_out, 8, fc * 128) for fc in range(8)], wo_body)
        T.ffn(first, fg, b_fg, w_up, cw, b_cw, cb, b_cb, w_down, False)
        k.dma(h1v[:, :, g0:g0 + TB], T.h[:], [T.b_h], (), eng="pool")
        T.rmsnorm(mg, b_mg)
        def gb_body(fc):
            def evac(nt, cs, ps, pb, fc=fc):
                k.act(gg[:, cs], ps, AF.Gelu_apprx_tanh, [pb], [b_gg])

            T.linear(T.hn, T.b_hn, 8, w_in, fc * 128, evac)
            k.dma(ggv[:, fc, g0:g0 + TB], gg[:], [b_gg], (), eng="pool")

        T.run_stage([(w_in, 8, fc * 128) for fc in range(8)], gb_body)
        reqs = []
        for n in range(4):
            reqs += [(w_in, 8, 1024 + (2 * n + c2i) * 128) for c2i in range(2)]
            for c2i in range(2):
                reqs += [(wa_d[n * 256:(n + 1) * 256, :], 2, c2i * 128), (wx_d[n * 256:(n + 1) * 256, :], 2, c2i * 128)]

        def rg_body(i, blk=blk, first=first, g0=g0):
            n, r = i // 6, i % 6
            if r < 2:
                c2i = r
                c8 = 2 * n + c2i
                pc, b_pc = T.pc[T.pci]
                T.pci = (T.pci + 1) % 2
                k.cp(pc[:, 0:3], rcar[:, c8, :], [b_rcar], [b_pc], eng="act")

                def evac(nt, cs, ps, pb, pc=pc, b_pc=b_pc):
                    k.cp(pc[:, 3 + cs.start:3 + cs.stop], ps, [pb], [b_pc], eng="act")

                T.linear(T.hn, T.b_hn, 8, w_in, 1024 + c8 * 128, evac)
                if first:
                    k.ts(pc[:, 3:3 + HALO], pc[:, 3:3 + HALO], T.m[:, 0:1], ALU.mult, [b_pc, T.b_m], [b_pc])
                k.cp(rcar[:, c8, :], pc[:, TB:TB + 3], [b_pc], [b_rcar], eng="act")
                yt, b_yt = T.yt[c2i]
                T.conv(pc, b_pc, 4, rcw, b_rcw, rcb, b_rcb, c8, yt, b_yt)
                k.cp(xrc[:, c2i, :], yt[:], [b_yt], [b_xrc], eng="pool")
                k.cp(xrb[:, c2i, :], yt[:], [b_yt], [b_xrb], eng="act")
                return
            c2i, which = (r - 2) // 2, (r - 2) % 2
            fc = 2 * n + c2i
            wd, bias_t, b_bias, dst, b_dst = ((wa_d, ba, b_ba, rr, b_rr), (wx_d, bx, b_bx, ii, b_ii))[which]

            def evac(nt, cs, ps, pb, dst=dst, b_dst=b_dst, bias_t=bias_t, b_bias=b_bias, fc=fc):
                k.act(dst[:, cs], ps, AF.Sigmoid, [pb, b_bias], [b_dst], bias=bias_t[:, fc:fc + 1], scale=1.0)

            T.linear(xrb, b_xrb, 2, wd[n * 256:(n + 1) * 256, :], c2i * 128, evac)
            if which == 0:
                return
            k.act(aa[:], rr[:], AF.Exp, [b_rr, b_c1], [b_aa], scale=c1[:, fc:fc + 1])
            k.act(rr[:], rr[:], AF.Exp, [b_rr, b_c2], [b_rr], scale=c2[:, fc:fc + 1])
            k.act(rr[:], rr[:], AF.Sqrt, [b_rr], [b_rr], bias=1.0, scale=-1.0)
            k.tt(uu[:], xrc[:, c2i, :], ii[:], ALU.mult, [b_xrc, b_ii], [b_uu])
            k.tt(uu[:], uu[:], rr[:], ALU.mult, [b_uu, b_rr], [b_uu])
            if first:
                k.ts(uu[:, 0:HALO], uu[:, 0:HALO], T.m[:, 0:1], ALU.mult, [b_uu, T.b_m], [b_uu])
                k.memset(hl[:, 0:5], 0.0, [b_hl])
                k.memset(pl[:, 0:5], 0.0, [b_pl])
                s0, hi, pi = 5, 0.0, 1.0
            else:
                s0, hi, pi = 0, hcar[:, fc:fc + 1], pcar[:, fc:fc + 1]
            k.scan(hl[:, s0:TB], aa[:, s0:TB], uu[:, s0:TB], hi, [b_aa, b_uu, b_hcar], [b_hl])
            k.scan(pl[:, s0:TB], aa[:, s0:TB], zer[:, s0:TB], pi, [b_aa, b_zer, b_pcar], [b_pl])
            k.cp(hcar[:, fc:fc + 1], hl[:, TB - 1:TB], [b_hl], [b_hcar], eng="pool")
            k.cp(pcar[:, fc:fc + 1], pl[:, TB - 1:TB], [b_pl], [b_pcar], eng="pool")
            k.dma(hlv[:, fc, g0:g0 + TB], hl[:], [b_hl], (), eng="pool")
            k.dma(plv[:, fc, g0:g0 + TB], pl[:], [b_pl], (), eng="pool")
            if blk == NB - 1:
                k.cp(ext[:, fc, 0:1], hl[:, TB - 4:TB - 3], [b_hl], [b_ext], eng="pool")
                k.cp(ext[:, fc, 1:2], pl[:, TB - 4:TB - 3], [b_pl], [b_ext], eng="pool")

        T.run_stage(reqs, rg_body)
    k.dma(exv[:, :, :], ext[:], [b_ext], (), eng="pool")


def l3_io(k, W):
    d = {}
    for name, shape in (("srank", [128, 4]), ("oms", [128, 4]), ("w_out1", [1024, 1024]), ("fg1", [128, 8]),
                        ("w_up1", [1024, 6144]), ("cw1", [128, 48, 3]), ("cb1", [128, 48]), ("w_down1", [3072, 1024]),
                        ("fin", [128, 8])):
        d[name] = k.din(name, shape)
    return d


def emit_l3(k, W, d, m_d, h1w, ggw, hlw, plw, exg, out_o):
    nc = k.nc
    T = Trunk(k, W)
    NB, TB, TW = T.NB, T.TB, T.TW
    TOK = W - HALO
    w_out, fg_d, w_up, cw_d, cb_d, w_down, fin_d = (d["w_out1"], d["fg1"], d["w_up1"], d["cw1"], d["cb1"], d["w_down1"], d["fin"])
    k.dma(T.m[:], m_d[:, :], (), [T.b_m])
    fg, b_fg = T.small("fg", fg_d[:, :], [128, 8])
    cw, b_cw = T.small("cw", cw_d[:, :, :], [128, 48, 3])
    cb, b_cb = T.small("cb", cb_d[:, :], [128, 48])
    fin, b_fin = T.small("fin", fin_d[:, :], [128, 8])
    sr, b_sr = T.small("sr", d["srank"][:, :], [128, 4])
    oms, b_oms = T.small("oms", d["oms"][:, :], [128, 4])
    pe, b_pe = k.sb([128, 4, 8, 2], F32, "pe")
    exv = exg.rearrange("(r kc p) t -> p r kc t", p=128, kc=8)
    for r in range(4):
        k.dma(pe[:, r, :, :], exv[:, r, :, :], (), [b_pe])
    Hc, b_Hc = k.sb([128, 8], F32, "Hc")
    Pm, b_Pm = k.sb([128, 8], F32, "Pm")
    Em, b_Em = k.sb([128, 8], F32, "Em")
    k.memset(Hc[:], 0.0, [b_Hc])
    for r in range(4):
        k.ts(Pm[:], pe[:, r, :, 1], sr[:, r:r + 1], ALU.mult, [b_pe, b_sr], [b_Pm], s2=oms[:, r:r + 1], op1=ALU.add)
        k.ts(Em[:], pe[:, r, :, 0], sr[:, r:r + 1], ALU.mult, [b_pe, b_sr], [b_Em])
        k.tt(Hc[:], Hc[:], Pm[:], ALU.mult, [b_Hc, b_Pm], [b_Hc])
        k.tt(Hc[:], Hc[:], Em[:], ALU.add, [b_Hc, b_Em], [b_Hc])
    yb, b_yb = T.hn, T.b_hn
    ybufs = [(k.sb([128, TB], F32, "hl%d" % i), k.sb([128, TB], F32, "pl%d" % i), k.sb([128, TB], BF16, "ggt%d" % i)) for i in range(2)]
    outt, b_outt = T.h, T.b_h

    h1v = h1w.rearrange("(kc p) s -> p kc s", p=128)
    ggv = ggw.rearrange("(kc p) s -> p kc s", p=128)
    hlv = hlw.rearrange("(kc p) s -> p kc s", p=128)
    plv = plw.rearrange("(kc p) s -> p kc s", p=128)
    outv = out_o.rearrange("(kc p) s -> p kc s", p=128)

    for blk in range(NB):
        first = blk == 0
        g0 = blk * TB
        k.dma(T.h[:, 0:4, :], h1v[:, 0:4, g0:g0 + TB], (), [T.b_h])
        k.dma(T.h[:, 4:8, :], h1v[:, 4:8, g0:g0 + TB], (), [T.b_h])
        for fc in range(8):
            (hl, b_hl), (pl, b_pl), (ggt, b_ggt) = ybufs[fc % 2]
            k.dma(hl[:], hlv[:, fc, g0:g0 + TB], (), [b_hl])
            k.dma(pl[:], plv[:, fc, g0:g0 + TB], (), [b_pl])
            k.dma(ggt[:], ggv[:, fc, g0:g0 + TB], (), [b_ggt])
            k.stt(hl[:], pl[:], Hc[:, fc:fc + 1], hl[:], ALU.mult, ALU.add, [b_pl, b_Hc, b_hl], [b_hl])
            k.tt(yb[:, fc, :], hl[:], ggt[:], ALU.mult, [b_hl, b_ggt], [b_yb])
        T.plan([(w_out, 8, fc * 128) for fc in range(8)] + T.ffn_reqs(w_up, w_down))

        def wo_body(fc):
            def evac(nt, cs, ps, pb, fc=fc):
                k.tt(T.h[:, fc, cs], T.h[:, fc, cs], ps, ALU.add, [T.b_h, pb], [T.b_h])

            T.linear(yb, b_yb, 8, w_out, fc * 128, evac)

        T.run_stage([(w_out, 8, fc * 128) for fc in range(8)], wo_body)
        T.ffn(first, fg, b_fg, w_up, cw, b_cw, cb, b_cb, w_down, True)
        T.rmsnorm(fin, b_fin, out_f32=(outt, b_outt))
        lo = HALO if first else 0
        k.dma(outv[:, :, g0 + lo - HALO:g0 + TB - HALO], outt[:, :, lo:TB], [b_outt], (), eng="pool")


_F = {}


def build_fused(S):
    TOK = S // 4
    W = TOK + HALO
    nc = bass.Bass("TRN2", target_bir_lowering=False)
    k = K(nc)
    io1 = l1_io(k, S)
    io2 = l2_io(k, W)
    io3 = l3_io(k, W)
    NCH = max(1, S // OCS)
    osrc = [k.dint("osrc%d" % j, [256, min(S, OCS)], BF16) for j in range(NCH)]
    og = [k.dint("og%d" % j, [1024, min(S, OCS)], BF16) for j in range(NCH)]
    h1 = k.dint("h1s", [1024, W])
    gg = k.dint("ggs", [1024, W], BF16)
    hl = k.dint("hls", [1024, W])
    pl = k.dint("pls", [1024, W])
    exs = k.dint("exs", [1024, 2])
    exg = k.dint("exg", [4096, 2])
    out = k.dout("outT", [1024, TOK])
    groups = [[0, 1, 2, 3], [4, 5, 6, 7]]
    upto = int(os.environ.get("FUSE_UPTO", "3"))
    tile_bufs = {}

    def o_out(g, otile, b_ot):
        j, off = (g * 512) // OCS, (g * 512) % OCS
        ov = osrc[j].rearrange("(r p) s -> p r s", p=64)
        bb = Buf()
        tile_bufs.setdefault(j, []).append(bb)
        k.dma(ov[:, :, off:off + 512], otile[:], [b_ot], [bb], eng="pool")
        if off + 512 == min(S, OCS):
            k.P.dma((lambda j=j: nc.gpsimd.collective_compute("AllGather", ALU.bypass, replica_groups=groups,
                                                               ins=[osrc[j][:, :]], outs=[og[j][:, :]])),
                    tile_bufs[j], (), eng="pool", inc=1)

    emit_l1(k, S, io1, o_out)
    k.end_phase()
    emit_l2(k, W, S, io2, og, h1, gg, hl, pl, exs)
    k.end_phase()
    if upto == 2:
        return nc, k.P.finish()
    if os.environ.get("NOCC2"):
        k.dma(exg[0:1024, :], exs[:, :], (), ())
    else:
        k.P.dma(lambda: nc.gpsimd.collective_compute("AllGather", ALU.bypass, replica_groups=groups, ins=[exs[:, :]], outs=[exg[:, :]]),
                (), (), eng="pool", inc=1)
    k.P.barrier()
    emit_l3(k, W, io3, io2["m"], h1, gg, hl, pl, exg, out)
    cnt = k.P.finish()
    return nc, cnt


def _pk(v):
    return np.ascontiguousarray(v.reshape(-1, 128).T)


def _pkw(w):
    t, C = w.shape
    return np.ascontiguousarray(w.T.reshape(C // 128, 128, t).transpose(1, 0, 2))


def kernel(x, mix_norm_g, ffn_norm_g, final_norm_g,
           ev_w_in, ev_w_gk2, ev_b_gk2, ev_gla_norm_g, ev_w_out,
           od_w_in, od_conv_w, od_conv_b, od_w_a, od_b_a, od_w_x, od_b_x, od_lambda, od_w_out,
           ffn_w_up, ffn_conv_w, ffn_conv_b, ffn_w_down):
    f = lambda a: np.asarray(a, dtype=np.float32)
    x = f(x)
    Bsz, S, D = x.shape
    TOK = S // 4
    W = TOK + HALO
    if S not in _F:
        _F[S] = build_fused(S)
    nc, _ = _F[S]
    consts = l1_consts(S)
    w = f(ev_w_in)[0]
    gn = _pk(f(mix_norm_g)[0])
    perm = []
    for g in range(4):
        for j in range(4):
            head = 2 * g + (j % 2)
            base = head * 64 if j < 2 else 512 + head * 64
            perm += list(range(base, base + 64))
    w_out0 = np.ascontiguousarray(f(ev_w_out)[0][np.asarray(perm)])
    shared = {
        "gn": gn, "w_out0": w_out0, "fg0": _pk(f(ffn_norm_g)[0]), "w_up0": f(ffn_w_up)[0],
        "cw0": _pkw(f(ffn_conv_w)[0]), "cb0": _pk(f(ffn_conv_b)[0]), "w_down0": f(ffn_w_down)[0],
        "mg": _pk(f(mix_norm_g)[1]), "w_in": f(od_w_in)[0], "rcw": _pkw(f(od_conv_w)[0]), "rcb": _pk(f(od_conv_b)[0]),
        "wa": np.ascontiguousarray(f(od_w_a)[0].reshape(1024, 256)), "ba": _pk(f(od_b_a)[0]),
        "wx": np.ascontiguousarray(f(od_w_x)[0].reshape(1024, 256)), "bx": _pk(f(od_b_x)[0]),
        "lam": _pk(f(od_lambda)[0]),
        "w_out1": f(od_w_out)[0], "fg1": _pk(f(ffn_norm_g)[1]), "w_up1": f(ffn_w_up)[1],
        "cw1": _pkw(f(ffn_conv_w)[1]), "cb1": _pk(f(ffn_conv_b)[1]), "w_down1": f(ffn_w_down)[1],
        "fin": _pk(f(final_norm_g)),
    }
    shared.update(consts)
    in_maps = []
    for b in range(Bsz):
        xTb = np.ascontiguousarray(x[b].T)
        for c in range(4):
            h0, h1 = 2 * c, 2 * c + 1
            cols = []
            for base in (0, 512, 1024):
                cols += [np.arange(base + h0 * 64, base + h0 * 64 + 64), np.arange(base + h1 * 64, base + h1 * 64 + 64)]
            for base in (1536, 1792):
                cols += [np.arange(base + h0 * 32, base + h0 * 32 + 32), np.arange(base + h1 * 32, base + h1 * 32 + 32)]
            cols += [np.arange(2048 + h0 * 64, 2048 + h0 * 64 + 64), np.arange(2048 + h1 * 64, 2048 + h1 * 64 + 64)]
            cols += [np.arange(2560, 2576)]
            cols += [np.arange(2576 + h0 * 64, 2576 + h0 * 64 + 64), np.arange(2576 + h1 * 64, 2576 + h1 * 64 + 64)]
            cols = np.concatenate(cols)
            t0 = c * TOK
            xw = np.zeros((1024, W), np.float32)
            if c == 0:
                xw[:, HALO:] = xTb[:, 0:TOK]
            else:
                xw[:] = xTb[:, t0 - HALO:t0 + TOK]
            sel = np.zeros((128, 4), np.float32)
            sel[:, c] = 1.0
            sr = np.zeros((128, 4), np.float32)
            sr[:, :c] = 1.0
            m = dict(shared)
            m.update({
                "xT": xTb, "w1": np.ascontiguousarray(w[:, cols]),
                "wgk2": np.ascontiguousarray(f(ev_w_gk2)[0][:, h0 * 32:h0 * 32 + 64]),
                "bgk2": np.ascontiguousarray(f(ev_b_gk2)[0][h0 * 32:h0 * 32 + 64].reshape(64, 1)),
                "glag": np.ascontiguousarray(f(ev_gla_norm_g)[0][h0:h0 + 2].T),
                "xTw": xw, "m": np.full((128, 1), 0.0 if c == 0 else 1.0, np.float32),
                "sel": sel, "srank": sr, "oms": 1.0 - sr,
            })
            in_maps.append(m)
    res = run_bass_kernel_spmd(nc, in_maps, core_ids=list(range(len(in_maps)))).results
    out = np.zeros((Bsz, S, D), np.float32)
    for b in range(Bsz):
        for c in range(4):
            out[b, c * TOK:(c + 1) * TOK, :] = np.asarray(res[b * 4 + c]["outT"]).T
    return out
```

```python
import contextlib
import os
import numpy as np
import ml_dtypes
import concourse.bass as bass
import concourse.mybir as mybir
from concourse.bass_utils import run_bass_kernel_spmd

F32 = mybir.dt.float32
BF16 = mybir.dt.bfloat16
AF = mybir.ActivationFunctionType
ALU = mybir.AluOpType
AX = mybir.AxisListType

SAME_SYNC = True
N_DMA_SEMS = 24
SEM_EPOCH = 2000
NEG = -240000.0
EPS = 1e-6


class Buf:
    __slots__ = ("name", "lw", "rd")

    def __init__(self, name=""):
        self.name = name
        self.lw = None
        self.rd = []


class Prog:
    ENGS = ("pe", "act", "dve", "pool", "sp")

    def __init__(self, nc):
        self.nc = nc
        self.h = {"pe": nc.tensor, "act": nc.scalar, "dve": nc.vector, "pool": nc.gpsimd, "sp": nc.sync}
        self.items = {e: [] for e in self.ENGS}
        self.known = {}
        self.sem = {e: nc.alloc_semaphore("s_" + e) for e in self.ENGS}
        self.dsem = [nc.alloc_semaphore("d%d" % i) for i in range(N_DMA_SEMS)]
        self.duse = [0] * N_DMA_SEMS
        self.dval = [0] * N_DMA_SEMS
        self.pending = {e: [] for e in self.ENGS}
        self.rank = {e: [] for e in self.ENGS}
        self.emitted = {e: 0 for e in self.ENGS}
        self.esem = {}
        self.dnext = 0

    def _need(self, eng, tok, waits):
        if tok is None:
            return
        if tok[0] == "c":
            _, e2, seq = tok
            if e2 == eng and (eng == "pe" or not SAME_SYNC):
                return
            key = (eng, "c", e2)
            if self.known.get(key, -1) >= seq:
                return
            self.known[key] = seq
            self.items[e2][seq]["flag"] = True
            waits.append(tok)
        else:
            _, idx, val = tok
            key = (eng, "d", idx)
            if self.known.get(key, -1) >= val:
                return
            self.known[key] = val
            waits.append(tok)

    def _deps(self, eng, reads, writes):
        waits = []
        for b in reads:
            self._need(eng, b.lw, waits)
        for b in writes:
            self._need(eng, b.lw, waits)
            for t in b.rd:
                self._need(eng, t, waits)
        return waits

    def op(self, eng, fn, reads=(), writes=()):
        waits = self.pending[eng] + self._deps(eng, reads, writes)
        self.pending[eng] = []
        seq = len(self.items[eng])
        self.items[eng].append({"waits": waits, "fn": fn, "flag": False, "dma": None})
        tok = ("c", eng, seq)
        for b in reads:
            b.rd.append(tok)
        for b in writes:
            b.lw = tok
            b.rd = []
        return tok

    def dma(self, fn, reads=(), writes=(), eng="sp", inc=16):
        idx = self.dnext
        self.dnext = (self.dnext + 1) % N_DMA_SEMS
        pv = self.dval[idx]
        waits = self.pending[eng] + self._deps(eng, reads, writes)
        self.pending[eng] = []
        if pv > 0:
            self._need(eng, ("d", idx, pv), waits)
        self.duse[idx] += 1
        self.dval[idx] = pv + inc
        tok = ("d", idx, pv + inc)
        self.items[eng].append({"waits": waits, "fn": fn, "flag": False, "dma": idx, "inc": inc})
        for b in reads:
            b.rd.append(tok)
        for b in writes:
            b.lw = tok
            b.rd = []
        return tok

    def _all_tokens(self):
        toks = []
        for e in ("pe", "act", "dve", "pool"):
            items = self.items[e]
            for i in range(len(items) - 1, -1, -1):
                if items[i]["dma"] is None and items[i]["fn"] is not None or (items[i]["dma"] is None and i < self.emitted[e]):
                    toks.append(("c", e, i))
                    break
        for idx in range(N_DMA_SEMS):
            if self.dval[idx]:
                toks.append(("d", idx, self.dval[idx]))
        return toks

    def barrier(self, engines=None):
        toks = self._all_tokens()
        for e in (engines or self.ENGS):
            for t in toks:
                if t[0] == "c" and t[1] == e:
                    continue
                self._need(e, t, self.pending[e])

    def _sem_of(self, e, r):
        ep = (r - 1) // SEM_EPOCH
        if (e, ep) not in self.esem:
            self.esem[(e, ep)] = self.sem[e] if ep == 0 else self.nc.alloc_semaphore("s_%s_%d" % (e, ep))
        return self.esem[(e, ep)], (r - 1) % SEM_EPOCH + 1

    def flush(self):
        for e in self.ENGS:
            items = self.items[e]
            rk = self.rank[e]
            c = rk[-1] if rk else 0
            for i in range(len(rk), len(items)):
                if items[i]["flag"]:
                    c += 1
                rk.append(c)
        for e in self.ENGS:
            h = self.h[e]
            items = self.items[e]
            for i in range(self.emitted[e], len(items)):
                it = items[i]
                for t in it["waits"]:
                    if t[0] == "c":
                        sm, v = self._sem_of(t[1], self.rank[t[1]][t[2]])
                        h.wait_ge(sm, v)
                    else:
                        h.wait_ge(self.dsem[t[1]], t[2])
                if it["fn"] is None:
                    continue
                ins = it["fn"]()
                if it["dma"] is not None:
                    ins.then_inc(self.dsem[it["dma"]], it["inc"])
                elif it["flag"]:
                    sm, _ = self._sem_of(e, self.rank[e][i])
                    ins.then_inc(sm, 1)
                it["fn"] = None
            self.emitted[e] = len(items)

    def finish(self):
        self.barrier(engines=("sp",))
        self.items["sp"].append({"waits": self.pending["sp"], "fn": None, "flag": False, "dma": None})
        self.pending["sp"] = []
        self.flush()
        return {e: (len(self.items[e]), self.rank[e][-1] if self.rank[e] else 0) for e in self.ENGS}


class K:
    def __init__(self, nc):
        self.nc = nc
        self.P = Prog(nc)
        self._n = 0
        self.stack = contextlib.ExitStack()

    def end_phase(self):
        self.P.barrier()
        self.P.flush()
        self.stack.close()
        self.stack = contextlib.ExitStack()

    def sb(self, shape, dt, name=None):
        self._n += 1
        return self.stack.enter_context(self.nc.sbuf_tensor("sb%d_" % self._n + (name or "t"), list(shape), dt)), Buf(name or "")

    def ps(self, shape, name=None):
        self._n += 1
        return self.stack.enter_context(self.nc.psum_tensor("ps%d_" % self._n + (name or "p"), list(shape), F32)), Buf(name or "")

    def din(self, name, shape, dt=F32):
        return self.nc.dram_tensor(name, list(shape), dt, kind="ExternalInput").ap()

    def dint(self, name, shape, dt=F32, **kw):
        return self.nc.dram_tensor(name, list(shape), dt, kind="Internal", **kw).ap()

    def dout(self, name, shape, dt=F32):
        return self.nc.dram_tensor(name, list(shape), dt, kind="ExternalOutput").ap()

    def mm(self, out, lhsT, rhs, st, sp, r, w):
        nc = self.nc
        return self.P.op("pe", lambda: nc.tensor.matmul(out, lhsT=lhsT, rhs=rhs, start=st, stop=sp), r, w)

    def tr(self, out, in_, ident, r, w):
        nc = self.nc
        return self.P.op("pe", lambda: nc.tensor.transpose(out, in_, ident), r, w)

    def act(self, out, in_, func, r, w, bias=None, scale=None):
        nc = self.nc
        kw = {}
        if bias is not None:
            kw["bias"] = bias
        if scale is not None:
            kw["scale"] = scale
        return self.P.op("act", lambda: nc.scalar.activation(out=out, in_=in_, func=func, **kw), r, w)

    def tt(self, out, in0, in1, op, r, w, eng="dve"):
        h = self.P.h[eng]
        return self.P.op(eng, lambda: h.tensor_tensor(out=out, in0=in0, in1=in1, op=op), r, w)

    def ts(self, out, in0, s1, op0, r, w, s2=None, op1=None, eng="dve"):
        h = self.P.h[eng]
        if op1 is None:
            return self.P.op(eng, lambda: h.tensor_scalar(out=out, in0=in0, scalar1=s1, scalar2=None, op0=op0), r, w)
        return self.P.op(eng, lambda: h.tensor_scalar(out=out, in0=in0, scalar1=s1, scalar2=s2, op0=op0, op1=op1), r, w)

    def stt(self, out, in0, scalar, in1, op0, op1, r, w):
        nc = self.nc
        return self.P.op("dve", lambda: nc.vector.scalar_tensor_tensor(out=out, in0=in0, scalar=scalar, in1=in1, op0=op0, op1=op1), r, w)

    def cp(self, out, in_, r, w, eng="dve"):
        if eng == "act":
            nc = self.nc
            return self.P.op("act", lambda: nc.scalar.copy(out=out, in_=in_), r, w)
        h = self.P.h[eng]
        return self.P.op(eng, lambda: h.tensor_copy(out=out, in_=in_), r, w)

    def memset(self, ap, val, w, eng="pool"):
        h = self.P.h[eng]
        return self.P.op(eng, lambda: h.memset(ap, val), (), w)

    def scan(self, out, d0, d1, init, r, w):
        nc = self.nc
        return self.P.op("dve", lambda: nc.vector.tensor_tensor_scan(out=out, data0=d0, data1=d1, initial=init, op0=ALU.mult, op1=ALU.add), r, w)

    def dma(self, out, in_, r, w, eng="sp"):
        h = self.P.h[eng]
        return self.P.dma(lambda: h.dma_start(out=out, in_=in_), r, w, eng=eng)


def l1_io(k, S):
    d = {}
    for name, shape in (("xT", [1024, S]), ("gn", [128, 8]), ("w1", [1024, 784]), ("wgk2", [16, 64]), ("bgk2", [64, 1]),
                        ("glag", [64, 2]), ("cosd", [64, S]), ("sind", [64, S]), ("cmd", [128, 2048]), ("ed", [64, S]),
                        ("identd", [128, 128]), ("rmd", [64, 512]), ("amd", [64, 512]), ("hm2d", [64, 128]), ("hmd", [64, 2])):
        d[name] = k.din(name, shape)
    return d


def emit_l1(k, S, d, oT):
    NT = S // 512
    NKT = S // 128
    nc = k.nc
    P = k.P
    xT, gn_d, w1_d, wgk2_d, bgk2_d, glag_d = d["xT"], d["gn"], d["w1"], d["wgk2"], d["bgk2"], d["glag"]
    cos_d, sin_d, cm_d, e_d, id_d, rm_d, am_d, hm2_d, hm_d = (d["cosd"], d["sind"], d["cmd"], d["ed"], d["identd"], d["rmd"],
                                                              d["amd"], d["hm2d"], d["hmd"])

    xt, b_xt = k.sb([128, 8, 512], F32, "xt")
    rstd, b_rstd = k.sb([128, 512], F32, "rstd")
    lnv, b_lnv = rstd, b_rstd
    hn, b_hn = k.sb([128, 8, 512], BF16, "hn")
    sq, b_sq = hn, b_hn
    wb, b_wb = k.sb([128, 8, 784], BF16, "wb")
    gn, b_gn = k.sb([128, 8], F32, "gn")
    Kaug = [k.sb([128, S], BF16, "kaug%d" % h) for h in range(2)]
    KB = [[Buf() for _ in range(NT)] for h in range(2)]
    b_kE = [Buf(), Buf()]
    Vaug = [k.sb([128, NKT, 66], BF16, "vaug%d" % h) for h in range(2)]
    VB = [[Buf() for _ in range(NT)] for h in range(2)]
    b_vones = [Buf(), Buf()]
    Qaug = [[k.sb([128, 512], BF16, "qaug%d_%d" % (h, p)) for p in range(2)] for h in range(2)]
    kmT = [k.sb([64, 64], BF16, "kmT%d" % h) for h in range(2)]
    km32, b_km32 = k.sb([64, 2], F32, "km32")
    cosT, b_cos = k.sb([64, 512], F32, "cos")
    sinT, b_sin = k.sb([64, 512], F32, "sin")
    t1, b_t1 = k.sb([64, 512], F32, "t1")
    t2, b_t2 = k.sb([64, 512], F32, "t2")
    pTs = [k.sb([128, 512], BF16, "pT%d" % i) for i in range(4)]
    cm, b_cm = k.sb([128, 4, 512], BF16, "cm")
    id_f, b_idf = k.sb([128, 128], F32, "idf")
    id_b, b_idb = k.sb([128, 128], BF16, "idb")
    ones_b, b_onesb = k.sb([128, 128], BF16, "onesb")
    ones_f, b_onesf = k.sb([128, 64], F32, "onesf")
    bq, b_bq = k.sb([128, 4, 128], F32, "bq")
    gsb, b_gsb = k.sb([128, 4, 64], F32, "gsb")
    m8, b_m8 = k.sb([128, 4, 8], F32, "m8")
    fin_t, b_rden = k.sb([128, 512], F32, "fin")
    rden = fin_t
    osb, b_osb = fin_t[0:64, :], Buf("osb")
    otiles = [k.sb([64, 4, 512], BF16, "otile%d" % p) for p in range(2)]
    QG32, b_qg = k.sb([64, 512], F32, "qg32")
    KG32, b_kg = k.sb([64, 512], F32, "kg32")
    spl, b_spl = k.sb([64, 512], F32, "spl")
    bpos, b_bpos = k.sb([64, 512], F32, "bpos")
    eb, b_eb = k.sb([64, 512], F32, "eb")
    enb, b_enb = k.sb([64, 512], F32, "enb")
    Ac, b_ac = k.sb([64, 8], F32, "Ac")
    ke32, b_ke = k.sb([64, 512], F32, "ke32")
    qt, b_qt = k.sb([64, 512], BF16, "qt")
    kpad, b_kpad = k.sb([64, 2, 512], BF16, "kpad")
    khat, b_khat = k.sb([64, 512], BF16, "khat")
    rmk, b_rmk = k.sb([64, 512], F32, "rmk")
    amk, b_amk = k.sb([64, 512], BF16, "amk")
    hm2, b_hm2 = k.sb([64, 128], F32, "hm2")
    hm, b_hm = k.sb([64, 2], F32, "hm")
    attm, b_attm = k.sb([64, 2, 512], BF16, "attm")
    gvt, b_gvt = k.sb([64, 8, 128], BF16, "gvt")
    KTt, b_ktt = k.sb([64, 8, 64], BF16, "KTt")
    gk16, b_gk16 = k.sb([16, 512], BF16, "gk16")
    wgk2f, b_wgk2f = k.sb([16, 64], F32, "wgk2f")
    wgk2b, b_wgk2b = k.sb([16, 64], BF16, "wgk2b")
    nbg, b_nbg = k.sb([64, 1], F32, "nbg")
    glag, b_glag = k.sb([64, 2], F32, "glag")
    sbog, b_sbog = k.sb([64, 2, 512], BF16, "sbog")
    st32, b_st32 = k.sb([64, 128], F32, "st32")
    stall, b_stall = k.sb([64, 9, 128], BF16, "stall")
    stmp, b_stmp = k.sb([64, 128], F32, "stmp")
    o32, b_o32 = t1, b_t1
    osq, b_osq = k.sb([64, 512], BF16, "osq")
    on32, b_on32 = t2, b_t2
    B = [k.ps([128, 512], "bank%d" % i) for i in range(8)]

    stg = xt
    k.dma(id_f[:], id_d[:, :], (), [b_idf])
    k.cp(id_b[:], id_f[:], [b_idf], [b_idb])
    k.memset(ones_b[:], 1.0, [b_onesb])
    k.memset(ones_f[:], 1.0, [b_onesf])
    k.memset(bq[:], 0.0, [b_bq])
    k.memset(st32[:], 0.0, [b_st32])
    k.memset(stall[:], 0.0, [b_stall])
    k.dma(gn[:], gn_d[:, :], (), [b_gn])
    k.dma(glag[:], glag_d[:, :], (), [b_glag])
    k.dma(hm2[:], hm2_d[:, :], (), [b_hm2])
    k.dma(hm[:], hm_d[:, :], (), [b_hm])
    k.dma(rmk[:], rm_d[:, :], (), [b_rmk])
    k.dma(wgk2f[:], wgk2_d[:, :], (), [b_wgk2f])
    k.cp(wgk2b[:], wgk2f[:], [b_wgk2f], [b_wgk2b])
    k.dma(nbg[:], bgk2_d[:, :], (), [b_nbg])
    k.ts(nbg[:], nbg[:], -1.0, ALU.mult, [b_nbg], [b_nbg])
    k.dma(t1[:], am_d[:, :], (), [b_t1])
    k.cp(amk[:], t1[:], [b_t1], [b_amk])
    sflat = stg[:].rearrange("p a b -> p (a b)")
    k.dma(sflat[:, 0:2048], cm_d[:, :], (), [b_xt])
    k.cp(cm[:].rearrange("p a b -> p (a b)"), sflat[:, 0:2048], [b_xt], [b_cm])
    w1v = w1_d.rearrange("(kc p) f -> p kc f", p=128)
    for half in range(2):
        sv = sflat[:, 0:4 * 784].rearrange("p (a b) -> p a b", b=784)
        k.dma(sv, w1v[:, half * 4:(half + 1) * 4, :], (), [b_xt])
        k.cp(wb[:, half * 4:(half + 1) * 4, :], sv, [b_xt], [b_wb], eng="dve" if half == 0 else "pool")
    for pc in range(S // 2048):
        k.dma(sflat[64:128, 0:2048], e_d[:, pc * 2048:(pc + 1) * 2048], (), [b_xt])
        for h in range(2):
            k.cp(Kaug[h][0][64:128, pc * 2048:(pc + 1) * 2048], sflat[64:128, 0:2048], [b_xt], [b_kE[h]],
                 eng="dve" if h == 0 else "pool")
    for h in range(2):
        k.memset(Vaug[h][0][:, :, 64:65], 1.0, [b_vones[h]])
        k.memset(kmT[h][0][:], 0.0, [kmT[h][1]])

    xTv = xT.rearrange("(kc p) s -> p kc s", p=128)
    def proj(bank, M, col0, ncols=None):
        pt, pb = bank
        for kc in range(8):
            k.mm(pt[0:M, 0:512], wb[:, kc, col0:col0 + M], hn[:, kc, :], kc == 0, kc == 7, [b_wb, b_hn], [pb])
        return pt, pb

    FB0, FB1, FB2 = B[0], B[4], B[7]

    def gen_F(g):
        c0 = g * 512
        par = g % 2
        otile, b_ot = otiles[par]
        k.dma(xt[:, 0:4, :], xTv[:, 0:4, c0:c0 + 512], (), [b_xt])
        k.dma(xt[:, 4:8, :], xTv[:, 4:8, c0:c0 + 512], (), [b_xt])
        k.dma(cosT[:], cos_d[:, c0:c0 + 512], (), [b_cos])
        k.dma(sinT[:], sin_d[:, c0:c0 + 512], (), [b_sin])
        k.act(sq[:], xt[:], AF.Square, [b_xt], [b_sq])
        pt, pb = FB0
        for kc in range(8):
            k.mm(pt[:, :], ones_b[:], sq[:, kc, :], kc == 0, kc == 7, [b_onesb, b_sq], [pb])
        yield
        k.act(lnv[:], pt[:, :], AF.Ln, [pb], [b_lnv], bias=EPS, scale=1.0 / 1024)
        k.act(rstd[:], lnv[:], AF.Exp, [b_lnv], [b_rstd], scale=-0.5)
        for kc in range(8):
            k.stt(hn[:, kc, :], xt[:, kc, :], gn[:, kc:kc + 1], rstd[:], ALU.mult, ALU.mult, [b_xt, b_gn, b_rstd], [b_hn])
            if kc % 4 == 3:
                yield
        for idx in range(4):
            h = idx % 2
            isk = idx >= 2
            pt, pb = proj((FB1, FB2)[idx % 2], 64, idx * 64)
            yield
            if isk:
                dest = Kaug[h][0][0:64, c0:c0 + 512]
                dbuf = KB[h][g]
            else:
                dest = Qaug[h][par][0][0:64, :]
                dbuf = Qaug[h][par][1]
            k.tt(t1[:], pt[0:64, 0:512], cosT[:], ALU.mult, [pb, b_cos], [b_t1])
            k.tt(t2[0:32, :], pt[32:64, 0:512], sinT[32:64, :], ALU.mult, [pb, b_sin], [b_t2])
            k.tt(t2[32:64, :], pt[0:32, 0:512], sinT[0:32, :], ALU.mult, [pb, b_sin], [b_t2])
            k.tt(dest, t1[:], t2[:], ALU.add, [b_t1, b_t2], [dbuf])
            yield
        pt, pb = FB0
        for st in range(4):
            for kc in range(8):
                k.mm(pt[:, st * 128:(st + 1) * 128], hn[:, kc, st * 128:(st + 1) * 128], wb[:, kc, 256:384],
                     kc == 0, kc == 7, [b_hn, b_wb], [pb])
            yield
        pv_ = pt[:, 0:512].rearrange("p (a b) -> p a b", b=128)
        for h in range(2):
            k.cp(Vaug[h][0][:, 4 * g:4 * g + 4, 0:64], pv_[:, :, h * 64:(h + 1) * 64], [pb], [VB[h][g]], eng="act")
        pt, pb = proj(FB1, 128, 384)
        k.cp(QG32[:], pt[0:64, 0:512], [pb], [b_qg], eng="act")
        k.cp(KG32[:], pt[64:128, 0:512], [pb], [b_kg], eng="act")
        yield
        for c in range(8):
            pt, pb = FB2 if c < 4 else FB0
            for kc in range(8):
                k.mm(pt[0:64, (c % 4) * 128:(c % 4 + 1) * 128], hn[:, kc, c * 64:(c + 1) * 64], wb[:, kc, 512:640],
                     kc == 0, kc == 7, [b_hn, b_wb], [pb])
            if c % 2 == 1:
                yield
        for hf in range(2):
            pt, pb = FB2 if hf == 0 else FB0
            k.cp(gvt[:, hf * 4:(hf + 1) * 4, :].rearrange("p a b -> p (a b)"), pt[0:64, 0:512], [pb], [b_gvt], eng="act")
        pt, pb = proj(FB1, 16, 640)
        k.cp(gk16[:], pt[0:16, 0:512], [pb], [b_gk16], eng="act")
        k.mm(pt[0:64, 0:512], wgk2b[:], gk16[:], True, True, [b_wgk2b, b_gk16], [pb])
        k.act(spl[:], pt[0:64, 0:512], AF.Exp, [pb, b_nbg], [b_spl], bias=nbg[:, 0:1], scale=-1.0)
        k.act(spl[:], spl[:], AF.Ln, [b_spl], [b_spl], bias=1.0, scale=1.0)
        yield
        pt, pb = proj(FB2, 128, 656)
        k.act(sbog[:, 0, :], pt[0:64, 0:512], AF.Silu, [pb], [b_sbog])
        k.act(sbog[:, 1, :], pt[64:128, 0:512], AF.Silu, [pb], [b_sbog])
        yield
        k.scan(bpos[:], rmk[:], spl[:], 0.0, [b_rmk, b_spl], [b_bpos])
        k.act(eb[:], bpos[:], AF.Exp, [b_bpos], [b_eb], scale=-1.0 / 16)
        k.act(enb[:], bpos[:], AF.Exp, [b_bpos], [b_enb], scale=1.0 / 16)
        blast = bpos[:].rearrange("p (c t) -> p c t", t=64)[:, :, 63:64].rearrange("p c o -> p (c o)")
        k.act(Ac[:], blast, AF.Exp, [b_bpos], [b_ac], scale=-1.0 / 16)
        yield
        k.stt(qt[:], QG32[:], 32.0 ** -0.5, eb[:], ALU.mult, ALU.mult, [b_qg, b_eb], [b_qt])
        k.tt(ke32[:], KG32[:], enb[:], ALU.mult, [b_kg, b_enb], [b_ke])
        for h in range(2):
            k.ts(kpad[:, h, :], ke32[:], hm[:, h:h + 1], ALU.mult, [b_ke, b_hm], [b_kpad])
        yield
        for c in range(8):
            k.act(khat[:, c * 64:(c + 1) * 64], ke32[:, c * 64:(c + 1) * 64], AF.Copy, [b_ke, b_ac], [b_khat], scale=Ac[:, c:c + 1])
        yield
        pt, pb = FB0
        for c in range(8):
            k.mm(pt[0:64, c * 64:(c + 1) * 64], khat[:, c * 64:(c + 1) * 64], id_b[0:64, 0:64], True, True, [b_khat, b_idb], [pb])
        k.cp(KTt[:].rearrange("p a b -> p (a b)"), pt[0:64, 0:512], [pb], [b_ktt], eng="act")
        yield
        for h in range(2):
            pt, pb = (FB1, FB2)[h]
            for c in range(8):
                k.mm(pt[0:64, c * 64:(c + 1) * 64], kpad[:, h, c * 64:(c + 1) * 64], qt[:, c * 64:(c + 1) * 64], True, True,
                     [b_kpad, b_qt], [pb])
            k.tt(attm[:, h, :], pt[0:64, 0:512], amk[:], ALU.mult, [pb, b_amk], [b_attm])
            yield
        pso = [FB2, FB0]
        for c in range(8):
            psd, b_psd = FB0 if c < 4 else FB1
            for h in range(2):
                k.mm(psd[0:64, (c % 4) * 128 + h * 64:(c % 4) * 128 + (h + 1) * 64], KTt[:, c, :], gvt[:, c, h * 64:(h + 1) * 64],
                     True, True, [b_ktt, b_gvt], [b_psd])
            if c % 4 == 3:
                yield
        k.cp(stall[:, 0, :], stall[:, 8, :], [b_stall], [b_stall])
        for c in range(8):
            psd, b_psd = FB0 if c < 4 else FB1
            k.tt(stmp[:], psd[0:64, (c % 4) * 128:(c % 4 + 1) * 128], hm2[:], ALU.mult, [b_psd, b_hm2], [b_stmp])
            k.stt(st32[:], st32[:], Ac[:, c:c + 1], stmp[:], ALU.mult, ALU.add, [b_st32, b_ac, b_stmp], [b_st32])
            k.cp(stall[:, c + 1, :], st32[:], [b_st32], [b_stall])
            if c % 2 == 1:
                yield
        for c in range(8):
            for h in range(2):
                po, pbo = pso[h]
                k.mm(po[0:64, c * 64:(c + 1) * 64], gvt[:, c, h * 64:(h + 1) * 64], attm[:, h, c * 64:(c + 1) * 64], True, False,
                     [b_gvt, b_attm], [pbo])
                k.mm(po[0:64, c * 64:(c + 1) * 64], stall[:, c, h * 64:(h + 1) * 64], qt[:, c * 64:(c + 1) * 64], False, True,
                     [b_stall, b_qt], [pbo])
            if c % 2 == 1:
                yield
        for h in range(2):
            po, pbo = pso[h]
            k.cp(o32[:], po[0:64, 0:512], [pbo], [b_o32], eng="act")
            k.act(osq[:], po[0:64, 0:512], AF.Square, [pbo], [b_osq])
            pt, pb = FB1
            k.mm(pt[0:64, 0:512], ones_b[0:64, 0:64], osq[:], True, True, [b_onesb, b_osq], [pb])
            yield
            k.act(lnv[0:64, :], pt[0:64, 0:512], AF.Ln, [pb], [b_lnv], bias=EPS, scale=1.0 / 64)
            k.act(lnv[0:64, :], lnv[0:64, :], AF.Exp, [b_lnv], [b_lnv], scale=-0.5)
            k.tt(on32[:], o32[:], lnv[0:64, :], ALU.mult, [b_o32, b_lnv], [b_on32])
            k.stt(otile[:, 2 + h, :], on32[:], glag[:, h:h + 1], sbog[:, h, :], ALU.mult, ALU.mult, [b_on32, b_glag, b_sbog], [b_ot])
            yield
        for h in range(2):
            KA, _ = Kaug[h]
            QA, b_QA = Qaug[h][par]
            kmt, b_kmt = kmT[h]
            k.P.op("dve", (lambda o=km32[:], i=KA[0:64, c0:c0 + 512].rearrange("p (a b) -> p a b", b=256):
                           nc.vector.tensor_reduce(out=o, in_=i, axis=AX.X, op=ALU.add)), [KB[h][g]], [b_km32])
            k.cp(kmt[:, 2 * g:2 * g + 2], km32[:], [b_km32], [b_kmt])
            pg, b_pg = FB1
            for st in range(4):
                k.mm(pg[:, st * 64:(st + 1) * 64], QA[0:64, st * 128:(st + 1) * 128], kmt[:, :], True, True, [b_QA, b_kmt], [b_pg])
            yield
            k.memset(gsb[:], -1e30, [b_gsb], eng="dve")
            for st in range(4):
                blk = 2 * g + st // 2
                if blk > 0:
                    k.cp(gsb[:, st, 0:blk], pg[:, st * 64:st * 64 + blk], [b_pg], [b_gsb])
            yield
            for st in range(4):
                blk = 2 * g + st // 2
                k.P.op("dve", (lambda o=m8[:, st, :], i=gsb[:, st, :]: nc.vector.max(out=o, in_=i)), [b_gsb], [b_m8])
                k.ts(bq[:, st, 64:128], gsb[:, st, :], m8[:, st, 2:3], ALU.is_ge, [b_gsb, b_m8], [b_bq], s2=-NEG, op1=ALU.mult)
            yield
            k.ts(bq[:, :, 64:128], bq[:, :, 64:128], NEG, ALU.add, [b_bq], [b_bq])
            for st in range(4):
                blk = 2 * g + st // 2
                k.memset(bq[:, st, 64 + blk:65 + blk], 0.0, [b_bq], eng="dve")
            pg2, b_pg2 = FB2
            for st in range(4):
                k.tr(pg2[:, st * 128:(st + 1) * 128], bq[:, st, :], id_f[:], [b_bq, b_idf], [b_pg2])
            k.cp(QA[64:128, :], pg2[64:128, 0:512], [b_pg2], [b_QA], eng="act")
            yield

    def gen_A(g):
        par = g % 2
        otile, b_ot = otiles[par]
        for h in range(2):
            KA, _ = Kaug[h]
            VA, _ = Vaug[h]
            QA, b_QA = Qaug[h][par]
            pO, b_pO = B[5 + h]
            nkt = 4 * g + 4
            LA = 2

            def qk(kt):
                j = kt - 4 * g
                q0 = 256 if j >= 2 else 0
                pS, b_pS = B[1 + (kt % 3)]
                k.mm(pS[:, q0:512], KA[:, kt * 128:(kt + 1) * 128], QA[:, q0:512], True, j < 0,
                     [KB[h][kt // 4], b_kE[h], b_QA], [b_pS])
                if j >= 0:
                    k.mm(pS[:, q0:512], id_b[:], cm[:, j, q0:512], False, True, [b_idb, b_cm], [b_pS])

            def pv(kt):
                j = kt - 4 * g
                q0 = 256 if j >= 2 else 0
                pS, b_pS = B[1 + (kt % 3)]
                pTt, b_pT = pTs[kt % 4]
                k.act(pTt[:, q0:512], pS[:, q0:512], AF.Exp, [b_pS], [b_pT], scale=0.125)
                k.mm(pO[0:65, q0:512], VA[:, kt, 0:65], pTt[:, q0:512], kt == 0, kt == nkt - 1,
                     [VB[h][kt // 4], b_vones[h], b_pT], [b_pO])

            for i in range(nkt + LA):
                if i < nkt:
                    qk(i)
                if i >= LA:
                    pv(i - LA)
                yield
            k.P.op("dve", (lambda o=rden[64:65, :], i=pO[64:65, 0:512]: nc.vector.reciprocal(out=o, in_=i)), [b_pO], [b_rden])
            pt, pb = B[1]
            k.mm(pt[0:64, 0:512], ones_f[64:65, 0:64], rden[64:65, :], True, True, [b_onesf, b_rden], [pb])
            k.cp(osb[:], pO[0:64, 0:512], [b_pO], [b_osb], eng="act")
            k.tt(otile[:, h, :], osb[:], pt[0:64, 0:512], ALU.mult, [b_osb, pb], [b_ot])
            yield
        oT(g, otile, b_ot)

    def drive(*gens):
        live = list(gens)
        while live:
            for it in list(live):
                try:
                    next(it)
                except StopIteration:
                    live.remove(it)

    drive(gen_F(0))
    for g in range(NT):
        if g + 1 < NT:
            drive(gen_A(g), gen_F(g + 1))
        else:
            drive(gen_A(g))


def l1_consts(S):
    half = 32
    inv = (10000.0 ** (-np.arange(half, dtype=np.float32) / half)).astype(np.float32)
    ang = np.arange(S, dtype=np.float32)[None, :] * inv[:, None]
    cos = np.cos(ang).astype(np.float32)
    sin = np.sin(ang).astype(np.float32)
    cosd = np.concatenate([cos, cos], 0)
    sind = np.concatenate([sin, -sin], 0)
    kk = np.arange(128)[:, None]
    qq = np.arange(512)[None, :]
    cm = np.concatenate([np.where(qq < j * 128 + kk, NEG, 0.0) for j in range(4)], 1).astype(np.float32)
    ed = (np.arange(S)[None, :] // 256 == np.arange(64)[:, None]).astype(np.float32)
    ident = np.eye(128, dtype=np.float32)
    rm = np.tile((np.arange(512) % 64 != 0).astype(np.float32)[None, :], (64, 1))
    s_ = np.arange(64)[:, None]
    t_ = np.arange(64)[None, :]
    am = np.tile((s_ <= t_).astype(np.float32), (1, 8))
    hm = np.zeros((64, 2), np.float32)
    hm[0:32, 0] = 1
    hm[32:64, 1] = 1
    hm2 = np.repeat(hm, 64, axis=1)
    return dict(cosd=cosd, sind=sind, cmd=cm, ed=ed, identd=ident, rmd=rm, amd=am, hm2d=hm2, hmd=hm)


HALO = 8
OCS = 2048


def choose_tiles(W):
    if W == 4104:
        return 4, 3, 342
    if W == 520:
        return 2, 1, 260
    raise ValueError(W)


class Trunk:
    def __init__(self, k, W):
        self.k = k
        self.nc = k.nc
        self.W = W
        self.NB, self.NTB, self.TW = choose_tiles(W)
        self.TB = self.NTB * self.TW
        TB, TW = self.TB, self.TW
        self.h, self.b_h = k.sb([128, 8, TB], F32, "h")
        self.hn, self.b_hn = k.sb([128, 8, TB], BF16, "hn")
        self.act, self.b_act = k.sb([128, 24, TB], BF16, "act")
        self.wst = [k.sb([128, 8, 128], F32, "wst%d" % i) for i in range(3)]
        self.wbf = [k.sb([128, 24, 128], BF16, "wbf%d" % i) for i in range(3)]
        self.wi = 0
        self.si = 0
        self.wq = []
        self.todo = []
        self.pc = [k.sb([128, 4 + TB], F32, "pc%d" % i) for i in range(2)]
        self.pci = 0
        self.yt = [k.sb([128, TB], F32, "yt%d" % i) for i in range(2)]
        self.gel, self.b_gel = k.sb([128, TB], F32, "gel")
        self.sqt, self.b_sqt = k.sb([128, 8, TW], BF16, "sqt")
        self.lnv, self.b_lnv = k.sb([128, TW], F32, "lnvt")
        self.rstd, self.b_rstd = k.sb([128, TW], F32, "rstdt")
        self.ones_b, self.b_ones = k.sb([128, 128], BF16, "onesb")
        self.m, self.b_m = k.sb([128, 1], F32, "hmask")
        self.carry, self.b_carry = k.sb([128, 48, 2], F32, "carry")
        self.B = [k.ps([128, 512], "bank%d" % i) for i in range(8)]
        self.bi = 0
        k.memset(self.ones_b[:], 1.0, [self.b_ones])
        k.memset(self.carry[:], 0.0, [self.b_carry])

    def bank(self):
        b = self.B[self.bi]
        self.bi = (self.bi + 1) % 8
        return b

    def small(self, name, dram_ap, shape):
        t, b = self.k.sb(shape, F32, name)
        self.k.dma(t[:], dram_ap, (), [b])
        return t, b

    def request(self, wd, KC, col0):
        k = self.k
        wb, b_wb = self.wbf[self.wi]
        self.wi = (self.wi + 1) % 3
        wv = wd.rearrange("(kc p) f -> p kc f", p=128)
        for k0 in range(0, KC, 8):
            k1 = min(KC, k0 + 8)
            st, b_st = self.wst[self.si]
            self.si = (self.si + 1) % 3
            k.dma(st[:, 0:k1 - k0, :], wv[:, k0:k1, col0:col0 + 128], (), [b_st])
            k.cp(wb[:, k0:k1, :], st[:, 0:k1 - k0, :], [b_st], [b_wb], eng="act")
        self.wq.append((wb, b_wb))

    def plan(self, reqs):
        assert not self.wq and not getattr(self, "todo", None), "previous plan not fully consumed"
        self.todo = list(reqs)
        for _ in range(2):
            if self.todo:
                self.request(*self.todo.pop(0))

    def run_stage(self, reqs, body):
        for i in range(len(reqs)):
            body(i)

    def rmsnorm(self, g_t, b_g, out_f32=None):
        k = self.k
        TW = self.TW
        for nt in range(self.NTB):
            cs = slice(nt * TW, (nt + 1) * TW)
            k.act(self.sqt[:], self.h[:, :, cs], AF.Square, [self.b_h], [self.b_sqt])
            pt, pb = self.bank()
            for kc in range(8):
                k.mm(pt[:, 0:TW], self.ones_b[:], self.sqt[:, kc, :], kc == 0, kc == 7, [self.b_ones, self.b_sqt], [pb])
            k.act(self.lnv[:], pt[:, 0:TW], AF.Ln, [pb], [self.b_lnv], bias=EPS, scale=1.0 / 1024)
            k.act(self.rstd[:], self.lnv[:], AF.Exp, [self.b_lnv], [self.b_rstd], scale=-0.5)
            for kc in range(8):
                if out_f32 is None:
                    k.stt(self.hn[:, kc, cs], self.h[:, kc, cs], g_t[:, kc:kc + 1], self.rstd[:], ALU.mult, ALU.mult,
                          [self.b_h, b_g, self.b_rstd], [self.b_hn])
                else:
                    k.stt(out_f32[0][:, kc, cs], self.h[:, kc, cs], g_t[:, kc:kc + 1], self.rstd[:], ALU.mult, ALU.mult,
                          [self.b_h, b_g, self.b_rstd], [out_f32[1]])

    def linear(self, src, b_src, KC, wd, col0, evac):
        k = self.k
        TW = self.TW
        if self.todo:
            self.request(*self.todo.pop(0))
        wb, b_wb = self.wq.pop(0)
        for nt in range(self.NTB):
            cs = slice(nt * TW, (nt + 1) * TW)
            pt, pb = self.bank()
            for kc in range(KC):
                k.mm(pt[:, 0:TW], wb[:, kc, :], src[:, kc, cs], kc == 0, kc == KC - 1, [b_wb, b_src], [pb])
            evac(nt, cs, pt[:, 0:TW], pb)

    @staticmethod
    def ffn_reqs(w_up, w_down):
        return ([(w_up, 8, (part * 24 + j) * 128) for j in range(24) for part in range(2)]
                + [(w_down, 24, fc * 128) for fc in range(8)])

    def conv(self, pc, b_pc, ntap, cw_t, b_cw, cb_t, b_cb, idx, yt, b_yt):
        k = self.k
        TB = self.TB
        last = ntap - 1
        k.act(yt[:], pc[:, last:last + TB], AF.Identity, [b_pc, b_cw, b_cb], [b_yt],
              bias=cb_t[:, idx:idx + 1], scale=cw_t[:, idx, last:last + 1])
        for i in range(last - 1, -1, -1):
            k.stt(yt[:], pc[:, i:i + TB], cw_t[:, idx, i:i + 1], yt[:], ALU.mult, ALU.add, [b_pc, b_cw, b_yt], [b_yt])

    def ffn(self, first, gn_t, b_gn, w_up, cw_t, b_cw, cb_t, b_cb, w_down, mask_halo):
        k = self.k
        TB, TW = self.TB, self.TW
        self.rmsnorm(gn_t, b_gn)
        ys = [None, None]

        def up_body(i):
            j, part = i // 2, i % 2
            idx = part * 24 + j
            pc, b_pc = self.pc[self.pci]
            self.pci = (self.pci + 1) % 2
            yt, b_yt = self.yt[part]
            k.cp(pc[:, 0:2], self.carry[:, idx, :], [self.b_carry], [b_pc], eng="act")

            def evac(nt, cs, ps, pb, pc=pc, b_pc=b_pc):
                k.cp(pc[:, 2 + cs.start:2 + cs.stop], ps, [pb], [b_pc], eng="act")

            self.linear(self.hn, self.b_hn, 8, w_up, idx * 128, evac)
            if first and mask_halo:
                k.ts(pc[:, 2:2 + HALO], pc[:, 2:2 + HALO], self.m[:, 0:1], ALU.mult, [b_pc, self.b_m], [b_pc])
            k.cp(self.carry[:, idx, :], pc[:, TB:TB + 2], [b_pc], [self.b_carry], eng="act")
            self.conv(pc, b_pc, 3, cw_t, b_cw, cb_t, b_cb, idx, yt, b_yt)
            ys[part] = (yt, b_yt)
            if part == 1:
                (yu, b_yu), (yg, b_yg) = ys
                k.act(self.gel[:], yg[:], AF.Gelu_apprx_tanh, [b_yg], [self.b_gel])
                k.tt(self.act[:, j, :], yu[:], self.gel[:], ALU.mult, [b_yu, self.b_gel], [self.b_act])

        self.run_stage([(w_up, 8, (part * 24 + j) * 128) for j in range(24) for part in range(2)], up_body)

        def down_body(fc):
            def evac(nt, cs, ps, pb, fc=fc):
                k.tt(self.h[:, fc, cs], self.h[:, fc, cs], ps, ALU.add, [self.b_h, pb], [self.b_h])

            self.linear(self.act, self.b_act, 24, w_down, fc * 128, evac)

        self.run_stage([(w_down, 24, fc * 128) for fc in range(8)], down_body)


def l2_io(k, W):
    d = {}
    for name, shape in (("xTw", [1024, W]), ("m", [128, 1]), ("sel", [128, 4]), ("w_out0", [1024, 1024]), ("fg0", [128, 8]),
                        ("w_up0", [1024, 6144]), ("cw0", [128, 48, 3]), ("cb0", [128, 48]), ("w_down0", [3072, 1024]),
                        ("mg", [128, 8]), ("w_in", [1024, 2048]), ("rcw", [128, 8, 4]), ("rcb", [128, 8]), ("wa", [1024, 256]),
                        ("ba", [128, 8]), ("wx", [1024, 256]), ("bx", [128, 8]), ("lam", [128, 8])):
        d[name] = k.din(name, shape)
    return d


def emit_l2(k, W, S, d, og, h1_o, gg_o, hl_o, pl_o, ex_o):
    nc = k.nc
    T = Trunk(k, W)
    NB, TB, TW = T.NB, T.TB, T.TW
    TOK = W - HALO
    xTw, m_d, sel_d, w_out, fg_d, w_up, cw_d, cb_d, w_down = (d["xTw"], d["m"], d["sel"], d["w_out0"], d["fg0"], d["w_up0"],
                                                               d["cw0"], d["cb0"], d["w_down0"])
    mg_d, w_in, rcw_d, rcb_d, wa_d, ba_d, wx_d, bx_d, lam_d = (d["mg"], d["w_in"], d["rcw"], d["rcb"], d["wa"], d["ba"],
                                                               d["wx"], d["bx"], d["lam"])
    sel, b_sel = T.small("sel", sel_d[:, :], [128, 4])

    k.dma(T.m[:], m_d[:, :], (), [T.b_m])
    fg, b_fg = T.small("fg", fg_d[:, :], [128, 8])
    cw, b_cw = T.small("cw", cw_d[:, :, :], [128, 48, 3])
    cb, b_cb = T.small("cb", cb_d[:, :], [128, 48])
    mg, b_mg = T.small("mg", mg_d[:, :], [128, 8])
    rcw, b_rcw = T.small("rcw", rcw_d[:, :, :], [128, 8, 4])
    rcb, b_rcb = T.small("rcb", rcb_d[:, :], [128, 8])
    ba, b_ba = T.small("ba", ba_d[:, :], [128, 8])
    bx, b_bx = T.small("bx", bx_d[:, :], [128, 8])
    lam, b_lam = T.small("lam", lam_d[:, :], [128, 8])
    c1, b_c1 = k.sb([128, 8], F32, "c1")
    c2, b_c2 = k.sb([128, 8], F32, "c2")
    k.act(c1[:], lam[:], AF.Exp, [b_lam], [b_c1], scale=-1.0)
    k.act(c1[:], c1[:], AF.Ln, [b_c1], [b_c1], bias=1.0, scale=1.0)
    k.ts(c2[:], c1[:], -16.0, ALU.mult, [b_c1], [b_c2])
    k.ts(c1[:], c1[:], -8.0, ALU.mult, [b_c1, b_c2], [b_c1])
    ob, b_ob = T.hn, T.b_hn
    gg, b_gg = k.sb([128, TB], BF16, "gg")
    rcar, b_rcar = k.sb([128, 8, 3], F32, "rcar")
    hcar, b_hcar = k.sb([128, 8], F32, "hcar")
    pcar, b_pcar = k.sb([128, 8], F32, "pcar")
    zer, b_zer = k.sb([128, TB], F32, "zer")
    xrc, b_xrc = k.sb([128, 2, TB], F32, "xrc")
    xrb, b_xrb = k.sb([128, 2, TB], BF16, "xrb")
    rr, b_rr = k.sb([128, TB], F32, "rr")
    ii, b_ii = k.sb([128, TB], F32, "ii")
    aa, b_aa = k.sb([128, TB], F32, "aa")
    uu, b_uu = k.sb([128, TB], F32, "uu")
    hl, b_hl = k.sb([128, TB], F32, "hl")
    pl, b_pl = k.sb([128, TB], F32, "pl")
    k.memset(rcar[:], 0.0, [b_rcar])
    k.memset(zer[:], 0.0, [b_zer])
    ext, b_ext = k.sb([128, 8, 2], F32, "ext")

    oTv = [o_.rearrange("(kc p) s -> p kc s", p=128) for o_ in og]
    xTv = xTw.rearrange("(kc p) s -> p kc s", p=128)
    h1v = h1_o.rearrange("(kc p) s -> p kc s", p=128)
    ggv = gg_o.rearrange("(kc p) s -> p kc s", p=128)
    hlv = hl_o.rearrange("(kc p) s -> p kc s", p=128)
    plv = pl_o.rearrange("(kc p) s -> p kc s", p=128)
    exv = ex_o.rearrange("(kc p) s -> p kc s", p=128)

    for blk in range(NB):
        first = blk == 0
        g0 = blk * TB
        for c in range(4):
            cand = T.act[:, 8 * (c % 3):8 * (c % 3) + 8, :]
            lo = c * TOK - HALO + g0
            skip = max(0, -lo)
            if skip:
                k.memset(cand[:, :, 0:skip], 0.0, [T.b_act])
            a = lo + skip
            while a < lo + TB:
                j = a // OCS
                e = min(lo + TB, (j + 1) * OCS)
                k.dma(cand[:, :, a - lo:e - lo], oTv[j][:, :, a - j * OCS:e - j * OCS], (), [T.b_act])
                a = e
            if c == 0:
                k.ts(ob[:], cand, sel[:, 0:1], ALU.mult, [T.b_act, b_sel], [b_ob])
            else:
                k.stt(ob[:], cand, sel[:, c:c + 1], ob[:], ALU.mult, ALU.add, [T.b_act, b_sel, b_ob], [b_ob])
        k.dma(T.h[:, 0:4, :], xTv[:, 0:4, g0:g0 + TB], (), [T.b_h])
        k.dma(T.h[:, 4:8, :], xTv[:, 4:8, g0:g0 + TB], (), [T.b_h])
        rg_reqs = []
        for n in range(4):
            rg_reqs += [(w_in, 8, 1024 + (2 * n + c2i) * 128) for c2i in range(2)]
            for c2i in range(2):
                rg_reqs += [(wa_d[n * 256:(n + 1) * 256, :], 2, c2i * 128), (wx_d[n * 256:(n + 1) * 256, :], 2, c2i * 128)]
        T.plan([(w_out, 8, fc * 128) for fc in range(8)] + T.ffn_reqs(w_up, w_down)
               + [(w_in, 8, fc * 128) for fc in range(8)] + rg_reqs)

        def wo_body(fc):
            def evac(nt, cs, ps, pb, fc=fc):
                k.tt(T.h[:, fc, cs], T.h[:, fc, cs], ps, ALU.add, [T.b_h, pb], [T.b_h])

            T.linear(ob, b_ob, 8, w_out, fc * 128, evac)

        T.run_stage([(w_out, 8, fc * 128) for fc in range(8)], wo_body)
        T.ffn(first, fg, b_fg, w_up, cw, b_cw, cb, b_cb, w_down, False)
        k.dma(h1v[:, :, g0:g0 + TB], T.h[:], [T.b_h], (), eng="pool")
        T.rmsnorm(mg, b_mg)
        def gb_body(fc):
            def evac(nt, cs, ps, pb, fc=fc):
                k.act(gg[:, cs], ps, AF.Gelu_apprx_tanh, [pb], [b_gg])

            T.linear(T.hn, T.b_hn, 8, w_in, fc * 128, evac)
            k.dma(ggv[:, fc, g0:g0 + TB], gg[:], [b_gg], (), eng="pool")

        T.run_stage([(w_in, 8, fc * 128) for fc in range(8)], gb_body)
        reqs = []
        for n in range(4):
            reqs += [(w_in, 8, 1024 + (2 * n + c2i) * 128) for c2i in range(2)]
            for c2i in range(2):
                reqs += [(wa_d[n * 256:(n + 1) * 256, :], 2, c2i * 128), (wx_d[n * 256:(n + 1) * 256, :], 2, c2i * 128)]

        def rg_body(i, blk=blk, first=first, g0=g0):
            n, r = i // 6, i % 6
            if r < 2:
                c2i = r
                c8 = 2 * n + c2i
                pc, b_pc = T.pc[T.pci]
                T.pci = (T.pci + 1) % 2
                k.cp(pc[:, 0:3], rcar[:, c8, :], [b_rcar], [b_pc], eng="act")

                def evac(nt, cs, ps, pb, pc=pc, b_pc=b_pc):
                    k.cp(pc[:, 3 + cs.start:3 + cs.stop], ps, [pb], [b_pc], eng="act")

                T.linear(T.hn, T.b_hn, 8, w_in, 1024 + c8 * 128, evac)
                if first:
                    k.ts(pc[:, 3:3 + HALO], pc[:, 3:3 + HALO], T.m[:, 0:1], ALU.mult, [b_pc, T.b_m], [b_pc])
                k.cp(rcar[:, c8, :], pc[:, TB:TB + 3], [b_pc], [b_rcar], eng="act")
                yt, b_yt = T.yt[c2i]
                T.conv(pc, b_pc, 4, rcw, b_rcw, rcb, b_rcb, c8, yt, b_yt)
                k.cp(xrc[:, c2i, :], yt[:], [b_yt], [b_xrc], eng="pool")
                k.cp(xrb[:, c2i, :], yt[:], [b_yt], [b_xrb], eng="act")
                return
            c2i, which = (r - 2) // 2, (r - 2) % 2
            fc = 2 * n + c2i
            wd, bias_t, b_bias, dst, b_dst = ((wa_d, ba, b_ba, rr, b_rr), (wx_d, bx, b_bx, ii, b_ii))[which]

            def evac(nt, cs, ps, pb, dst=dst, b_dst=b_dst, bias_t=bias_t, b_bias=b_bias, fc=fc):
                k.act(dst[:, cs], ps, AF.Sigmoid, [pb, b_bias], [b_dst], bias=bias_t[:, fc:fc + 1], scale=1.0)

            T.linear(xrb, b_xrb, 2, wd[n * 256:(n + 1) * 256, :], c2i * 128, evac)
            if which == 0:
                return
            k.act(aa[:], rr[:], AF.Exp, [b_rr, b_c1], [b_aa], scale=c1[:, fc:fc + 1])
            k.act(rr[:], rr[:], AF.Exp, [b_rr, b_c2], [b_rr], scale=c2[:, fc:fc + 1])
            k.act(rr[:], rr[:], AF.Sqrt, [b_rr], [b_rr], bias=1.0, scale=-1.0)
            k.tt(uu[:], xrc[:, c2i, :], ii[:], ALU.mult, [b_xrc, b_ii], [b_uu])
            k.tt(uu[:], uu[:], rr[:], ALU.mult, [b_uu, b_rr], [b_uu])
            if first:
                k.ts(uu[:, 0:HALO], uu[:, 0:HALO], T.m[:, 0:1], ALU.mult, [b_uu, T.b_m], [b_uu])
                k.memset(hl[:, 0:5], 0.0, [b_hl])
                k.memset(pl[:, 0:5], 0.0, [b_pl])
                s0, hi, pi = 5, 0.0, 1.0
            else:
                s0, hi, pi = 0, hcar[:, fc:fc + 1], pcar[:, fc:fc + 1]
            k.scan(hl[:, s0:TB], aa[:, s0:TB], uu[:, s0:TB], hi, [b_aa, b_uu, b_hcar], [b_hl])
            k.scan(pl[:, s0:TB], aa[:, s0:TB], zer[:, s0:TB], pi, [b_aa, b_zer, b_pcar], [b_pl])
            k.cp(hcar[:, fc:fc + 1], hl[:, TB - 1:TB], [b_hl], [b_hcar], eng="pool")
            k.cp(pcar[:, fc:fc + 1], pl[:, TB - 1:TB], [b_pl], [b_pcar], eng="pool")
            k.dma(hlv[:, fc, g0:g0 + TB], hl[:], [b_hl], (), eng="pool")
            k.dma(plv[:, fc, g0:g0 + TB], pl[:], [b_pl], (), eng="pool")
            if blk == NB - 1:
                k.cp(ext[:, fc, 0:1], hl[:, TB - 4:TB - 3], [b_hl], [b_ext], eng="pool")
                k.cp(ext[:, fc, 1:2], pl[:, TB - 4:TB - 3], [b_pl], [b_ext], eng="pool")

        T.run_stage(reqs, rg_body)
    k.dma(exv[:, :, :], ext[:], [b_ext], (), eng="pool")


def l3_io(k, W):
    d = {}
    for name, shape in (("srank", [128, 4]), ("oms", [128, 4]), ("w_out1", [1024, 1024]), ("fg1", [128, 8]),
                        ("w_up1", [1024, 6144]), ("cw1", [128, 48, 3]), ("cb1", [128, 48]), ("w_down1", [3072, 1024]),
                        ("fin", [128, 8])):
        d[name] = k.din(name, shape)
    return d


def emit_l3(k, W, d, m_d, h1w, ggw, hlw, plw, exg, out_o):
    nc = k.nc
    T = Trunk(k, W)
    NB, TB, TW = T.NB, T.TB, T.TW
    TOK = W - HALO
    w_out, fg_d, w_up, cw_d, cb_d, w_down, fin_d = (d["w_out1"], d["fg1"], d["w_up1"], d["cw1"], d["cb1"], d["w_down1"], d["fin"])
    k.dma(T.m[:], m_d[:, :], (), [T.b_m])
    fg, b_fg = T.small("fg", fg_d[:, :], [128, 8])
    cw, b_cw = T.small("cw", cw_d[:, :, :], [128, 48, 3])
    cb, b_cb = T.small("cb", cb_d[:, :], [128, 48])
    fin, b_fin = T.small("fin", fin_d[:, :], [128, 8])
    sr, b_sr = T.small("sr", d["srank"][:, :], [128, 4])
    oms, b_oms = T.small("oms", d["oms"][:, :], [128, 4])
    pe, b_pe = k.sb([128, 4, 8, 2], F32, "pe")
    exv = exg.rearrange("(r kc p) t -> p r kc t", p=128, kc=8)
    for r in range(4):
        k.dma(pe[:, r, :, :], exv[:, r, :, :], (), [b_pe])
    Hc, b_Hc = k.sb([128, 8], F32, "Hc")
    Pm, b_Pm = k.sb([128, 8], F32, "Pm")
    Em, b_Em = k.sb([128, 8], F32, "Em")
    k.memset(Hc[:], 0.0, [b_Hc])
    for r in range(4):
        k.ts(Pm[:], pe[:, r, :, 1], sr[:, r:r + 1], ALU.mult, [b_pe, b_sr], [b_Pm], s2=oms[:, r:r + 1], op1=ALU.add)
        k.ts(Em[:], pe[:, r, :, 0], sr[:, r:r + 1], ALU.mult, [b_pe, b_sr], [b_Em])
        k.tt(Hc[:], Hc[:], Pm[:], ALU.mult, [b_Hc, b_Pm], [b_Hc])
        k.tt(Hc[:], Hc[:], Em[:], ALU.add, [b_Hc, b_Em], [b_Hc])
    yb, b_yb = T.hn, T.b_hn
    ybufs = [(k.sb([128, TB], F32, "hl%d" % i), k.sb([128, TB], F32, "pl%d" % i), k.sb([128, TB], BF16, "ggt%d" % i)) for i in range(2)]
    outt, b_outt = T.h, T.b_h

    h1v = h1w.rearrange("(kc p) s -> p kc s", p=128)
    ggv = ggw.rearrange("(kc p) s -> p kc s", p=128)
    hlv = hlw.rearrange("(kc p) s -> p kc s", p=128)
    plv = plw.rearrange("(kc p) s -> p kc s", p=128)
    outv = out_o.rearrange("(kc p) s -> p kc s", p=128)

    for blk in range(NB):
        first = blk == 0
        g0 = blk * TB
        k.dma(T.h[:, 0:4, :], h1v[:, 0:4, g0:g0 + TB], (), [T.b_h])
        k.dma(T.h[:, 4:8, :], h1v[:, 4:8, g0:g0 + TB], (), [T.b_h])
        for fc in range(8):
            (hl, b_hl), (pl, b_pl), (ggt, b_ggt) = ybufs[fc % 2]
            k.dma(hl[:], hlv[:, fc, g0:g0 + TB], (), [b_hl])
            k.dma(pl[:], plv[:, fc, g0:g0 + TB], (), [b_pl])
            k.dma(ggt[:], ggv[:, fc, g0:g0 + TB], (), [b_ggt])
            k.stt(hl[:], pl[:], Hc[:, fc:fc + 1], hl[:], ALU.mult, ALU.add, [b_pl, b_Hc, b_hl], [b_hl])
            k.tt(yb[:, fc, :], hl[:], ggt[:], ALU.mult, [b_hl, b_ggt], [b_yb])
        T.plan([(w_out, 8, fc * 128) for fc in range(8)] + T.ffn_reqs(w_up, w_down))

        def wo_body(fc):
            def evac(nt, cs, ps, pb, fc=fc):
                k.tt(T.h[:, fc, cs], T.h[:, fc, cs], ps, ALU.add, [T.b_h, pb], [T.b_h])

            T.linear(yb, b_yb, 8, w_out, fc * 128, evac)

        T.run_stage([(w_out, 8, fc * 128) for fc in range(8)], wo_body)
        T.ffn(first, fg, b_fg, w_up, cw, b_cw, cb, b_cb, w_down, True)
        T.rmsnorm(fin, b_fin, out_f32=(outt, b_outt))
        lo = HALO if first else 0
        k.dma(outv[:, :, g0 + lo - HALO:g0 + TB - HALO], outt[:, :, lo:TB], [b_outt], (), eng="pool")


_F = {}


def build_fused(S):
    TOK = S // 4
    W = TOK + HALO
    nc = bass.Bass("TRN2", target_bir_lowering=False)
    k = K(nc)
    io1 = l1_io(k, S)
    io2 = l2_io(k, W)
    io3 = l3_io(k, W)
    NCH = max(1, S // OCS)
    osrc = [k.dint("osrc%d" % j, [256, min(S, OCS)], BF16) for j in range(NCH)]
    og = [k.dint("og%d" % j, [1024, min(S, OCS)], BF16) for j in range(NCH)]
    h1 = k.dint("h1s", [1024, W])
    gg = k.dint("ggs", [1024, W], BF16)
    hl = k.dint("hls", [1024, W])
    pl = k.dint("pls", [1024, W])
    exs = k.dint("exs", [1024, 2])
    exg = k.dint("exg", [4096, 2])
    out = k.dout("outT", [1024, TOK])
    groups = [[0, 1, 2, 3], [4, 5, 6, 7]]
    upto = int(os.environ.get("FUSE_UPTO", "3"))
    tile_bufs = {}

    def o_out(g, otile, b_ot):
        j, off = (g * 512) // OCS, (g * 512) % OCS
        ov = osrc[j].rearrange("(r p) s -> p r s", p=64)
        bb = Buf()
        tile_bufs.setdefault(j, []).append(bb)
        k.dma(ov[:, :, off:off + 512], otile[:], [b_ot], [bb], eng="pool")
        if off + 512 == min(S, OCS):
            k.P.dma((lambda j=j: nc.gpsimd.collective_compute("AllGather", ALU.bypass, replica_groups=groups,
                                                               ins=[osrc[j][:, :]], outs=[og[j][:, :]])),
                    tile_bufs[j], (), eng="pool", inc=1)

    emit_l1(k, S, io1, o_out)
    k.end_phase()
    emit_l2(k, W, S, io2, og, h1, gg, hl, pl, exs)
    k.end_phase()
    if upto == 2:
        return nc, k.P.finish()
    if os.environ.get("NOCC2"):
        k.dma(exg[0:1024, :], exs[:, :], (), ())
    else:
        k.P.dma(lambda: nc.gpsimd.collective_compute("AllGather", ALU.bypass, replica_groups=groups, ins=[exs[:, :]], outs=[exg[:, :]]),
                (), (), eng="pool", inc=1)
    k.P.barrier()
    emit_l3(k, W, io3, io2["m"], h1, gg, hl, pl, exg, out)
    cnt = k.P.finish()
    return nc, cnt


def _pk(v):
    return np.ascontiguousarray(v.reshape(-1, 128).T)


def _pkw(w):
    t, C = w.shape
    return np.ascontiguousarray(w.T.reshape(C // 128, 128, t).transpose(1, 0, 2))


def kernel(x, mix_norm_g, ffn_norm_g, final_norm_g,
           ev_w_in, ev_w_gk2, ev_b_gk2, ev_gla_norm_g, ev_w_out,
           od_w_in, od_conv_w, od_conv_b, od_w_a, od_b_a, od_w_x, od_b_x, od_lambda, od_w_out,
           ffn_w_up, ffn_conv_w, ffn_conv_b, ffn_w_down):
    f = lambda a: np.asarray(a, dtype=np.float32)
    x = f(x)
    Bsz, S, D = x.shape
    TOK = S // 4
    W = TOK + HALO
    if S not in _F:
        _F[S] = build_fused(S)
    nc, _ = _F[S]
    consts = l1_consts(S)
    w = f(ev_w_in)[0]
    gn = _pk(f(mix_norm_g)[0])
    perm = []
    for g in range(4):
        for j in range(4):
            head = 2 * g + (j % 2)
            base = head * 64 if j < 2 else 512 + head * 64
            perm += list(range(base, base + 64))
    w_out0 = np.ascontiguousarray(f(ev_w_out)[0][np.asarray(perm)])
    shared = {
        "gn": gn, "w_out0": w_out0, "fg0": _pk(f(ffn_norm_g)[0]), "w_up0": f(ffn_w_up)[0],
        "cw0": _pkw(f(ffn_conv_w)[0]), "cb0": _pk(f(ffn_conv_b)[0]), "w_down0": f(ffn_w_down)[0],
        "mg": _pk(f(mix_norm_g)[1]), "w_in": f(od_w_in)[0], "rcw": _pkw(f(od_conv_w)[0]), "rcb": _pk(f(od_conv_b)[0]),
        "wa": np.ascontiguousarray(f(od_w_a)[0].reshape(1024, 256)), "ba": _pk(f(od_b_a)[0]),
        "wx": np.ascontiguousarray(f(od_w_x)[0].reshape(1024, 256)), "bx": _pk(f(od_b_x)[0]),
        "lam": _pk(f(od_lambda)[0]),
        "w_out1": f(od_w_out)[0], "fg1": _pk(f(ffn_norm_g)[1]), "w_up1": f(ffn_w_up)[1],
        "cw1": _pkw(f(ffn_conv_w)[1]), "cb1": _pk(f(ffn_conv_b)[1]), "w_down1": f(ffn_w_down)[1],
        "fin": _pk(f(final_norm_g)),
    }
    shared.update(consts)
    in_maps = []
    for b in range(Bsz):
        xTb = np.ascontiguousarray(x[b].T)
        for c in range(4):
            h0, h1 = 2 * c, 2 * c + 1
            cols = []
            for base in (0, 512, 1024):
                cols += [np.arange(base + h0 * 64, base + h0 * 64 + 64), np.arange(base + h1 * 64, base + h1 * 64 + 64)]
            for base in (1536, 1792):
                cols += [np.arange(base + h0 * 32, base + h0 * 32 + 32), np.arange(base + h1 * 32, base + h1 * 32 + 32)]
            cols += [np.arange(2048 + h0 * 64, 2048 + h0 * 64 + 64), np.arange(2048 + h1 * 64, 2048 + h1 * 64 + 64)]
            cols += [np.arange(2560, 2576)]
            cols += [np.arange(2576 + h0 * 64, 2576 + h0 * 64 + 64), np.arange(2576 + h1 * 64, 2576 + h1 * 64 + 64)]
            cols = np.concatenate(cols)
            t0 = c * TOK
            xw = np.zeros((1024, W), np.float32)
            if c == 0:
                xw[:, HALO:] = xTb[:, 0:TOK]
            else:
                xw[:] = xTb[:, t0 - HALO:t0 + TOK]
            sel = np.zeros((128, 4), np.float32)
            sel[:, c] = 1.0
            sr = np.zeros((128, 4), np.float32)
            sr[:, :c] = 1.0
            m = dict(shared)
            m.update({
                "xT": xTb, "w1": np.ascontiguousarray(w[:, cols]),
                "wgk2": np.ascontiguousarray(f(ev_w_gk2)[0][:, h0 * 32:h0 * 32 + 64]),
                "bgk2": np.ascontiguousarray(f(ev_b_gk2)[0][h0 * 32:h0 * 32 + 64].reshape(64, 1)),
                "glag": np.ascontiguousarray(f(ev_gla_norm_g)[0][h0:h0 + 2].T),
                "xTw": xw, "m": np.full((128, 1), 0.0 if c == 0 else 1.0, np.float32),
                "sel": sel, "srank": sr, "oms": 1.0 - sr,
            })
            in_maps.append(m)
    res = run_bass_kernel_spmd(nc, in_maps, core_ids=list(range(len(in_maps)))).results
    out = np.zeros((Bsz, S, D), np.float32)
    for b in range(Bsz):
        for c in range(4):
            out[b, c * TOK:(c + 1) * TOK, :] = np.asarray(res[b * 4 + c]["outT"]).T
    return out
```

```python
import contextlib
import os
import numpy as np
import ml_dtypes
import concourse.bass as bass
import concourse.mybir as mybir
from concourse.bass_utils import run_bass_kernel_spmd

F32 = mybir.dt.float32
BF16 = mybir.dt.bfloat16
AF = mybir.ActivationFunctionType
ALU = mybir.AluOpType
AX = mybir.AxisListType

SAME_SYNC = True
N_DMA_SEMS = 24
SEM_EPOCH = 2000
NEG = -240000.0
EPS = 1e-6


class Buf:
    __slots__ = ("name", "lw", "rd")

    def __init__(self, name=""):
        self.name = name
        self.lw = None
        self.rd = []


class Prog:
    ENGS = ("pe", "act", "dve", "pool", "sp")

    def __init__(self, nc):
        self.nc = nc
        self.h = {"pe": nc.tensor, "act": nc.scalar, "dve": nc.vector, "pool": nc.gpsimd, "sp": nc.sync}
        self.items = {e: [] for e in self.ENGS}
        self.known = {}
        self.sem = {e: nc.alloc_semaphore("s_" + e) for e in self.ENGS}
        self.dsem = [nc.alloc_semaphore("d%d" % i) for i in range(N_DMA_SEMS)]
        self.duse = [0] * N_DMA_SEMS
        self.dval = [0] * N_DMA_SEMS
        self.pending = {e: [] for e in self.ENGS}
        self.rank = {e: [] for e in self.ENGS}
        self.emitted = {e: 0 for e in self.ENGS}
        self.esem = {}
        self.dnext = 0

    def _need(self, eng, tok, waits):
        if tok is None:
            return
        if tok[0] == "c":
            _, e2, seq = tok
            if e2 == eng and (eng == "pe" or not SAME_SYNC):
                return
            key = (eng, "c", e2)
            if self.known.get(key, -1) >= seq:
                return
            self.known[key] = seq
            self.items[e2][seq]["flag"] = True
            waits.append(tok)
        else:
            _, idx, val = tok
            key = (eng, "d", idx)
            if self.known.get(key, -1) >= val:
                return
            self.known[key] = val
            waits.append(tok)

    def _deps(self, eng, reads, writes):
        waits = []
        for b in reads:
            self._need(eng, b.lw, waits)
        for b in writes:
            self._need(eng, b.lw, waits)
            for t in b.rd:
                self._need(eng, t, waits)
        return waits

    def op(self, eng, fn, reads=(), writes=()):
        waits = self.pending[eng] + self._deps(eng, reads, writes)
        self.pending[eng] = []
        seq = len(self.items[eng])
        self.items[eng].append({"waits": waits, "fn": fn, "flag": False, "dma": None})
        tok = ("c", eng, seq)
        for b in reads:
            b.rd.append(tok)
        for b in writes:
            b.lw = tok
            b.rd = []
        return tok

    def dma(self, fn, reads=(), writes=(), eng="sp", inc=16):
        idx = self.dnext
        self.dnext = (self.dnext + 1) % N_DMA_SEMS
        pv = self.dval[idx]
        waits = self.pending[eng] + self._deps(eng, reads, writes)
        self.pending[eng] = []
        if pv > 0:
            self._need(eng, ("d", idx, pv), waits)
        self.duse[idx] += 1
        self.dval[idx] = pv + inc
        tok = ("d", idx, pv + inc)
        self.items[eng].append({"waits": waits, "fn": fn, "flag": False, "dma": idx, "inc": inc})
        for b in reads:
            b.rd.append(tok)
        for b in writes:
            b.lw = tok
            b.rd = []
        return tok

    def _all_tokens(self):
        toks = []
        for e in ("pe", "act", "dve", "pool"):
            items = self.items[e]
            for i in range(len(items) - 1, -1, -1):
                if items[i]["dma"] is None and items[i]["fn"] is not None or (items[i]["dma"] is None and i < self.emitted[e]):
                    toks.append(("c", e, i))
                    break
        for idx in range(N_DMA_SEMS):
            if self.dval[idx]:
                toks.append(("d", idx, self.dval[idx]))
        return toks

    def barrier(self, engines=None):
        toks = self._all_tokens()
        for e in (engines or self.ENGS):
            for t in toks:
                if t[0] == "c" and t[1] == e:
                    continue
                self._need(e, t, self.pending[e])

    def _sem_of(self, e, r):
        ep = (r - 1) // SEM_EPOCH
        if (e, ep) not in self.esem:
            self.esem[(e, ep)] = self.sem[e] if ep == 0 else self.nc.alloc_semaphore("s_%s_%d" % (e, ep))
        return self.esem[(e, ep)], (r - 1) % SEM_EPOCH + 1

    def flush(self):
        for e in self.ENGS:
            items = self.items[e]
            rk = self.rank[e]
            c = rk[-1] if rk else 0
            for i in range(len(rk), len(items)):
                if items[i]["flag"]:
                    c += 1
                rk.append(c)
        for e in self.ENGS:
            h = self.h[e]
            items = self.items[e]
            for i in range(self.emitted[e], len(items)):
                it = items[i]
                for t in it["waits"]:
                    if t[0] == "c":
                        sm, v = self._sem_of(t[1], self.rank[t[1]][t[2]])
                        h.wait_ge(sm, v)
                    else:
                        h.wait_ge(self.dsem[t[1]], t[2])
                if it["fn"] is None:
                    continue
                ins = it["fn"]()
                if it["dma"] is not None:
                    ins.then_inc(self.dsem[it["dma"]], it["inc"])
                elif it["flag"]:
                    sm, _ = self._sem_of(e, self.rank[e][i])
                    ins.then_inc(sm, 1)
                it["fn"] = None
            self.emitted[e] = len(items)

    def finish(self):
        self.barrier(engines=("sp",))
        self.items["sp"].append({"waits": self.pending["sp"], "fn": None, "flag": False, "dma": None})
        self.pending["sp"] = []
        self.flush()
        return {e: (len(self.items[e]), self.rank[e][-1] if self.rank[e] else 0) for e in self.ENGS}


class K:
    def __init__(self, nc):
        self.nc = nc
        self.P = Prog(nc)
        self._n = 0
        self.stack = contextlib.ExitStack()

    def end_phase(self):
        self.P.barrier()
        self.P.flush()
        self.stack.close()
        self.stack = contextlib.ExitStack()

    def sb(self, shape, dt, name=None):
        self._n += 1
        return self.stack.enter_context(self.nc.sbuf_tensor("sb%d_" % self._n + (name or "t"), list(shape), dt)), Buf(name or "")

    def ps(self, shape, name=None):
        self._n += 1
        return self.stack.enter_context(self.nc.psum_tensor("ps%d_" % self._n + (name or "p"), list(shape), F32)), Buf(name or "")

    def din(self, name, shape, dt=F32):
        return self.nc.dram_tensor(name, list(shape), dt, kind="ExternalInput").ap()

    def dint(self, name, shape, dt=F32, **kw):
        return self.nc.dram_tensor(name, list(shape), dt, kind="Internal", **kw).ap()

    def dout(self, name, shape, dt=F32):
        return self.nc.dram_tensor(name, list(shape), dt, kind="ExternalOutput").ap()

    def mm(self, out, lhsT, rhs, st, sp, r, w):
        nc = self.nc
        return self.P.op("pe", lambda: nc.tensor.matmul(out, lhsT=lhsT, rhs=rhs, start=st, stop=sp), r, w)

    def tr(self, out, in_, ident, r, w):
        nc = self.nc
        return self.P.op("pe", lambda: nc.tensor.transpose(out, in_, ident), r, w)

    def act(self, out, in_, func, r, w, bias=None, scale=None):
        nc = self.nc
        kw = {}
        if bias is not None:
            kw["bias"] = bias
        if scale is not None:
            kw["scale"] = scale
        return self.P.op("act", lambda: nc.scalar.activation(out=out, in_=in_, func=func, **kw), r, w)

    def tt(self, out, in0, in1, op, r, w, eng="dve"):
        h = self.P.h[eng]
        return self.P.op(eng, lambda: h.tensor_tensor(out=out, in0=in0, in1=in1, op=op), r, w)

    def ts(self, out, in0, s1, op0, r, w, s2=None, op1=None, eng="dve"):
        h = self.P.h[eng]
        if op1 is None:
            return self.P.op(eng, lambda: h.tensor_scalar(out=out, in0=in0, scalar1=s1, scalar2=None, op0=op0), r, w)
        return self.P.op(eng, lambda: h.tensor_scalar(out=out, in0=in0, scalar1=s1, scalar2=s2, op0=op0, op1=op1), r, w)

    def stt(self, out, in0, scalar, in1, op0, op1, r, w):
        nc = self.nc
        return self.P.op("dve", lambda: nc.vector.scalar_tensor_tensor(out=out, in0=in0, scalar=scalar, in1=in1, op0=op0, op1=op1), r, w)

    def cp(self, out, in_, r, w, eng="dve"):
        if eng == "act":
            nc = self.nc
            return self.P.op("act", lambda: nc.scalar.copy(out=out, in_=in_), r, w)
        h = self.P.h[eng]
        return self.P.op(eng, lambda: h.tensor_copy(out=out, in_=in_), r, w)

    def memset(self, ap, val, w, eng="pool"):
        h = self.P.h[eng]
        return self.P.op(eng, lambda: h.memset(ap, val), (), w)

    def scan(self, out, d0, d1, init, r, w):
        nc = self.nc
        return self.P.op("dve", lambda: nc.vector.tensor_tensor_scan(out=out, data0=d0, data1=d1, initial=init, op0=ALU.mult, op1=ALU.add), r, w)

    def dma(self, out, in_, r, w, eng="sp"):
        h = self.P.h[eng]
        return self.P.dma(lambda: h.dma_start(out=out, in_=in_), r, w, eng=eng)


def l1_io(k, S):
    d = {}
    for name, shape in (("xT", [1024, S]), ("gn", [128, 8]), ("w1", [1024, 784]), ("wgk2", [16, 64]), ("bgk2", [64, 1]),
                        ("glag", [64, 2]), ("cosd", [64, S]), ("sind", [64, S]), ("cmd", [128, 2048]), ("ed", [64, S]),
                        ("identd", [128, 128]), ("rmd", [64, 512]), ("amd", [64, 512]), ("hm2d", [64, 128]), ("hmd", [64, 2])):
        d[name] = k.din(name, shape)
    return d


def emit_l1(k, S, d, oT):
    NT = S // 512
    NKT = S // 128
    nc = k.nc
    P = k.P
    xT, gn_d, w1_d, wgk2_d, bgk2_d, glag_d = d["xT"], d["gn"], d["w1"], d["wgk2"], d["bgk2"], d["glag"]
    cos_d, sin_d, cm_d, e_d, id_d, rm_d, am_d, hm2_d, hm_d = (d["cosd"], d["sind"], d["cmd"], d["ed"], d["identd"], d["rmd"],
                                                              d["amd"], d["hm2d"], d["hmd"])

    xt, b_xt = k.sb([128, 8, 512], F32, "xt")
    rstd, b_rstd = k.sb([128, 512], F32, "rstd")
    lnv, b_lnv = rstd, b_rstd
    hn, b_hn = k.sb([128, 8, 512], BF16, "hn")
    sq, b_sq = hn, b_hn
    wb, b_wb = k.sb([128, 8, 784], BF16, "wb")
    gn, b_gn = k.sb([128, 8], F32, "gn")
    Kaug = [k.sb([128, S], BF16, "kaug%d" % h) for h in range(2)]
    KB = [[Buf() for _ in range(NT)] for h in range(2)]
    b_kE = [Buf(), Buf()]
    Vaug = [k.sb([128, NKT, 66], BF16, "vaug%d" % h) for h in range(2)]
    VB = [[Buf() for _ in range(NT)] for h in range(2)]
    b_vones = [Buf(), Buf()]
    Qaug = [[k.sb([128, 512], BF16, "qaug%d_%d" % (h, p)) for p in range(2)] for h in range(2)]
    kmT = [k.sb([64, 64], BF16, "kmT%d" % h) for h in range(2)]
    km32, b_km32 = k.sb([64, 2], F32, "km32")
    cosT, b_cos = k.sb([64, 512], F32, "cos")
    sinT, b_sin = k.sb([64, 512], F32, "sin")
    t1, b_t1 = k.sb([64, 512], F32, "t1")
    t2, b_t2 = k.sb([64, 512], F32, "t2")
    pTs = [k.sb([128, 512], BF16, "pT%d" % i) for i in range(4)]
    cm, b_cm = k.sb([128, 4, 512], BF16, "cm")
    id_f, b_idf = k.sb([128, 128], F32, "idf")
    id_b, b_idb = k.sb([128, 128], BF16, "idb")
    ones_b, b_onesb = k.sb([128, 128], BF16, "onesb")
    ones_f, b_onesf = k.sb([128, 64], F32, "onesf")
    bq, b_bq = k.sb([128, 4, 128], F32, "bq")
    gsb, b_gsb = k.sb([128, 4, 64], F32, "gsb")
    m8, b_m8 = k.sb([128, 4, 8], F32, "m8")
    fin_t, b_rden = k.sb([128, 512], F32, "fin")
    rden = fin_t
    osb, b_osb = fin_t[0:64, :], Buf("osb")
    otiles = [k.sb([64, 4, 512], BF16, "otile%d" % p) for p in range(2)]
    QG32, b_qg = k.sb([64, 512], F32, "qg32")
    KG32, b_kg = k.sb([64, 512], F32, "kg32")
    spl, b_spl = k.sb([64, 512], F32, "spl")
    bpos, b_bpos = k.sb([64, 512], F32, "bpos")
    eb, b_eb = k.sb([64, 512], F32, "eb")
    enb, b_enb = k.sb([64, 512], F32, "enb")
    Ac, b_ac = k.sb([64, 8], F32, "Ac")
    ke32, b_ke = k.sb([64, 512], F32, "ke32")
    qt, b_qt = k.sb([64, 512], BF16, "qt")
    kpad, b_kpad = k.sb([64, 2, 512], BF16, "kpad")
    khat, b_khat = k.sb([64, 512], BF16, "khat")
    rmk, b_rmk = k.sb([64, 512], F32, "rmk")
    amk, b_amk = k.sb([64, 512], BF16, "amk")
    hm2, b_hm2 = k.sb([64, 128], F32, "hm2")
    hm, b_hm = k.sb([64, 2], F32, "hm")
    attm, b_attm = k.sb([64, 2, 512], BF16, "attm")
    gvt, b_gvt = k.sb([64, 8, 128], BF16, "gvt")
    KTt, b_ktt = k.sb([64, 8, 64], BF16, "KTt")
    gk16, b_gk16 = k.sb([16, 512], BF16, "gk16")
    wgk2f, b_wgk2f = k.sb([16, 64], F32, "wgk2f")
    wgk2b, b_wgk2b = k.sb([16, 64], BF16, "wgk2b")
    nbg, b_nbg = k.sb([64, 1], F32, "nbg")
    glag, b_glag = k.sb([64, 2], F32, "glag")
    sbog, b_sbog = k.sb([64, 2, 512], BF16, "sbog")
    st32, b_st32 = k.sb([64, 128], F32, "st32")
    stall, b_stall = k.sb([64, 9, 128], BF16, "stall")
    stmp, b_stmp = k.sb([64, 128], F32, "stmp")
    o32, b_o32 = t1, b_t1
    osq, b_osq = k.sb([64, 512], BF16, "osq")
    on32, b_on32 = t2, b_t2
    B = [k.ps([128, 512], "bank%d" % i) for i in range(8)]

    stg = xt
    k.dma(id_f[:], id_d[:, :], (), [b_idf])
    k.cp(id_b[:], id_f[:], [b_idf], [b_idb])
    k.memset(ones_b[:], 1.0, [b_onesb])
    k.memset(ones_f[:], 1.0, [b_onesf])
    k.memset(bq[:], 0.0, [b_bq])
    k.memset(st32[:], 0.0, [b_st32])
    k.memset(stall[:], 0.0, [b_stall])
    k.dma(gn[:], gn_d[:, :], (), [b_gn])
    k.dma(glag[:], glag_d[:, :], (), [b_glag])
    k.dma(hm2[:], hm2_d[:, :], (), [b_hm2])
    k.dma(hm[:], hm_d[:, :], (), [b_hm])
    k.dma(rmk[:], rm_d[:, :], (), [b_rmk])
    k.dma(wgk2f[:], wgk2_d[:, :], (), [b_wgk2f])
    k.cp(wgk2b[:], wgk2f[:], [b_wgk2f], [b_wgk2b])
    k.dma(nbg[:], bgk2_d[:, :], (), [b_nbg])
    k.ts(nbg[:], nbg[:], -1.0, ALU.mult, [b_nbg], [b_nbg])
    k.dma(t1[:], am_d[:, :], (), [b_t1])
    k.cp(amk[:], t1[:], [b_t1], [b_amk])
    sflat = stg[:].rearrange("p a b -> p (a b)")
    k.dma(sflat[:, 0:2048], cm_d[:, :], (), [b_xt])
    k.cp(cm[:].rearrange("p a b -> p (a b)"), sflat[:, 0:2048], [b_xt], [b_cm])
    w1v = w1_d.rearrange("(kc p) f -> p kc f", p=128)
    for half in range(2):
        sv = sflat[:, 0:4 * 784].rearrange("p (a b) -> p a b", b=784)
        k.dma(sv, w1v[:, half * 4:(half + 1) * 4, :], (), [b_xt])
        k.cp(wb[:, half * 4:(half + 1) * 4, :], sv, [b_xt], [b_wb], eng="dve" if half == 0 else "pool")
    for pc in range(S // 2048):
        k.dma(sflat[64:128, 0:2048], e_d[:, pc * 2048:(pc + 1) * 2048], (), [b_xt])
        for h in range(2):
            k.cp(Kaug[h][0][64:128, pc * 2048:(pc + 1) * 2048], sflat[64:128, 0:2048], [b_xt], [b_kE[h]],
                 eng="dve" if h == 0 else "pool")
    for h in range(2):
        k.memset(Vaug[h][0][:, :, 64:65], 1.0, [b_vones[h]])
        k.memset(kmT[h][0][:], 0.0, [kmT[h][1]])

    xTv = xT.rearrange("(kc p) s -> p kc s", p=128)
    def proj(bank, M, col0, ncols=None):
        pt, pb = bank
        for kc in range(8):
            k.mm(pt[0:M, 0:512], wb[:, kc, col0:col0 + M], hn[:, kc, :], kc == 0, kc == 7, [b_wb, b_hn], [pb])
        return pt, pb

    FB0, FB1, FB2 = B[0], B[4], B[7]

    def gen_F(g):
        c0 = g * 512
        par = g % 2
        otile, b_ot = otiles[par]
        k.dma(xt[:, 0:4, :], xTv[:, 0:4, c0:c0 + 512], (), [b_xt])
        k.dma(xt[:, 4:8, :], xTv[:, 4:8, c0:c0 + 512], (), [b_xt])
        k.dma(cosT[:], cos_d[:, c0:c0 + 512], (), [b_cos])
        k.dma(sinT[:], sin_d[:, c0:c0 + 512], (), [b_sin])
        k.act(sq[:], xt[:], AF.Square, [b_xt], [b_sq])
        pt, pb = FB0
        for kc in range(8):
            k.mm(pt[:, :], ones_b[:], sq[:, kc, :], kc == 0, kc == 7, [b_onesb, b_sq], [pb])
        yield
        k.act(lnv[:], pt[:, :], AF.Ln, [pb], [b_lnv], bias=EPS, scale=1.0 / 1024)
        k.act(rstd[:], lnv[:], AF.Exp, [b_lnv], [b_rstd], scale=-0.5)
        for kc in range(8):
            k.stt(hn[:, kc, :], xt[:, kc, :], gn[:, kc:kc + 1], rstd[:], ALU.mult, ALU.mult, [b_xt, b_gn, b_rstd], [b_hn])
            if kc % 4 == 3:
                yield
        for idx in range(4):
            h = idx % 2
            isk = idx >= 2
            pt, pb = proj((FB1, FB2)[idx % 2], 64, idx * 64)
            yield
            if isk:
                dest = Kaug[h][0][0:64, c0:c0 + 512]
                dbuf = KB[h][g]
            else:
                dest = Qaug[h][par][0][0:64, :]
                dbuf = Qaug[h][par][1]
            k.tt(t1[:], pt[0:64, 0:512], cosT[:], ALU.mult, [pb, b_cos], [b_t1])
            k.tt(t2[0:32, :], pt[32:64, 0:512], sinT[32:64, :], ALU.mult, [pb, b_sin], [b_t2])
            k.tt(t2[32:64, :], pt[0:32, 0:512], sinT[0:32, :], ALU.mult, [pb, b_sin], [b_t2])
            k.tt(dest, t1[:], t2[:], ALU.add, [b_t1, b_t2], [dbuf])
            yield
        pt, pb = FB0
        for st in range(4):
            for kc in range(8):
                k.mm(pt[:, st * 128:(st + 1) * 128], hn[:, kc, st * 128:(st + 1) * 128], wb[:, kc, 256:384],
                     kc == 0, kc == 7, [b_hn, b_wb], [pb])
            yield
        pv_ = pt[:, 0:512].rearrange("p (a b) -> p a b", b=128)
        for h in range(2):
            k.cp(Vaug[h][0][:, 4 * g:4 * g + 4, 0:64], pv_[:, :, h * 64:(h + 1) * 64], [pb], [VB[h][g]], eng="act")
        pt, pb = proj(FB1, 128, 384)
        k.cp(QG32[:], pt[0:64, 0:512], [pb], [b_qg], eng="act")
        k.cp(KG32[:], pt[64:128, 0:512], [pb], [b_kg], eng="act")
        yield
        for c in range(8):
            pt, pb = FB2 if c < 4 else FB0
            for kc in range(8):
                k.mm(pt[0:64, (c % 4) * 128:(c % 4 + 1) * 128], hn[:, kc, c * 64:(c + 1) * 64], wb[:, kc, 512:640],
                     kc == 0, kc == 7, [b_hn, b_wb], [pb])
            if c % 2 == 1:
                yield
        for hf in range(2):
            pt, pb = FB2 if hf == 0 else FB0
            k.cp(gvt[:, hf * 4:(hf + 1) * 4, :].rearrange("p a b -> p (a b)"), pt[0:64, 0:512], [pb], [b_gvt], eng="act")
        pt, pb = proj(FB1, 16, 640)
        k.cp(gk16[:], pt[0:16, 0:512], [pb], [b_gk16], eng="act")
        k.mm(pt[0:64, 0:512], wgk2b[:], gk16[:], True, True, [b_wgk2b, b_gk16], [pb])
        k.act(spl[:], pt[0:64, 0:512], AF.Exp, [pb, b_nbg], [b_spl], bias=nbg[:, 0:1], scale=-1.0)
        k.act(spl[:], spl[:], AF.Ln, [b_spl], [b_spl], bias=1.0, scale=1.0)
        yield
        pt, pb = proj(FB2, 128, 656)
        k.act(sbog[:, 0, :], pt[0:64, 0:512], AF.Silu, [pb], [b_sbog])
        k.act(sbog[:, 1, :], pt[64:128, 0:512], AF.Silu, [pb], [b_sbog])
        yield
        k.scan(bpos[:], rmk[:], spl[:], 0.0, [b_rmk, b_spl], [b_bpos])
        k.act(eb[:], bpos[:], AF.Exp, [b_bpos], [b_eb], scale=-1.0 / 16)
        k.act(enb[:], bpos[:], AF.Exp, [b_bpos], [b_enb], scale=1.0 / 16)
        blast = bpos[:].rearrange("p (c t) -> p c t", t=64)[:, :, 63:64].rearrange("p c o -> p (c o)")
        k.act(Ac[:], blast, AF.Exp, [b_bpos], [b_ac], scale=-1.0 / 16)
        yield
        k.stt(qt[:], QG32[:], 32.0 ** -0.5, eb[:], ALU.mult, ALU.mult, [b_qg, b_eb], [b_qt])
        k.tt(ke32[:], KG32[:], enb[:], ALU.mult, [b_kg, b_enb], [b_ke])
        for h in range(2):
            k.ts(kpad[:, h, :], ke32[:], hm[:, h:h + 1], ALU.mult, [b_ke, b_hm], [b_kpad])
        yield
        for c in range(8):
            k.act(khat[:, c * 64:(c + 1) * 64], ke32[:, c * 64:(c + 1) * 64], AF.Copy, [b_ke, b_ac], [b_khat], scale=Ac[:, c:c + 1])
        yield
        pt, pb = FB0
        for c in range(8):
            k.mm(pt[0:64, c * 64:(c + 1) * 64], khat[:, c * 64:(c + 1) * 64], id_b[0:64, 0:64], True, True, [b_khat, b_idb], [pb])
        k.cp(KTt[:].rearrange("p a b -> p (a b)"), pt[0:64, 0:512], [pb], [b_ktt], eng="act")
        yield
        for h in range(2):
            pt, pb = (FB1, FB2)[h]
            for c in range(8):
                k.mm(pt[0:64, c * 64:(c + 1) * 64], kpad[:, h, c * 64:(c + 1) * 64], qt[:, c * 64:(c + 1) * 64], True, True,
                     [b_kpad, b_qt], [pb])
            k.tt(attm[:, h, :], pt[0:64, 0:512], amk[:], ALU.mult, [pb, b_amk], [b_attm])
            yield
        pso = [FB2, FB0]
        for c in range(8):
            psd, b_psd = FB0 if c < 4 else FB1
            for h in range(2):
                k.mm(psd[0:64, (c % 4) * 128 + h * 64:(c % 4) * 128 + (h + 1) * 64], KTt[:, c, :], gvt[:, c, h * 64:(h + 1) * 64],
                     True, True, [b_ktt, b_gvt], [b_psd])
            if c % 4 == 3:
                yield
        k.cp(stall[:, 0, :], stall[:, 8, :], [b_stall], [b_stall])
        for c in range(8):
            psd, b_psd = FB0 if c < 4 else FB1
            k.tt(stmp[:], psd[0:64, (c % 4) * 128:(c % 4 + 1) * 128], hm2[:], ALU.mult, [b_psd, b_hm2], [b_stmp])
            k.stt(st32[:], st32[:], Ac[:, c:c + 1], stmp[:], ALU.mult, ALU.add, [b_st32, b_ac, b_stmp], [b_st32])
            k.cp(stall[:, c + 1, :], st32[:], [b_st32], [b_stall])
            if c % 2 == 1:
                yield
        for c in range(8):
            for h in range(2):
                po, pbo = pso[h]
                k.mm(po[0:64, c * 64:(c + 1) * 64], gvt[:, c, h * 64:(h + 1) * 64], attm[:, h, c * 64:(c + 1) * 64], True, False,
                     [b_gvt, b_attm], [pbo])
                k.mm(po[0:64, c * 64:(c + 1) * 64], stall[:, c, h * 64:(h + 1) * 64], qt[:, c * 64:(c + 1) * 64], False, True,
                     [b_stall, b_qt], [pbo])
            if c % 2 == 1:
                yield
        for h in range(2):
            po, pbo = pso[h]
            k.cp(o32[:], po[0:64, 0:512], [pbo], [b_o32], eng="act")
            k.act(osq[:], po[0:64, 0:512], AF.Square, [pbo], [b_osq])
            pt, pb = FB1
            k.mm(pt[0:64, 0:512], ones_b[0:64, 0:64], osq[:], True, True, [b_onesb, b_osq], [pb])
            yield
            k.act(lnv[0:64, :], pt[0:64, 0:512], AF.Ln, [pb], [b_lnv], bias=EPS, scale=1.0 / 64)
            k.act(lnv[0:64, :], lnv[0:64, :], AF.Exp, [b_lnv], [b_lnv], scale=-0.5)
            k.tt(on32[:], o32[:], lnv[0:64, :], ALU.mult, [b_o32, b_lnv], [b_on32])
            k.stt(otile[:, 2 + h, :], on32[:], glag[:, h:h + 1], sbog[:, h, :], ALU.mult, ALU.mult, [b_on32, b_glag, b_sbog], [b_ot])
            yield
        for h in range(2):
            KA, _ = Kaug[h]
            QA, b_QA = Qaug[h][par]
            kmt, b_kmt = kmT[h]
            k.P.op("dve", (lambda o=km32[:], i=KA[0:64, c0:c0 + 512].rearrange("p (a b) -> p a b", b=256):
                           nc.vector.tensor_reduce(out=o, in_=i, axis=AX.X, op=ALU.add)), [KB[h][g]], [b_km32])
            k.cp(kmt[:, 2 * g:2 * g + 2], km32[:], [b_km32], [b_kmt])
            pg, b_pg = FB1
            for st in range(4):
                k.mm(pg[:, st * 64:(st + 1) * 64], QA[0:64, st * 128:(st + 1) * 128], kmt[:, :], True, True, [b_QA, b_kmt], [b_pg])
            yield
            k.memset(gsb[:], -1e30, [b_gsb], eng="dve")
            for st in range(4):
                blk = 2 * g + st // 2
                if blk > 0:
                    k.cp(gsb[:, st, 0:blk], pg[:, st * 64:st * 64 + blk], [b_pg], [b_gsb])
            yield
            for st in range(4):
                blk = 2 * g + st // 2
                k.P.op("dve", (lambda o=m8[:, st, :], i=gsb[:, st, :]: nc.vector.max(out=o, in_=i)), [b_gsb], [b_m8])
                k.ts(bq[:, st, 64:128], gsb[:, st, :], m8[:, st, 2:3], ALU.is_ge, [b_gsb, b_m8], [b_bq], s2=-NEG, op1=ALU.mult)
            yield
            k.ts(bq[:, :, 64:128], bq[:, :, 64:128], NEG, ALU.add, [b_bq], [b_bq])
            for st in range(4):
                blk = 2 * g + st // 2
                k.memset(bq[:, st, 64 + blk:65 + blk], 0.0, [b_bq], eng="dve")
            pg2, b_pg2 = FB2
            for st in range(4):
                k.tr(pg2[:, st * 128:(st + 1) * 128], bq[:, st, :], id_f[:], [b_bq, b_idf], [b_pg2])
            k.cp(QA[64:128, :], pg2[64:128, 0:512], [b_pg2], [b_QA], eng="act")
            yield

    def gen_A(g):
        par = g % 2
        otile, b_ot = otiles[par]
        for h in range(2):
            KA, _ = Kaug[h]
            VA, _ = Vaug[h]
            QA, b_QA = Qaug[h][par]
            pO, b_pO = B[5 + h]
            nkt = 4 * g + 4
            LA = 2

            def qk(kt):
                j = kt - 4 * g
                q0 = 256 if j >= 2 else 0
                pS, b_pS = B[1 + (kt % 3)]
                k.mm(pS[:, q0:512], KA[:, kt * 128:(kt + 1) * 128], QA[:, q0:512], True, j < 0,
                     [KB[h][kt // 4], b_kE[h], b_QA], [b_pS])
                if j >= 0:
                    k.mm(pS[:, q0:512], id_b[:], cm[:, j, q0:512], False, True, [b_idb, b_cm], [b_pS])

            def pv(kt):
                j = kt - 4 * g
                q0 = 256 if j >= 2 else 0
                pS, b_pS = B[1 + (kt % 3)]
                pTt, b_pT = pTs[kt % 4]
                k.act(pTt[:, q0:512], pS[:, q0:512], AF.Exp, [b_pS], [b_pT], scale=0.125)
                k.mm(pO[0:65, q0:512], VA[:, kt, 0:65], pTt[:, q0:512], kt == 0, kt == nkt - 1,
                     [VB[h][kt // 4], b_vones[h], b_pT], [b_pO])

            for i in range(nkt + LA):
                if i < nkt:
                    qk(i)
                if i >= LA:
                    pv(i - LA)
                yield
            k.P.op("dve", (lambda o=rden[64:65, :], i=pO[64:65, 0:512]: nc.vector.reciprocal(out=o, in_=i)), [b_pO], [b_rden])
            pt, pb = B[1]
            k.mm(pt[0:64, 0:512], ones_f[64:65, 0:64], rden[64:65, :], True, True, [b_onesf, b_rden], [pb])
            k.cp(osb[:], pO[0:64, 0:512], [b_pO], [b_osb], eng="act")
            k.tt(otile[:, h, :], osb[:], pt[0:64, 0:512], ALU.mult, [b_osb, pb], [b_ot])
            yield
        oT(g, otile, b_ot)

    def drive(a, f=None, ratio=1):
        n = 0
        a_live, f_live = a is not None, f is not None
        while a_live or f_live:
            if a_live:
                try:
                    next(a)
                except StopIteration:
                    a_live = False
                n += 1
            if f_live and (not a_live or n % ratio == 0):
                try:
                    next(f)
                except StopIteration:
                    f_live = False

    NF_STEPS = 48
    drive(None, gen_F(0))
    for g in range(NT):
        na = 2 * (4 * g + 4 + 3)
        drive(gen_A(g), gen_F(g + 1) if g + 1 < NT else None, ratio=max(1, na // NF_STEPS))


def l1_consts(S):
    half = 32
    inv = (10000.0 ** (-np.arange(half, dtype=np.float32) / half)).astype(np.float32)
    ang = np.arange(S, dtype=np.float32)[None, :] * inv[:, None]
    cos = np.cos(ang).astype(np.float32)
    sin = np.sin(ang).astype(np.float32)
    cosd = np.concatenate([cos, cos], 0)
    sind = np.concatenate([sin, -sin], 0)
    kk = np.arange(128)[:, None]
    qq = np.arange(512)[None, :]
    cm = np.concatenate([np.where(qq < j * 128 + kk, NEG, 0.0) for j in range(4)], 1).astype(np.float32)
    ed = (np.arange(S)[None, :] // 256 == np.arange(64)[:, None]).astype(np.float32)
    ident = np.eye(128, dtype=np.float32)
    rm = np.tile((np.arange(512) % 64 != 0).astype(np.float32)[None, :], (64, 1))
    s_ = np.arange(64)[:, None]
    t_ = np.arange(64)[None, :]
    am = np.tile((s_ <= t_).astype(np.float32), (1, 8))
    hm = np.zeros((64, 2), np.float32)
    hm[0:32, 0] = 1
    hm[32:64, 1] = 1
    hm2 = np.repeat(hm, 64, axis=1)
    return dict(cosd=cosd, sind=sind, cmd=cm, ed=ed, identd=ident, rmd=rm, amd=am, hm2d=hm2, hmd=hm)


HALO = 8
OCS = 2048


def choose_tiles(W):
    if W == 4104:
        return 4, 3, 342
    if W == 520:
        return 2, 1, 260
    raise ValueError(W)


class Trunk:
    def __init__(self, k, W):
        self.k = k
        self.nc = k.nc
        self.W = W
        self.NB, self.NTB, self.TW = choose_tiles(W)
        self.TB = self.NTB * self.TW
        TB, TW = self.TB, self.TW
        self.h, self.b_h = k.sb([128, 8, TB], F32, "h")
        self.hn, self.b_hn = k.sb([128, 8, TB], BF16, "hn")
        self.act, self.b_act = k.sb([128, 24, TB], BF16, "act")
        self.wst = [k.sb([128, 8, 128], F32, "wst%d" % i) for i in range(3)]
        self.wbf = [k.sb([128, 24, 128], BF16, "wbf%d" % i) for i in range(3)]
        self.wi = 0
        self.si = 0
        self.wq = []
        self.todo = []
        self.pc = [k.sb([128, 4 + TB], F32, "pc%d" % i) for i in range(2)]
        self.pci = 0
        self.yt = [k.sb([128, TB], F32, "yt%d" % i) for i in range(2)]
        self.gel, self.b_gel = k.sb([128, TB], F32, "gel")
        self.sqt, self.b_sqt = k.sb([128, 8, TW], BF16, "sqt")
        self.lnv, self.b_lnv = k.sb([128, TW], F32, "lnvt")
        self.rstd, self.b_rstd = k.sb([128, TW], F32, "rstdt")
        self.ones_b, self.b_ones = k.sb([128, 128], BF16, "onesb")
        self.m, self.b_m = k.sb([128, 1], F32, "hmask")
        self.carry, self.b_carry = k.sb([128, 48, 2], F32, "carry")
        self.B = [k.ps([128, 512], "bank%d" % i) for i in range(8)]
        self.bi = 0
        k.memset(self.ones_b[:], 1.0, [self.b_ones])
        k.memset(self.carry[:], 0.0, [self.b_carry])

    def bank(self):
        b = self.B[self.bi]
        self.bi = (self.bi + 1) % 8
        return b

    def small(self, name, dram_ap, shape):
        t, b = self.k.sb(shape, F32, name)
        self.k.dma(t[:], dram_ap, (), [b])
        return t, b

    def request(self, wd, KC, col0):
        k = self.k
        wb, b_wb = self.wbf[self.wi]
        self.wi = (self.wi + 1) % 3
        wv = wd.rearrange("(kc p) f -> p kc f", p=128)
        for k0 in range(0, KC, 8):
            k1 = min(KC, k0 + 8)
            st, b_st = self.wst[self.si]
            self.si = (self.si + 1) % 3
            k.dma(st[:, 0:k1 - k0, :], wv[:, k0:k1, col0:col0 + 128], (), [b_st])
            k.cp(wb[:, k0:k1, :], st[:, 0:k1 - k0, :], [b_st], [b_wb], eng="act")
        self.wq.append((wb, b_wb))

    def plan(self, reqs):
        assert not self.wq and not getattr(self, "todo", None), "previous plan not fully consumed"
        self.todo = list(reqs)
        for _ in range(2):
            if self.todo:
                self.request(*self.todo.pop(0))

    def run_stage(self, reqs, body):
        for i in range(len(reqs)):
            body(i)

    def rmsnorm(self, g_t, b_g, out_f32=None):
        k = self.k
        TW = self.TW
        for nt in range(self.NTB):
            cs = slice(nt * TW, (nt + 1) * TW)
            k.act(self.sqt[:], self.h[:, :, cs], AF.Square, [self.b_h], [self.b_sqt])
            pt, pb = self.bank()
            for kc in range(8):
                k.mm(pt[:, 0:TW], self.ones_b[:], self.sqt[:, kc, :], kc == 0, kc == 7, [self.b_ones, self.b_sqt], [pb])
            k.act(self.lnv[:], pt[:, 0:TW], AF.Ln, [pb], [self.b_lnv], bias=EPS, scale=1.0 / 1024)
            k.act(self.rstd[:], self.lnv[:], AF.Exp, [self.b_lnv], [self.b_rstd], scale=-0.5)
            for kc in range(8):
                if out_f32 is None:
                    k.stt(self.hn[:, kc, cs], self.h[:, kc, cs], g_t[:, kc:kc + 1], self.rstd[:], ALU.mult, ALU.mult,
                          [self.b_h, b_g, self.b_rstd], [self.b_hn])
                else:
                    k.stt(out_f32[0][:, kc, cs], self.h[:, kc, cs], g_t[:, kc:kc + 1], self.rstd[:], ALU.mult, ALU.mult,
                          [self.b_h, b_g, self.b_rstd], [out_f32[1]])

    def linear(self, src, b_src, KC, wd, col0, evac):
        k = self.k
        TW = self.TW
        if self.todo:
            self.request(*self.todo.pop(0))
        wb, b_wb = self.wq.pop(0)
        for nt in range(self.NTB):
            cs = slice(nt * TW, (nt + 1) * TW)
            pt, pb = self.bank()
            for kc in range(KC):
                k.mm(pt[:, 0:TW], wb[:, kc, :], src[:, kc, cs], kc == 0, kc == KC - 1, [b_wb, b_src], [pb])
            evac(nt, cs, pt[:, 0:TW], pb)

    @staticmethod
    def ffn_reqs(w_up, w_down):
        return ([(w_up, 8, (part * 24 + j) * 128) for j in range(24) for part in range(2)]
                + [(w_down, 24, fc * 128) for fc in range(8)])

    def conv(self, pc, b_pc, ntap, cw_t, b_cw, cb_t, b_cb, idx, yt, b_yt):
        k = self.k
        TB = self.TB
        last = ntap - 1
        k.act(yt[:], pc[:, last:last + TB], AF.Identity, [b_pc, b_cw, b_cb], [b_yt],
              bias=cb_t[:, idx:idx + 1], scale=cw_t[:, idx, last:last + 1])
        for i in range(last - 1, -1, -1):
            k.stt(yt[:], pc[:, i:i + TB], cw_t[:, idx, i:i + 1], yt[:], ALU.mult, ALU.add, [b_pc, b_cw, b_yt], [b_yt])

    def ffn(self, first, gn_t, b_gn, w_up, cw_t, b_cw, cb_t, b_cb, w_down, mask_halo):
        k = self.k
        TB, TW = self.TB, self.TW
        self.rmsnorm(gn_t, b_gn)
        ys = [None, None]

        def up_body(i):
            j, part = i // 2, i % 2
            idx = part * 24 + j
            pc, b_pc = self.pc[self.pci]
            self.pci = (self.pci + 1) % 2
            yt, b_yt = self.yt[part]
            k.cp(pc[:, 0:2], self.carry[:, idx, :], [self.b_carry], [b_pc], eng="act")

            def evac(nt, cs, ps, pb, pc=pc, b_pc=b_pc):
                k.cp(pc[:, 2 + cs.start:2 + cs.stop], ps, [pb], [b_pc], eng="act")

            self.linear(self.hn, self.b_hn, 8, w_up, idx * 128, evac)
            if first and mask_halo:
                k.ts(pc[:, 2:2 + HALO], pc[:, 2:2 + HALO], self.m[:, 0:1], ALU.mult, [b_pc, self.b_m], [b_pc])
            k.cp(self.carry[:, idx, :], pc[:, TB:TB + 2], [b_pc], [self.b_carry], eng="act")
            self.conv(pc, b_pc, 3, cw_t, b_cw, cb_t, b_cb, idx, yt, b_yt)
            ys[part] = (yt, b_yt)
            if part == 1:
                (yu, b_yu), (yg, b_yg) = ys
                k.act(self.gel[:], yg[:], AF.Gelu_apprx_tanh, [b_yg], [self.b_gel])
                k.tt(self.act[:, j, :], yu[:], self.gel[:], ALU.mult, [b_yu, self.b_gel], [self.b_act])

        self.run_stage([(w_up, 8, (part * 24 + j) * 128) for j in range(24) for part in range(2)], up_body)

        def down_body(fc):
            def evac(nt, cs, ps, pb, fc=fc):
                k.tt(self.h[:, fc, cs], self.h[:, fc, cs], ps, ALU.add, [self.b_h, pb], [self.b_h])

            self.linear(self.act, self.b_act, 24, w_down, fc * 128, evac)

        self.run_stage([(w_down, 24, fc * 128) for fc in range(8)], down_body)


def l2_io(k, W):
    d = {}
    for name, shape in (("xTw", [1024, W]), ("m", [128, 1]), ("sel", [128, 4]), ("w_out0", [1024, 1024]), ("fg0", [128, 8]),
                        ("w_up0", [1024, 6144]), ("cw0", [128, 48, 3]), ("cb0", [128, 48]), ("w_down0", [3072, 1024]),
                        ("mg", [128, 8]), ("w_in", [1024, 2048]), ("rcw", [128, 8, 4]), ("rcb", [128, 8]), ("wa", [1024, 256]),
                        ("ba", [128, 8]), ("wx", [1024, 256]), ("bx", [128, 8]), ("lam", [128, 8])):
        d[name] = k.din(name, shape)
    return d


def emit_l2(k, W, S, d, og, h1_o, gg_o, hl_o, pl_o, ex_o):
    nc = k.nc
    T = Trunk(k, W)
    NB, TB, TW = T.NB, T.TB, T.TW
    TOK = W - HALO
    xTw, m_d, sel_d, w_out, fg_d, w_up, cw_d, cb_d, w_down = (d["xTw"], d["m"], d["sel"], d["w_out0"], d["fg0"], d["w_up0"],
                                                               d["cw0"], d["cb0"], d["w_down0"])
    mg_d, w_in, rcw_d, rcb_d, wa_d, ba_d, wx_d, bx_d, lam_d = (d["mg"], d["w_in"], d["rcw"], d["rcb"], d["wa"], d["ba"],
                                                               d["wx"], d["bx"], d["lam"])
    sel, b_sel = T.small("sel", sel_d[:, :], [128, 4])

    k.dma(T.m[:], m_d[:, :], (), [T.b_m])
    fg, b_fg = T.small("fg", fg_d[:, :], [128, 8])
    cw, b_cw = T.small("cw", cw_d[:, :, :], [128, 48, 3])
    cb, b_cb = T.small("cb", cb_d[:, :], [128, 48])
    mg, b_mg = T.small("mg", mg_d[:, :], [128, 8])
    rcw, b_rcw = T.small("rcw", rcw_d[:, :, :], [128, 8, 4])
    rcb, b_rcb = T.small("rcb", rcb_d[:, :], [128, 8])
    ba, b_ba = T.small("ba", ba_d[:, :], [128, 8])
    bx, b_bx = T.small("bx", bx_d[:, :], [128, 8])
    lam, b_lam = T.small("lam", lam_d[:, :], [128, 8])
    c1, b_c1 = k.sb([128, 8], F32, "c1")
    c2, b_c2 = k.sb([128, 8], F32, "c2")
    k.act(c1[:], lam[:], AF.Exp, [b_lam], [b_c1], scale=-1.0)
    k.act(c1[:], c1[:], AF.Ln, [b_c1], [b_c1], bias=1.0, scale=1.0)
    k.ts(c2[:], c1[:], -16.0, ALU.mult, [b_c1], [b_c2])
    k.ts(c1[:], c1[:], -8.0, ALU.mult, [b_c1, b_c2], [b_c1])
    ob, b_ob = T.hn, T.b_hn
    gg, b_gg = k.sb([128, TB], BF16, "gg")
    rcar, b_rcar = k.sb([128, 8, 3], F32, "rcar")
    hcar, b_hcar = k.sb([128, 8], F32, "hcar")
    pcar, b_pcar = k.sb([128, 8], F32, "pcar")
    zer, b_zer = k.sb([128, TB], F32, "zer")
    xrc, b_xrc = k.sb([128, 2, TB], F32, "xrc")
    xrb, b_xrb = k.sb([128, 2, TB], BF16, "xrb")
    rr, b_rr = k.sb([128, TB], F32, "rr")
    ii, b_ii = k.sb([128, TB], F32, "ii")
    aa, b_aa = k.sb([128, TB], F32, "aa")
    uu, b_uu = k.sb([128, TB], F32, "uu")
    hl, b_hl = k.sb([128, TB], F32, "hl")
    pl, b_pl = k.sb([128, TB], F32, "pl")
    k.memset(rcar[:], 0.0, [b_rcar])
    k.memset(zer[:], 0.0, [b_zer])
    ext, b_ext = k.sb([128, 8, 2], F32, "ext")

    oTv = [o_.rearrange("(kc p) s -> p kc s", p=128) for o_ in og]
    xTv = xTw.rearrange("(kc p) s -> p kc s", p=128)
    h1v = h1_o.rearrange("(kc p) s -> p kc s", p=128)
    ggv = gg_o.rearrange("(kc p) s -> p kc s", p=128)
    hlv = hl_o.rearrange("(kc p) s -> p kc s", p=128)
    plv = pl_o.rearrange("(kc p) s -> p kc s", p=128)
    exv = ex_o.rearrange("(kc p) s -> p kc s", p=128)

    for blk in range(NB):
        first = blk == 0
        g0 = blk * TB
        for c in range(4):
            cand = T.act[:, 8 * (c % 3):8 * (c % 3) + 8, :]
            lo = c * TOK - HALO + g0
            skip = max(0, -lo)
            if skip:
                k.memset(cand[:, :, 0:skip], 0.0, [T.b_act])
            a = lo + skip
            while a < lo + TB:
                j = a // OCS
                e = min(lo + TB, (j + 1) * OCS)
                k.dma(cand[:, :, a - lo:e - lo], oTv[j][:, :, a - j * OCS:e - j * OCS], (), [T.b_act])
                a = e
            if c == 0:
                k.ts(ob[:], cand, sel[:, 0:1], ALU.mult, [T.b_act, b_sel], [b_ob])
            else:
                k.stt(ob[:], cand, sel[:, c:c + 1], ob[:], ALU.mult, ALU.add, [T.b_act, b_sel, b_ob], [b_ob])
        k.dma(T.h[:, 0:4, :], xTv[:, 0:4, g0:g0 + TB], (), [T.b_h])
        k.dma(T.h[:, 4:8, :], xTv[:, 4:8, g0:g0 + TB], (), [T.b_h])
        rg_reqs = []
        for n in range(4):
            rg_reqs += [(w_in, 8, 1024 + (2 * n + c2i) * 128) for c2i in range(2)]
            for c2i in range(2):
                rg_reqs += [(wa_d[n * 256:(n + 1) * 256, :], 2, c2i * 128), (wx_d[n * 256:(n + 1) * 256, :], 2, c2i * 128)]
        T.plan([(w_out, 8, fc * 128) for fc in range(8)] + T.ffn_reqs(w_up, w_down)
               + [(w_in, 8, fc * 128) for fc in range(8)] + rg_reqs)

        def wo_body(fc):
            def evac(nt, cs, ps, pb, fc=fc):
                k.tt(T.h[:, fc, cs], T.h[:, fc, cs], ps, ALU.add, [T.b_h, pb], [T.b_h])

            T.linear(ob, b_ob, 8, w_out, fc * 128, evac)

        T.run_stage([(w_out, 8, fc * 128) for fc in range(8)], wo_body)
        T.ffn(first, fg, b_fg, w_up, cw, b_cw, cb, b_cb, w_down, False)
        k.dma(h1v[:, :, g0:g0 + TB], T.h[:], [T.b_h], (), eng="pool")
        T.rmsnorm(mg, b_mg)
        def gb_body(fc):
            def evac(nt, cs, ps, pb, fc=fc):
                k.act(gg[:, cs], ps, AF.Gelu_apprx_tanh, [pb], [b_gg])

            T.linear(T.hn, T.b_hn, 8, w_in, fc * 128, evac)
            k.dma(ggv[:, fc, g0:g0 + TB], gg[:], [b_gg], (), eng="pool")

        T.run_stage([(w_in, 8, fc * 128) for fc in range(8)], gb_body)
        reqs = []
        for n in range(4):
            reqs += [(w_in, 8, 1024 + (2 * n + c2i) * 128) for c2i in range(2)]
            for c2i in range(2):
                reqs += [(wa_d[n * 256:(n + 1) * 256, :], 2, c2i * 128), (wx_d[n * 256:(n + 1) * 256, :], 2, c2i * 128)]

        def rg_body(i, blk=blk, first=first, g0=g0):
            n, r = i // 6, i % 6
            if r < 2:
                c2i = r
                c8 = 2 * n + c2i
                pc, b_pc = T.pc[T.pci]
                T.pci = (T.pci + 1) % 2
                k.cp(pc[:, 0:3], rcar[:, c8, :], [b_rcar], [b_pc], eng="act")

                def evac(nt, cs, ps, pb, pc=pc, b_pc=b_pc):
                    k.cp(pc[:, 3 + cs.start:3 + cs.stop], ps, [pb], [b_pc], eng="act")

                T.linear(T.hn, T.b_hn, 8, w_in, 1024 + c8 * 128, evac)
                if first:
                    k.ts(pc[:, 3:3 + HALO], pc[:, 3:3 + HALO], T.m[:, 0:1], ALU.mult, [b_pc, T.b_m], [b_pc])
                k.cp(rcar[:, c8, :], pc[:, TB:TB + 3], [b_pc], [b_rcar], eng="act")
                yt, b_yt = T.yt[c2i]
                T.conv(pc, b_pc, 4, rcw, b_rcw, rcb, b_rcb, c8, yt, b_yt)
                k.cp(xrc[:, c2i, :], yt[:], [b_yt], [b_xrc], eng="pool")
                k.cp(xrb[:, c2i, :], yt[:], [b_yt], [b_xrb], eng="act")
                return
            c2i, which = (r - 2) // 2, (r - 2) % 2
            fc = 2 * n + c2i
            wd, bias_t, b_bias, dst, b_dst = ((wa_d, ba, b_ba, rr, b_rr), (wx_d, bx, b_bx, ii, b_ii))[which]

            def evac(nt, cs, ps, pb, dst=dst, b_dst=b_dst, bias_t=bias_t, b_bias=b_bias, fc=fc):
                k.act(dst[:, cs], ps, AF.Sigmoid, [pb, b_bias], [b_dst], bias=bias_t[:, fc:fc + 1], scale=1.0)

            T.linear(xrb, b_xrb, 2, wd[n * 256:(n + 1) * 256, :], c2i * 128, evac)
            if which == 0:
                return
            k.act(aa[:], rr[:], AF.Exp, [b_rr, b_c1], [b_aa], scale=c1[:, fc:fc + 1])
            k.act(rr[:], rr[:], AF.Exp, [b_rr, b_c2], [b_rr], scale=c2[:, fc:fc + 1])
            k.act(rr[:], rr[:], AF.Sqrt, [b_rr], [b_rr], bias=1.0, scale=-1.0)
            k.tt(uu[:], xrc[:, c2i, :], ii[:], ALU.mult, [b_xrc, b_ii], [b_uu])
            k.tt(uu[:], uu[:], rr[:], ALU.mult, [b_uu, b_rr], [b_uu])
            if first:
                k.ts(uu[:, 0:HALO], uu[:, 0:HALO], T.m[:, 0:1], ALU.mult, [b_uu, T.b_m], [b_uu])
                k.memset(hl[:, 0:5], 0.0, [b_hl])
                k.memset(pl[:, 0:5], 0.0, [b_pl])
                s0, hi, pi = 5, 0.0, 1.0
            else:
                s0, hi, pi = 0, hcar[:, fc:fc + 1], pcar[:, fc:fc + 1]
            k.scan(hl[:, s0:TB], aa[:, s0:TB], uu[:, s0:TB], hi, [b_aa, b_uu, b_hcar], [b_hl])
            k.scan(pl[:, s0:TB], aa[:, s0:TB], zer[:, s0:TB], pi, [b_aa, b_zer, b_pcar], [b_pl])
            k.cp(hcar[:, fc:fc + 1], hl[:, TB - 1:TB], [b_hl], [b_hcar], eng="pool")
            k.cp(pcar[:, fc:fc + 1], pl[:, TB - 1:TB], [b_pl], [b_pcar], eng="pool")
            k.dma(hlv[:, fc, g0:g0 + TB], hl[:], [b_hl], (), eng="pool")
            k.dma(plv[:, fc, g0:g0 + TB], pl[:], [b_pl], (), eng="pool")
            if blk == NB - 1:
                k.cp(ext[:, fc, 0:1], hl[:, TB - 4:TB - 3], [b_hl], [b_ext], eng="pool")
                k.cp(ext[:, fc, 1:2], pl[:, TB - 4:TB - 3], [b_pl], [b_ext], eng="pool")

        T.run_stage(reqs, rg_body)
    k.dma(exv[:, :, :], ext[:], [b_ext], (), eng="pool")


def l3_io(k, W):
    d = {}
    for name, shape in (("srank", [128, 4]), ("oms", [128, 4]), ("w_out1", [1024, 1024]), ("fg1", [128, 8]),
                        ("w_up1", [1024, 6144]), ("cw1", [128, 48, 3]), ("cb1", [128, 48]), ("w_down1", [3072, 1024]),
                        ("fin", [128, 8])):
        d[name] = k.din(name, shape)
    return d


def emit_l3(k, W, d, m_d, h1w, ggw, hlw, plw, exg, out_o):
    nc = k.nc
    T = Trunk(k, W)
    NB, TB, TW = T.NB, T.TB, T.TW
    TOK = W - HALO
    w_out, fg_d, w_up, cw_d, cb_d, w_down, fin_d = (d["w_out1"], d["fg1"], d["w_up1"], d["cw1"], d["cb1"], d["w_down1"], d["fin"])
    k.dma(T.m[:], m_d[:, :], (), [T.b_m])
    fg, b_fg = T.small("fg", fg_d[:, :], [128, 8])
    cw, b_cw = T.small("cw", cw_d[:, :, :], [128, 48, 3])
    cb, b_cb = T.small("cb", cb_d[:, :], [128, 48])
    fin, b_fin = T.small("fin", fin_d[:, :], [128, 8])
    sr, b_sr = T.small("sr", d["srank"][:, :], [128, 4])
    oms, b_oms = T.small("oms", d["oms"][:, :], [128, 4])
    pe, b_pe = k.sb([128, 4, 8, 2], F32, "pe")
    exv = exg.rearrange("(r kc p) t -> p r kc t", p=128, kc=8)
    for r in range(4):
        k.dma(pe[:, r, :, :], exv[:, r, :, :], (), [b_pe])
    Hc, b_Hc = k.sb([128, 8], F32, "Hc")
    Pm, b_Pm = k.sb([128, 8], F32, "Pm")
    Em, b_Em = k.sb([128, 8], F32, "Em")
    k.memset(Hc[:], 0.0, [b_Hc])
    for r in range(4):
        k.ts(Pm[:], pe[:, r, :, 1], sr[:, r:r + 1], ALU.mult, [b_pe, b_sr], [b_Pm], s2=oms[:, r:r + 1], op1=ALU.add)
        k.ts(Em[:], pe[:, r, :, 0], sr[:, r:r + 1], ALU.mult, [b_pe, b_sr], [b_Em])
        k.tt(Hc[:], Hc[:], Pm[:], ALU.mult, [b_Hc, b_Pm], [b_Hc])
        k.tt(Hc[:], Hc[:], Em[:], ALU.add, [b_Hc, b_Em], [b_Hc])
    yb, b_yb = T.hn, T.b_hn
    ybufs = [(k.sb([128, TB], F32, "hl%d" % i), k.sb([128, TB], F32, "pl%d" % i), k.sb([128, TB], BF16, "ggt%d" % i)) for i in range(2)]
    outt, b_outt = T.h, T.b_h

    h1v = h1w.rearrange("(kc p) s -> p kc s", p=128)
    ggv = ggw.rearrange("(kc p) s -> p kc s", p=128)
    hlv = hlw.rearrange("(kc p) s -> p kc s", p=128)
    plv = plw.rearrange("(kc p) s -> p kc s", p=128)
    outv = out_o.rearrange("(kc p) s -> p kc s", p=128)

    for blk in range(NB):
        first = blk == 0
        g0 = blk * TB
        k.dma(T.h[:, 0:4, :], h1v[:, 0:4, g0:g0 + TB], (), [T.b_h])
        k.dma(T.h[:, 4:8, :], h1v[:, 4:8, g0:g0 + TB], (), [T.b_h])
        for fc in range(8):
            (hl, b_hl), (pl, b_pl), (ggt, b_ggt) = ybufs[fc % 2]
            k.dma(hl[:], hlv[:, fc, g0:g0 + TB], (), [b_hl])
            k.dma(pl[:], plv[:, fc, g0:g0 + TB], (), [b_pl])
            k.dma(ggt[:], ggv[:, fc, g0:g0 + TB], (), [b_ggt])
            k.stt(hl[:], pl[:], Hc[:, fc:fc + 1], hl[:], ALU.mult, ALU.add, [b_pl, b_Hc, b_hl], [b_hl])
            k.tt(yb[:, fc, :], hl[:], ggt[:], ALU.mult, [b_hl, b_ggt], [b_yb])
        T.plan([(w_out, 8, fc * 128) for fc in range(8)] + T.ffn_reqs(w_up, w_down))

        def wo_body(fc):
            def evac(nt, cs, ps, pb, fc=fc):
                k.tt(T.h[:, fc, cs], T.h[:, fc, cs], ps, ALU.add, [T.b_h, pb], [T.b_h])

            T.linear(yb, b_yb, 8, w_out, fc * 128, evac)

        T.run_stage([(w_out, 8, fc * 128) for fc in range(8)], wo_body)
        T.ffn(first, fg, b_fg, w_up, cw, b_cw, cb, b_cb, w_down, True)
        T.rmsnorm(fin, b_fin, out_f32=(outt, b_outt))
        lo = HALO if first else 0
        k.dma(outv[:, :, g0 + lo - HALO:g0 + TB - HALO], outt[:, :, lo:TB], [b_outt], (), eng="pool")


_F = {}


def build_fused(S):
    TOK = S // 4
    W = TOK + HALO
    nc = bass.Bass("TRN2", target_bir_lowering=False)
    k = K(nc)
    io1 = l1_io(k, S)
    io2 = l2_io(k, W)
    io3 = l3_io(k, W)
    NCH = max(1, S // OCS)
    osrc = [k.dint("osrc%d" % j, [256, min(S, OCS)], BF16) for j in range(NCH)]
    og = [k.dint("og%d" % j, [1024, min(S, OCS)], BF16) for j in range(NCH)]
    h1 = k.dint("h1s", [1024, W])
    gg = k.dint("ggs", [1024, W], BF16)
    hl = k.dint("hls", [1024, W])
    pl = k.dint("pls", [1024, W])
    exs = k.dint("exs", [1024, 2])
    exg = k.dint("exg", [4096, 2])
    out = k.dout("outT", [1024, TOK])
    groups = [[0, 1, 2, 3], [4, 5, 6, 7]]
    upto = int(os.environ.get("FUSE_UPTO", "3"))
    tile_bufs = {}

    def o_out(g, otile, b_ot):
        j, off = (g * 512) // OCS, (g * 512) % OCS
        ov = osrc[j].rearrange("(r p) s -> p r s", p=64)
        bb = Buf()
        tile_bufs.setdefault(j, []).append(bb)
        k.dma(ov[:, :, off:off + 512], otile[:], [b_ot], [bb], eng="pool")
        if off + 512 == min(S, OCS):
            k.P.dma((lambda j=j: nc.gpsimd.collective_compute("AllGather", ALU.bypass, replica_groups=groups,
                                                               ins=[osrc[j][:, :]], outs=[og[j][:, :]])),
                    tile_bufs[j], (), eng="pool", inc=1)

    emit_l1(k, S, io1, o_out)
    k.end_phase()
    emit_l2(k, W, S, io2, og, h1, gg, hl, pl, exs)
    k.end_phase()
    if upto == 2:
        return nc, k.P.finish()
    if os.environ.get("NOCC2"):
        k.dma(exg[0:1024, :], exs[:, :], (), ())
    else:
        k.P.dma(lambda: nc.gpsimd.collective_compute("AllGather", ALU.bypass, replica_groups=groups, ins=[exs[:, :]], outs=[exg[:, :]]),
                (), (), eng="pool", inc=1)
    k.P.barrier()
    emit_l3(k, W, io3, io2["m"], h1, gg, hl, pl, exg, out)
    cnt = k.P.finish()
    return nc, cnt


def _pk(v):
    return np.ascontiguousarray(v.reshape(-1, 128).T)


def _pkw(w):
    t, C = w.shape
    return np.ascontiguousarray(w.T.reshape(C // 128, 128, t).transpose(1, 0, 2))


def kernel(x, mix_norm_g, ffn_norm_g, final_norm_g,
           ev_w_in, ev_w_gk2, ev_b_gk2, ev_gla_norm_g, ev_w_out,
           od_w_in, od_conv_w, od_conv_b, od_w_a, od_b_a, od_w_x, od_b_x, od_lambda, od_w_out,
           ffn_w_up, ffn_conv_w, ffn_conv_b, ffn_w_down):
    f = lambda a: np.asarray(a, dtype=np.float32)
    x = f(x)
    Bsz, S, D = x.shape
    TOK = S // 4
    W = TOK + HALO
    if S not in _F:
        _F[S] = build_fused(S)
    nc, _ = _F[S]
    consts = l1_consts(S)
    w = f(ev_w_in)[0]
    gn = _pk(f(mix_norm_g)[0])
    perm = []
    for g in range(4):
        for j in range(4):
            head = 2 * g + (j % 2)
            base = head * 64 if j < 2 else 512 + head * 64
            perm += list(range(base, base + 64))
    w_out0 = np.ascontiguousarray(f(ev_w_out)[0][np.asarray(perm)])
    shared = {
        "gn": gn, "w_out0": w_out0, "fg0": _pk(f(ffn_norm_g)[0]), "w_up0": f(ffn_w_up)[0],
        "cw0": _pkw(f(ffn_conv_w)[0]), "cb0": _pk(f(ffn_conv_b)[0]), "w_down0": f(ffn_w_down)[0],
        "mg": _pk(f(mix_norm_g)[1]), "w_in": f(od_w_in)[0], "rcw": _pkw(f(od_conv_w)[0]), "rcb": _pk(f(od_conv_b)[0]),
        "wa": np.ascontiguousarray(f(od_w_a)[0].reshape(1024, 256)), "ba": _pk(f(od_b_a)[0]),
        "wx": np.ascontiguousarray(f(od_w_x)[0].reshape(1024, 256)), "bx": _pk(f(od_b_x)[0]),
        "lam": _pk(f(od_lambda)[0]),
        "w_out1": f(od_w_out)[0], "fg1": _pk(f(ffn_norm_g)[1]), "w_up1": f(ffn_w_up)[1],
        "cw1": _pkw(f(ffn_conv_w)[1]), "cb1": _pk(f(ffn_conv_b)[1]), "w_down1": f(ffn_w_down)[1],
        "fin": _pk(f(final_norm_g)),
    }
    shared.update(consts)
    in_maps = []
    for b in range(Bsz):
        xTb = np.ascontiguousarray(x[b].T)
        for c in range(4):
            h0, h1 = 2 * c, 2 * c + 1
            cols = []
            for base in (0, 512, 1024):
                cols += [np.arange(base + h0 * 64, base + h0 * 64 + 64), np.arange(base + h1 * 64, base + h1 * 64 + 64)]
            for base in (1536, 1792):
                cols += [np.arange(base + h0 * 32, base + h0 * 32 + 32), np.arange(base + h1 * 32, base + h1 * 32 + 32)]
            cols += [np.arange(2048 + h0 * 64, 2048 + h0 * 64 + 64), np.arange(2048 + h1 * 64, 2048 + h1 * 64 + 64)]
            cols += [np.arange(2560, 2576)]
            cols += [np.arange(2576 + h0 * 64, 2576 + h0 * 64 + 64), np.arange(2576 + h1 * 64, 2576 + h1 * 64 + 64)]
            cols = np.concatenate(cols)
            t0 = c * TOK
            xw = np.zeros((1024, W), np.float32)
            if c == 0:
                xw[:, HALO:] = xTb[:, 0:TOK]
            else:
                xw[:] = xTb[:, t0 - HALO:t0 + TOK]
            sel = np.zeros((128, 4), np.float32)
            sel[:, c] = 1.0
            sr = np.zeros((128, 4), np.float32)
            sr[:, :c] = 1.0
            m = dict(shared)
            m.update({
                "xT": xTb, "w1": np.ascontiguousarray(w[:, cols]),
                "wgk2": np.ascontiguousarray(f(ev_w_gk2)[0][:, h0 * 32:h0 * 32 + 64]),
                "bgk2": np.ascontiguousarray(f(ev_b_gk2)[0][h0 * 32:h0 * 32 + 64].reshape(64, 1)),
                "glag": np.ascontiguousarray(f(ev_gla_norm_g)[0][h0:h0 + 2].T),
                "xTw": xw, "m": np.full((128, 1), 0.0 if c == 0 else 1.0, np.float32),
                "sel": sel, "srank": sr, "oms": 1.0 - sr,
            })
            in_maps.append(m)
    res = run_bass_kernel_spmd(nc, in_maps, core_ids=list(range(len(in_maps)))).results
    out = np.zeros((Bsz, S, D), np.float32)
    for b in range(Bsz):
        for c in range(4):
            out[b, c * TOK:(c + 1) * TOK, :] = np.asarray(res[b * 4 + c]["outT"]).T
    return out
```

```python
import contextlib
import os
import numpy as np
import ml_dtypes
import concourse.bass as bass
import concourse.mybir as mybir
from concourse.bass_utils import run_bass_kernel_spmd

F32 = mybir.dt.float32
BF16 = mybir.dt.bfloat16
AF = mybir.ActivationFunctionType
ALU = mybir.AluOpType
AX = mybir.AxisListType

SAME_SYNC = True
N_DMA_SEMS = 24
SEM_EPOCH = 2000
NEG = -240000.0
EPS = 1e-6


class Buf:
    __slots__ = ("name", "lw", "rd")

    def __init__(self, name=""):
        self.name = name
        self.lw = None
        self.rd = []


class Prog:
    ENGS = ("pe", "act", "dve", "pool", "sp")

    def __init__(self, nc):
        self.nc = nc
        self.h = {"pe": nc.tensor, "act": nc.scalar, "dve": nc.vector, "pool": nc.gpsimd, "sp": nc.sync}
        self.items = {e: [] for e in self.ENGS}
        self.known = {}
        self.sem = {e: nc.alloc_semaphore("s_" + e) for e in self.ENGS}
        self.dsem = [nc.alloc_semaphore("d%d" % i) for i in range(N_DMA_SEMS)]
        self.duse = [0] * N_DMA_SEMS
        self.dval = [0] * N_DMA_SEMS
        self.pending = {e: [] for e in self.ENGS}
        self.rank = {e: [] for e in self.ENGS}
        self.emitted = {e: 0 for e in self.ENGS}
        self.esem = {}
        self.dnext = 0

    def _need(self, eng, tok, waits):
        if tok is None:
            return
        if tok[0] == "c":
            _, e2, seq = tok
            if e2 == eng and (eng == "pe" or not SAME_SYNC):
                return
            key = (eng, "c", e2)
            if self.known.get(key, -1) >= seq:
                return
            self.known[key] = seq
            self.items[e2][seq]["flag"] = True
            waits.append(tok)
        else:
            _, idx, val = tok
            key = (eng, "d", idx)
            if self.known.get(key, -1) >= val:
                return
            self.known[key] = val
            waits.append(tok)

    def _deps(self, eng, reads, writes):
        waits = []
        for b in reads:
            self._need(eng, b.lw, waits)
        for b in writes:
            self._need(eng, b.lw, waits)
            for t in b.rd:
                self._need(eng, t, waits)
        return waits

    def op(self, eng, fn, reads=(), writes=()):
        waits = self.pending[eng] + self._deps(eng, reads, writes)
        self.pending[eng] = []
        seq = len(self.items[eng])
        self.items[eng].append({"waits": waits, "fn": fn, "flag": False, "dma": None})
        tok = ("c", eng, seq)
        for b in reads:
            b.rd.append(tok)
        for b in writes:
            b.lw = tok
            b.rd = []
        return tok

    def dma(self, fn, reads=(), writes=(), eng="sp", inc=16):
        idx = self.dnext
        self.dnext = (self.dnext + 1) % N_DMA_SEMS
        pv = self.dval[idx]
        waits = self.pending[eng] + self._deps(eng, reads, writes)
        self.pending[eng] = []
        if pv > 0:
            self._need(eng, ("d", idx, pv), waits)
        self.duse[idx] += 1
        self.dval[idx] = pv + inc
        tok = ("d", idx, pv + inc)
        self.items[eng].append({"waits": waits, "fn": fn, "flag": False, "dma": idx, "inc": inc})
        for b in reads:
            b.rd.append(tok)
        for b in writes:
            b.lw = tok
            b.rd = []
        return tok

    def _all_tokens(self):
        toks = []
        for e in ("pe", "act", "dve", "pool"):
            items = self.items[e]
            for i in range(len(items) - 1, -1, -1):
                if items[i]["dma"] is None and items[i]["fn"] is not None or (items[i]["dma"] is None and i < self.emitted[e]):
                    toks.append(("c", e, i))
                    break
        for idx in range(N_DMA_SEMS):
            if self.dval[idx]:
                toks.append(("d", idx, self.dval[idx]))
        return toks

    def barrier(self, engines=None):
        toks = self._all_tokens()
        for e in (engines or self.ENGS):
            for t in toks:
                if t[0] == "c" and t[1] == e:
                    continue
                self._need(e, t, self.pending[e])

    def _sem_of(self, e, r):
        ep = (r - 1) // SEM_EPOCH
        if (e, ep) not in self.esem:
            self.esem[(e, ep)] = self.sem[e] if ep == 0 else self.nc.alloc_semaphore("s_%s_%d" % (e, ep))
        return self.esem[(e, ep)], (r - 1) % SEM_EPOCH + 1

    def flush(self):
        for e in self.ENGS:
            items = self.items[e]
            rk = self.rank[e]
            c = rk[-1] if rk else 0
            for i in range(len(rk), len(items)):
                if items[i]["flag"]:
                    c += 1
                rk.append(c)
        for e in self.ENGS:
            h = self.h[e]
            items = self.items[e]
            for i in range(self.emitted[e], len(items)):
                it = items[i]
                for t in it["waits"]:
                    if t[0] == "c":
                        sm, v = self._sem_of(t[1], self.rank[t[1]][t[2]])
                        h.wait_ge(sm, v)
                    else:
                        h.wait_ge(self.dsem[t[1]], t[2])
                if it["fn"] is None:
                    continue
                ins = it["fn"]()
                if it["dma"] is not None:
                    ins.then_inc(self.dsem[it["dma"]], it["inc"])
                elif it["flag"]:
                    sm, _ = self._sem_of(e, self.rank[e][i])
                    ins.then_inc(sm, 1)
                it["fn"] = None
            self.emitted[e] = len(items)

    def finish(self):
        self.barrier(engines=("sp",))
        self.items["sp"].append({"waits": self.pending["sp"], "fn": None, "flag": False, "dma": None})
        self.pending["sp"] = []
        self.flush()
        return {e: (len(self.items[e]), self.rank[e][-1] if self.rank[e] else 0) for e in self.ENGS}


class K:
    def __init__(self, nc):
        self.nc = nc
        self.P = Prog(nc)
        self._n = 0
        self.stack = contextlib.ExitStack()

    def end_phase(self):
        self.P.barrier()
        self.P.flush()
        self.stack.close()
        self.stack = contextlib.ExitStack()

    def sb(self, shape, dt, name=None):
        self._n += 1
        return self.stack.enter_context(self.nc.sbuf_tensor("sb%d_" % self._n + (name or "t"), list(shape), dt)), Buf(name or "")

    def ps(self, shape, name=None):
        self._n += 1
        return self.stack.enter_context(self.nc.psum_tensor("ps%d_" % self._n + (name or "p"), list(shape), F32)), Buf(name or "")

    def din(self, name, shape, dt=F32):
        return self.nc.dram_tensor(name, list(shape), dt, kind="ExternalInput").ap()

    def dint(self, name, shape, dt=F32, **kw):
        return self.nc.dram_tensor(name, list(shape), dt, kind="Internal", **kw).ap()

    def dout(self, name, shape, dt=F32):
        return self.nc.dram_tensor(name, list(shape), dt, kind="ExternalOutput").ap()

    def mm(self, out, lhsT, rhs, st, sp, r, w):
        nc = self.nc
        return self.P.op("pe", lambda: nc.tensor.matmul(out, lhsT=lhsT, rhs=rhs, start=st, stop=sp), r, w)

    def tr(self, out, in_, ident, r, w):
        nc = self.nc
        return self.P.op("pe", lambda: nc.tensor.transpose(out, in_, ident), r, w)

    def act(self, out, in_, func, r, w, bias=None, scale=None):
        nc = self.nc
        kw = {}
        if bias is not None:
            kw["bias"] = bias
        if scale is not None:
            kw["scale"] = scale
        return self.P.op("act", lambda: nc.scalar.activation(out=out, in_=in_, func=func, **kw), r, w)

    def tt(self, out, in0, in1, op, r, w, eng="dve"):
        h = self.P.h[eng]
        return self.P.op(eng, lambda: h.tensor_tensor(out=out, in0=in0, in1=in1, op=op), r, w)

    def ts(self, out, in0, s1, op0, r, w, s2=None, op1=None, eng="dve"):
        h = self.P.h[eng]
        if op1 is None:
            return self.P.op(eng, lambda: h.tensor_scalar(out=out, in0=in0, scalar1=s1, scalar2=None, op0=op0), r, w)
        return self.P.op(eng, lambda: h.tensor_scalar(out=out, in0=in0, scalar1=s1, scalar2=s2, op0=op0, op1=op1), r, w)

    def stt(self, out, in0, scalar, in1, op0, op1, r, w):
        nc = self.nc
        return self.P.op("dve", lambda: nc.vector.scalar_tensor_tensor(out=out, in0=in0, scalar=scalar, in1=in1, op0=op0, op1=op1), r, w)

    def cp(self, out, in_, r, w, eng="dve"):
        if eng == "act":
            nc = self.nc
            return self.P.op("act", lambda: nc.scalar.copy(out=out, in_=in_), r, w)
        h = self.P.h[eng]
        return self.P.op(eng, lambda: h.tensor_copy(out=out, in_=in_), r, w)

    def memset(self, ap, val, w, eng="pool"):
        h = self.P.h[eng]
        return self.P.op(eng, lambda: h.memset(ap, val), (), w)

    def scan(self, out, d0, d1, init, r, w):
        nc = self.nc
        return self.P.op("dve", lambda: nc.vector.tensor_tensor_scan(out=out, data0=d0, data1=d1, initial=init, op0=ALU.mult, op1=ALU.add), r, w)

    def dma(self, out, in_, r, w, eng="sp"):
        h = self.P.h[eng]
        return self.P.dma(lambda: h.dma_start(out=out, in_=in_), r, w, eng=eng)


def l1_io(k, S):
    d = {}
    for name, shape in (("xT", [1024, S]), ("gn", [128, 8]), ("w1", [1024, 784]), ("wgk2", [16, 64]), ("bgk2", [64, 1]),
                        ("glag", [64, 2]), ("cosd", [64, S]), ("sind", [64, S]), ("cmd", [128, 2048]), ("ed", [64, S]),
                        ("identd", [128, 128]), ("rmd", [64, 512]), ("amd", [64, 512]), ("hm2d", [64, 128]), ("hmd", [64, 2])):
        d[name] = k.din(name, shape)
    return d


def emit_l1(k, S, d, oT):
    NT = S // 512
    NKT = S // 128
    nc = k.nc
    P = k.P
    xT, gn_d, w1_d, wgk2_d, bgk2_d, glag_d = d["xT"], d["gn"], d["w1"], d["wgk2"], d["bgk2"], d["glag"]
    cos_d, sin_d, cm_d, e_d, id_d, rm_d, am_d, hm2_d, hm_d = (d["cosd"], d["sind"], d["cmd"], d["ed"], d["identd"], d["rmd"],
                                                              d["amd"], d["hm2d"], d["hmd"])

    xt, b_xt = k.sb([128, 8, 512], F32, "xt")
    rstd, b_rstd = k.sb([128, 512], F32, "rstd")
    lnv, b_lnv = rstd, b_rstd
    hn, b_hn = k.sb([128, 8, 512], BF16, "hn")
    sq, b_sq = hn, b_hn
    wb, b_wb = k.sb([128, 8, 784], BF16, "wb")
    gn, b_gn = k.sb([128, 8], F32, "gn")
    Kaug = [k.sb([128, S], BF16, "kaug%d" % h) for h in range(2)]
    KB = [[Buf() for _ in range(NT)] for h in range(2)]
    b_kE = [Buf(), Buf()]
    Vaug = [k.sb([128, NKT, 66], BF16, "vaug%d" % h) for h in range(2)]
    VB = [[Buf() for _ in range(NT)] for h in range(2)]
    b_vones = [Buf(), Buf()]
    Qaug = [[k.sb([128, 512], BF16, "qaug%d_%d" % (h, p)) for p in range(2)] for h in range(2)]
    kmT = [k.sb([64, 64], BF16, "kmT%d" % h) for h in range(2)]
    km32, b_km32 = k.sb([64, 2], F32, "km32")
    cosT, b_cos = k.sb([64, 512], F32, "cos")
    sinT, b_sin = k.sb([64, 512], F32, "sin")
    t1, b_t1 = k.sb([64, 512], F32, "t1")
    t2, b_t2 = k.sb([64, 512], F32, "t2")
    pTs = [k.sb([128, 512], BF16, "pT%d" % i) for i in range(4)]
    cm, b_cm = k.sb([128, 4, 512], BF16, "cm")
    id_f, b_idf = k.sb([128, 128], F32, "idf")
    id_b, b_idb = k.sb([128, 128], BF16, "idb")
    ones_b, b_onesb = k.sb([128, 128], BF16, "onesb")
    ones_f, b_onesf = k.sb([128, 64], F32, "onesf")
    bq, b_bq = k.sb([128, 4, 128], F32, "bq")
    gsb, b_gsb = k.sb([128, 4, 64], F32, "gsb")
    m8, b_m8 = k.sb([128, 4, 8], F32, "m8")
    fin_t, b_rden = k.sb([128, 512], F32, "fin")
    rden = fin_t
    osb, b_osb = fin_t[0:64, :], Buf("osb")
    otiles = [k.sb([64, 4, 512], BF16, "otile%d" % p) for p in range(2)]
    QG32, b_qg = k.sb([64, 512], F32, "qg32")
    KG32, b_kg = k.sb([64, 512], F32, "kg32")
    spl, b_spl = k.sb([64, 512], F32, "spl")
    bpos, b_bpos = k.sb([64, 512], F32, "bpos")
    eb, b_eb = k.sb([64, 512], F32, "eb")
    enb, b_enb = k.sb([64, 512], F32, "enb")
    Ac, b_ac = k.sb([64, 8], F32, "Ac")
    ke32, b_ke = k.sb([64, 512], F32, "ke32")
    qt, b_qt = k.sb([64, 512], BF16, "qt")
    kpad, b_kpad = k.sb([64, 2, 512], BF16, "kpad")
    khat, b_khat = k.sb([64, 512], BF16, "khat")
    rmk, b_rmk = k.sb([64, 512], F32, "rmk")
    amk, b_amk = k.sb([64, 512], BF16, "amk")
    hm2, b_hm2 = k.sb([64, 128], F32, "hm2")
    hm, b_hm = k.sb([64, 2], F32, "hm")
    attm, b_attm = k.sb([64, 2, 512], BF16, "attm")
    gvt, b_gvt = k.sb([64, 8, 128], BF16, "gvt")
    KTt, b_ktt = k.sb([64, 8, 64], BF16, "KTt")
    gk16, b_gk16 = k.sb([16, 512], BF16, "gk16")
    wgk2f, b_wgk2f = k.sb([16, 64], F32, "wgk2f")
    wgk2b, b_wgk2b = k.sb([16, 64], BF16, "wgk2b")
    nbg, b_nbg = k.sb([64, 1], F32, "nbg")
    glag, b_glag = k.sb([64, 2], F32, "glag")
    sbog, b_sbog = k.sb([64, 2, 512], BF16, "sbog")
    st32, b_st32 = k.sb([64, 128], F32, "st32")
    stall, b_stall = k.sb([64, 9, 128], BF16, "stall")
    stmp, b_stmp = k.sb([64, 128], F32, "stmp")
    o32, b_o32 = t1, b_t1
    osq, b_osq = k.sb([64, 512], BF16, "osq")
    on32, b_on32 = t2, b_t2
    B = [k.ps([128, 512], "bank%d" % i) for i in range(8)]

    stg = xt
    k.dma(id_f[:], id_d[:, :], (), [b_idf])
    k.cp(id_b[:], id_f[:], [b_idf], [b_idb])
    k.memset(ones_b[:], 1.0, [b_onesb])
    k.memset(ones_f[:], 1.0, [b_onesf])
    k.memset(bq[:], 0.0, [b_bq])
    k.memset(st32[:], 0.0, [b_st32])
    k.memset(stall[:], 0.0, [b_stall])
    k.dma(gn[:], gn_d[:, :], (), [b_gn])
    k.dma(glag[:], glag_d[:, :], (), [b_glag])
    k.dma(hm2[:], hm2_d[:, :], (), [b_hm2])
    k.dma(hm[:], hm_d[:, :], (), [b_hm])
    k.dma(rmk[:], rm_d[:, :], (), [b_rmk])
    k.dma(wgk2f[:], wgk2_d[:, :], (), [b_wgk2f])
    k.cp(wgk2b[:], wgk2f[:], [b_wgk2f], [b_wgk2b])
    k.dma(nbg[:], bgk2_d[:, :], (), [b_nbg])
    k.ts(nbg[:], nbg[:], -1.0, ALU.mult, [b_nbg], [b_nbg])
    k.dma(t1[:], am_d[:, :], (), [b_t1])
    k.cp(amk[:], t1[:], [b_t1], [b_amk])
    sflat = stg[:].rearrange("p a b -> p (a b)")
    k.dma(sflat[:, 0:2048], cm_d[:, :], (), [b_xt])
    k.cp(cm[:].rearrange("p a b -> p (a b)"), sflat[:, 0:2048], [b_xt], [b_cm])
    w1v = w1_d.rearrange("(kc p) f -> p kc f", p=128)
    for half in range(2):
        sv = sflat[:, 0:4 * 784].rearrange("p (a b) -> p a b", b=784)
        k.dma(sv, w1v[:, half * 4:(half + 1) * 4, :], (), [b_xt])
        k.cp(wb[:, half * 4:(half + 1) * 4, :], sv, [b_xt], [b_wb], eng="dve" if half == 0 else "pool")
    for pc in range(S // 2048):
        k.dma(sflat[64:128, 0:2048], e_d[:, pc * 2048:(pc + 1) * 2048], (), [b_xt])
        for h in range(2):
            k.cp(Kaug[h][0][64:128, pc * 2048:(pc + 1) * 2048], sflat[64:128, 0:2048], [b_xt], [b_kE[h]],
                 eng="dve" if h == 0 else "pool")
    for h in range(2):
        k.memset(Vaug[h][0][:, :, 64:65], 1.0, [b_vones[h]])
        k.memset(kmT[h][0][:], 0.0, [kmT[h][1]])

    xTv = xT.rearrange("(kc p) s -> p kc s", p=128)
    def proj(bank, M, col0, ncols=None):
        pt, pb = bank
        for kc in range(8):
            k.mm(pt[0:M, 0:512], wb[:, kc, col0:col0 + M], hn[:, kc, :], kc == 0, kc == 7, [b_wb, b_hn], [pb])
        return pt, pb

    FB0, FB1, FB2 = B[0], B[4], B[7]

    def gen_F(g):
        c0 = g * 512
        par = g % 2
        otile, b_ot = otiles[par]
        k.dma(xt[:, 0:4, :], xTv[:, 0:4, c0:c0 + 512], (), [b_xt])
        k.dma(xt[:, 4:8, :], xTv[:, 4:8, c0:c0 + 512], (), [b_xt])
        k.dma(cosT[:], cos_d[:, c0:c0 + 512], (), [b_cos])
        k.dma(sinT[:], sin_d[:, c0:c0 + 512], (), [b_sin])
        k.act(sq[:], xt[:], AF.Square, [b_xt], [b_sq])
        pt, pb = FB0
        for kc in range(8):
            k.mm(pt[:, :], ones_b[:], sq[:, kc, :], kc == 0, kc == 7, [b_onesb, b_sq], [pb])
        yield
        k.act(lnv[:], pt[:, :], AF.Ln, [pb], [b_lnv], bias=EPS, scale=1.0 / 1024)
        k.act(rstd[:], lnv[:], AF.Exp, [b_lnv], [b_rstd], scale=-0.5)
        for kc in range(8):
            k.stt(hn[:, kc, :], xt[:, kc, :], gn[:, kc:kc + 1], rstd[:], ALU.mult, ALU.mult, [b_xt, b_gn, b_rstd], [b_hn])
            if kc % 4 == 3:
                yield
        for idx in range(4):
            h = idx % 2
            isk = idx >= 2
            pt, pb = proj((FB1, FB2)[idx % 2], 64, idx * 64)
            yield
            if isk:
                dest = Kaug[h][0][0:64, c0:c0 + 512]
                dbuf = KB[h][g]
            else:
                dest = Qaug[h][par][0][0:64, :]
                dbuf = Qaug[h][par][1]
            k.tt(t1[:], pt[0:64, 0:512], cosT[:], ALU.mult, [pb, b_cos], [b_t1])
            k.tt(t2[0:32, :], pt[32:64, 0:512], sinT[32:64, :], ALU.mult, [pb, b_sin], [b_t2])
            k.tt(t2[32:64, :], pt[0:32, 0:512], sinT[0:32, :], ALU.mult, [pb, b_sin], [b_t2])
            k.tt(dest, t1[:], t2[:], ALU.add, [b_t1, b_t2], [dbuf])
            yield
        pt, pb = FB0
        for st in range(4):
            for kc in range(8):
                k.mm(pt[:, st * 128:(st + 1) * 128], hn[:, kc, st * 128:(st + 1) * 128], wb[:, kc, 256:384],
                     kc == 0, kc == 7, [b_hn, b_wb], [pb])
            yield
        pv_ = pt[:, 0:512].rearrange("p (a b) -> p a b", b=128)
        for h in range(2):
            k.cp(Vaug[h][0][:, 4 * g:4 * g + 4, 0:64], pv_[:, :, h * 64:(h + 1) * 64], [pb], [VB[h][g]], eng="act")
        pt, pb = proj(FB1, 128, 384)
        k.cp(QG32[:], pt[0:64, 0:512], [pb], [b_qg], eng="act")
        k.cp(KG32[:], pt[64:128, 0:512], [pb], [b_kg], eng="act")
        yield
        for c in range(8):
            pt, pb = FB2 if c < 4 else FB0
            for kc in range(8):
                k.mm(pt[0:64, (c % 4) * 128:(c % 4 + 1) * 128], hn[:, kc, c * 64:(c + 1) * 64], wb[:, kc, 512:640],
                     kc == 0, kc == 7, [b_hn, b_wb], [pb])
            if c % 2 == 1:
                yield
        for hf in range(2):
            pt, pb = FB2 if hf == 0 else FB0
            k.cp(gvt[:, hf * 4:(hf + 1) * 4, :].rearrange("p a b -> p (a b)"), pt[0:64, 0:512], [pb], [b_gvt], eng="act")
        pt, pb = proj(FB1, 16, 640)
        k.cp(gk16[:], pt[0:16, 0:512], [pb], [b_gk16], eng="act")
        k.mm(pt[0:64, 0:512], wgk2b[:], gk16[:], True, True, [b_wgk2b, b_gk16], [pb])
        k.act(spl[:], pt[0:64, 0:512], AF.Exp, [pb, b_nbg], [b_spl], bias=nbg[:, 0:1], scale=-1.0)
        k.act(spl[:], spl[:], AF.Ln, [b_spl], [b_spl], bias=1.0, scale=1.0)
        yield
        pt, pb = proj(FB2, 128, 656)
        k.act(sbog[:, 0, :], pt[0:64, 0:512], AF.Silu, [pb], [b_sbog])
        k.act(sbog[:, 1, :], pt[64:128, 0:512], AF.Silu, [pb], [b_sbog])
        yield
        k.scan(bpos[:], rmk[:], spl[:], 0.0, [b_rmk, b_spl], [b_bpos])
        k.act(eb[:], bpos[:], AF.Exp, [b_bpos], [b_eb], scale=-1.0 / 16)
        k.act(enb[:], bpos[:], AF.Exp, [b_bpos], [b_enb], scale=1.0 / 16)
        blast = bpos[:].rearrange("p (c t) -> p c t", t=64)[:, :, 63:64].rearrange("p c o -> p (c o)")
        k.act(Ac[:], blast, AF.Exp, [b_bpos], [b_ac], scale=-1.0 / 16)
        yield
        k.stt(qt[:], QG32[:], 32.0 ** -0.5, eb[:], ALU.mult, ALU.mult, [b_qg, b_eb], [b_qt])
        k.tt(ke32[:], KG32[:], enb[:], ALU.mult, [b_kg, b_enb], [b_ke])
        for h in range(2):
            k.ts(kpad[:, h, :], ke32[:], hm[:, h:h + 1], ALU.mult, [b_ke, b_hm], [b_kpad])
        yield
        for c in range(8):
            k.act(khat[:, c * 64:(c + 1) * 64], ke32[:, c * 64:(c + 1) * 64], AF.Copy, [b_ke, b_ac], [b_khat], scale=Ac[:, c:c + 1])
        yield
        pt, pb = FB0
        for c in range(8):
            k.mm(pt[0:64, c * 64:(c + 1) * 64], khat[:, c * 64:(c + 1) * 64], id_b[0:64, 0:64], True, True, [b_khat, b_idb], [pb])
        k.cp(KTt[:].rearrange("p a b -> p (a b)"), pt[0:64, 0:512], [pb], [b_ktt], eng="act")
        yield
        for h in range(2):
            pt, pb = (FB1, FB2)[h]
            for c in range(8):
                k.mm(pt[0:64, c * 64:(c + 1) * 64], kpad[:, h, c * 64:(c + 1) * 64], qt[:, c * 64:(c + 1) * 64], True, True,
                     [b_kpad, b_qt], [pb])
            k.tt(attm[:, h, :], pt[0:64, 0:512], amk[:], ALU.mult, [pb, b_amk], [b_attm])
            yield
        pso = [FB2, FB0]
        for c in range(8):
            psd, b_psd = FB0 if c < 4 else FB1
            for h in range(2):
                k.mm(psd[0:64, (c % 4) * 128 + h * 64:(c % 4) * 128 + (h + 1) * 64], KTt[:, c, :], gvt[:, c, h * 64:(h + 1) * 64],
                     True, True, [b_ktt, b_gvt], [b_psd])
            if c % 4 == 3:
                yield
        k.cp(stall[:, 0, :], stall[:, 8, :], [b_stall], [b_stall])
        for c in range(8):
            psd, b_psd = FB0 if c < 4 else FB1
            k.tt(stmp[:], psd[0:64, (c % 4) * 128:(c % 4 + 1) * 128], hm2[:], ALU.mult, [b_psd, b_hm2], [b_stmp])
            k.stt(st32[:], st32[:], Ac[:, c:c + 1], stmp[:], ALU.mult, ALU.add, [b_st32, b_ac, b_stmp], [b_st32])
            k.cp(stall[:, c + 1, :], st32[:], [b_st32], [b_stall])
            if c % 2 == 1:
                yield
        for c in range(8):
            for h in range(2):
                po, pbo = pso[h]
                k.mm(po[0:64, c * 64:(c + 1) * 64], gvt[:, c, h * 64:(h + 1) * 64], attm[:, h, c * 64:(c + 1) * 64], True, False,
                     [b_gvt, b_attm], [pbo])
                k.mm(po[0:64, c * 64:(c + 1) * 64], stall[:, c, h * 64:(h + 1) * 64], qt[:, c * 64:(c + 1) * 64], False, True,
                     [b_stall, b_qt], [pbo])
            if c % 2 == 1:
                yield
        for h in range(2):
            po, pbo = pso[h]
            k.cp(o32[:], po[0:64, 0:512], [pbo], [b_o32], eng="act")
            k.act(osq[:], po[0:64, 0:512], AF.Square, [pbo], [b_osq])
            pt, pb = FB1
            k.mm(pt[0:64, 0:512], ones_b[0:64, 0:64], osq[:], True, True, [b_onesb, b_osq], [pb])
            yield
            k.act(lnv[0:64, :], pt[0:64, 0:512], AF.Ln, [pb], [b_lnv], bias=EPS, scale=1.0 / 64)
            k.act(lnv[0:64, :], lnv[0:64, :], AF.Exp, [b_lnv], [b_lnv], scale=-0.5)
            k.tt(on32[:], o32[:], lnv[0:64, :], ALU.mult, [b_o32, b_lnv], [b_on32])
            k.stt(otile[:, 2 + h, :], on32[:], glag[:, h:h + 1], sbog[:, h, :], ALU.mult, ALU.mult, [b_on32, b_glag, b_sbog], [b_ot])
            yield
        for h in range(2):
            KA, _ = Kaug[h]
            QA, b_QA = Qaug[h][par]
            kmt, b_kmt = kmT[h]
            k.P.op("dve", (lambda o=km32[:], i=KA[0:64, c0:c0 + 512].rearrange("p (a b) -> p a b", b=256):
                           nc.vector.tensor_reduce(out=o, in_=i, axis=AX.X, op=ALU.add)), [KB[h][g]], [b_km32])
            k.cp(kmt[:, 2 * g:2 * g + 2], km32[:], [b_km32], [b_kmt])
            pg, b_pg = FB1
            for st in range(4):
                k.mm(pg[:, st * 64:(st + 1) * 64], QA[0:64, st * 128:(st + 1) * 128], kmt[:, :], True, True, [b_QA, b_kmt], [b_pg])
            yield
            k.memset(gsb[:], -1e30, [b_gsb], eng="dve")
            for st in range(4):
                blk = 2 * g + st // 2
                if blk > 0:
                    k.cp(gsb[:, st, 0:blk], pg[:, st * 64:st * 64 + blk], [b_pg], [b_gsb])
            yield
            for st in range(4):
                blk = 2 * g + st // 2
                k.P.op("dve", (lambda o=m8[:, st, :], i=gsb[:, st, :]: nc.vector.max(out=o, in_=i)), [b_gsb], [b_m8])
                k.ts(bq[:, st, 64:128], gsb[:, st, :], m8[:, st, 2:3], ALU.is_ge, [b_gsb, b_m8], [b_bq], s2=-NEG, op1=ALU.mult)
            yield
            k.ts(bq[:, :, 64:128], bq[:, :, 64:128], NEG, ALU.add, [b_bq], [b_bq])
            for st in range(4):
                blk = 2 * g + st // 2
                k.memset(bq[:, st, 64 + blk:65 + blk], 0.0, [b_bq], eng="dve")
            pg2, b_pg2 = FB2
            for st in range(4):
                k.tr(pg2[:, st * 128:(st + 1) * 128], bq[:, st, :], id_f[:], [b_bq, b_idf], [b_pg2])
            k.cp(QA[64:128, :], pg2[64:128, 0:512], [b_pg2], [b_QA], eng="act")
            yield

    def gen_A(g):
        par = g % 2
        otile, b_ot = otiles[par]
        for h in range(2):
            KA, _ = Kaug[h]
            VA, _ = Vaug[h]
            QA, b_QA = Qaug[h][par]
            pO, b_pO = B[5 + h]
            nkt = 4 * g + 4
            LA = 2

            def qk(kt):
                j = kt - 4 * g
                q0 = 256 if j >= 2 else 0
                pS, b_pS = B[1 + (kt % 3)]
                k.mm(pS[:, q0:512], KA[:, kt * 128:(kt + 1) * 128], QA[:, q0:512], True, j < 0,
                     [KB[h][kt // 4], b_kE[h], b_QA], [b_pS])
                if j >= 0:
                    k.mm(pS[:, q0:512], id_b[:], cm[:, j, q0:512], False, True, [b_idb, b_cm], [b_pS])

            def pv(kt):
                j = kt - 4 * g
                q0 = 256 if j >= 2 else 0
                pS, b_pS = B[1 + (kt % 3)]
                pTt, b_pT = pTs[kt % 4]
                k.act(pTt[:, q0:512], pS[:, q0:512], AF.Exp, [b_pS], [b_pT], scale=0.125)
                k.mm(pO[0:65, q0:512], VA[:, kt, 0:65], pTt[:, q0:512], kt == 0, kt == nkt - 1,
                     [VB[h][kt // 4], b_vones[h], b_pT], [b_pO])

            for i in range(nkt + LA):
                if i < nkt:
                    qk(i)
                if i >= LA:
                    pv(i - LA)
                yield
            k.P.op("dve", (lambda o=rden[64:65, :], i=pO[64:65, 0:512]: nc.vector.reciprocal(out=o, in_=i)), [b_pO], [b_rden])
            pt, pb = B[1]
            k.mm(pt[0:64, 0:512], ones_f[64:65, 0:64], rden[64:65, :], True, True, [b_onesf, b_rden], [pb])
            k.cp(osb[:], pO[0:64, 0:512], [b_pO], [b_osb], eng="act")
            k.tt(otile[:, h, :], osb[:], pt[0:64, 0:512], ALU.mult, [b_osb, pb], [b_ot])
            yield
        oT(g, otile, b_ot)

    def drive(a, f=None, ratio=1):
        n = 0
        a_live, f_live = a is not None, f is not None
        while a_live or f_live:
            if a_live:
                try:
                    next(a)
                except StopIteration:
                    a_live = False
                n += 1
            if f_live and (not a_live or n % ratio == 0):
                try:
                    next(f)
                except StopIteration:
                    f_live = False

    NF_STEPS = 48
    drive(None, gen_F(0))
    for g in range(NT):
        na = 2 * (4 * g + 4 + 3)
        drive(gen_A(g), gen_F(g + 1) if g + 1 < NT else None, ratio=max(1, na // NF_STEPS))


def l1_consts(S):
    half = 32
    inv = (10000.0 ** (-np.arange(half, dtype=np.float32) / half)).astype(np.float32)
    ang = np.arange(S, dtype=np.float32)[None, :] * inv[:, None]
    cos = np.cos(ang).astype(np.float32)
    sin = np.sin(ang).astype(np.float32)
    cosd = np.concatenate([cos, cos], 0)
    sind = np.concatenate([sin, -sin], 0)
    kk = np.arange(128)[:, None]
    qq = np.arange(512)[None, :]
    cm = np.concatenate([np.where(qq < j * 128 + kk, NEG, 0.0) for j in range(4)], 1).astype(np.float32)
    ed = (np.arange(S)[None, :] // 256 == np.arange(64)[:, None]).astype(np.float32)
    ident = np.eye(128, dtype=np.float32)
    rm = np.tile((np.arange(512) % 64 != 0).astype(np.float32)[None, :], (64, 1))
    s_ = np.arange(64)[:, None]
    t_ = np.arange(64)[None, :]
    am = np.tile((s_ <= t_).astype(np.float32), (1, 8))
    hm = np.zeros((64, 2), np.float32)
    hm[0:32, 0] = 1
    hm[32:64, 1] = 1
    hm2 = np.repeat(hm, 64, axis=1)
    return dict(cosd=cosd, sind=sind, cmd=cm, ed=ed, identd=ident, rmd=rm, amd=am, hm2d=hm2, hmd=hm)


HALO = 8
OCS = 2048


def choose_tiles(W):
    if W == 4104:
        return 4, 3, 342
    if W == 520:
        return 2, 1, 260
    raise ValueError(W)


class Trunk:
    def __init__(self, k, W):
        self.k = k
        self.nc = k.nc
        self.W = W
        self.NB, self.NTB, self.TW = choose_tiles(W)
        self.TB = self.NTB * self.TW
        TB, TW = self.TB, self.TW
        self.h, self.b_h = k.sb([128, 8, TB], F32, "h")
        self.hn, self.b_hn = k.sb([128, 8, TB], BF16, "hn")
        self.act, self.b_act = k.sb([128, 24, TB], BF16, "act")
        self.wst = [k.sb([128, 8, 128], F32, "wst%d" % i) for i in range(3)]
        self.wbf = [k.sb([128, 24, 128], BF16, "wbf%d" % i) for i in range(3)]
        self.wi = 0
        self.si = 0
        self.wq = []
        self.todo = []
        self.pc = [k.sb([128, 4 + TB], F32, "pc%d" % i) for i in range(2)]
        self.pci = 0
        self.yt = [k.sb([128, TB], F32, "yt%d" % i) for i in range(2)]
        self.gel, self.b_gel = k.sb([128, TB], F32, "gel")
        self.sqt, self.b_sqt = k.sb([128, 8, TW], BF16, "sqt")
        self.lnv, self.b_lnv = k.sb([128, TW], F32, "lnvt")
        self.rstd, self.b_rstd = k.sb([128, TW], F32, "rstdt")
        self.ones_b, self.b_ones = k.sb([128, 128], BF16, "onesb")
        self.m, self.b_m = k.sb([128, 1], F32, "hmask")
        self.carry, self.b_carry = k.sb([128, 48, 2], F32, "carry")
        self.B = [k.ps([128, 512], "bank%d" % i) for i in range(8)]
        self.bi = 0
        k.memset(self.ones_b[:], 1.0, [self.b_ones])
        k.memset(self.carry[:], 0.0, [self.b_carry])

    def bank(self):
        b = self.B[self.bi]
        self.bi = (self.bi + 1) % 8
        return b

    def small(self, name, dram_ap, shape):
        t, b = self.k.sb(shape, F32, name)
        self.k.dma(t[:], dram_ap, (), [b])
        return t, b

    def request(self, wd, KC, col0):
        k = self.k
        wb, b_wb = self.wbf[self.wi]
        self.wi = (self.wi + 1) % 3
        wv = wd.rearrange("(kc p) f -> p kc f", p=128)
        for k0 in range(0, KC, 8):
            k1 = min(KC, k0 + 8)
            st, b_st = self.wst[self.si]
            self.si = (self.si + 1) % 3
            k.dma(st[:, 0:k1 - k0, :], wv[:, k0:k1, col0:col0 + 128], (), [b_st])
            k.cp(wb[:, k0:k1, :], st[:, 0:k1 - k0, :], [b_st], [b_wb], eng="act")
        self.wq.append((wb, b_wb))

    def plan(self, reqs):
        assert not self.wq and not getattr(self, "todo", None), "previous plan not fully consumed"
        self.todo = list(reqs)
        for _ in range(2):
            if self.todo:
                self.request(*self.todo.pop(0))

    def run_stage(self, reqs, body):
        for i in range(len(reqs)):
            body(i)

    def rmsnorm(self, g_t, b_g, out_f32=None):
        k = self.k
        TW = self.TW
        for nt in range(self.NTB):
            cs = slice(nt * TW, (nt + 1) * TW)
            k.act(self.sqt[:], self.h[:, :, cs], AF.Square, [self.b_h], [self.b_sqt])
            pt, pb = self.bank()
            for kc in range(8):
                k.mm(pt[:, 0:TW], self.ones_b[:], self.sqt[:, kc, :], kc == 0, kc == 7, [self.b_ones, self.b_sqt], [pb])
            k.act(self.lnv[:], pt[:, 0:TW], AF.Ln, [pb], [self.b_lnv], bias=EPS, scale=1.0 / 1024)
            k.act(self.rstd[:], self.lnv[:], AF.Exp, [self.b_lnv], [self.b_rstd], scale=-0.5)
            for kc in range(8):
                if out_f32 is None:
                    k.stt(self.hn[:, kc, cs], self.h[:, kc, cs], g_t[:, kc:kc + 1], self.rstd[:], ALU.mult, ALU.mult,
                          [self.b_h, b_g, self.b_rstd], [self.b_hn])
                else:
                    k.stt(out_f32[0][:, kc, cs], self.h[:, kc, cs], g_t[:, kc:kc + 1], self.rstd[:], ALU.mult, ALU.mult,
                          [self.b_h, b_g, self.b_rstd], [out_f32[1]])

    def linear(self, src, b_src, KC, wd, col0, evac):
        k = self.k
        TW = self.TW
        if self.todo:
            self.request(*self.todo.pop(0))
        wb, b_wb = self.wq.pop(0)
        for nt in range(self.NTB):
            cs = slice(nt * TW, (nt + 1) * TW)
            pt, pb = self.bank()
            for kc in range(KC):
                k.mm(pt[:, 0:TW], wb[:, kc, :], src[:, kc, cs], kc == 0, kc == KC - 1, [b_wb, b_src], [pb])
            evac(nt, cs, pt[:, 0:TW], pb)

    @staticmethod
    def ffn_reqs(w_up, w_down):
        return ([(w_up, 8, (part * 24 + j) * 128) for j in range(24) for part in range(2)]
                + [(w_down, 24, fc * 128) for fc in range(8)])

    def conv(self, pc, b_pc, ntap, cw_t, b_cw, cb_t, b_cb, idx, yt, b_yt):
        k = self.k
        TB = self.TB
        last = ntap - 1
        k.act(yt[:], pc[:, last:last + TB], AF.Identity, [b_pc, b_cw, b_cb], [b_yt],
              bias=cb_t[:, idx:idx + 1], scale=cw_t[:, idx, last:last + 1])
        for i in range(last - 1, -1, -1):
            k.stt(yt[:], pc[:, i:i + TB], cw_t[:, idx, i:i + 1], yt[:], ALU.mult, ALU.add, [b_pc, b_cw, b_yt], [b_yt])

    def ffn(self, first, gn_t, b_gn, w_up, cw_t, b_cw, cb_t, b_cb, w_down, mask_halo):
        k = self.k
        TB, TW = self.TB, self.TW
        self.rmsnorm(gn_t, b_gn)
        ys = [None, None]

        def up_body(i):
            j, part = i // 2, i % 2
            idx = part * 24 + j
            pc, b_pc = self.pc[self.pci]
            self.pci = (self.pci + 1) % 2
            yt, b_yt = self.yt[part]
            k.cp(pc[:, 0:2], self.carry[:, idx, :], [self.b_carry], [b_pc], eng="act")

            def evac(nt, cs, ps, pb, pc=pc, b_pc=b_pc):
                k.cp(pc[:, 2 + cs.start:2 + cs.stop], ps, [pb], [b_pc], eng="act")

            self.linear(self.hn, self.b_hn, 8, w_up, idx * 128, evac)
            if first and mask_halo:
                k.ts(pc[:, 2:2 + HALO], pc[:, 2:2 + HALO], self.m[:, 0:1], ALU.mult, [b_pc, self.b_m], [b_pc])
            k.cp(self.carry[:, idx, :], pc[:, TB:TB + 2], [b_pc], [self.b_carry], eng="act")
            self.conv(pc, b_pc, 3, cw_t, b_cw, cb_t, b_cb, idx, yt, b_yt)
            ys[part] = (yt, b_yt)
            if part == 1:
                (yu, b_yu), (yg, b_yg) = ys
                k.act(self.gel[:], yg[:], AF.Gelu_apprx_tanh, [b_yg], [self.b_gel])
                k.tt(self.act[:, j, :], yu[:], self.gel[:], ALU.mult, [b_yu, self.b_gel], [self.b_act])

        self.run_stage([(w_up, 8, (part * 24 + j) * 128) for j in range(24) for part in range(2)], up_body)

        def down_body(fc):
            def evac(nt, cs, ps, pb, fc=fc):
                k.tt(self.h[:, fc, cs], self.h[:, fc, cs], ps, ALU.add, [self.b_h, pb], [self.b_h])

            self.linear(self.act, self.b_act, 24, w_down, fc * 128, evac)

        self.run_stage([(w_down, 24, fc * 128) for fc in range(8)], down_body)


def l2_io(k, W):
    d = {}
    for name, shape in (("xTw", [1024, W]), ("m", [128, 1]), ("sel", [128, 4]), ("w_out0", [1024, 1024]), ("fg0", [128, 8]),
                        ("w_up0", [1024, 6144]), ("cw0", [128, 48, 3]), ("cb0", [128, 48]), ("w_down0", [3072, 1024]),
                        ("mg", [128, 8]), ("w_in", [1024, 2048]), ("rcw", [128, 8, 4]), ("rcb", [128, 8]), ("wa", [1024, 256]),
                        ("ba", [128, 8]), ("wx", [1024, 256]), ("bx", [128, 8]), ("lam", [128, 8])):
        d[name] = k.din(name, shape)
    return d


def emit_l2(k, W, S, d, og, h1_o, gg_o, hl_o, pl_o, ex_o):
    nc = k.nc
    T = Trunk(k, W)
    NB, TB, TW = T.NB, T.TB, T.TW
    TOK = W - HALO
    xTw, m_d, sel_d, w_out, fg_d, w_up, cw_d, cb_d, w_down = (d["xTw"], d["m"], d["sel"], d["w_out0"], d["fg0"], d["w_up0"],
                                                               d["cw0"], d["cb0"], d["w_down0"])
    mg_d, w_in, rcw_d, rcb_d, wa_d, ba_d, wx_d, bx_d, lam_d = (d["mg"], d["w_in"], d["rcw"], d["rcb"], d["wa"], d["ba"],
                                                               d["wx"], d["bx"], d["lam"])
    sel, b_sel = T.small("sel", sel_d[:, :], [128, 4])

    k.dma(T.m[:], m_d[:, :], (), [T.b_m])
    fg, b_fg = T.small("fg", fg_d[:, :], [128, 8])
    cw, b_cw = T.small("cw", cw_d[:, :, :], [128, 48, 3])
    cb, b_cb = T.small("cb", cb_d[:, :], [128, 48])
    mg, b_mg = T.small("mg", mg_d[:, :], [128, 8])
    rcw, b_rcw = T.small("rcw", rcw_d[:, :, :], [128, 8, 4])
    rcb, b_rcb = T.small("rcb", rcb_d[:, :], [128, 8])
    ba, b_ba = T.small("ba", ba_d[:, :], [128, 8])
    bx, b_bx = T.small("bx", bx_d[:, :], [128, 8])
    lam, b_lam = T.small("lam", lam_d[:, :], [128, 8])
    c1, b_c1 = k.sb([128, 8], F32, "c1")
    c2, b_c2 = k.sb([128, 8], F32, "c2")
    k.act(c1[:], lam[:], AF.Exp, [b_lam], [b_c1], scale=-1.0)
    k.act(c1[:], c1[:], AF.Ln, [b_c1], [b_c1], bias=1.0, scale=1.0)
    k.ts(c2[:], c1[:], -16.0, ALU.mult, [b_c1], [b_c2])
    k.ts(c1[:], c1[:], -8.0, ALU.mult, [b_c1, b_c2], [b_c1])
    gg, b_gg = k.sb([128, TB], BF16, "gg")
    rcar, b_rcar = k.sb([128, 8, 3], F32, "rcar")
    hcar, b_hcar = k.sb([128, 8], F32, "hcar")
    pcar, b_pcar = k.sb([128, 8], F32, "pcar")
    zer, b_zer = k.sb([128, TB], F32, "zer")
    ob_x, b_ob_x = k.sb([128, 8, TB], BF16, "ob_x")
    xrb, b_xrb = k.sb([128, 2, TB], BF16, "xrb")
    rr, b_rr = k.sb([128, TB], F32, "rr")
    ii, b_ii = k.sb([128, TB], F32, "ii")
    aa, b_aa = k.sb([128, TB], F32, "aa")
    uu, b_uu = k.sb([128, TB], F32, "uu")
    hl, b_hl = k.sb([128, TB], F32, "hl")
    pl, b_pl = k.sb([128, TB], F32, "pl")
    k.memset(rcar[:], 0.0, [b_rcar])
    k.memset(zer[:], 0.0, [b_zer])
    ext, b_ext = k.sb([128, 8, 2], F32, "ext")

    oTv = [o_.rearrange("(kc p) s -> p kc s", p=128) for o_ in og]
    xTv = xTw.rearrange("(kc p) s -> p kc s", p=128)
    h1v = h1_o.rearrange("(kc p) s -> p kc s", p=128)
    ggv = gg_o.rearrange("(kc p) s -> p kc s", p=128)
    hlv = hl_o.rearrange("(kc p) s -> p kc s", p=128)
    plv = pl_o.rearrange("(kc p) s -> p kc s", p=128)
    exv = ex_o.rearrange("(kc p) s -> p kc s", p=128)

    def load_o(blk, dst, b_dst):
        g0 = blk * TB
        for c in range(4):
            cand = T.act[:, 8 * (c % 3):8 * (c % 3) + 8, :]
            lo = c * TOK - HALO + g0
            skip = max(0, -lo)
            if skip:
                k.memset(cand[:, :, 0:skip], 0.0, [T.b_act])
            a = lo + skip
            while a < lo + TB:
                j = a // OCS
                e = min(lo + TB, (j + 1) * OCS)
                k.dma(cand[:, :, a - lo:e - lo], oTv[j][:, :, a - j * OCS:e - j * OCS], (), [T.b_act])
                a = e
            if c == 0:
                k.ts(dst[:], cand, sel[:, 0:1], ALU.mult, [T.b_act, b_sel], [b_dst])
            else:
                k.stt(dst[:], cand, sel[:, c:c + 1], dst[:], ALU.mult, ALU.add, [T.b_act, b_sel, b_dst], [b_dst])

    def load_x(blk):
        g0 = blk * TB
        k.dma(T.h[:, 0:4, :], xTv[:, 0:4, g0:g0 + TB], (), [T.b_h])
        k.dma(T.h[:, 4:8, :], xTv[:, 4:8, g0:g0 + TB], (), [T.b_h])

    for blk in range(NB):
        first = blk == 0
        g0 = blk * TB
        if first:
            load_o(0, T.hn, T.b_hn)
            load_x(0)
            ob, b_ob = T.hn, T.b_hn
        else:
            ob, b_ob = ob_x, b_ob_x
        rg_reqs = []
        for n in range(4):
            rg_reqs += [(w_in, 8, 1024 + (2 * n + c2i) * 128) for c2i in range(2)]
            for c2i in range(2):
                rg_reqs += [(wa_d[n * 256:(n + 1) * 256, :], 2, c2i * 128), (wx_d[n * 256:(n + 1) * 256, :], 2, c2i * 128)]
        T.plan([(w_out, 8, fc * 128) for fc in range(8)] + T.ffn_reqs(w_up, w_down)
               + [(w_in, 8, fc * 128) for fc in range(8)] + rg_reqs)

        def wo_body(fc):
            def evac(nt, cs, ps, pb, fc=fc):
                k.tt(T.h[:, fc, cs], T.h[:, fc, cs], ps, ALU.add, [T.b_h, pb], [T.b_h])

            T.linear(ob, b_ob, 8, w_out, fc * 128, evac)

        T.run_stage([(w_out, 8, fc * 128) for fc in range(8)], wo_body)
        T.ffn(first, fg, b_fg, w_up, cw, b_cw, cb, b_cb, w_down, False)
        k.dma(h1v[:, :, g0:g0 + TB], T.h[:], [T.b_h], (), eng="pool")
        T.rmsnorm(mg, b_mg)
        if blk + 1 < NB:
            load_x(blk + 1)
        def gb_body(fc):
            def evac(nt, cs, ps, pb, fc=fc):
                k.act(gg[:, cs], ps, AF.Gelu_apprx_tanh, [pb], [b_gg])

            T.linear(T.hn, T.b_hn, 8, w_in, fc * 128, evac)
            k.dma(ggv[:, fc, g0:g0 + TB], gg[:], [b_gg], (), eng="pool")

        T.run_stage([(w_in, 8, fc * 128) for fc in range(8)], gb_body)
        reqs = []
        for n in range(4):
            reqs += [(w_in, 8, 1024 + (2 * n + c2i) * 128) for c2i in range(2)]
            for c2i in range(2):
                reqs += [(wa_d[n * 256:(n + 1) * 256, :], 2, c2i * 128), (wx_d[n * 256:(n + 1) * 256, :], 2, c2i * 128)]

        def rg_body(i, blk=blk, first=first, g0=g0):
            n, r = i // 6, i % 6
            if r < 2:
                c2i = r
                c8 = 2 * n + c2i
                pc, b_pc = T.pc[T.pci]
                T.pci = (T.pci + 1) % 2
                k.cp(pc[:, 0:3], rcar[:, c8, :], [b_rcar], [b_pc], eng="act")

                def evac(nt, cs, ps, pb, pc=pc, b_pc=b_pc):
                    k.cp(pc[:, 3 + cs.start:3 + cs.stop], ps, [pb], [b_pc], eng="act")

                T.linear(T.hn, T.b_hn, 8, w_in, 1024 + c8 * 128, evac)
                if first:
                    k.ts(pc[:, 3:3 + HALO], pc[:, 3:3 + HALO], T.m[:, 0:1], ALU.mult, [b_pc, T.b_m], [b_pc])
                k.cp(rcar[:, c8, :], pc[:, TB:TB + 3], [b_pc], [b_rcar], eng="act")
                yt, b_yt = T.yt[c2i]
                T.conv(pc, b_pc, 4, rcw, b_rcw, rcb, b_rcb, c8, yt, b_yt)
                k.cp(xrb[:, c2i, :], yt[:], [b_yt], [b_xrb], eng="act")
                return
            c2i, which = (r - 2) // 2, (r - 2) % 2
            fc = 2 * n + c2i
            wd, bias_t, b_bias, dst, b_dst = ((wa_d, ba, b_ba, rr, b_rr), (wx_d, bx, b_bx, ii, b_ii))[which]

            def evac(nt, cs, ps, pb, dst=dst, b_dst=b_dst, bias_t=bias_t, b_bias=b_bias, fc=fc):
                k.act(dst[:, cs], ps, AF.Sigmoid, [pb, b_bias], [b_dst], bias=bias_t[:, fc:fc + 1], scale=1.0)

            T.linear(xrb, b_xrb, 2, wd[n * 256:(n + 1) * 256, :], c2i * 128, evac)
            if which == 0:
                return
            k.act(aa[:], rr[:], AF.Exp, [b_rr, b_c1], [b_aa], scale=c1[:, fc:fc + 1])
            k.act(rr[:], rr[:], AF.Exp, [b_rr, b_c2], [b_rr], scale=c2[:, fc:fc + 1])
            k.act(rr[:], rr[:], AF.Sqrt, [b_rr], [b_rr], bias=1.0, scale=-1.0)
            k.tt(uu[:], T.yt[c2i][0][:], ii[:], ALU.mult, [T.yt[c2i][1], b_ii], [b_uu])
            k.tt(uu[:], uu[:], rr[:], ALU.mult, [b_uu, b_rr], [b_uu])
            if first:
                k.ts(uu[:, 0:HALO], uu[:, 0:HALO], T.m[:, 0:1], ALU.mult, [b_uu, T.b_m], [b_uu])
                k.memset(hl[:, 0:5], 0.0, [b_hl])
                k.memset(pl[:, 0:5], 0.0, [b_pl])
                s0, hi, pi = 5, 0.0, 1.0
            else:
                s0, hi, pi = 0, hcar[:, fc:fc + 1], pcar[:, fc:fc + 1]
            k.scan(hl[:, s0:TB], aa[:, s0:TB], uu[:, s0:TB], hi, [b_aa, b_uu, b_hcar], [b_hl])
            k.scan(pl[:, s0:TB], aa[:, s0:TB], zer[:, s0:TB], pi, [b_aa, b_zer, b_pcar], [b_pl])
            k.cp(hcar[:, fc:fc + 1], hl[:, TB - 1:TB], [b_hl], [b_hcar], eng="pool")
            k.cp(pcar[:, fc:fc + 1], pl[:, TB - 1:TB], [b_pl], [b_pcar], eng="pool")
            k.dma(hlv[:, fc, g0:g0 + TB], hl[:], [b_hl], (), eng="pool")
            k.dma(plv[:, fc, g0:g0 + TB], pl[:], [b_pl], (), eng="pool")
            if blk == NB - 1:
                k.cp(ext[:, fc, 0:1], hl[:, TB - 4:TB - 3], [b_hl], [b_ext], eng="pool")
                k.cp(ext[:, fc, 1:2], pl[:, TB - 4:TB - 3], [b_pl], [b_ext], eng="pool")

        if blk + 1 < NB:
            load_o(blk + 1, ob_x, b_ob_x)
        T.run_stage(reqs, rg_body)
    k.dma(exv[:, :, :], ext[:], [b_ext], (), eng="pool")


def l3_io(k, W):
    d = {}
    for name, shape in (("srank", [128, 4]), ("oms", [128, 4]), ("w_out1", [1024, 1024]), ("fg1", [128, 8]),
                        ("w_up1", [1024, 6144]), ("cw1", [128, 48, 3]), ("cb1", [128, 48]), ("w_down1", [3072, 1024]),
                        ("fin", [128, 8])):
        d[name] = k.din(name, shape)
    return d


def emit_l3(k, W, d, m_d, h1w, ggw, hlw, plw, exg, out_o):
    nc = k.nc
    T = Trunk(k, W)
    NB, TB, TW = T.NB, T.TB, T.TW
    TOK = W - HALO
    w_out, fg_d, w_up, cw_d, cb_d, w_down, fin_d = (d["w_out1"], d["fg1"], d["w_up1"], d["cw1"], d["cb1"], d["w_down1"], d["fin"])
    k.dma(T.m[:], m_d[:, :], (), [T.b_m])
    fg, b_fg = T.small("fg", fg_d[:, :], [128, 8])
    cw, b_cw = T.small("cw", cw_d[:, :, :], [128, 48, 3])
    cb, b_cb = T.small("cb", cb_d[:, :], [128, 48])
    fin, b_fin = T.small("fin", fin_d[:, :], [128, 8])
    sr, b_sr = T.small("sr", d["srank"][:, :], [128, 4])
    oms, b_oms = T.small("oms", d["oms"][:, :], [128, 4])
    pe, b_pe = k.sb([128, 4, 8, 2], F32, "pe")
    exv = exg.rearrange("(r kc p) t -> p r kc t", p=128, kc=8)
    for r in range(4):
        k.dma(pe[:, r, :, :], exv[:, r, :, :], (), [b_pe])
    Hc, b_Hc = k.sb([128, 8], F32, "Hc")
    Pm, b_Pm = k.sb([128, 8], F32, "Pm")
    Em, b_Em = k.sb([128, 8], F32, "Em")
    k.memset(Hc[:], 0.0, [b_Hc])
    for r in range(4):
        k.ts(Pm[:], pe[:, r, :, 1], sr[:, r:r + 1], ALU.mult, [b_pe, b_sr], [b_Pm], s2=oms[:, r:r + 1], op1=ALU.add)
        k.ts(Em[:], pe[:, r, :, 0], sr[:, r:r + 1], ALU.mult, [b_pe, b_sr], [b_Em])
        k.tt(Hc[:], Hc[:], Pm[:], ALU.mult, [b_Hc, b_Pm], [b_Hc])
        k.tt(Hc[:], Hc[:], Em[:], ALU.add, [b_Hc, b_Em], [b_Hc])
    yb, b_yb = T.hn, T.b_hn
    ybufs = [(k.sb([128, TB], F32, "hl%d" % i), k.sb([128, TB], F32, "pl%d" % i), k.sb([128, TB], BF16, "ggt%d" % i)) for i in range(2)]
    outt, b_outt = T.h, T.b_h

    h1v = h1w.rearrange("(kc p) s -> p kc s", p=128)
    ggv = ggw.rearrange("(kc p) s -> p kc s", p=128)
    hlv = hlw.rearrange("(kc p) s -> p kc s", p=128)
    plv = plw.rearrange("(kc p) s -> p kc s", p=128)
    outv = out_o.rearrange("(kc p) s -> p kc s", p=128)

    for blk in range(NB):
        first = blk == 0
        g0 = blk * TB
        k.dma(T.h[:, 0:4, :], h1v[:, 0:4, g0:g0 + TB], (), [T.b_h])
        k.dma(T.h[:, 4:8, :], h1v[:, 4:8, g0:g0 + TB], (), [T.b_h])
        for fc in range(8):
            (hl, b_hl), (pl, b_pl), (ggt, b_ggt) = ybufs[fc % 2]
            k.dma(hl[:], hlv[:, fc, g0:g0 + TB], (), [b_hl])
            k.dma(pl[:], plv[:, fc, g0:g0 + TB], (), [b_pl])
            k.dma(ggt[:], ggv[:, fc, g0:g0 + TB], (), [b_ggt])
            k.stt(hl[:], pl[:], Hc[:, fc:fc + 1], hl[:], ALU.mult, ALU.add, [b_pl, b_Hc, b_hl], [b_hl])
            k.tt(yb[:, fc, :], hl[:], ggt[:], ALU.mult, [b_hl, b_ggt], [b_yb])
        T.plan([(w_out, 8, fc * 128) for fc in range(8)] + T.ffn_reqs(w_up, w_down))

        def wo_body(fc):
            def evac(nt, cs, ps, pb, fc=fc):
                k.tt(T.h[:, fc, cs], T.h[:, fc, cs], ps, ALU.add, [T.b_h, pb], [T.b_h])

            T.linear(yb, b_yb, 8, w_out, fc * 128, evac)

        T.run_stage([(w_out, 8, fc * 128) for fc in range(8)], wo_body)
        T.ffn(first, fg, b_fg, w_up, cw, b_cw, cb, b_cb, w_down, True)
        T.rmsnorm(fin, b_fin, out_f32=(outt, b_outt))
        lo = HALO if first else 0
        k.dma(outv[:, :, g0 + lo - HALO:g0 + TB - HALO], outt[:, :, lo:TB], [b_outt], (), eng="pool")


_F = {}


def build_fused(S):
    TOK = S // 4
    W = TOK + HALO
    nc = bass.Bass("TRN2", target_bir_lowering=False)
    k = K(nc)
    io1 = l1_io(k, S)
    io2 = l2_io(k, W)
    io3 = l3_io(k, W)
    NCH = max(1, S // OCS)
    osrc = [k.dint("osrc%d" % j, [256, min(S, OCS)], BF16) for j in range(NCH)]
    og = [k.dint("og%d" % j, [1024, min(S, OCS)], BF16) for j in range(NCH)]
    h1 = k.dint("h1s", [1024, W])
    gg = k.dint("ggs", [1024, W], BF16)
    hl = k.dint("hls", [1024, W])
    pl = k.dint("pls", [1024, W])
    exs = k.dint("exs", [1024, 2])
    exg = k.dint("exg", [4096, 2])
    out = k.dout("outT", [1024, TOK])
    groups = [[0, 1, 2, 3], [4, 5, 6, 7]]
    upto = int(os.environ.get("FUSE_UPTO", "3"))
    tile_bufs = {}

    def o_out(g, otile, b_ot):
        j, off = (g * 512) // OCS, (g * 512) % OCS
        ov = osrc[j].rearrange("(r p) s -> p r s", p=64)
        bb = Buf()
        tile_bufs.setdefault(j, []).append(bb)
        k.dma(ov[:, :, off:off + 512], otile[:], [b_ot], [bb], eng="pool")
        if off + 512 == min(S, OCS):
            k.P.dma((lambda j=j: nc.gpsimd.collective_compute("AllGather", ALU.bypass, replica_groups=groups,
                                                               ins=[osrc[j][:, :]], outs=[og[j][:, :]])),
                    tile_bufs[j], (), eng="pool", inc=1)

    emit_l1(k, S, io1, o_out)
    k.end_phase()
    emit_l2(k, W, S, io2, og, h1, gg, hl, pl, exs)
    k.end_phase()
    if upto == 2:
        return nc, k.P.finish()
    if os.environ.get("NOCC2"):
        k.dma(exg[0:1024, :], exs[:, :], (), ())
    else:
        k.P.dma(lambda: nc.gpsimd.collective_compute("AllGather", ALU.bypass, replica_groups=groups, ins=[exs[:, :]], outs=[exg[:, :]]),
                (), (), eng="pool", inc=1)
    k.P.barrier()
    emit_l3(k, W, io3, io2["m"], h1, gg, hl, pl, exg, out)
    cnt = k.P.finish()
    return nc, cnt


def _pk(v):
    return np.ascontiguousarray(v.reshape(-1, 128).T)


def _pkw(w):
    t, C = w.shape
    return np.ascontiguousarray(w.T.reshape(C // 128, 128, t).transpose(1, 0, 2))


def kernel(x, mix_norm_g, ffn_norm_g, final_norm_g,
           ev_w_in, ev_w_gk2, ev_b_gk2, ev_gla_norm_g, ev_w_out,
           od_w_in, od_conv_w, od_conv_b, od_w_a, od_b_a, od_w_x, od_b_x, od_lambda, od_w_out,
           ffn_w_up, ffn_conv_w, ffn_conv_b, ffn_w_down):
    f = lambda a: np.asarray(a, dtype=np.float32)
    x = f(x)
    Bsz, S, D = x.shape
    TOK = S // 4
    W = TOK + HALO
    if S not in _F:
        _F[S] = build_fused(S)
    nc, _ = _F[S]
    consts = l1_consts(S)
    w = f(ev_w_in)[0]
    gn = _pk(f(mix_norm_g)[0])
    perm = []
    for g in range(4):
        for j in range(4):
            head = 2 * g + (j % 2)
            base = head * 64 if j < 2 else 512 + head * 64
            perm += list(range(base, base + 64))
    w_out0 = np.ascontiguousarray(f(ev_w_out)[0][np.asarray(perm)])
    shared = {
        "gn": gn, "w_out0": w_out0, "fg0": _pk(f(ffn_norm_g)[0]), "w_up0": f(ffn_w_up)[0],
        "cw0": _pkw(f(ffn_conv_w)[0]), "cb0": _pk(f(ffn_conv_b)[0]), "w_down0": f(ffn_w_down)[0],
        "mg": _pk(f(mix_norm_g)[1]), "w_in": f(od_w_in)[0], "rcw": _pkw(f(od_conv_w)[0]), "rcb": _pk(f(od_conv_b)[0]),
        "wa": np.ascontiguousarray(f(od_w_a)[0].reshape(1024, 256)), "ba": _pk(f(od_b_a)[0]),
        "wx": np.ascontiguousarray(f(od_w_x)[0].reshape(1024, 256)), "bx": _pk(f(od_b_x)[0]),
        "lam": _pk(f(od_lambda)[0]),
        "w_out1": f(od_w_out)[0], "fg1": _pk(f(ffn_norm_g)[1]), "w_up1": f(ffn_w_up)[1],
        "cw1": _pkw(f(ffn_conv_w)[1]), "cb1": _pk(f(ffn_conv_b)[1]), "w_down1": f(ffn_w_down)[1],
        "fin": _pk(f(final_norm_g)),
    }
    shared.update(consts)
    in_maps = []
    for b in range(Bsz):
        xTb = np.ascontiguousarray(x[b].T)
        for c in range(4):
            h0, h1 = 2 * c, 2 * c + 1
            cols = []
            for base in (0, 512, 1024):
                cols += [np.arange(base + h0 * 64, base + h0 * 64 + 64), np.arange(base + h1 * 64, base + h1 * 64 + 64)]
            for base in (1536, 1792):
                cols += [np.arange(base + h0 * 32, base + h0 * 32 + 32), np.arange(base + h1 * 32, base + h1 * 32 + 32)]
            cols += [np.arange(2048 + h0 * 64, 2048 + h0 * 64 + 64), np.arange(2048 + h1 * 64, 2048 + h1 * 64 + 64)]
            cols += [np.arange(2560, 2576)]
            cols += [np.arange(2576 + h0 * 64, 2576 + h0 * 64 + 64), np.arange(2576 + h1 * 64, 2576 + h1 * 64 + 64)]
            cols = np.concatenate(cols)
            t0 = c * TOK
            xw = np.zeros((1024, W), np.float32)
            if c == 0:
                xw[:, HALO:] = xTb[:, 0:TOK]
            else:
                xw[:] = xTb[:, t0 - HALO:t0 + TOK]
            sel = np.zeros((128, 4), np.float32)
            sel[:, c] = 1.0
            sr = np.zeros((128, 4), np.float32)
            sr[:, :c] = 1.0
            m = dict(shared)
            m.update({
                "xT": xTb, "w1": np.ascontiguousarray(w[:, cols]),
                "wgk2": np.ascontiguousarray(f(ev_w_gk2)[0][:, h0 * 32:h0 * 32 + 64]),
                "bgk2": np.ascontiguousarray(f(ev_b_gk2)[0][h0 * 32:h0 * 32 + 64].reshape(64, 1)),
                "glag": np.ascontiguousarray(f(ev_gla_norm_g)[0][h0:h0 + 2].T),
                "xTw": xw, "m": np.full((128, 1), 0.0 if c == 0 else 1.0, np.float32),
                "sel": sel, "srank": sr, "oms": 1.0 - sr,
            })
            in_maps.append(m)
    res = run_bass_kernel_spmd(nc, in_maps, core_ids=list(range(len(in_maps)))).results
    out = np.zeros((Bsz, S, D), np.float32)
    for b in range(Bsz):
        for c in range(4):
            out[b, c * TOK:(c + 1) * TOK, :] = np.asarray(res[b * 4 + c]["outT"]).T
    return out
```

```python
import contextlib
import os
import numpy as np
import ml_dtypes
import concourse.bass as bass
import concourse.mybir as mybir
from concourse.bass_utils import run_bass_kernel_spmd

F32 = mybir.dt.float32
BF16 = mybir.dt.bfloat16
AF = mybir.ActivationFunctionType
ALU = mybir.AluOpType
AX = mybir.AxisListType

SAME_SYNC = True
N_DMA_SEMS = 24
SEM_EPOCH = 2000
NEG = -240000.0
EPS = 1e-6


class Buf:
    __slots__ = ("name", "lw", "rd")

    def __init__(self, name=""):
        self.name = name
        self.lw = None
        self.rd = []


class Prog:
    ENGS = ("pe", "act", "dve", "pool", "sp")

    def __init__(self, nc):
        self.nc = nc
        self.h = {"pe": nc.tensor, "act": nc.scalar, "dve": nc.vector, "pool": nc.gpsimd, "sp": nc.sync}
        self.items = {e: [] for e in self.ENGS}
        self.known = {}
        self.sem = {e: nc.alloc_semaphore("s_" + e) for e in self.ENGS}
        self.dsem = [nc.alloc_semaphore("d%d" % i) for i in range(N_DMA_SEMS)]
        self.duse = [0] * N_DMA_SEMS
        self.dval = [0] * N_DMA_SEMS
        self.pending = {e: [] for e in self.ENGS}
        self.rank = {e: [] for e in self.ENGS}
        self.emitted = {e: 0 for e in self.ENGS}
        self.esem = {}
        self.dnext = 0

    def _need(self, eng, tok, waits):
        if tok is None:
            return
        if tok[0] == "c":
            _, e2, seq = tok
            if e2 == eng and (eng == "pe" or not SAME_SYNC):
                return
            key = (eng, "c", e2)
            if self.known.get(key, -1) >= seq:
                return
            self.known[key] = seq
            self.items[e2][seq]["flag"] = True
            waits.append(tok)
        else:
            _, idx, val = tok
            key = (eng, "d", idx)
            if self.known.get(key, -1) >= val:
                return
            self.known[key] = val
            waits.append(tok)

    def _deps(self, eng, reads, writes):
        waits = []
        for b in reads:
            self._need(eng, b.lw, waits)
        for b in writes:
            self._need(eng, b.lw, waits)
            for t in b.rd:
                self._need(eng, t, waits)
        return waits

    def op(self, eng, fn, reads=(), writes=()):
        waits = self.pending[eng] + self._deps(eng, reads, writes)
        self.pending[eng] = []
        seq = len(self.items[eng])
        self.items[eng].append({"waits": waits, "fn": fn, "flag": False, "dma": None})
        tok = ("c", eng, seq)
        for b in reads:
            b.rd.append(tok)
        for b in writes:
            b.lw = tok
            b.rd = []
        return tok

    def dma(self, fn, reads=(), writes=(), eng="sp", inc=16):
        idx = self.dnext
        self.dnext = (self.dnext + 1) % N_DMA_SEMS
        pv = self.dval[idx]
        waits = self.pending[eng] + self._deps(eng, reads, writes)
        self.pending[eng] = []
        if pv > 0:
            self._need(eng, ("d", idx, pv), waits)
        self.duse[idx] += 1
        self.dval[idx] = pv + inc
        tok = ("d", idx, pv + inc)
        self.items[eng].append({"waits": waits, "fn": fn, "flag": False, "dma": idx, "inc": inc})
        for b in reads:
            b.rd.append(tok)
        for b in writes:
            b.lw = tok
            b.rd = []
        return tok

    def _all_tokens(self):
        toks = []
        for e in ("pe", "act", "dve", "pool"):
            items = self.items[e]
            for i in range(len(items) - 1, -1, -1):
                if items[i]["dma"] is None and items[i]["fn"] is not None or (items[i]["dma"] is None and i < self.emitted[e]):
                    toks.append(("c", e, i))
                    break
        for idx in range(N_DMA_SEMS):
            if self.dval[idx]:
                toks.append(("d", idx, self.dval[idx]))
        return toks

    def barrier(self, engines=None):
        toks = self._all_tokens()
        for e in (engines or self.ENGS):
            for t in toks:
                if t[0] == "c" and t[1] == e:
                    continue
                self._need(e, t, self.pending[e])

    def _sem_of(self, e, r):
        ep = (r - 1) // SEM_EPOCH
        if (e, ep) not in self.esem:
            self.esem[(e, ep)] = self.sem[e] if ep == 0 else self.nc.alloc_semaphore("s_%s_%d" % (e, ep))
        return self.esem[(e, ep)], (r - 1) % SEM_EPOCH + 1

    def flush(self):
        for e in self.ENGS:
            items = self.items[e]
            rk = self.rank[e]
            c = rk[-1] if rk else 0
            for i in range(len(rk), len(items)):
                if items[i]["flag"]:
                    c += 1
                rk.append(c)
        for e in self.ENGS:
            h = self.h[e]
            items = self.items[e]
            for i in range(self.emitted[e], len(items)):
                it = items[i]
                for t in it["waits"]:
                    if t[0] == "c":
                        sm, v = self._sem_of(t[1], self.rank[t[1]][t[2]])
                        h.wait_ge(sm, v)
                    else:
                        h.wait_ge(self.dsem[t[1]], t[2])
                if it["fn"] is None:
                    continue
                ins = it["fn"]()
                if it["dma"] is not None:
                    ins.then_inc(self.dsem[it["dma"]], it["inc"])
                elif it["flag"]:
                    sm, _ = self._sem_of(e, self.rank[e][i])
                    ins.then_inc(sm, 1)
                it["fn"] = None
            self.emitted[e] = len(items)

    def finish(self):
        self.barrier(engines=("sp",))
        self.items["sp"].append({"waits": self.pending["sp"], "fn": None, "flag": False, "dma": None})
        self.pending["sp"] = []
        self.flush()
        return {e: (len(self.items[e]), self.rank[e][-1] if self.rank[e] else 0) for e in self.ENGS}


class K:
    def __init__(self, nc):
        self.nc = nc
        self.P = Prog(nc)
        self._n = 0
        self.stack = contextlib.ExitStack()

    def end_phase(self):
        self.P.barrier()
        self.P.flush()
        self.stack.close()
        self.stack = contextlib.ExitStack()

    def sb(self, shape, dt, name=None):
        self._n += 1
        return self.stack.enter_context(self.nc.sbuf_tensor("sb%d_" % self._n + (name or "t"), list(shape), dt)), Buf(name or "")

    def ps(self, shape, name=None):
        self._n += 1
        return self.stack.enter_context(self.nc.psum_tensor("ps%d_" % self._n + (name or "p"), list(shape), F32)), Buf(name or "")

    def din(self, name, shape, dt=F32):
        return self.nc.dram_tensor(name, list(shape), dt, kind="ExternalInput").ap()

    def dint(self, name, shape, dt=F32, **kw):
        return self.nc.dram_tensor(name, list(shape), dt, kind="Internal", **kw).ap()

    def dout(self, name, shape, dt=F32):
        return self.nc.dram_tensor(name, list(shape), dt, kind="ExternalOutput").ap()

    def mm(self, out, lhsT, rhs, st, sp, r, w):
        nc = self.nc
        return self.P.op("pe", lambda: nc.tensor.matmul(out, lhsT=lhsT, rhs=rhs, start=st, stop=sp), r, w)

    def tr(self, out, in_, ident, r, w):
        nc = self.nc
        return self.P.op("pe", lambda: nc.tensor.transpose(out, in_, ident), r, w)

    def act(self, out, in_, func, r, w, bias=None, scale=None):
        nc = self.nc
        kw = {}
        if bias is not None:
            kw["bias"] = bias
        if scale is not None:
            kw["scale"] = scale
        return self.P.op("act", lambda: nc.scalar.activation(out=out, in_=in_, func=func, **kw), r, w)

    def tt(self, out, in0, in1, op, r, w, eng="dve"):
        h = self.P.h[eng]
        return self.P.op(eng, lambda: h.tensor_tensor(out=out, in0=in0, in1=in1, op=op), r, w)

    def ts(self, out, in0, s1, op0, r, w, s2=None, op1=None, eng="dve"):
        h = self.P.h[eng]
        if op1 is None:
            return self.P.op(eng, lambda: h.tensor_scalar(out=out, in0=in0, scalar1=s1, scalar2=None, op0=op0), r, w)
        return self.P.op(eng, lambda: h.tensor_scalar(out=out, in0=in0, scalar1=s1, scalar2=s2, op0=op0, op1=op1), r, w)

    def stt(self, out, in0, scalar, in1, op0, op1, r, w):
        nc = self.nc
        return self.P.op("dve", lambda: nc.vector.scalar_tensor_tensor(out=out, in0=in0, scalar=scalar, in1=in1, op0=op0, op1=op1), r, w)

    def cp(self, out, in_, r, w, eng="dve"):
        if eng == "act":
            nc = self.nc
            return self.P.op("act", lambda: nc.scalar.copy(out=out, in_=in_), r, w)
        h = self.P.h[eng]
        return self.P.op(eng, lambda: h.tensor_copy(out=out, in_=in_), r, w)

    def memset(self, ap, val, w, eng="pool"):
        h = self.P.h[eng]
        return self.P.op(eng, lambda: h.memset(ap, val), (), w)

    def scan(self, out, d0, d1, init, r, w):
        nc = self.nc
        return self.P.op("dve", lambda: nc.vector.tensor_tensor_scan(out=out, data0=d0, data1=d1, initial=init, op0=ALU.mult, op1=ALU.add), r, w)

    def dma(self, out, in_, r, w, eng="sp"):
        h = self.P.h[eng]
        return self.P.dma(lambda: h.dma_start(out=out, in_=in_), r, w, eng=eng)


def l1_io(k, S):
    d = {}
    for name, shape in (("xT", [1024, S]), ("gn", [128, 8]), ("w1", [1024, 784]), ("wgk2", [16, 64]), ("bgk2", [64, 1]),
                        ("glag", [64, 2]), ("cosd", [64, S]), ("sind", [64, S]), ("cmd", [128, 2048]), ("ed", [64, S]),
                        ("identd", [128, 128]), ("rmd", [64, 512]), ("amd", [64, 512]), ("hm2d", [64, 128]), ("hmd", [64, 2])):
        d[name] = k.din(name, shape)
    return d


def emit_l1(k, S, d, oT):
    NT = S // 512
    NKT = S // 128
    nc = k.nc
    P = k.P
    xT, gn_d, w1_d, wgk2_d, bgk2_d, glag_d = d["xT"], d["gn"], d["w1"], d["wgk2"], d["bgk2"], d["glag"]
    cos_d, sin_d, cm_d, e_d, id_d, rm_d, am_d, hm2_d, hm_d = (d["cosd"], d["sind"], d["cmd"], d["ed"], d["identd"], d["rmd"],
                                                              d["amd"], d["hm2d"], d["hmd"])

    xt, b_xt = k.sb([128, 8, 512], F32, "xt")
    rstd, b_rstd = k.sb([128, 512], F32, "rstd")
    lnv, b_lnv = rstd, b_rstd
    hn, b_hn = k.sb([128, 8, 512], BF16, "hn")
    sq, b_sq = hn, b_hn
    wb, b_wb = k.sb([128, 8, 784], BF16, "wb")
    gn, b_gn = k.sb([128, 8], F32, "gn")
    Kaug = [k.sb([128, S], BF16, "kaug%d" % h) for h in range(2)]
    KB = [[Buf() for _ in range(NT)] for h in range(2)]
    b_kE = [Buf(), Buf()]
    Vaug = [k.sb([128, NKT, 66], BF16, "vaug%d" % h) for h in range(2)]
    VB = [[Buf() for _ in range(NT)] for h in range(2)]
    b_vones = [Buf(), Buf()]
    Qaug = [[k.sb([128, 512], BF16, "qaug%d_%d" % (h, p)) for p in range(2)] for h in range(2)]
    kmT = [k.sb([64, 64], BF16, "kmT%d" % h) for h in range(2)]
    km32, b_km32 = k.sb([64, 2], F32, "km32")
    cosT, b_cos = k.sb([64, 512], F32, "cos")
    sinT, b_sin = k.sb([64, 512], F32, "sin")
    t1, b_t1 = k.sb([64, 512], F32, "t1")
    t2, b_t2 = k.sb([64, 512], F32, "t2")
    pTs = [k.sb([128, 512], BF16, "pT%d" % i) for i in range(4)]
    cm, b_cm = k.sb([128, 4, 512], BF16, "cm")
    id_f, b_idf = k.sb([128, 128], F32, "idf")
    id_b, b_idb = k.sb([128, 128], BF16, "idb")
    ones_b, b_onesb = k.sb([128, 128], BF16, "onesb")
    ones_f, b_onesf = k.sb([128, 64], F32, "onesf")
    bq, b_bq = k.sb([128, 4, 128], F32, "bq")
    gsb, b_gsb = k.sb([128, 4, 64], F32, "gsb")
    m8, b_m8 = k.sb([128, 4, 8], F32, "m8")
    fin_t, b_rden = k.sb([128, 512], F32, "fin")
    rden = fin_t
    osb, b_osb = fin_t[0:64, :], Buf("osb")
    otiles = [k.sb([64, 4, 512], BF16, "otile%d" % p) for p in range(2)]
    QG32, b_qg = k.sb([64, 512], F32, "qg32")
    KG32, b_kg = k.sb([64, 512], F32, "kg32")
    spl, b_spl = k.sb([64, 512], F32, "spl")
    bpos, b_bpos = k.sb([64, 512], F32, "bpos")
    eb, b_eb = k.sb([64, 512], F32, "eb")
    enb, b_enb = k.sb([64, 512], F32, "enb")
    Ac, b_ac = k.sb([64, 8], F32, "Ac")
    ke32, b_ke = k.sb([64, 512], F32, "ke32")
    qt, b_qt = k.sb([64, 512], BF16, "qt")
    kpad, b_kpad = k.sb([64, 2, 512], BF16, "kpad")
    khat, b_khat = k.sb([64, 512], BF16, "khat")
    rmk, b_rmk = k.sb([64, 512], F32, "rmk")
    amk, b_amk = k.sb([64, 512], BF16, "amk")
    hm2, b_hm2 = k.sb([64, 128], F32, "hm2")
    hm, b_hm = k.sb([64, 2], F32, "hm")
    attm, b_attm = k.sb([64, 2, 512], BF16, "attm")
    gvt, b_gvt = k.sb([64, 8, 128], BF16, "gvt")
    KTt, b_ktt = k.sb([64, 8, 64], BF16, "KTt")
    gk16, b_gk16 = k.sb([16, 512], BF16, "gk16")
    wgk2f, b_wgk2f = k.sb([16, 64], F32, "wgk2f")
    wgk2b, b_wgk2b = k.sb([16, 64], BF16, "wgk2b")
    nbg, b_nbg = k.sb([64, 1], F32, "nbg")
    glag, b_glag = k.sb([64, 2], F32, "glag")
    sbog, b_sbog = k.sb([64, 2, 512], BF16, "sbog")
    st32, b_st32 = k.sb([64, 128], F32, "st32")
    stall, b_stall = k.sb([64, 9, 128], BF16, "stall")
    stmp, b_stmp = k.sb([64, 128], F32, "stmp")
    o32, b_o32 = t1, b_t1
    osq, b_osq = k.sb([64, 512], BF16, "osq")
    on32, b_on32 = t2, b_t2
    B = [k.ps([128, 512], "bank%d" % i) for i in range(8)]

    stg = xt
    k.dma(id_f[:], id_d[:, :], (), [b_idf])
    k.cp(id_b[:], id_f[:], [b_idf], [b_idb])
    k.memset(ones_b[:], 1.0, [b_onesb])
    k.memset(ones_f[:], 1.0, [b_onesf])
    k.memset(bq[:], 0.0, [b_bq])
    k.memset(st32[:], 0.0, [b_st32])
    k.memset(stall[:], 0.0, [b_stall])
    k.dma(gn[:], gn_d[:, :], (), [b_gn])
    k.dma(glag[:], glag_d[:, :], (), [b_glag])
    k.dma(hm2[:], hm2_d[:, :], (), [b_hm2])
    k.dma(hm[:], hm_d[:, :], (), [b_hm])
    k.dma(rmk[:], rm_d[:, :], (), [b_rmk])
    k.dma(wgk2f[:], wgk2_d[:, :], (), [b_wgk2f])
    k.cp(wgk2b[:], wgk2f[:], [b_wgk2f], [b_wgk2b])
    k.dma(nbg[:], bgk2_d[:, :], (), [b_nbg])
    k.ts(nbg[:], nbg[:], -1.0, ALU.mult, [b_nbg], [b_nbg])
    k.dma(t1[:], am_d[:, :], (), [b_t1])
    k.cp(amk[:], t1[:], [b_t1], [b_amk])
    sflat = stg[:].rearrange("p a b -> p (a b)")
    k.dma(sflat[:, 0:2048], cm_d[:, :], (), [b_xt])
    k.cp(cm[:].rearrange("p a b -> p (a b)"), sflat[:, 0:2048], [b_xt], [b_cm])
    w1v = w1_d.rearrange("(kc p) f -> p kc f", p=128)
    for half in range(2):
        sv = sflat[:, 0:4 * 784].rearrange("p (a b) -> p a b", b=784)
        k.dma(sv, w1v[:, half * 4:(half + 1) * 4, :], (), [b_xt])
        k.cp(wb[:, half * 4:(half + 1) * 4, :], sv, [b_xt], [b_wb], eng="dve" if half == 0 else "pool")
    for pc in range(S // 2048):
        k.dma(sflat[64:128, 0:2048], e_d[:, pc * 2048:(pc + 1) * 2048], (), [b_xt])
        for h in range(2):
            k.cp(Kaug[h][0][64:128, pc * 2048:(pc + 1) * 2048], sflat[64:128, 0:2048], [b_xt], [b_kE[h]],
                 eng="dve" if h == 0 else "pool")
    for h in range(2):
        k.memset(Vaug[h][0][:, :, 64:65], 1.0, [b_vones[h]])
        k.memset(kmT[h][0][:], 0.0, [kmT[h][1]])

    xTv = xT.rearrange("(kc p) s -> p kc s", p=128)
    def proj(bank, M, col0, ncols=None):
        pt, pb = bank
        for kc in range(8):
            k.mm(pt[0:M, 0:512], wb[:, kc, col0:col0 + M], hn[:, kc, :], kc == 0, kc == 7, [b_wb, b_hn], [pb])
        return pt, pb

    FB0, FB1, FB2 = B[0], B[4], B[7]

    def gen_F(g):
        c0 = g * 512
        par = g % 2
        otile, b_ot = otiles[par]
        k.dma(xt[:, 0:4, :], xTv[:, 0:4, c0:c0 + 512], (), [b_xt])
        k.dma(xt[:, 4:8, :], xTv[:, 4:8, c0:c0 + 512], (), [b_xt])
        k.dma(cosT[:], cos_d[:, c0:c0 + 512], (), [b_cos])
        k.dma(sinT[:], sin_d[:, c0:c0 + 512], (), [b_sin])
        k.act(sq[:], xt[:], AF.Square, [b_xt], [b_sq])
        pt, pb = FB0
        for kc in range(8):
            k.mm(pt[:, :], ones_b[:], sq[:, kc, :], kc == 0, kc == 7, [b_onesb, b_sq], [pb])
        yield
        k.act(lnv[:], pt[:, :], AF.Ln, [pb], [b_lnv], bias=EPS, scale=1.0 / 1024)
        k.act(rstd[:], lnv[:], AF.Exp, [b_lnv], [b_rstd], scale=-0.5)
        for kc in range(8):
            k.stt(hn[:, kc, :], xt[:, kc, :], gn[:, kc:kc + 1], rstd[:], ALU.mult, ALU.mult, [b_xt, b_gn, b_rstd], [b_hn])
            if kc % 4 == 3:
                yield
        for idx in range(4):
            h = idx % 2
            isk = idx >= 2
            pt, pb = proj((FB1, FB2)[idx % 2], 64, idx * 64)
            yield
            if isk:
                dest = Kaug[h][0][0:64, c0:c0 + 512]
                dbuf = KB[h][g]
            else:
                dest = Qaug[h][par][0][0:64, :]
                dbuf = Qaug[h][par][1]
            k.tt(t1[:], pt[0:64, 0:512], cosT[:], ALU.mult, [pb, b_cos], [b_t1])
            k.tt(t2[0:32, :], pt[32:64, 0:512], sinT[32:64, :], ALU.mult, [pb, b_sin], [b_t2])
            k.tt(t2[32:64, :], pt[0:32, 0:512], sinT[0:32, :], ALU.mult, [pb, b_sin], [b_t2])
            k.tt(dest, t1[:], t2[:], ALU.add, [b_t1, b_t2], [dbuf])
            yield
        pt, pb = FB0
        for st in range(4):
            for kc in range(8):
                k.mm(pt[:, st * 128:(st + 1) * 128], hn[:, kc, st * 128:(st + 1) * 128], wb[:, kc, 256:384],
                     kc == 0, kc == 7, [b_hn, b_wb], [pb])
            yield
        pv_ = pt[:, 0:512].rearrange("p (a b) -> p a b", b=128)
        for h in range(2):
            k.cp(Vaug[h][0][:, 4 * g:4 * g + 4, 0:64], pv_[:, :, h * 64:(h + 1) * 64], [pb], [VB[h][g]], eng="act")
        pt, pb = proj(FB1, 128, 384)
        k.cp(QG32[:], pt[0:64, 0:512], [pb], [b_qg], eng="act")
        k.cp(KG32[:], pt[64:128, 0:512], [pb], [b_kg], eng="act")
        yield
        for c in range(8):
            pt, pb = FB2 if c < 4 else FB0
            for kc in range(8):
                k.mm(pt[0:64, (c % 4) * 128:(c % 4 + 1) * 128], hn[:, kc, c * 64:(c + 1) * 64], wb[:, kc, 512:640],
                     kc == 0, kc == 7, [b_hn, b_wb], [pb])
            if c % 2 == 1:
                yield
        for hf in range(2):
            pt, pb = FB2 if hf == 0 else FB0
            k.cp(gvt[:, hf * 4:(hf + 1) * 4, :].rearrange("p a b -> p (a b)"), pt[0:64, 0:512], [pb], [b_gvt], eng="act")
        pt, pb = proj(FB1, 16, 640)
        k.cp(gk16[:], pt[0:16, 0:512], [pb], [b_gk16], eng="act")
        k.mm(pt[0:64, 0:512], wgk2b[:], gk16[:], True, True, [b_wgk2b, b_gk16], [pb])
        k.act(spl[:], pt[0:64, 0:512], AF.Exp, [pb, b_nbg], [b_spl], bias=nbg[:, 0:1], scale=-1.0)
        k.act(spl[:], spl[:], AF.Ln, [b_spl], [b_spl], bias=1.0, scale=1.0)
        yield
        pt, pb = proj(FB2, 128, 656)
        k.act(sbog[:, 0, :], pt[0:64, 0:512], AF.Silu, [pb], [b_sbog])
        k.act(sbog[:, 1, :], pt[64:128, 0:512], AF.Silu, [pb], [b_sbog])
        yield
        k.scan(bpos[:], rmk[:], spl[:], 0.0, [b_rmk, b_spl], [b_bpos])
        k.act(eb[:], bpos[:], AF.Exp, [b_bpos], [b_eb], scale=-1.0 / 16)
        k.act(enb[:], bpos[:], AF.Exp, [b_bpos], [b_enb], scale=1.0 / 16)
        blast = bpos[:].rearrange("p (c t) -> p c t", t=64)[:, :, 63:64].rearrange("p c o -> p (c o)")
        k.act(Ac[:], blast, AF.Exp, [b_bpos], [b_ac], scale=-1.0 / 16)
        yield
        k.stt(qt[:], QG32[:], 32.0 ** -0.5, eb[:], ALU.mult, ALU.mult, [b_qg, b_eb], [b_qt])
        k.tt(ke32[:], KG32[:], enb[:], ALU.mult, [b_kg, b_enb], [b_ke])
        for h in range(2):
            k.ts(kpad[:, h, :], ke32[:], hm[:, h:h + 1], ALU.mult, [b_ke, b_hm], [b_kpad])
        yield
        for c in range(8):
            k.act(khat[:, c * 64:(c + 1) * 64], ke32[:, c * 64:(c + 1) * 64], AF.Copy, [b_ke, b_ac], [b_khat], scale=Ac[:, c:c + 1])
        yield
        pt, pb = FB0
        for c in range(8):
            k.mm(pt[0:64, c * 64:(c + 1) * 64], khat[:, c * 64:(c + 1) * 64], id_b[0:64, 0:64], True, True, [b_khat, b_idb], [pb])
        k.cp(KTt[:].rearrange("p a b -> p (a b)"), pt[0:64, 0:512], [pb], [b_ktt], eng="act")
        yield
        for h in range(2):
            pt, pb = (FB1, FB2)[h]
            for c in range(8):
                k.mm(pt[0:64, c * 64:(c + 1) * 64], kpad[:, h, c * 64:(c + 1) * 64], qt[:, c * 64:(c + 1) * 64], True, True,
                     [b_kpad, b_qt], [pb])
            k.tt(attm[:, h, :], pt[0:64, 0:512], amk[:], ALU.mult, [pb, b_amk], [b_attm])
            yield
        pso = [FB2, FB0]
        for c in range(8):
            psd, b_psd = FB0 if c < 4 else FB1
            for h in range(2):
                k.mm(psd[0:64, (c % 4) * 128 + h * 64:(c % 4) * 128 + (h + 1) * 64], KTt[:, c, :], gvt[:, c, h * 64:(h + 1) * 64],
                     True, True, [b_ktt, b_gvt], [b_psd])
            if c % 4 == 3:
                yield
        k.cp(stall[:, 0, :], stall[:, 8, :], [b_stall], [b_stall])
        for c in range(8):
            psd, b_psd = FB0 if c < 4 else FB1
            k.tt(stmp[:], psd[0:64, (c % 4) * 128:(c % 4 + 1) * 128], hm2[:], ALU.mult, [b_psd, b_hm2], [b_stmp])
            k.stt(st32[:], st32[:], Ac[:, c:c + 1], stmp[:], ALU.mult, ALU.add, [b_st32, b_ac, b_stmp], [b_st32])
            k.cp(stall[:, c + 1, :], st32[:], [b_st32], [b_stall])
            if c % 2 == 1:
                yield
        for c in range(8):
            for h in range(2):
                po, pbo = pso[h]
                k.mm(po[0:64, c * 64:(c + 1) * 64], gvt[:, c, h * 64:(h + 1) * 64], attm[:, h, c * 64:(c + 1) * 64], True, False,
                     [b_gvt, b_attm], [pbo])
                k.mm(po[0:64, c * 64:(c + 1) * 64], stall[:, c, h * 64:(h + 1) * 64], qt[:, c * 64:(c + 1) * 64], False, True,
                     [b_stall, b_qt], [pbo])
            if c % 2 == 1:
                yield
        for h in range(2):
            po, pbo = pso[h]
            k.cp(o32[:], po[0:64, 0:512], [pbo], [b_o32], eng="act")
            k.act(osq[:], po[0:64, 0:512], AF.Square, [pbo], [b_osq])
            pt, pb = FB1
            k.mm(pt[0:64, 0:512], ones_b[0:64, 0:64], osq[:], True, True, [b_onesb, b_osq], [pb])
            yield
            k.act(lnv[0:64, :], pt[0:64, 0:512], AF.Ln, [pb], [b_lnv], bias=EPS, scale=1.0 / 64)
            k.act(lnv[0:64, :], lnv[0:64, :], AF.Exp, [b_lnv], [b_lnv], scale=-0.5)
            k.tt(on32[:], o32[:], lnv[0:64, :], ALU.mult, [b_o32, b_lnv], [b_on32])
            k.stt(otile[:, 2 + h, :], on32[:], glag[:, h:h + 1], sbog[:, h, :], ALU.mult, ALU.mult, [b_on32, b_glag, b_sbog], [b_ot])
            yield
        for h in range(2):
            KA, _ = Kaug[h]
            QA, b_QA = Qaug[h][par]
            kmt, b_kmt = kmT[h]
            k.P.op("dve", (lambda o=km32[:], i=KA[0:64, c0:c0 + 512].rearrange("p (a b) -> p a b", b=256):
                           nc.vector.tensor_reduce(out=o, in_=i, axis=AX.X, op=ALU.add)), [KB[h][g]], [b_km32])
            k.cp(kmt[:, 2 * g:2 * g + 2], km32[:], [b_km32], [b_kmt])
            pg, b_pg = FB1
            for st in range(4):
                k.mm(pg[:, st * 64:(st + 1) * 64], QA[0:64, st * 128:(st + 1) * 128], kmt[:, :], True, True, [b_QA, b_kmt], [b_pg])
            yield
            k.memset(gsb[:], -1e30, [b_gsb], eng="dve")
            for st in range(4):
                blk = 2 * g + st // 2
                if blk > 0:
                    k.cp(gsb[:, st, 0:blk], pg[:, st * 64:st * 64 + blk], [b_pg], [b_gsb])
            yield
            for st in range(4):
                blk = 2 * g + st // 2
                k.P.op("dve", (lambda o=m8[:, st, :], i=gsb[:, st, :]: nc.vector.max(out=o, in_=i)), [b_gsb], [b_m8])
                k.ts(bq[:, st, 64:128], gsb[:, st, :], m8[:, st, 2:3], ALU.is_ge, [b_gsb, b_m8], [b_bq], s2=-NEG, op1=ALU.mult)
            yield
            k.ts(bq[:, :, 64:128], bq[:, :, 64:128], NEG, ALU.add, [b_bq], [b_bq])
            for st in range(4):
                blk = 2 * g + st // 2
                k.memset(bq[:, st, 64 + blk:65 + blk], 0.0, [b_bq], eng="dve")
            pg2, b_pg2 = FB2
            for st in range(4):
                k.tr(pg2[:, st * 128:(st + 1) * 128], bq[:, st, :], id_f[:], [b_bq, b_idf], [b_pg2])
            k.cp(QA[64:128, :], pg2[64:128, 0:512], [b_pg2], [b_QA], eng="act")
            yield

    def gen_A(g):
        par = g % 2
        otile, b_ot = otiles[par]
        for h in range(2):
            KA, _ = Kaug[h]
            VA, _ = Vaug[h]
            QA, b_QA = Qaug[h][par]
            pO, b_pO = B[5 + h]
            nkt = 4 * g + 4
            LA = 2

            def qk(kt):
                j = kt - 4 * g
                q0 = 256 if j >= 2 else 0
                pS, b_pS = B[1 + (kt % 3)]
                k.mm(pS[:, q0:512], KA[:, kt * 128:(kt + 1) * 128], QA[:, q0:512], True, j < 0,
                     [KB[h][kt // 4], b_kE[h], b_QA], [b_pS])
                if j >= 0:
                    k.mm(pS[:, q0:512], id_b[:], cm[:, j, q0:512], False, True, [b_idb, b_cm], [b_pS])

            def pv(kt):
                j = kt - 4 * g
                q0 = 256 if j >= 2 else 0
                pS, b_pS = B[1 + (kt % 3)]
                pTt, b_pT = pTs[kt % 4]
                k.act(pTt[:, q0:512], pS[:, q0:512], AF.Exp, [b_pS], [b_pT], scale=0.125)
                k.mm(pO[0:65, q0:512], VA[:, kt, 0:65], pTt[:, q0:512], kt == 0, kt == nkt - 1,
                     [VB[h][kt // 4], b_vones[h], b_pT], [b_pO])

            for i in range(nkt + LA):
                if i < nkt:
                    qk(i)
                if i >= LA:
                    pv(i - LA)
                yield
            k.P.op("dve", (lambda o=rden[64:65, :], i=pO[64:65, 0:512]: nc.vector.reciprocal(out=o, in_=i)), [b_pO], [b_rden])
            pt, pb = B[1]
            k.mm(pt[0:64, 0:512], ones_f[64:65, 0:64], rden[64:65, :], True, True, [b_onesf, b_rden], [pb])
            k.cp(osb[:], pO[0:64, 0:512], [b_pO], [b_osb], eng="act")
            k.tt(otile[:, h, :], osb[:], pt[0:64, 0:512], ALU.mult, [b_osb, pb], [b_ot])
            yield
        oT(g, otile, b_ot)

    def drive(a, f=None, ratio=1):
        n = 0
        a_live, f_live = a is not None, f is not None
        while a_live or f_live:
            if a_live:
                try:
                    next(a)
                except StopIteration:
                    a_live = False
                n += 1
            if f_live and (not a_live or n % ratio == 0):
                try:
                    next(f)
                except StopIteration:
                    f_live = False

    NF_STEPS = 48
    drive(None, gen_F(0))
    for g in range(NT):
        na = 2 * (4 * g + 4 + 3)
        drive(gen_A(g), gen_F(g + 1) if g + 1 < NT else None, ratio=max(1, na // NF_STEPS))


def l1_consts(S):
    half = 32
    inv = (10000.0 ** (-np.arange(half, dtype=np.float32) / half)).astype(np.float32)
    ang = np.arange(S, dtype=np.float32)[None, :] * inv[:, None]
    cos = np.cos(ang).astype(np.float32)
    sin = np.sin(ang).astype(np.float32)
    cosd = np.concatenate([cos, cos], 0)
    sind = np.concatenate([sin, -sin], 0)
    kk = np.arange(128)[:, None]
    qq = np.arange(512)[None, :]
    cm = np.concatenate([np.where(qq < j * 128 + kk, NEG, 0.0) for j in range(4)], 1).astype(np.float32)
    ed = (np.arange(S)[None, :] // 256 == np.arange(64)[:, None]).astype(np.float32)
    ident = np.eye(128, dtype=np.float32)
    rm = np.tile((np.arange(512) % 64 != 0).astype(np.float32)[None, :], (64, 1))
    s_ = np.arange(64)[:, None]
    t_ = np.arange(64)[None, :]
    am = np.tile((s_ <= t_).astype(np.float32), (1, 8))
    hm = np.zeros((64, 2), np.float32)
    hm[0:32, 0] = 1
    hm[32:64, 1] = 1
    hm2 = np.repeat(hm, 64, axis=1)
    return dict(cosd=cosd, sind=sind, cmd=cm, ed=ed, identd=ident, rmd=rm, amd=am, hm2d=hm2, hmd=hm)


HALO = 8
OCS = 2048


def choose_tiles(W):
    if W == 4104:
        return 4, 3, 342
    if W == 520:
        return 2, 1, 260
    raise ValueError(W)


class Trunk:
    def __init__(self, k, W):
        self.k = k
        self.nc = k.nc
        self.W = W
        self.NB, self.NTB, self.TW = choose_tiles(W)
        self.TB = self.NTB * self.TW
        TB, TW = self.TB, self.TW
        self.h, self.b_h = k.sb([128, 8, TB], F32, "h")
        self.hn, self.b_hn = k.sb([128, 8, TB], BF16, "hn")
        self.act, self.b_act = k.sb([128, 24, TB], BF16, "act")
        self.wst = [k.sb([128, 8, 128], F32, "wst%d" % i) for i in range(3)]
        self.wbf = [k.sb([128, 24, 128], BF16, "wbf%d" % i) for i in range(3)]
        self.wi = 0
        self.si = 0
        self.wq = []
        self.todo = []
        self.pc = [k.sb([128, 4 + TB], F32, "pc%d" % i) for i in range(2)]
        self.pci = 0
        self.yt = [k.sb([128, TB], F32, "yt%d" % i) for i in range(2)]
        self.gel, self.b_gel = k.sb([128, TB], F32, "gel")
        self.sqt, self.b_sqt = k.sb([128, 8, TW], BF16, "sqt")
        self.lnv, self.b_lnv = k.sb([128, TW], F32, "lnvt")
        self.rstd, self.b_rstd = k.sb([128, TW], F32, "rstdt")
        self.ones_b, self.b_ones = k.sb([128, 128], BF16, "onesb")
        self.m, self.b_m = k.sb([128, 1], F32, "hmask")
        self.carry, self.b_carry = k.sb([128, 48, 2], F32, "carry")
        self.B = [k.ps([128, 512], "bank%d" % i) for i in range(8)]
        self.bi = 0
        k.memset(self.ones_b[:], 1.0, [self.b_ones])
        k.memset(self.carry[:], 0.0, [self.b_carry])

    def bank(self):
        b = self.B[self.bi]
        self.bi = (self.bi + 1) % 8
        return b

    def small(self, name, dram_ap, shape):
        t, b = self.k.sb(shape, F32, name)
        self.k.dma(t[:], dram_ap, (), [b])
        return t, b

    def request(self, wd, KC, col0):
        k = self.k
        wb, b_wb = self.wbf[self.wi]
        self.wi = (self.wi + 1) % 3
        wv = wd.rearrange("(kc p) f -> p kc f", p=128)
        for k0 in range(0, KC, 8):
            k1 = min(KC, k0 + 8)
            st, b_st = self.wst[self.si]
            self.si = (self.si + 1) % 3
            k.dma(st[:, 0:k1 - k0, :], wv[:, k0:k1, col0:col0 + 128], (), [b_st])
            k.cp(wb[:, k0:k1, :], st[:, 0:k1 - k0, :], [b_st], [b_wb], eng="act")
        self.wq.append((wb, b_wb))

    def plan(self, reqs):
        assert not self.wq and not getattr(self, "todo", None), "previous plan not fully consumed"
        self.todo = list(reqs)
        for _ in range(2):
            if self.todo:
                self.request(*self.todo.pop(0))

    def run_stage(self, reqs, body):
        for i in range(len(reqs)):
            body(i)

    def rmsnorm(self, g_t, b_g, out_f32=None):
        k = self.k
        TW = self.TW
        for nt in range(self.NTB):
            cs = slice(nt * TW, (nt + 1) * TW)
            k.act(self.sqt[:], self.h[:, :, cs], AF.Square, [self.b_h], [self.b_sqt])
            pt, pb = self.bank()
            for kc in range(8):
                k.mm(pt[:, 0:TW], self.ones_b[:], self.sqt[:, kc, :], kc == 0, kc == 7, [self.b_ones, self.b_sqt], [pb])
            k.act(self.lnv[:], pt[:, 0:TW], AF.Ln, [pb], [self.b_lnv], bias=EPS, scale=1.0 / 1024)
            k.act(self.rstd[:], self.lnv[:], AF.Exp, [self.b_lnv], [self.b_rstd], scale=-0.5)
            for kc in range(8):
                if out_f32 is None:
                    k.stt(self.hn[:, kc, cs], self.h[:, kc, cs], g_t[:, kc:kc + 1], self.rstd[:], ALU.mult, ALU.mult,
                          [self.b_h, b_g, self.b_rstd], [self.b_hn])
                else:
                    k.stt(out_f32[0][:, kc, cs], self.h[:, kc, cs], g_t[:, kc:kc + 1], self.rstd[:], ALU.mult, ALU.mult,
                          [self.b_h, b_g, self.b_rstd], [out_f32[1]])

    def linear(self, src, b_src, KC, wd, col0, evac):
        k = self.k
        TW = self.TW
        if self.todo:
            self.request(*self.todo.pop(0))
        wb, b_wb = self.wq.pop(0)
        for nt in range(self.NTB):
            cs = slice(nt * TW, (nt + 1) * TW)
            pt, pb = self.bank()
            for kc in range(KC):
                k.mm(pt[:, 0:TW], wb[:, kc, :], src[:, kc, cs], kc == 0, kc == KC - 1, [b_wb, b_src], [pb])
            evac(nt, cs, pt[:, 0:TW], pb)

    @staticmethod
    def ffn_reqs(w_up, w_down):
        return ([(w_up, 8, (part * 24 + j) * 128) for j in range(24) for part in range(2)]
                + [(w_down, 24, fc * 128) for fc in range(8)])

    def conv(self, pc, b_pc, ntap, cw_t, b_cw, cb_t, b_cb, idx, yt, b_yt):
        k = self.k
        TB = self.TB
        last = ntap - 1
        k.act(yt[:], pc[:, last:last + TB], AF.Identity, [b_pc, b_cw, b_cb], [b_yt],
              bias=cb_t[:, idx:idx + 1], scale=cw_t[:, idx, last:last + 1])
        for i in range(last - 1, -1, -1):
            k.stt(yt[:], pc[:, i:i + TB], cw_t[:, idx, i:i + 1], yt[:], ALU.mult, ALU.add, [b_pc, b_cw, b_yt], [b_yt])

    def ffn(self, first, gn_t, b_gn, w_up, cw_t, b_cw, cb_t, b_cb, w_down, mask_halo, down_hook=None):
        k = self.k
        TB, TW = self.TB, self.TW
        self.rmsnorm(gn_t, b_gn)
        ys = [None, None]

        def up_body(i):
            j, part = i // 2, i % 2
            idx = part * 24 + j
            pc, b_pc = self.pc[self.pci]
            self.pci = (self.pci + 1) % 2
            yt, b_yt = self.yt[part]
            k.cp(pc[:, 0:2], self.carry[:, idx, :], [self.b_carry], [b_pc], eng="act")

            def evac(nt, cs, ps, pb, pc=pc, b_pc=b_pc):
                k.cp(pc[:, 2 + cs.start:2 + cs.stop], ps, [pb], [b_pc], eng="act")

            self.linear(self.hn, self.b_hn, 8, w_up, idx * 128, evac)
            if first and mask_halo:
                k.ts(pc[:, 2:2 + HALO], pc[:, 2:2 + HALO], self.m[:, 0:1], ALU.mult, [b_pc, self.b_m], [b_pc])
            k.cp(self.carry[:, idx, :], pc[:, TB:TB + 2], [b_pc], [self.b_carry], eng="act")
            self.conv(pc, b_pc, 3, cw_t, b_cw, cb_t, b_cb, idx, yt, b_yt)
            ys[part] = (yt, b_yt)
            if part == 1:
                (yu, b_yu), (yg, b_yg) = ys
                k.act(self.gel[:], yg[:], AF.Gelu_apprx_tanh, [b_yg], [self.b_gel])
                k.tt(self.act[:, j, :], yu[:], self.gel[:], ALU.mult, [b_yu, self.b_gel], [self.b_act])

        self.run_stage([(w_up, 8, (part * 24 + j) * 128) for j in range(24) for part in range(2)], up_body)

        def down_body(fc):
            def evac(nt, cs, ps, pb, fc=fc):
                k.tt(self.h[:, fc, cs], self.h[:, fc, cs], ps, ALU.add, [self.b_h, pb], [self.b_h])

            self.linear(self.act, self.b_act, 24, w_down, fc * 128, evac)

        for fc in range(8):
            down_body(fc)
            if down_hook is not None:
                down_hook(fc)


def l2_io(k, W):
    d = {}
    for name, shape in (("xTw", [1024, W]), ("m", [128, 1]), ("sel", [128, 4]), ("w_out0", [1024, 1024]), ("fg0", [128, 8]),
                        ("w_up0", [1024, 6144]), ("cw0", [128, 48, 3]), ("cb0", [128, 48]), ("w_down0", [3072, 1024]),
                        ("mg", [128, 8]), ("w_in", [1024, 2048]), ("rcw", [128, 8, 4]), ("rcb", [128, 8]), ("wa", [1024, 256]),
                        ("ba", [128, 8]), ("wx", [1024, 256]), ("bx", [128, 8]), ("lam", [128, 8])):
        d[name] = k.din(name, shape)
    return d


def emit_l2(k, W, S, d, og, h1_o, gg_o, hl_o, pl_o, ex_o):
    nc = k.nc
    T = Trunk(k, W)
    NB, TB, TW = T.NB, T.TB, T.TW
    TOK = W - HALO
    xTw, m_d, sel_d, w_out, fg_d, w_up, cw_d, cb_d, w_down = (d["xTw"], d["m"], d["sel"], d["w_out0"], d["fg0"], d["w_up0"],
                                                               d["cw0"], d["cb0"], d["w_down0"])
    mg_d, w_in, rcw_d, rcb_d, wa_d, ba_d, wx_d, bx_d, lam_d = (d["mg"], d["w_in"], d["rcw"], d["rcb"], d["wa"], d["ba"],
                                                               d["wx"], d["bx"], d["lam"])
    sel, b_sel = T.small("sel", sel_d[:, :], [128, 4])

    k.dma(T.m[:], m_d[:, :], (), [T.b_m])
    fg, b_fg = T.small("fg", fg_d[:, :], [128, 8])
    cw, b_cw = T.small("cw", cw_d[:, :, :], [128, 48, 3])
    cb, b_cb = T.small("cb", cb_d[:, :], [128, 48])
    mg, b_mg = T.small("mg", mg_d[:, :], [128, 8])
    rcw, b_rcw = T.small("rcw", rcw_d[:, :, :], [128, 8, 4])
    rcb, b_rcb = T.small("rcb", rcb_d[:, :], [128, 8])
    ba, b_ba = T.small("ba", ba_d[:, :], [128, 8])
    bx, b_bx = T.small("bx", bx_d[:, :], [128, 8])
    lam, b_lam = T.small("lam", lam_d[:, :], [128, 8])
    c1, b_c1 = k.sb([128, 8], F32, "c1")
    c2, b_c2 = k.sb([128, 8], F32, "c2")
    k.act(c1[:], lam[:], AF.Exp, [b_lam], [b_c1], scale=-1.0)
    k.act(c1[:], c1[:], AF.Ln, [b_c1], [b_c1], bias=1.0, scale=1.0)
    k.ts(c2[:], c1[:], -16.0, ALU.mult, [b_c1], [b_c2])
    k.ts(c1[:], c1[:], -8.0, ALU.mult, [b_c1, b_c2], [b_c1])
    gg, b_gg = k.sb([128, TB], BF16, "gg")
    rcar, b_rcar = k.sb([128, 8, 3], F32, "rcar")
    hcar, b_hcar = k.sb([128, 8], F32, "hcar")
    pcar, b_pcar = k.sb([128, 8], F32, "pcar")
    zer, b_zer = k.sb([128, TB], F32, "zer")
    ob_x, b_ob_x = k.sb([128, 8, TB], BF16, "ob_x")
    xrb, b_xrb = k.sb([128, 2, TB], BF16, "xrb")
    rr, b_rr = k.sb([128, TB], F32, "rr")
    ii, b_ii = k.sb([128, TB], F32, "ii")
    aa, b_aa = k.sb([128, TB], F32, "aa")
    uu, b_uu = k.sb([128, TB], F32, "uu")
    hl, b_hl = k.sb([128, TB], F32, "hl")
    pl, b_pl = k.sb([128, TB], F32, "pl")
    k.memset(rcar[:], 0.0, [b_rcar])
    k.memset(zer[:], 0.0, [b_zer])
    ext, b_ext = k.sb([128, 8, 2], F32, "ext")

    oTv = [o_.rearrange("(kc p) s -> p kc s", p=128) for o_ in og]
    xTv = xTw.rearrange("(kc p) s -> p kc s", p=128)
    h1v = h1_o.rearrange("(kc p) s -> p kc s", p=128)
    ggv = gg_o.rearrange("(kc p) s -> p kc s", p=128)
    hlv = hl_o.rearrange("(kc p) s -> p kc s", p=128)
    plv = pl_o.rearrange("(kc p) s -> p kc s", p=128)
    exv = ex_o.rearrange("(kc p) s -> p kc s", p=128)

    def load_o(blk, dst, b_dst):
        g0 = blk * TB
        for c in range(4):
            cand = T.act[:, 8 * (c % 3):8 * (c % 3) + 8, :]
            lo = c * TOK - HALO + g0
            skip = max(0, -lo)
            if skip:
                k.memset(cand[:, :, 0:skip], 0.0, [T.b_act])
            a = lo + skip
            while a < lo + TB:
                j = a // OCS
                e = min(lo + TB, (j + 1) * OCS)
                k.dma(cand[:, :, a - lo:e - lo], oTv[j][:, :, a - j * OCS:e - j * OCS], (), [T.b_act], eng=("sp" if blk == 0 else "act"))
                a = e
            if c == 0:
                k.ts(dst[:], cand, sel[:, 0:1], ALU.mult, [T.b_act, b_sel], [b_dst])
            else:
                k.stt(dst[:], cand, sel[:, c:c + 1], dst[:], ALU.mult, ALU.add, [T.b_act, b_sel, b_dst], [b_dst])

    def load_x(blk):
        g0 = blk * TB
        k.dma(T.h[:, 0:4, :], xTv[:, 0:4, g0:g0 + TB], (), [T.b_h])
        k.dma(T.h[:, 4:8, :], xTv[:, 4:8, g0:g0 + TB], (), [T.b_h])

    for blk in range(NB):
        first = blk == 0
        g0 = blk * TB
        if first:
            load_o(0, T.hn, T.b_hn)
            load_x(0)
            ob, b_ob = T.hn, T.b_hn
        else:
            ob, b_ob = ob_x, b_ob_x
        rg_reqs = []
        for n in range(4):
            rg_reqs += [(w_in, 8, 1024 + (2 * n + c2i) * 128) for c2i in range(2)]
            for c2i in range(2):
                rg_reqs += [(wa_d[n * 256:(n + 1) * 256, :], 2, c2i * 128), (wx_d[n * 256:(n + 1) * 256, :], 2, c2i * 128)]
        T.plan([(w_out, 8, fc * 128) for fc in range(8)] + T.ffn_reqs(w_up, w_down)
               + [(w_in, 8, fc * 128) for fc in range(8)] + rg_reqs)

        def wo_body(fc):
            def evac(nt, cs, ps, pb, fc=fc):
                k.tt(T.h[:, fc, cs], T.h[:, fc, cs], ps, ALU.add, [T.b_h, pb], [T.b_h])

            T.linear(ob, b_ob, 8, w_out, fc * 128, evac)

        T.run_stage([(w_out, 8, fc * 128) for fc in range(8)], wo_body)
        T.ffn(first, fg, b_fg, w_up, cw, b_cw, cb, b_cb, w_down, False)
        k.dma(h1v[:, :, g0:g0 + TB], T.h[:], [T.b_h], (), eng="pool")
        T.rmsnorm(mg, b_mg)
        if blk + 1 < NB:
            load_x(blk + 1)
        def gb_body(fc):
            def evac(nt, cs, ps, pb, fc=fc):
                k.act(gg[:, cs], ps, AF.Gelu_apprx_tanh, [pb], [b_gg])

            T.linear(T.hn, T.b_hn, 8, w_in, fc * 128, evac)
            k.dma(ggv[:, fc, g0:g0 + TB], gg[:], [b_gg], (), eng="pool")

        T.run_stage([(w_in, 8, fc * 128) for fc in range(8)], gb_body)
        reqs = []
        for n in range(4):
            reqs += [(w_in, 8, 1024 + (2 * n + c2i) * 128) for c2i in range(2)]
            for c2i in range(2):
                reqs += [(wa_d[n * 256:(n + 1) * 256, :], 2, c2i * 128), (wx_d[n * 256:(n + 1) * 256, :], 2, c2i * 128)]

        def rg_body(i, blk=blk, first=first, g0=g0):
            n, r = i // 6, i % 6
            if r < 2:
                c2i = r
                c8 = 2 * n + c2i
                pc, b_pc = T.pc[T.pci]
                T.pci = (T.pci + 1) % 2
                k.cp(pc[:, 0:3], rcar[:, c8, :], [b_rcar], [b_pc], eng="act")

                def evac(nt, cs, ps, pb, pc=pc, b_pc=b_pc):
                    k.cp(pc[:, 3 + cs.start:3 + cs.stop], ps, [pb], [b_pc], eng="act")

                T.linear(T.hn, T.b_hn, 8, w_in, 1024 + c8 * 128, evac)
                if first:
                    k.ts(pc[:, 3:3 + HALO], pc[:, 3:3 + HALO], T.m[:, 0:1], ALU.mult, [b_pc, T.b_m], [b_pc])
                k.cp(rcar[:, c8, :], pc[:, TB:TB + 3], [b_pc], [b_rcar], eng="act")
                yt, b_yt = T.yt[c2i]
                T.conv(pc, b_pc, 4, rcw, b_rcw, rcb, b_rcb, c8, yt, b_yt)
                k.cp(xrb[:, c2i, :], yt[:], [b_yt], [b_xrb], eng="act")
                return
            c2i, which = (r - 2) // 2, (r - 2) % 2
            fc = 2 * n + c2i
            wd, bias_t, b_bias, dst, b_dst = ((wa_d, ba, b_ba, rr, b_rr), (wx_d, bx, b_bx, ii, b_ii))[which]

            def evac(nt, cs, ps, pb, dst=dst, b_dst=b_dst, bias_t=bias_t, b_bias=b_bias, fc=fc):
                k.act(dst[:, cs], ps, AF.Sigmoid, [pb, b_bias], [b_dst], bias=bias_t[:, fc:fc + 1], scale=1.0)

            T.linear(xrb, b_xrb, 2, wd[n * 256:(n + 1) * 256, :], c2i * 128, evac)
            if which == 0:
                return
            k.act(aa[:], rr[:], AF.Exp, [b_rr, b_c1], [b_aa], scale=c1[:, fc:fc + 1])
            k.act(rr[:], rr[:], AF.Exp, [b_rr, b_c2], [b_rr], scale=c2[:, fc:fc + 1])
            k.act(rr[:], rr[:], AF.Sqrt, [b_rr], [b_rr], bias=1.0, scale=-1.0)
            k.tt(uu[:], T.yt[c2i][0][:], ii[:], ALU.mult, [T.yt[c2i][1], b_ii], [b_uu])
            k.tt(uu[:], uu[:], rr[:], ALU.mult, [b_uu, b_rr], [b_uu])
            if first:
                k.ts(uu[:, 0:HALO], uu[:, 0:HALO], T.m[:, 0:1], ALU.mult, [b_uu, T.b_m], [b_uu])
                k.memset(hl[:, 0:5], 0.0, [b_hl])
                k.memset(pl[:, 0:5], 0.0, [b_pl])
                s0, hi, pi = 5, 0.0, 1.0
            else:
                s0, hi, pi = 0, hcar[:, fc:fc + 1], pcar[:, fc:fc + 1]
            k.scan(hl[:, s0:TB], aa[:, s0:TB], uu[:, s0:TB], hi, [b_aa, b_uu, b_hcar], [b_hl])
            k.scan(pl[:, s0:TB], aa[:, s0:TB], zer[:, s0:TB], pi, [b_aa, b_zer, b_pcar], [b_pl])
            k.cp(hcar[:, fc:fc + 1], hl[:, TB - 1:TB], [b_hl], [b_hcar], eng="pool")
            k.cp(pcar[:, fc:fc + 1], pl[:, TB - 1:TB], [b_pl], [b_pcar], eng="pool")
            k.dma(hlv[:, fc, g0:g0 + TB], hl[:], [b_hl], (), eng="pool")
            k.dma(plv[:, fc, g0:g0 + TB], pl[:], [b_pl], (), eng="pool")
            if blk == NB - 1:
                k.cp(ext[:, fc, 0:1], hl[:, TB - 4:TB - 3], [b_hl], [b_ext], eng="pool")
                k.cp(ext[:, fc, 1:2], pl[:, TB - 4:TB - 3], [b_pl], [b_ext], eng="pool")

        if blk + 1 < NB:
            load_o(blk + 1, ob_x, b_ob_x)
        T.run_stage(reqs, rg_body)
    k.dma(exv[:, :, :], ext[:], [b_ext], (), eng="pool")


def l3_io(k, W):
    d = {}
    for name, shape in (("srank", [128, 4]), ("oms", [128, 4]), ("w_out1", [1024, 1024]), ("fg1", [128, 8]),
                        ("w_up1", [1024, 6144]), ("cw1", [128, 48, 3]), ("cb1", [128, 48]), ("w_down1", [3072, 1024]),
                        ("fin", [128, 8])):
        d[name] = k.din(name, shape)
    return d


def emit_l3(k, W, d, m_d, h1w, ggw, hlw, plw, exg, out_o):
    nc = k.nc
    T = Trunk(k, W)
    NB, TB, TW = T.NB, T.TB, T.TW
    TOK = W - HALO
    w_out, fg_d, w_up, cw_d, cb_d, w_down, fin_d = (d["w_out1"], d["fg1"], d["w_up1"], d["cw1"], d["cb1"], d["w_down1"], d["fin"])
    k.dma(T.m[:], m_d[:, :], (), [T.b_m])
    fg, b_fg = T.small("fg", fg_d[:, :], [128, 8])
    cw, b_cw = T.small("cw", cw_d[:, :, :], [128, 48, 3])
    cb, b_cb = T.small("cb", cb_d[:, :], [128, 48])
    fin, b_fin = T.small("fin", fin_d[:, :], [128, 8])
    sr, b_sr = T.small("sr", d["srank"][:, :], [128, 4])
    oms, b_oms = T.small("oms", d["oms"][:, :], [128, 4])
    pe, b_pe = k.sb([128, 4, 8, 2], F32, "pe")
    exv = exg.rearrange("(r kc p) t -> p r kc t", p=128, kc=8)
    for r in range(4):
        k.dma(pe[:, r, :, :], exv[:, r, :, :], (), [b_pe])
    Hc, b_Hc = k.sb([128, 8], F32, "Hc")
    Pm, b_Pm = k.sb([128, 8], F32, "Pm")
    Em, b_Em = k.sb([128, 8], F32, "Em")
    k.memset(Hc[:], 0.0, [b_Hc])
    for r in range(4):
        k.ts(Pm[:], pe[:, r, :, 1], sr[:, r:r + 1], ALU.mult, [b_pe, b_sr], [b_Pm], s2=oms[:, r:r + 1], op1=ALU.add)
        k.ts(Em[:], pe[:, r, :, 0], sr[:, r:r + 1], ALU.mult, [b_pe, b_sr], [b_Em])
        k.tt(Hc[:], Hc[:], Pm[:], ALU.mult, [b_Hc, b_Pm], [b_Hc])
        k.tt(Hc[:], Hc[:], Em[:], ALU.add, [b_Hc, b_Em], [b_Hc])
    yb, b_yb = T.hn, T.b_hn
    ybufs = [(k.sb([128, TB], F32, "hl%d" % i), k.sb([128, TB], F32, "pl%d" % i), k.sb([128, TB], BF16, "ggt%d" % i)) for i in range(2)]
    outt, b_outt = T.h, T.b_h

    h1v = h1w.rearrange("(kc p) s -> p kc s", p=128)
    ggv = ggw.rearrange("(kc p) s -> p kc s", p=128)
    hlv = hlw.rearrange("(kc p) s -> p kc s", p=128)
    plv = plw.rearrange("(kc p) s -> p kc s", p=128)
    outv = out_o.rearrange("(kc p) s -> p kc s", p=128)

    for blk in range(NB):
        first = blk == 0
        g0 = blk * TB
        k.dma(T.h[:, 0:4, :], h1v[:, 0:4, g0:g0 + TB], (), [T.b_h])
        k.dma(T.h[:, 4:8, :], h1v[:, 4:8, g0:g0 + TB], (), [T.b_h])
        def y_load(bb, fc):
            (hl, b_hl), (pl, b_pl), (ggt, b_ggt) = ybufs[fc % 2]
            q0 = bb * TB
            k.dma(hl[:], hlv[:, fc, q0:q0 + TB], (), [b_hl])
            k.dma(pl[:], plv[:, fc, q0:q0 + TB], (), [b_pl])
            k.dma(ggt[:], ggv[:, fc, q0:q0 + TB], (), [b_ggt])

        def y_comp(fc):
            (hl, b_hl), (pl, b_pl), (ggt, b_ggt) = ybufs[fc % 2]
            k.stt(hl[:], pl[:], Hc[:, fc:fc + 1], hl[:], ALU.mult, ALU.add, [b_pl, b_Hc, b_hl], [b_hl])
            k.tt(yb[:, fc, :], hl[:], ggt[:], ALU.mult, [b_hl, b_ggt], [b_yb])

        if first:
            for fc in range(8):
                y_load(0, fc)
                y_comp(fc)

        def down_hook(fc, blk=blk):
            if blk + 1 >= NB:
                return
            if fc == 0:
                y_load(blk + 1, 0)
            if fc + 1 < 8:
                y_load(blk + 1, fc + 1)
            y_comp(fc)
        T.plan([(w_out, 8, fc * 128) for fc in range(8)] + T.ffn_reqs(w_up, w_down))

        def wo_body(fc):
            def evac(nt, cs, ps, pb, fc=fc):
                k.tt(T.h[:, fc, cs], T.h[:, fc, cs], ps, ALU.add, [T.b_h, pb], [T.b_h])

            T.linear(yb, b_yb, 8, w_out, fc * 128, evac)

        T.run_stage([(w_out, 8, fc * 128) for fc in range(8)], wo_body)
        T.ffn(first, fg, b_fg, w_up, cw, b_cw, cb, b_cb, w_down, True, down_hook=down_hook)
        T.rmsnorm(fin, b_fin, out_f32=(outt, b_outt))
        lo = HALO if first else 0
        k.dma(outv[:, :, g0 + lo - HALO:g0 + TB - HALO], outt[:, :, lo:TB], [b_outt], (), eng="pool")


_F = {}


def build_fused(S):
    TOK = S // 4
    W = TOK + HALO
    nc = bass.Bass("TRN2", target_bir_lowering=False)
    k = K(nc)
    io1 = l1_io(k, S)
    io2 = l2_io(k, W)
    io3 = l3_io(k, W)
    NCH = max(1, S // OCS)
    osrc = [k.dint("osrc%d" % j, [256, min(S, OCS)], BF16) for j in range(NCH)]
    og = [k.dint("og%d" % j, [1024, min(S, OCS)], BF16) for j in range(NCH)]
    h1 = k.dint("h1s", [1024, W])
    gg = k.dint("ggs", [1024, W], BF16)
    hl = k.dint("hls", [1024, W])
    pl = k.dint("pls", [1024, W])
    exs = k.dint("exs", [1024, 2])
    exg = k.dint("exg", [4096, 2])
    out = k.dout("outT", [1024, TOK])
    groups = [[0, 1, 2, 3], [4, 5, 6, 7]]
    upto = int(os.environ.get("FUSE_UPTO", "3"))
    tile_bufs = {}

    def o_out(g, otile, b_ot):
        j, off = (g * 512) // OCS, (g * 512) % OCS
        ov = osrc[j].rearrange("(r p) s -> p r s", p=64)
        bb = Buf()
        tile_bufs.setdefault(j, []).append(bb)
        k.dma(ov[:, :, off:off + 512], otile[:], [b_ot], [bb], eng="pool")
        if off + 512 == min(S, OCS):
            k.P.dma((lambda j=j: nc.gpsimd.collective_compute("AllGather", ALU.bypass, replica_groups=groups,
                                                               ins=[osrc[j][:, :]], outs=[og[j][:, :]])),
                    tile_bufs[j], (), eng="pool", inc=1)

    emit_l1(k, S, io1, o_out)
    k.end_phase()
    emit_l2(k, W, S, io2, og, h1, gg, hl, pl, exs)
    k.end_phase()
    if upto == 2:
        return nc, k.P.finish()
    if os.environ.get("NOCC2"):
        k.dma(exg[0:1024, :], exs[:, :], (), ())
    else:
        k.P.dma(lambda: nc.gpsimd.collective_compute("AllGather", ALU.bypass, replica_groups=groups, ins=[exs[:, :]], outs=[exg[:, :]]),
                (), (), eng="pool", inc=1)
    k.P.barrier()
    emit_l3(k, W, io3, io2["m"], h1, gg, hl, pl, exg, out)
    cnt = k.P.finish()
    return nc, cnt


def _pk(v):
    return np.ascontiguousarray(v.reshape(-1, 128).T)


def _pkw(w):
    t, C = w.shape
    return np.ascontiguousarray(w.T.reshape(C // 128, 128, t).transpose(1, 0, 2))


def kernel(x, mix_norm_g, ffn_norm_g, final_norm_g,
           ev_w_in, ev_w_gk2, ev_b_gk2, ev_gla_norm_g, ev_w_out,
           od_w_in, od_conv_w, od_conv_b, od_w_a, od_b_a, od_w_x, od_b_x, od_lambda, od_w_out,
           ffn_w_up, ffn_conv_w, ffn_conv_b, ffn_w_down):
    f = lambda a: np.asarray(a, dtype=np.float32)
    x = f(x)
    Bsz, S, D = x.shape
    TOK = S // 4
    W = TOK + HALO
    if S not in _F:
        _F[S] = build_fused(S)
    nc, _ = _F[S]
    consts = l1_consts(S)
    w = f(ev_w_in)[0]
    gn = _pk(f(mix_norm_g)[0])
    perm = []
    for g in range(4):
        for j in range(4):
            head = 2 * g + (j % 2)
            base = head * 64 if j < 2 else 512 + head * 64
            perm += list(range(base, base + 64))
    w_out0 = np.ascontiguousarray(f(ev_w_out)[0][np.asarray(perm)])
    shared = {
        "gn": gn, "w_out0": w_out0, "fg0": _pk(f(ffn_norm_g)[0]), "w_up0": f(ffn_w_up)[0],
        "cw0": _pkw(f(ffn_conv_w)[0]), "cb0": _pk(f(ffn_conv_b)[0]), "w_down0": f(ffn_w_down)[0],
        "mg": _pk(f(mix_norm_g)[1]), "w_in": f(od_w_in)[0], "rcw": _pkw(f(od_conv_w)[0]), "rcb": _pk(f(od_conv_b)[0]),
        "wa": np.ascontiguousarray(f(od_w_a)[0].reshape(1024, 256)), "ba": _pk(f(od_b_a)[0]),
        "wx": np.ascontiguousarray(f(od_w_x)[0].reshape(1024, 256)), "bx": _pk(f(od_b_x)[0]),
        "lam": _pk(f(od_lambda)[0]),
        "w_out1": f(od_w_out)[0], "fg1": _pk(f(ffn_norm_g)[1]), "w_up1": f(ffn_w_up)[1],
        "cw1": _pkw(f(ffn_conv_w)[1]), "cb1": _pk(f(ffn_conv_b)[1]), "w_down1": f(ffn_w_down)[1],
        "fin": _pk(f(final_norm_g)),
    }
    shared.update(consts)
    in_maps = []
    for b in range(Bsz):
        xTb = np.ascontiguousarray(x[b].T)
        for c in range(4):
            h0, h1 = 2 * c, 2 * c + 1
            cols = []
            for base in (0, 512, 1024):
                cols += [np.arange(base + h0 * 64, base + h0 * 64 + 64), np.arange(base + h1 * 64, base + h1 * 64 + 64)]
            for base in (1536, 1792):
                cols += [np.arange(base + h0 * 32, base + h0 * 32 + 32), np.arange(base + h1 * 32, base + h1 * 32 + 32)]
            cols += [np.arange(2048 + h0 * 64, 2048 + h0 * 64 + 64), np.arange(2048 + h1 * 64, 2048 + h1 * 64 + 64)]
            cols += [np.arange(2560, 2576)]
            cols += [np.arange(2576 + h0 * 64, 2576 + h0 * 64 + 64), np.arange(2576 + h1 * 64, 2576 + h1 * 64 + 64)]
            cols = np.concatenate(cols)
            t0 = c * TOK
            xw = np.zeros((1024, W), np.float32)
            if c == 0:
                xw[:, HALO:] = xTb[:, 0:TOK]
            else:
                xw[:] = xTb[:, t0 - HALO:t0 + TOK]
            sel = np.zeros((128, 4), np.float32)
            sel[:, c] = 1.0
            sr = np.zeros((128, 4), np.float32)
            sr[:, :c] = 1.0
            m = dict(shared)
            m.update({
                "xT": xTb, "w1": np.ascontiguousarray(w[:, cols]),
                "wgk2": np.ascontiguousarray(f(ev_w_gk2)[0][:, h0 * 32:h0 * 32 + 64]),
                "bgk2": np.ascontiguousarray(f(ev_b_gk2)[0][h0 * 32:h0 * 32 + 64].reshape(64, 1)),
                "glag": np.ascontiguousarray(f(ev_gla_norm_g)[0][h0:h0 + 2].T),
                "xTw": xw, "m": np.full((128, 1), 0.0 if c == 0 else 1.0, np.float32),
                "sel": sel, "srank": sr, "oms": 1.0 - sr,
            })
            in_maps.append(m)
    res = run_bass_kernel_spmd(nc, in_maps, core_ids=list(range(len(in_maps)))).results
    out = np.zeros((Bsz, S, D), np.float32)
    for b in range(Bsz):
        for c in range(4):
            out[b, c * TOK:(c + 1) * TOK, :] = np.asarray(res[b * 4 + c]["outT"]).T
    return out
```

```python
import contextlib
import os
import numpy as np
import ml_dtypes
import concourse.bass as bass
import concourse.mybir as mybir
from concourse.bass_utils import run_bass_kernel_spmd

F32 = mybir.dt.float32
BF16 = mybir.dt.bfloat16
AF = mybir.ActivationFunctionType
ALU = mybir.AluOpType
AX = mybir.AxisListType

SAME_SYNC = True
N_DMA_SEMS = 24
SEM_EPOCH = 2000
NEG = -240000.0
EPS = 1e-6


class Buf:
    __slots__ = ("name", "lw", "rd")

    def __init__(self, name=""):
        self.name = name
        self.lw = None
        self.rd = []


def _flat(bs):
    out = []
    for b in bs:
        if isinstance(b, (list, tuple)):
            out.extend(_flat(b))
        else:
            out.append(b)
    return out


class Prog:
    ENGS = ("pe", "act", "dve", "pool", "sp")

    def __init__(self, nc):
        self.nc = nc
        self.h = {"pe": nc.tensor, "act": nc.scalar, "dve": nc.vector, "pool": nc.gpsimd, "sp": nc.sync}
        self.items = {e: [] for e in self.ENGS}
        self.known = {}
        self.sem = {e: nc.alloc_semaphore("s_" + e) for e in self.ENGS}
        self.dsem = [nc.alloc_semaphore("d%d" % i) for i in range(N_DMA_SEMS)]
        self.duse = [0] * N_DMA_SEMS
        self.dval = [0] * N_DMA_SEMS
        self.pending = {e: [] for e in self.ENGS}
        self.rank = {e: [] for e in self.ENGS}
        self.emitted = {e: 0 for e in self.ENGS}
        self.esem = {}
        self.dnext = 0

    def _need(self, eng, tok, waits):
        if tok is None:
            return
        if tok[0] == "c":
            _, e2, seq = tok
            if e2 == eng and (eng == "pe" or not SAME_SYNC):
                return
            key = (eng, "c", e2)
            if self.known.get(key, -1) >= seq:
                return
            self.known[key] = seq
            self.items[e2][seq]["flag"] = True
            waits.append(tok)
        else:
            _, idx, val = tok
            key = (eng, "d", idx)
            if self.known.get(key, -1) >= val:
                return
            self.known[key] = val
            waits.append(tok)

    def _deps(self, eng, reads, writes):
        waits = []
        for b in reads:
            self._need(eng, b.lw, waits)
        for b in writes:
            self._need(eng, b.lw, waits)
            for t in b.rd:
                self._need(eng, t, waits)
        return waits

    def op(self, eng, fn, reads=(), writes=()):
        reads, writes = _flat(reads), _flat(writes)
        waits = self.pending[eng] + self._deps(eng, reads, writes)
        self.pending[eng] = []
        seq = len(self.items[eng])
        self.items[eng].append({"waits": waits, "fn": fn, "flag": False, "dma": None})
        tok = ("c", eng, seq)
        for b in reads:
            b.rd.append(tok)
        for b in writes:
            b.lw = tok
            b.rd = []
        return tok

    def dma(self, fn, reads=(), writes=(), eng="sp", inc=16):
        reads, writes = _flat(reads), _flat(writes)
        idx = self.dnext
        self.dnext = (self.dnext + 1) % N_DMA_SEMS
        pv = self.dval[idx]
        waits = self.pending[eng] + self._deps(eng, reads, writes)
        self.pending[eng] = []
        if pv > 0:
            self._need(eng, ("d", idx, pv), waits)
        self.duse[idx] += 1
        self.dval[idx] = pv + inc
        tok = ("d", idx, pv + inc)
        self.items[eng].append({"waits": waits, "fn": fn, "flag": False, "dma": idx, "inc": inc})
        for b in reads:
            b.rd.append(tok)
        for b in writes:
            b.lw = tok
            b.rd = []
        return tok

    def _all_tokens(self):
        toks = []
        for e in ("pe", "act", "dve", "pool"):
            items = self.items[e]
            for i in range(len(items) - 1, -1, -1):
                if items[i]["dma"] is None and items[i]["fn"] is not None or (items[i]["dma"] is None and i < self.emitted[e]):
                    toks.append(("c", e, i))
                    break
        for idx in range(N_DMA_SEMS):
            if self.dval[idx]:
                toks.append(("d", idx, self.dval[idx]))
        return toks

    def barrier(self, engines=None):
        toks = self._all_tokens()
        for e in (engines or self.ENGS):
            for t in toks:
                if t[0] == "c" and t[1] == e:
                    continue
                self._need(e, t, self.pending[e])

    def _sem_of(self, e, r):
        ep = (r - 1) // SEM_EPOCH
        if (e, ep) not in self.esem:
            self.esem[(e, ep)] = self.sem[e] if ep == 0 else self.nc.alloc_semaphore("s_%s_%d" % (e, ep))
        return self.esem[(e, ep)], (r - 1) % SEM_EPOCH + 1

    def flush(self):
        for e in self.ENGS:
            items = self.items[e]
            rk = self.rank[e]
            c = rk[-1] if rk else 0
            for i in range(len(rk), len(items)):
                if items[i]["flag"]:
                    c += 1
                rk.append(c)
        for e in self.ENGS:
            h = self.h[e]
            items = self.items[e]
            for i in range(self.emitted[e], len(items)):
                it = items[i]
                for t in it["waits"]:
                    if t[0] == "c":
                        sm, v = self._sem_of(t[1], self.rank[t[1]][t[2]])
                        h.wait_ge(sm, v)
                    else:
                        h.wait_ge(self.dsem[t[1]], t[2])
                if it["fn"] is None:
                    continue
                ins = it["fn"]()
                if it["dma"] is not None:
                    ins.then_inc(self.dsem[it["dma"]], it["inc"])
                elif it["flag"]:
                    sm, _ = self._sem_of(e, self.rank[e][i])
                    ins.then_inc(sm, 1)
                it["fn"] = None
            self.emitted[e] = len(items)

    def finish(self):
        self.barrier(engines=("sp",))
        self.items["sp"].append({"waits": self.pending["sp"], "fn": None, "flag": False, "dma": None})
        self.pending["sp"] = []
        self.flush()
        return {e: (len(self.items[e]), self.rank[e][-1] if self.rank[e] else 0) for e in self.ENGS}


class K:
    def __init__(self, nc):
        self.nc = nc
        self.P = Prog(nc)
        self._n = 0
        self.stack = contextlib.ExitStack()

    def end_phase(self):
        self.P.barrier()
        self.P.flush()
        self.stack.close()
        self.stack = contextlib.ExitStack()

    def sb(self, shape, dt, name=None):
        self._n += 1
        return self.stack.enter_context(self.nc.sbuf_tensor("sb%d_" % self._n + (name or "t"), list(shape), dt)), Buf(name or "")

    def ps(self, shape, name=None):
        self._n += 1
        return self.stack.enter_context(self.nc.psum_tensor("ps%d_" % self._n + (name or "p"), list(shape), F32)), Buf(name or "")

    def din(self, name, shape, dt=F32):
        return self.nc.dram_tensor(name, list(shape), dt, kind="ExternalInput").ap()

    def dint(self, name, shape, dt=F32, **kw):
        return self.nc.dram_tensor(name, list(shape), dt, kind="Internal", **kw).ap()

    def dout(self, name, shape, dt=F32):
        return self.nc.dram_tensor(name, list(shape), dt, kind="ExternalOutput").ap()

    def mm(self, out, lhsT, rhs, st, sp, r, w):
        nc = self.nc
        return self.P.op("pe", lambda: nc.tensor.matmul(out, lhsT=lhsT, rhs=rhs, start=st, stop=sp), r, w)

    def tr(self, out, in_, ident, r, w):
        nc = self.nc
        return self.P.op("pe", lambda: nc.tensor.transpose(out, in_, ident), r, w)

    def act(self, out, in_, func, r, w, bias=None, scale=None):
        nc = self.nc
        kw = {}
        if bias is not None:
            kw["bias"] = bias
        if scale is not None:
            kw["scale"] = scale
        return self.P.op("act", lambda: nc.scalar.activation(out=out, in_=in_, func=func, **kw), r, w)

    def tt(self, out, in0, in1, op, r, w, eng="dve"):
        h = self.P.h[eng]
        return self.P.op(eng, lambda: h.tensor_tensor(out=out, in0=in0, in1=in1, op=op), r, w)

    def ts(self, out, in0, s1, op0, r, w, s2=None, op1=None, eng="dve"):
        h = self.P.h[eng]
        if op1 is None:
            return self.P.op(eng, lambda: h.tensor_scalar(out=out, in0=in0, scalar1=s1, scalar2=None, op0=op0), r, w)
        return self.P.op(eng, lambda: h.tensor_scalar(out=out, in0=in0, scalar1=s1, scalar2=s2, op0=op0, op1=op1), r, w)

    def stt(self, out, in0, scalar, in1, op0, op1, r, w):
        nc = self.nc
        return self.P.op("dve", lambda: nc.vector.scalar_tensor_tensor(out=out, in0=in0, scalar=scalar, in1=in1, op0=op0, op1=op1), r, w)

    def cp(self, out, in_, r, w, eng="dve"):
        if eng == "act":
            nc = self.nc
            return self.P.op("act", lambda: nc.scalar.copy(out=out, in_=in_), r, w)
        h = self.P.h[eng]
        return self.P.op(eng, lambda: h.tensor_copy(out=out, in_=in_), r, w)

    def memset(self, ap, val, w, eng="pool"):
        h = self.P.h[eng]
        return self.P.op(eng, lambda: h.memset(ap, val), (), w)

    def scan(self, out, d0, d1, init, r, w):
        nc = self.nc
        return self.P.op("dve", lambda: nc.vector.tensor_tensor_scan(out=out, data0=d0, data1=d1, initial=init, op0=ALU.mult, op1=ALU.add), r, w)

    def dma(self, out, in_, r, w, eng="sp"):
        h = self.P.h[eng]
        return self.P.dma(lambda: h.dma_start(out=out, in_=in_), r, w, eng=eng)


def l1_io(k, S):
    d = {}
    for name, shape in (("xT", [1024, S]), ("gn", [128, 8]), ("w1", [1024, 784]), ("wgk2", [16, 64]), ("bgk2", [64, 1]),
                        ("glag", [64, 2]), ("cosd", [64, S]), ("sind", [64, S]), ("cmd", [128, 2048]), ("ed", [64, S]),
                        ("identd", [128, 128]), ("rmd", [64, 512]), ("amd", [64, 512]), ("hm2d", [64, 128]), ("hmd", [64, 2])):
        d[name] = k.din(name, shape)
    return d


def emit_l1(k, S, d, oT):
    NT = S // 512
    NKT = S // 128
    nc = k.nc
    P = k.P
    xT, gn_d, w1_d, wgk2_d, bgk2_d, glag_d = d["xT"], d["gn"], d["w1"], d["wgk2"], d["bgk2"], d["glag"]
    cos_d, sin_d, cm_d, e_d, id_d, rm_d, am_d, hm2_d, hm_d = (d["cosd"], d["sind"], d["cmd"], d["ed"], d["identd"], d["rmd"],
                                                              d["amd"], d["hm2d"], d["hmd"])

    xt, b_xt = k.sb([128, 8, 512], F32, "xt")
    rstd, b_rstd = k.sb([128, 512], F32, "rstd")
    lnv, b_lnv = rstd, b_rstd
    hn, b_hn_whole = k.sb([128, 8, 512], BF16, "hn")
    b_hn_kc = [Buf("hn_kc%d" % i) for i in range(8)]
    b_hn = [b_hn_whole] + b_hn_kc
    sq, b_sq = hn, b_hn
    wb, b_wb = k.sb([128, 8, 784], BF16, "wb")
    gn, b_gn = k.sb([128, 8], F32, "gn")
    Kaug = [k.sb([128, S], BF16, "kaug%d" % h) for h in range(2)]
    KB = [[Buf() for _ in range(NT)] for h in range(2)]
    b_kE = [Buf(), Buf()]
    Vaug = [k.sb([128, NKT, 66], BF16, "vaug%d" % h) for h in range(2)]
    VB = [[Buf() for _ in range(NT)] for h in range(2)]
    b_vones = [Buf(), Buf()]
    Qaug = [[k.sb([128, 512], BF16, "qaug%d_%d" % (h, p)) for p in range(2)] for h in range(2)]
    kmT = [k.sb([64, 64], BF16, "kmT%d" % h) for h in range(2)]
    km32, b_km32 = k.sb([64, 2], F32, "km32")
    cosT, b_cos = k.sb([64, 512], F32, "cos")
    sinT, b_sin = k.sb([64, 512], F32, "sin")
    t1, b_t1 = k.sb([64, 512], F32, "t1")
    t2, b_t2 = k.sb([64, 512], F32, "t2")
    pTs = [k.sb([128, 512], BF16, "pT%d" % i) for i in range(4)]
    cm, b_cm = k.sb([128, 4, 512], BF16, "cm")
    id_f, b_idf = k.sb([128, 128], F32, "idf")
    id_b, b_idb = k.sb([128, 128], BF16, "idb")
    ones_b, b_onesb = k.sb([128, 128], BF16, "onesb")
    ones_f, b_onesf = k.sb([128, 64], F32, "onesf")
    bq, b_bq = k.sb([128, 4, 128], F32, "bq")
    gsb, b_gsb = k.sb([128, 4, 64], F32, "gsb")
    m8, b_m8 = k.sb([128, 4, 8], F32, "m8")
    fin_t, b_rden = k.sb([128, 512], F32, "fin")
    rden = fin_t
    osb, b_osb = fin_t[0:64, :], Buf("osb")
    otiles = [k.sb([64, 4, 512], BF16, "otile%d" % p) for p in range(2)]
    QG32, b_qg = k.sb([64, 512], F32, "qg32")
    KG32, b_kg = k.sb([64, 512], F32, "kg32")
    spl, b_spl = k.sb([64, 512], F32, "spl")
    bpos, b_bpos = k.sb([64, 512], F32, "bpos")
    eb, b_eb = k.sb([64, 512], F32, "eb")
    enb, b_enb = k.sb([64, 512], F32, "enb")
    Ac, b_ac = k.sb([64, 8], F32, "Ac")
    ke32, b_ke = k.sb([64, 512], F32, "ke32")
    qt, b_qt = k.sb([64, 512], BF16, "qt")
    kpad, b_kpad = k.sb([64, 2, 512], BF16, "kpad")
    khat, b_khat = k.sb([64, 512], BF16, "khat")
    rmk, b_rmk = k.sb([64, 512], F32, "rmk")
    amk, b_amk = k.sb([64, 512], BF16, "amk")
    hm2, b_hm2 = k.sb([64, 128], F32, "hm2")
    hm, b_hm = k.sb([64, 2], F32, "hm")
    attm, b_attm = k.sb([64, 2, 512], BF16, "attm")
    gvt, b_gvt = k.sb([64, 8, 128], BF16, "gvt")
    KTt, b_ktt = k.sb([64, 8, 64], BF16, "KTt")
    gk16, b_gk16 = k.sb([16, 512], BF16, "gk16")
    wgk2f, b_wgk2f = k.sb([16, 64], F32, "wgk2f")
    wgk2b, b_wgk2b = k.sb([16, 64], BF16, "wgk2b")
    nbg, b_nbg = k.sb([64, 1], F32, "nbg")
    glag, b_glag = k.sb([64, 2], F32, "glag")
    sbog, b_sbog = k.sb([64, 2, 512], BF16, "sbog")
    st32, b_st32 = k.sb([64, 128], F32, "st32")
    stall, b_stall = k.sb([64, 9, 128], BF16, "stall")
    stmp, b_stmp = k.sb([64, 128], F32, "stmp")
    o32, b_o32 = t1, b_t1
    osq, b_osq = k.sb([64, 512], BF16, "osq")
    on32, b_on32 = t2, b_t2
    B = [k.ps([128, 512], "bank%d" % i) for i in range(8)]

    stg = xt
    k.dma(id_f[:], id_d[:, :], (), [b_idf])
    k.cp(id_b[:], id_f[:], [b_idf], [b_idb])
    k.memset(ones_b[:], 1.0, [b_onesb])
    k.memset(ones_f[:], 1.0, [b_onesf])
    k.memset(bq[:], 0.0, [b_bq])
    k.memset(st32[:], 0.0, [b_st32])
    k.memset(stall[:], 0.0, [b_stall])
    k.dma(gn[:], gn_d[:, :], (), [b_gn])
    k.dma(glag[:], glag_d[:, :], (), [b_glag])
    k.dma(hm2[:], hm2_d[:, :], (), [b_hm2])
    k.dma(hm[:], hm_d[:, :], (), [b_hm])
    k.dma(rmk[:], rm_d[:, :], (), [b_rmk])
    k.dma(wgk2f[:], wgk2_d[:, :], (), [b_wgk2f])
    k.cp(wgk2b[:], wgk2f[:], [b_wgk2f], [b_wgk2b])
    k.dma(nbg[:], bgk2_d[:, :], (), [b_nbg])
    k.ts(nbg[:], nbg[:], -1.0, ALU.mult, [b_nbg], [b_nbg])
    k.dma(t1[:], am_d[:, :], (), [b_t1])
    k.cp(amk[:], t1[:], [b_t1], [b_amk])
    sflat = stg[:].rearrange("p a b -> p (a b)")
    k.dma(sflat[:, 0:2048], cm_d[:, :], (), [b_xt])
    k.cp(cm[:].rearrange("p a b -> p (a b)"), sflat[:, 0:2048], [b_xt], [b_cm])
    w1v = w1_d.rearrange("(kc p) f -> p kc f", p=128)
    for half in range(2):
        sv = sflat[:, 0:4 * 784].rearrange("p (a b) -> p a b", b=784)
        k.dma(sv, w1v[:, half * 4:(half + 1) * 4, :], (), [b_xt])
        k.cp(wb[:, half * 4:(half + 1) * 4, :], sv, [b_xt], [b_wb], eng="dve" if half == 0 else "pool")
    for pc in range(S // 2048):
        k.dma(sflat[64:128, 0:2048], e_d[:, pc * 2048:(pc + 1) * 2048], (), [b_xt])
        for h in range(2):
            k.cp(Kaug[h][0][64:128, pc * 2048:(pc + 1) * 2048], sflat[64:128, 0:2048], [b_xt], [b_kE[h]],
                 eng="dve" if h == 0 else "pool")
    for h in range(2):
        k.memset(Vaug[h][0][:, :, 64:65], 1.0, [b_vones[h]])
        k.memset(kmT[h][0][:], 0.0, [kmT[h][1]])

    xTv = xT.rearrange("(kc p) s -> p kc s", p=128)
    def proj(bank, M, col0, ncols=None):
        pt, pb = bank
        for kc in range(8):
            k.mm(pt[0:M, 0:512], wb[:, kc, col0:col0 + M], hn[:, kc, :], kc == 0, kc == 7, [b_wb, b_hn_kc[kc]], [pb])
        return pt, pb

    FB0, FB1, FB2 = B[0], B[4], B[7]

    def gen_F(g):
        c0 = g * 512
        par = g % 2
        otile, b_ot = otiles[par]
        k.dma(xt[:, 0:4, :], xTv[:, 0:4, c0:c0 + 512], (), [b_xt])
        k.dma(xt[:, 4:8, :], xTv[:, 4:8, c0:c0 + 512], (), [b_xt])
        k.dma(cosT[:], cos_d[:, c0:c0 + 512], (), [b_cos])
        k.dma(sinT[:], sin_d[:, c0:c0 + 512], (), [b_sin])
        k.act(sq[:], xt[:], AF.Square, [b_xt], [b_sq])
        pt, pb = FB0
        for kc in range(8):
            k.mm(pt[:, :], ones_b[:], sq[:, kc, :], kc == 0, kc == 7, [b_onesb, b_sq], [pb])
        yield
        k.act(lnv[:], pt[:, :], AF.Ln, [pb], [b_lnv], bias=EPS, scale=1.0 / 1024)
        k.act(rstd[:], lnv[:], AF.Exp, [b_lnv], [b_rstd], scale=-0.5)
        for kc in range(8):
            k.stt(hn[:, kc, :], xt[:, kc, :], gn[:, kc:kc + 1], rstd[:], ALU.mult, ALU.mult, [b_xt, b_gn, b_rstd], [b_hn_whole, b_hn_kc[kc]])
            if kc % 4 == 3:
                yield
        for idx in range(4):
            h = idx % 2
            isk = idx >= 2
            pt, pb = proj((FB1, FB2)[idx % 2], 64, idx * 64)
            yield
            if isk:
                dest = Kaug[h][0][0:64, c0:c0 + 512]
                dbuf = KB[h][g]
            else:
                dest = Qaug[h][par][0][0:64, :]
                dbuf = Qaug[h][par][1]
            k.tt(t1[:], pt[0:64, 0:512], cosT[:], ALU.mult, [pb, b_cos], [b_t1])
            k.tt(t2[0:32, :], pt[32:64, 0:512], sinT[32:64, :], ALU.mult, [pb, b_sin], [b_t2])
            k.tt(t2[32:64, :], pt[0:32, 0:512], sinT[0:32, :], ALU.mult, [pb, b_sin], [b_t2])
            k.tt(dest, t1[:], t2[:], ALU.add, [b_t1, b_t2], [dbuf])
            yield
        pt, pb = FB0
        for st in range(4):
            for kc in range(8):
                k.mm(pt[:, st * 128:(st + 1) * 128], hn[:, kc, st * 128:(st + 1) * 128], wb[:, kc, 256:384],
                     kc == 0, kc == 7, [b_hn_kc[kc], b_wb], [pb])
            yield
        pv_ = pt[:, 0:512].rearrange("p (a b) -> p a b", b=128)
        for h in range(2):
            k.cp(Vaug[h][0][:, 4 * g:4 * g + 4, 0:64], pv_[:, :, h * 64:(h + 1) * 64], [pb], [VB[h][g]], eng="act")
        pt, pb = proj(FB1, 128, 384)
        k.cp(QG32[:], pt[0:64, 0:512], [pb], [b_qg], eng="act")
        k.cp(KG32[:], pt[64:128, 0:512], [pb], [b_kg], eng="act")
        yield
        for c in range(8):
            pt, pb = FB2 if c < 4 else FB0
            for kc in range(8):
                k.mm(pt[0:64, (c % 4) * 128:(c % 4 + 1) * 128], hn[:, kc, c * 64:(c + 1) * 64], wb[:, kc, 512:640],
                     kc == 0, kc == 7, [b_hn_kc[kc], b_wb], [pb])
            if c % 2 == 1:
                yield
        for hf in range(2):
            pt, pb = FB2 if hf == 0 else FB0
            k.cp(gvt[:, hf * 4:(hf + 1) * 4, :].rearrange("p a b -> p (a b)"), pt[0:64, 0:512], [pb], [b_gvt], eng="act")
        pt, pb = proj(FB1, 16, 640)
        k.cp(gk16[:], pt[0:16, 0:512], [pb], [b_gk16], eng="act")
        k.mm(pt[0:64, 0:512], wgk2b[:], gk16[:], True, True, [b_wgk2b, b_gk16], [pb])
        k.act(spl[:], pt[0:64, 0:512], AF.Exp, [pb, b_nbg], [b_spl], bias=nbg[:, 0:1], scale=-1.0)
        k.act(spl[:], spl[:], AF.Ln, [b_spl], [b_spl], bias=1.0, scale=1.0)
        yield
        pt, pb = proj(FB2, 128, 656)
        k.act(sbog[:, 0, :], pt[0:64, 0:512], AF.Silu, [pb], [b_sbog])
        k.act(sbog[:, 1, :], pt[64:128, 0:512], AF.Silu, [pb], [b_sbog])
        yield
        k.scan(bpos[:], rmk[:], spl[:], 0.0, [b_rmk, b_spl], [b_bpos])
        k.act(eb[:], bpos[:], AF.Exp, [b_bpos], [b_eb], scale=-1.0 / 16)
        k.act(enb[:], bpos[:], AF.Exp, [b_bpos], [b_enb], scale=1.0 / 16)
        blast = bpos[:].rearrange("p (c t) -> p c t", t=64)[:, :, 63:64].rearrange("p c o -> p (c o)")
        k.act(Ac[:], blast, AF.Exp, [b_bpos], [b_ac], scale=-1.0 / 16)
        yield
        k.stt(qt[:], QG32[:], 32.0 ** -0.5, eb[:], ALU.mult, ALU.mult, [b_qg, b_eb], [b_qt])
        k.tt(ke32[:], KG32[:], enb[:], ALU.mult, [b_kg, b_enb], [b_ke])
        for h in range(2):
            k.ts(kpad[:, h, :], ke32[:], hm[:, h:h + 1], ALU.mult, [b_ke, b_hm], [b_kpad])
        yield
        for c in range(8):
            k.act(khat[:, c * 64:(c + 1) * 64], ke32[:, c * 64:(c + 1) * 64], AF.Copy, [b_ke, b_ac], [b_khat], scale=Ac[:, c:c + 1])
        yield
        pt, pb = FB0
        for c in range(8):
            k.mm(pt[0:64, c * 64:(c + 1) * 64], khat[:, c * 64:(c + 1) * 64], id_b[0:64, 0:64], True, True, [b_khat, b_idb], [pb])
        k.cp(KTt[:].rearrange("p a b -> p (a b)"), pt[0:64, 0:512], [pb], [b_ktt], eng="act")
        yield
        for h in range(2):
            pt, pb = (FB1, FB2)[h]
            for c in range(8):
                k.mm(pt[0:64, c * 64:(c + 1) * 64], kpad[:, h, c * 64:(c + 1) * 64], qt[:, c * 64:(c + 1) * 64], True, True,
                     [b_kpad, b_qt], [pb])
            k.tt(attm[:, h, :], pt[0:64, 0:512], amk[:], ALU.mult, [pb, b_amk], [b_attm])
            yield
        pso = [FB2, FB0]
        for c in range(8):
            psd, b_psd = FB0 if c < 4 else FB1
            for h in range(2):
                k.mm(psd[0:64, (c % 4) * 128 + h * 64:(c % 4) * 128 + (h + 1) * 64], KTt[:, c, :], gvt[:, c, h * 64:(h + 1) * 64],
                     True, True, [b_ktt, b_gvt], [b_psd])
            if c % 4 == 3:
                yield
        k.cp(stall[:, 0, :], stall[:, 8, :], [b_stall], [b_stall])
        for c in range(8):
            psd, b_psd = FB0 if c < 4 else FB1
            k.tt(stmp[:], psd[0:64, (c % 4) * 128:(c % 4 + 1) * 128], hm2[:], ALU.mult, [b_psd, b_hm2], [b_stmp])
            k.stt(st32[:], st32[:], Ac[:, c:c + 1], stmp[:], ALU.mult, ALU.add, [b_st32, b_ac, b_stmp], [b_st32])
            k.cp(stall[:, c + 1, :], st32[:], [b_st32], [b_stall])
            if c % 2 == 1:
                yield
        for c in range(8):
            for h in range(2):
                po, pbo = pso[h]
                k.mm(po[0:64, c * 64:(c + 1) * 64], gvt[:, c, h * 64:(h + 1) * 64], attm[:, h, c * 64:(c + 1) * 64], True, False,
                     [b_gvt, b_attm], [pbo])
                k.mm(po[0:64, c * 64:(c + 1) * 64], stall[:, c, h * 64:(h + 1) * 64], qt[:, c * 64:(c + 1) * 64], False, True,
                     [b_stall, b_qt], [pbo])
            if c % 2 == 1:
                yield
        for h in range(2):
            po, pbo = pso[h]
            k.cp(o32[:], po[0:64, 0:512], [pbo], [b_o32], eng="act")
            k.act(osq[:], po[0:64, 0:512], AF.Square, [pbo], [b_osq])
            pt, pb = FB1
            k.mm(pt[0:64, 0:512], ones_b[0:64, 0:64], osq[:], True, True, [b_onesb, b_osq], [pb])
            yield
            k.act(lnv[0:64, :], pt[0:64, 0:512], AF.Ln, [pb], [b_lnv], bias=EPS, scale=1.0 / 64)
            k.act(lnv[0:64, :], lnv[0:64, :], AF.Exp, [b_lnv], [b_lnv], scale=-0.5)
            k.tt(on32[:], o32[:], lnv[0:64, :], ALU.mult, [b_o32, b_lnv], [b_on32])
            k.stt(otile[:, 2 + h, :], on32[:], glag[:, h:h + 1], sbog[:, h, :], ALU.mult, ALU.mult, [b_on32, b_glag, b_sbog], [b_ot])
            yield
        for h in range(2):
            KA, _ = Kaug[h]
            QA, b_QA = Qaug[h][par]
            kmt, b_kmt = kmT[h]
            k.P.op("dve", (lambda o=km32[:], i=KA[0:64, c0:c0 + 512].rearrange("p (a b) -> p a b", b=256):
                           nc.vector.tensor_reduce(out=o, in_=i, axis=AX.X, op=ALU.add)), [KB[h][g]], [b_km32])
            k.cp(kmt[:, 2 * g:2 * g + 2], km32[:], [b_km32], [b_kmt])
            pg, b_pg = FB1
            for st in range(4):
                k.mm(pg[:, st * 64:(st + 1) * 64], QA[0:64, st * 128:(st + 1) * 128], kmt[:, :], True, True, [b_QA, b_kmt], [b_pg])
            yield
            k.memset(gsb[:], -1e30, [b_gsb], eng="dve")
            for st in range(4):
                blk = 2 * g + st // 2
                if blk > 0:
                    k.cp(gsb[:, st, 0:blk], pg[:, st * 64:st * 64 + blk], [b_pg], [b_gsb])
            yield
            for st in range(4):
                blk = 2 * g + st // 2
                k.P.op("dve", (lambda o=m8[:, st, :], i=gsb[:, st, :]: nc.vector.max(out=o, in_=i)), [b_gsb], [b_m8])
                k.ts(bq[:, st, 64:128], gsb[:, st, :], m8[:, st, 2:3], ALU.is_ge, [b_gsb, b_m8], [b_bq], s2=-NEG, op1=ALU.mult)
            yield
            k.ts(bq[:, :, 64:128], bq[:, :, 64:128], NEG, ALU.add, [b_bq], [b_bq])
            for st in range(4):
                blk = 2 * g + st // 2
                k.memset(bq[:, st, 64 + blk:65 + blk], 0.0, [b_bq], eng="dve")
            pg2, b_pg2 = FB2
            for st in range(4):
                k.tr(pg2[:, st * 128:(st + 1) * 128], bq[:, st, :], id_f[:], [b_bq, b_idf], [b_pg2])
            k.cp(QA[64:128, :], pg2[64:128, 0:512], [b_pg2], [b_QA], eng="act")
            yield

    def gen_A(g):
        par = g % 2
        otile, b_ot = otiles[par]
        for h in range(2):
            KA, _ = Kaug[h]
            VA, _ = Vaug[h]
            QA, b_QA = Qaug[h][par]
            pO, b_pO = B[5 + h]
            nkt = 4 * g + 4
            LA = 2

            def qk(kt):
                j = kt - 4 * g
                q0 = 256 if j >= 2 else 0
                pS, b_pS = B[1 + (kt % 3)]
                k.mm(pS[:, q0:512], KA[:, kt * 128:(kt + 1) * 128], QA[:, q0:512], True, j < 0,
                     [KB[h][kt // 4], b_kE[h], b_QA], [b_pS])
                if j >= 0:
                    k.mm(pS[:, q0:512], id_b[:], cm[:, j, q0:512], False, True, [b_idb, b_cm], [b_pS])

            def pv(kt):
                j = kt - 4 * g
                q0 = 256 if j >= 2 else 0
                pS, b_pS = B[1 + (kt % 3)]
                pTt, b_pT = pTs[kt % 4]
                k.act(pTt[:, q0:512], pS[:, q0:512], AF.Exp, [b_pS], [b_pT], scale=0.125)
                k.mm(pO[0:65, q0:512], VA[:, kt, 0:65], pTt[:, q0:512], kt == 0, kt == nkt - 1,
                     [VB[h][kt // 4], b_vones[h], b_pT], [b_pO])

            for i in range(nkt + LA):
                if i < nkt:
                    qk(i)
                if i >= LA:
                    pv(i - LA)
                yield
            k.P.op("dve", (lambda o=rden[64:65, :], i=pO[64:65, 0:512]: nc.vector.reciprocal(out=o, in_=i)), [b_pO], [b_rden])
            pt, pb = B[1]
            k.mm(pt[0:64, 0:512], ones_f[64:65, 0:64], rden[64:65, :], True, True, [b_onesf, b_rden], [pb])
            k.cp(osb[:], pO[0:64, 0:512], [b_pO], [b_osb], eng="act")
            k.tt(otile[:, h, :], osb[:], pt[0:64, 0:512], ALU.mult, [b_osb, pb], [b_ot])
            yield
        oT(g, otile, b_ot)

    def drive(a, f=None, ratio=1):
        n = 0
        a_live, f_live = a is not None, f is not None
        while a_live or f_live:
            if a_live:
                try:
                    next(a)
                except StopIteration:
                    a_live = False
                n += 1
            if f_live and (not a_live or n % ratio == 0):
                try:
                    next(f)
                except StopIteration:
                    f_live = False

    NF_STEPS = 48
    drive(None, gen_F(0))
    for g in range(NT):
        na = 2 * (4 * g + 4 + 3)
        drive(gen_A(g), gen_F(g + 1) if g + 1 < NT else None, ratio=max(1, na // NF_STEPS))


def l1_consts(S):
    half = 32
    inv = (10000.0 ** (-np.arange(half, dtype=np.float32) / half)).astype(np.float32)
    ang = np.arange(S, dtype=np.float32)[None, :] * inv[:, None]
    cos = np.cos(ang).astype(np.float32)
    sin = np.sin(ang).astype(np.float32)
    cosd = np.concatenate([cos, cos], 0)
    sind = np.concatenate([sin, -sin], 0)
    kk = np.arange(128)[:, None]
    qq = np.arange(512)[None, :]
    cm = np.concatenate([np.where(qq < j * 128 + kk, NEG, 0.0) for j in range(4)], 1).astype(np.float32)
    ed = (np.arange(S)[None, :] // 256 == np.arange(64)[:, None]).astype(np.float32)
    ident = np.eye(128, dtype=np.float32)
    rm = np.tile((np.arange(512) % 64 != 0).astype(np.float32)[None, :], (64, 1))
    s_ = np.arange(64)[:, None]
    t_ = np.arange(64)[None, :]
    am = np.tile((s_ <= t_).astype(np.float32), (1, 8))
    hm = np.zeros((64, 2), np.float32)
    hm[0:32, 0] = 1
    hm[32:64, 1] = 1
    hm2 = np.repeat(hm, 64, axis=1)
    return dict(cosd=cosd, sind=sind, cmd=cm, ed=ed, identd=ident, rmd=rm, amd=am, hm2d=hm2, hmd=hm)


HALO = 8
OCS = 2048


def choose_tiles(W):
    if W == 4104:
        return 4, 3, 342
    if W == 520:
        return 2, 1, 260
    raise ValueError(W)


class Trunk:
    def __init__(self, k, W):
        self.k = k
        self.nc = k.nc
        self.W = W
        self.NB, self.NTB, self.TW = choose_tiles(W)
        self.TB = self.NTB * self.TW
        TB, TW = self.TB, self.TW
        self.h, self.b_h = k.sb([128, 8, TB], F32, "h")
        self.hn, b_hn_whole = k.sb([128, 8, TB], BF16, "hn")
        self.b_hn_nt = [Buf("hn_nt%d" % i) for i in range(self.NTB)]
        self.b_hn = [b_hn_whole] + self.b_hn_nt
        self.act, self.b_act = k.sb([128, 24, TB], BF16, "act")
        self.wst = [k.sb([128, 8, 128], F32, "wst%d" % i) for i in range(3)]
        self.wbf = [k.sb([128, 24, 128], BF16, "wbf%d" % i) for i in range(3)]
        self.wi = 0
        self.si = 0
        self.wq = []
        self.todo = []
        self.pc = [k.sb([128, 4 + TB], F32, "pc%d" % i) for i in range(2)]
        self.pci = 0
        self.yt = [k.sb([128, TB], F32, "yt%d" % i) for i in range(2)]
        self.gel, self.b_gel = k.sb([128, TB], F32, "gel")
        self.sqt, self.b_sqt = k.sb([128, 8, TW], BF16, "sqt")
        self.lnv, self.b_lnv = k.sb([128, TW], F32, "lnvt")
        self.rstd, self.b_rstd = k.sb([128, TW], F32, "rstdt")
        self.ones_b, self.b_ones = k.sb([128, 128], BF16, "onesb")
        self.m, self.b_m = k.sb([128, 1], F32, "hmask")
        self.carry, self.b_carry = k.sb([128, 48, 2], F32, "carry")
        self.B = [k.ps([128, 512], "bank%d" % i) for i in range(8)]
        self.bi = 0
        k.memset(self.ones_b[:], 1.0, [self.b_ones])
        k.memset(self.carry[:], 0.0, [self.b_carry])

    def bank(self):
        b = self.B[self.bi]
        self.bi = (self.bi + 1) % 8
        return b

    def small(self, name, dram_ap, shape):
        t, b = self.k.sb(shape, F32, name)
        self.k.dma(t[:], dram_ap, (), [b])
        return t, b

    def request(self, wd, KC, col0):
        k = self.k
        wb, b_wb = self.wbf[self.wi]
        self.wi = (self.wi + 1) % 3
        wv = wd.rearrange("(kc p) f -> p kc f", p=128)
        for k0 in range(0, KC, 8):
            k1 = min(KC, k0 + 8)
            st, b_st = self.wst[self.si]
            self.si = (self.si + 1) % 3
            k.dma(st[:, 0:k1 - k0, :], wv[:, k0:k1, col0:col0 + 128], (), [b_st])
            k.cp(wb[:, k0:k1, :], st[:, 0:k1 - k0, :], [b_st], [b_wb], eng="act")
        self.wq.append((wb, b_wb))

    def plan(self, reqs):
        assert not self.wq and not getattr(self, "todo", None), "previous plan not fully consumed"
        self.todo = list(reqs)
        for _ in range(2):
            if self.todo:
                self.request(*self.todo.pop(0))

    def run_stage(self, reqs, body):
        for i in range(len(reqs)):
            body(i)

    def rmsnorm(self, g_t, b_g, out_f32=None):
        k = self.k
        TW = self.TW
        for nt in range(self.NTB):
            cs = slice(nt * TW, (nt + 1) * TW)
            k.act(self.sqt[:], self.h[:, :, cs], AF.Square, [self.b_h], [self.b_sqt])
            pt, pb = self.bank()
            for kc in range(8):
                k.mm(pt[:, 0:TW], self.ones_b[:], self.sqt[:, kc, :], kc == 0, kc == 7, [self.b_ones, self.b_sqt], [pb])
            k.act(self.lnv[:], pt[:, 0:TW], AF.Ln, [pb], [self.b_lnv], bias=EPS, scale=1.0 / 1024)
            k.act(self.rstd[:], self.lnv[:], AF.Exp, [self.b_lnv], [self.b_rstd], scale=-0.5)
            for kc in range(8):
                if out_f32 is None:
                    k.stt(self.hn[:, kc, cs], self.h[:, kc, cs], g_t[:, kc:kc + 1], self.rstd[:], ALU.mult, ALU.mult,
                          [self.b_h, b_g, self.b_rstd], [self.b_hn[0], self.b_hn_nt[nt]])
                else:
                    k.stt(out_f32[0][:, kc, cs], self.h[:, kc, cs], g_t[:, kc:kc + 1], self.rstd[:], ALU.mult, ALU.mult,
                          [self.b_h, b_g, self.b_rstd], [out_f32[1]])

    def linear(self, src, b_src, KC, wd, col0, evac):
        k = self.k
        TW = self.TW
        if self.todo:
            self.request(*self.todo.pop(0))
        wb, b_wb = self.wq.pop(0)
        for nt in range(self.NTB):
            cs = slice(nt * TW, (nt + 1) * TW)
            pt, pb = self.bank()
            rb = self.b_hn_nt[nt] if b_src is self.b_hn else b_src
            for kc in range(KC):
                k.mm(pt[:, 0:TW], wb[:, kc, :], src[:, kc, cs], kc == 0, kc == KC - 1, [b_wb, rb], [pb])
            evac(nt, cs, pt[:, 0:TW], pb)

    @staticmethod
    def ffn_reqs(w_up, w_down):
        return ([(w_up, 8, (part * 24 + j) * 128) for j in range(24) for part in range(2)]
                + [(w_down, 24, fc * 128) for fc in range(8)])

    def conv(self, pc, b_pc, ntap, cw_t, b_cw, cb_t, b_cb, idx, yt, b_yt):
        k = self.k
        TB = self.TB
        last = ntap - 1
        k.act(yt[:], pc[:, last:last + TB], AF.Identity, [b_pc, b_cw, b_cb], [b_yt],
              bias=cb_t[:, idx:idx + 1], scale=cw_t[:, idx, last:last + 1])
        for i in range(last - 1, -1, -1):
            k.stt(yt[:], pc[:, i:i + TB], cw_t[:, idx, i:i + 1], yt[:], ALU.mult, ALU.add, [b_pc, b_cw, b_yt], [b_yt])

    def ffn(self, first, gn_t, b_gn, w_up, cw_t, b_cw, cb_t, b_cb, w_down, mask_halo, down_hook=None):
        k = self.k
        TB, TW = self.TB, self.TW
        self.rmsnorm(gn_t, b_gn)
        ys = [None, None]

        def up_body(i):
            j, part = i // 2, i % 2
            idx = part * 24 + j
            pc, b_pc = self.pc[self.pci]
            self.pci = (self.pci + 1) % 2
            yt, b_yt = self.yt[part]
            k.cp(pc[:, 0:2], self.carry[:, idx, :], [self.b_carry], [b_pc], eng="act")

            def evac(nt, cs, ps, pb, pc=pc, b_pc=b_pc):
                k.cp(pc[:, 2 + cs.start:2 + cs.stop], ps, [pb], [b_pc], eng="act")

            self.linear(self.hn, self.b_hn, 8, w_up, idx * 128, evac)
            if first and mask_halo:
                k.ts(pc[:, 2:2 + HALO], pc[:, 2:2 + HALO], self.m[:, 0:1], ALU.mult, [b_pc, self.b_m], [b_pc])
            k.cp(self.carry[:, idx, :], pc[:, TB:TB + 2], [b_pc], [self.b_carry], eng="act")
            self.conv(pc, b_pc, 3, cw_t, b_cw, cb_t, b_cb, idx, yt, b_yt)
            ys[part] = (yt, b_yt)
            if part == 1:
                (yu, b_yu), (yg, b_yg) = ys
                k.act(self.gel[:], yg[:], AF.Gelu_apprx_tanh, [b_yg], [self.b_gel])
                k.tt(self.act[:, j, :], yu[:], self.gel[:], ALU.mult, [b_yu, self.b_gel], [self.b_act])

        self.run_stage([(w_up, 8, (part * 24 + j) * 128) for j in range(24) for part in range(2)], up_body)

        def down_body(fc):
            def evac(nt, cs, ps, pb, fc=fc):
                k.tt(self.h[:, fc, cs], self.h[:, fc, cs], ps, ALU.add, [self.b_h, pb], [self.b_h])

            self.linear(self.act, self.b_act, 24, w_down, fc * 128, evac)

        for fc in range(8):
            down_body(fc)
            if down_hook is not None:
                down_hook(fc)


def l2_io(k, W):
    d = {}
    for name, shape in (("xTw", [1024, W]), ("m", [128, 1]), ("sel", [128, 4]), ("w_out0", [1024, 1024]), ("fg0", [128, 8]),
                        ("w_up0", [1024, 6144]), ("cw0", [128, 48, 3]), ("cb0", [128, 48]), ("w_down0", [3072, 1024]),
                        ("mg", [128, 8]), ("w_in", [1024, 2048]), ("rcw", [128, 8, 4]), ("rcb", [128, 8]), ("wa", [1024, 256]),
                        ("ba", [128, 8]), ("wx", [1024, 256]), ("bx", [128, 8]), ("lam", [128, 8])):
        d[name] = k.din(name, shape)
    return d


def emit_l2(k, W, S, d, og, h1_o, gg_o, hl_o, pl_o, ex_o):
    nc = k.nc
    T = Trunk(k, W)
    NB, TB, TW = T.NB, T.TB, T.TW
    TOK = W - HALO
    xTw, m_d, sel_d, w_out, fg_d, w_up, cw_d, cb_d, w_down = (d["xTw"], d["m"], d["sel"], d["w_out0"], d["fg0"], d["w_up0"],
                                                               d["cw0"], d["cb0"], d["w_down0"])
    mg_d, w_in, rcw_d, rcb_d, wa_d, ba_d, wx_d, bx_d, lam_d = (d["mg"], d["w_in"], d["rcw"], d["rcb"], d["wa"], d["ba"],
                                                               d["wx"], d["bx"], d["lam"])
    sel, b_sel = T.small("sel", sel_d[:, :], [128, 4])

    k.dma(T.m[:], m_d[:, :], (), [T.b_m])
    fg, b_fg = T.small("fg", fg_d[:, :], [128, 8])
    cw, b_cw = T.small("cw", cw_d[:, :, :], [128, 48, 3])
    cb, b_cb = T.small("cb", cb_d[:, :], [128, 48])
    mg, b_mg = T.small("mg", mg_d[:, :], [128, 8])
    rcw, b_rcw = T.small("rcw", rcw_d[:, :, :], [128, 8, 4])
    rcb, b_rcb = T.small("rcb", rcb_d[:, :], [128, 8])
    ba, b_ba = T.small("ba", ba_d[:, :], [128, 8])
    bx, b_bx = T.small("bx", bx_d[:, :], [128, 8])
    lam, b_lam = T.small("lam", lam_d[:, :], [128, 8])
    c1, b_c1 = k.sb([128, 8], F32, "c1")
    c2, b_c2 = k.sb([128, 8], F32, "c2")
    k.act(c1[:], lam[:], AF.Exp, [b_lam], [b_c1], scale=-1.0)
    k.act(c1[:], c1[:], AF.Ln, [b_c1], [b_c1], bias=1.0, scale=1.0)
    k.ts(c2[:], c1[:], -16.0, ALU.mult, [b_c1], [b_c2])
    k.ts(c1[:], c1[:], -8.0, ALU.mult, [b_c1, b_c2], [b_c1])
    gg, b_gg = k.sb([128, TB], BF16, "gg")
    rcar, b_rcar = k.sb([128, 8, 3], F32, "rcar")
    hcar, b_hcar = k.sb([128, 8], F32, "hcar")
    pcar, b_pcar = k.sb([128, 8], F32, "pcar")
    zer, b_zer = k.sb([128, TB], F32, "zer")
    ob_x, b_ob_x = k.sb([128, 8, TB], BF16, "ob_x")
    xrb, b_xrb = k.sb([128, 2, TB], BF16, "xrb")
    rr, b_rr = k.sb([128, TB], F32, "rr")
    ii, b_ii = k.sb([128, TB], F32, "ii")
    aa, b_aa = k.sb([128, TB], F32, "aa")
    uu, b_uu = k.sb([128, TB], F32, "uu")
    hl, b_hl = k.sb([128, TB], F32, "hl")
    pl, b_pl = k.sb([128, TB], F32, "pl")
    k.memset(rcar[:], 0.0, [b_rcar])
    k.memset(zer[:], 0.0, [b_zer])
    ext, b_ext = k.sb([128, 8, 2], F32, "ext")

    oTv = [o_.rearrange("(kc p) s -> p kc s", p=128) for o_ in og]
    xTv = xTw.rearrange("(kc p) s -> p kc s", p=128)
    h1v = h1_o.rearrange("(kc p) s -> p kc s", p=128)
    ggv = gg_o.rearrange("(kc p) s -> p kc s", p=128)
    hlv = hl_o.rearrange("(kc p) s -> p kc s", p=128)
    plv = pl_o.rearrange("(kc p) s -> p kc s", p=128)
    exv = ex_o.rearrange("(kc p) s -> p kc s", p=128)

    def load_o(blk, dst, b_dst):
        g0 = blk * TB
        for c in range(4):
            cand = T.act[:, 8 * (c % 3):8 * (c % 3) + 8, :]
            lo = c * TOK - HALO + g0
            skip = max(0, -lo)
            if skip:
                k.memset(cand[:, :, 0:skip], 0.0, [T.b_act])
            a = lo + skip
            while a < lo + TB:
                j = a // OCS
                e = min(lo + TB, (j + 1) * OCS)
                k.dma(cand[:, :, a - lo:e - lo], oTv[j][:, :, a - j * OCS:e - j * OCS], (), [T.b_act], eng=("sp" if blk == 0 else "act"))
                a = e
            if c == 0:
                k.ts(dst[:], cand, sel[:, 0:1], ALU.mult, [T.b_act, b_sel], [b_dst])
            else:
                k.stt(dst[:], cand, sel[:, c:c + 1], dst[:], ALU.mult, ALU.add, [T.b_act, b_sel, b_dst], [b_dst])

    def load_x(blk):
        g0 = blk * TB
        k.dma(T.h[:, 0:4, :], xTv[:, 0:4, g0:g0 + TB], (), [T.b_h])
        k.dma(T.h[:, 4:8, :], xTv[:, 4:8, g0:g0 + TB], (), [T.b_h])

    for blk in range(NB):
        first = blk == 0
        g0 = blk * TB
        if first:
            load_o(0, T.hn, T.b_hn)
            load_x(0)
            ob, b_ob = T.hn, T.b_hn
        else:
            ob, b_ob = ob_x, b_ob_x
        rg_reqs = []
        for n in range(4):
            rg_reqs += [(w_in, 8, 1024 + (2 * n + c2i) * 128) for c2i in range(2)]
            for c2i in range(2):
                rg_reqs += [(wa_d[n * 256:(n + 1) * 256, :], 2, c2i * 128), (wx_d[n * 256:(n + 1) * 256, :], 2, c2i * 128)]
        T.plan([(w_out, 8, fc * 128) for fc in range(8)] + T.ffn_reqs(w_up, w_down)
               + [(w_in, 8, fc * 128) for fc in range(8)] + rg_reqs)

        def wo_body(fc):
            def evac(nt, cs, ps, pb, fc=fc):
                k.tt(T.h[:, fc, cs], T.h[:, fc, cs], ps, ALU.add, [T.b_h, pb], [T.b_h])

            T.linear(ob, b_ob, 8, w_out, fc * 128, evac)

        T.run_stage([(w_out, 8, fc * 128) for fc in range(8)], wo_body)
        T.ffn(first, fg, b_fg, w_up, cw, b_cw, cb, b_cb, w_down, False)
        k.dma(h1v[:, :, g0:g0 + TB], T.h[:], [T.b_h], (), eng="pool")
        T.rmsnorm(mg, b_mg)
        if blk + 1 < NB:
            load_x(blk + 1)
        def gb_body(fc):
            def evac(nt, cs, ps, pb, fc=fc):
                k.act(gg[:, cs], ps, AF.Gelu_apprx_tanh, [pb], [b_gg])

            T.linear(T.hn, T.b_hn, 8, w_in, fc * 128, evac)
            k.dma(ggv[:, fc, g0:g0 + TB], gg[:], [b_gg], (), eng="pool")

        T.run_stage([(w_in, 8, fc * 128) for fc in range(8)], gb_body)
        reqs = []
        for n in range(4):
            reqs += [(w_in, 8, 1024 + (2 * n + c2i) * 128) for c2i in range(2)]
            for c2i in range(2):
                reqs += [(wa_d[n * 256:(n + 1) * 256, :], 2, c2i * 128), (wx_d[n * 256:(n + 1) * 256, :], 2, c2i * 128)]

        def rg_body(i, blk=blk, first=first, g0=g0):
            n, r = i // 6, i % 6
            if r < 2:
                c2i = r
                c8 = 2 * n + c2i
                pc, b_pc = T.pc[T.pci]
                T.pci = (T.pci + 1) % 2
                k.cp(pc[:, 0:3], rcar[:, c8, :], [b_rcar], [b_pc], eng="act")

                def evac(nt, cs, ps, pb, pc=pc, b_pc=b_pc):
                    k.cp(pc[:, 3 + cs.start:3 + cs.stop], ps, [pb], [b_pc], eng="act")

                T.linear(T.hn, T.b_hn, 8, w_in, 1024 + c8 * 128, evac)
                if first:
                    k.ts(pc[:, 3:3 + HALO], pc[:, 3:3 + HALO], T.m[:, 0:1], ALU.mult, [b_pc, T.b_m], [b_pc])
                k.cp(rcar[:, c8, :], pc[:, TB:TB + 3], [b_pc], [b_rcar], eng="act")
                yt, b_yt = T.yt[c2i]
                T.conv(pc, b_pc, 4, rcw, b_rcw, rcb, b_rcb, c8, yt, b_yt)
                k.cp(xrb[:, c2i, :], yt[:], [b_yt], [b_xrb], eng="act")
                return
            c2i, which = (r - 2) // 2, (r - 2) % 2
            fc = 2 * n + c2i
            wd, bias_t, b_bias, dst, b_dst = ((wa_d, ba, b_ba, rr, b_rr), (wx_d, bx, b_bx, ii, b_ii))[which]

            def evac(nt, cs, ps, pb, dst=dst, b_dst=b_dst, bias_t=bias_t, b_bias=b_bias, fc=fc):
                k.act(dst[:, cs], ps, AF.Sigmoid, [pb, b_bias], [b_dst], bias=bias_t[:, fc:fc + 1], scale=1.0)

            T.linear(xrb, b_xrb, 2, wd[n * 256:(n + 1) * 256, :], c2i * 128, evac)
            if which == 0:
                return
            k.act(aa[:], rr[:], AF.Exp, [b_rr, b_c1], [b_aa], scale=c1[:, fc:fc + 1])
            k.act(rr[:], rr[:], AF.Exp, [b_rr, b_c2], [b_rr], scale=c2[:, fc:fc + 1])
            k.act(rr[:], rr[:], AF.Sqrt, [b_rr], [b_rr], bias=1.0, scale=-1.0)
            k.tt(uu[:], T.yt[c2i][0][:], ii[:], ALU.mult, [T.yt[c2i][1], b_ii], [b_uu])
            k.tt(uu[:], uu[:], rr[:], ALU.mult, [b_uu, b_rr], [b_uu])
            if first:
                k.ts(uu[:, 0:HALO], uu[:, 0:HALO], T.m[:, 0:1], ALU.mult, [b_uu, T.b_m], [b_uu])
                k.memset(hl[:, 0:5], 0.0, [b_hl])
                k.memset(pl[:, 0:5], 0.0, [b_pl])
                s0, hi, pi = 5, 0.0, 1.0
            else:
                s0, hi, pi = 0, hcar[:, fc:fc + 1], pcar[:, fc:fc + 1]
            k.scan(hl[:, s0:TB], aa[:, s0:TB], uu[:, s0:TB], hi, [b_aa, b_uu, b_hcar], [b_hl])
            k.scan(pl[:, s0:TB], aa[:, s0:TB], zer[:, s0:TB], pi, [b_aa, b_zer, b_pcar], [b_pl])
            k.cp(hcar[:, fc:fc + 1], hl[:, TB - 1:TB], [b_hl], [b_hcar], eng="pool")
            k.cp(pcar[:, fc:fc + 1], pl[:, TB - 1:TB], [b_pl], [b_pcar], eng="pool")
            k.dma(hlv[:, fc, g0:g0 + TB], hl[:], [b_hl], (), eng="pool")
            k.dma(plv[:, fc, g0:g0 + TB], pl[:], [b_pl], (), eng="pool")
            if blk == NB - 1:
                k.cp(ext[:, fc, 0:1], hl[:, TB - 4:TB - 3], [b_hl], [b_ext], eng="pool")
                k.cp(ext[:, fc, 1:2], pl[:, TB - 4:TB - 3], [b_pl], [b_ext], eng="pool")

        if blk + 1 < NB:
            load_o(blk + 1, ob_x, b_ob_x)
        T.run_stage(reqs, rg_body)
    k.dma(exv[:, :, :], ext[:], [b_ext], (), eng="pool")


def l3_io(k, W):
    d = {}
    for name, shape in (("srank", [128, 4]), ("oms", [128, 4]), ("w_out1", [1024, 1024]), ("fg1", [128, 8]),
                        ("w_up1", [1024, 6144]), ("cw1", [128, 48, 3]), ("cb1", [128, 48]), ("w_down1", [3072, 1024]),
                        ("fin", [128, 8])):
        d[name] = k.din(name, shape)
    return d


def emit_l3(k, W, d, m_d, h1w, ggw, hlw, plw, exg, out_o):
    nc = k.nc
    T = Trunk(k, W)
    NB, TB, TW = T.NB, T.TB, T.TW
    TOK = W - HALO
    w_out, fg_d, w_up, cw_d, cb_d, w_down, fin_d = (d["w_out1"], d["fg1"], d["w_up1"], d["cw1"], d["cb1"], d["w_down1"], d["fin"])
    k.dma(T.m[:], m_d[:, :], (), [T.b_m])
    fg, b_fg = T.small("fg", fg_d[:, :], [128, 8])
    cw, b_cw = T.small("cw", cw_d[:, :, :], [128, 48, 3])
    cb, b_cb = T.small("cb", cb_d[:, :], [128, 48])
    fin, b_fin = T.small("fin", fin_d[:, :], [128, 8])
    sr, b_sr = T.small("sr", d["srank"][:, :], [128, 4])
    oms, b_oms = T.small("oms", d["oms"][:, :], [128, 4])
    pe, b_pe = k.sb([128, 4, 8, 2], F32, "pe")
    exv = exg.rearrange("(r kc p) t -> p r kc t", p=128, kc=8)
    for r in range(4):
        k.dma(pe[:, r, :, :], exv[:, r, :, :], (), [b_pe])
    Hc, b_Hc = k.sb([128, 8], F32, "Hc")
    Pm, b_Pm = k.sb([128, 8], F32, "Pm")
    Em, b_Em = k.sb([128, 8], F32, "Em")
    k.memset(Hc[:], 0.0, [b_Hc])
    for r in range(4):
        k.ts(Pm[:], pe[:, r, :, 1], sr[:, r:r + 1], ALU.mult, [b_pe, b_sr], [b_Pm], s2=oms[:, r:r + 1], op1=ALU.add)
        k.ts(Em[:], pe[:, r, :, 0], sr[:, r:r + 1], ALU.mult, [b_pe, b_sr], [b_Em])
        k.tt(Hc[:], Hc[:], Pm[:], ALU.mult, [b_Hc, b_Pm], [b_Hc])
        k.tt(Hc[:], Hc[:], Em[:], ALU.add, [b_Hc, b_Em], [b_Hc])
    yb, b_yb = T.hn, T.b_hn
    ybufs = [(k.sb([128, TB], F32, "hl%d" % i), k.sb([128, TB], F32, "pl%d" % i), k.sb([128, TB], BF16, "ggt%d" % i)) for i in range(2)]
    outt, b_outt = T.h, T.b_h

    h1v = h1w.rearrange("(kc p) s -> p kc s", p=128)
    ggv = ggw.rearrange("(kc p) s -> p kc s", p=128)
    hlv = hlw.rearrange("(kc p) s -> p kc s", p=128)
    plv = plw.rearrange("(kc p) s -> p kc s", p=128)
    outv = out_o.rearrange("(kc p) s -> p kc s", p=128)

    for blk in range(NB):
        first = blk == 0
        g0 = blk * TB
        k.dma(T.h[:, 0:4, :], h1v[:, 0:4, g0:g0 + TB], (), [T.b_h])
        k.dma(T.h[:, 4:8, :], h1v[:, 4:8, g0:g0 + TB], (), [T.b_h])
        def y_load(bb, fc):
            (hl, b_hl), (pl, b_pl), (ggt, b_ggt) = ybufs[fc % 2]
            q0 = bb * TB
            k.dma(hl[:], hlv[:, fc, q0:q0 + TB], (), [b_hl])
            k.dma(pl[:], plv[:, fc, q0:q0 + TB], (), [b_pl])
            k.dma(ggt[:], ggv[:, fc, q0:q0 + TB], (), [b_ggt])

        def y_comp(fc):
            (hl, b_hl), (pl, b_pl), (ggt, b_ggt) = ybufs[fc % 2]
            k.stt(hl[:], pl[:], Hc[:, fc:fc + 1], hl[:], ALU.mult, ALU.add, [b_pl, b_Hc, b_hl], [b_hl])
            k.tt(yb[:, fc, :], hl[:], ggt[:], ALU.mult, [b_hl, b_ggt], [b_yb])

        if first:
            for fc in range(8):
                y_load(0, fc)
                y_comp(fc)

        def down_hook(fc, blk=blk):
            if blk + 1 >= NB:
                return
            if fc == 0:
                y_load(blk + 1, 0)
            if fc + 1 < 8:
                y_load(blk + 1, fc + 1)
            y_comp(fc)
        T.plan([(w_out, 8, fc * 128) for fc in range(8)] + T.ffn_reqs(w_up, w_down))

        def wo_body(fc):
            def evac(nt, cs, ps, pb, fc=fc):
                k.tt(T.h[:, fc, cs], T.h[:, fc, cs], ps, ALU.add, [T.b_h, pb], [T.b_h])

            T.linear(yb, b_yb, 8, w_out, fc * 128, evac)

        T.run_stage([(w_out, 8, fc * 128) for fc in range(8)], wo_body)
        T.ffn(first, fg, b_fg, w_up, cw, b_cw, cb, b_cb, w_down, True, down_hook=down_hook)
        T.rmsnorm(fin, b_fin, out_f32=(outt, b_outt))
        lo = HALO if first else 0
        k.dma(outv[:, :, g0 + lo - HALO:g0 + TB - HALO], outt[:, :, lo:TB], [b_outt], (), eng="pool")


_F = {}


def build_fused(S):
    TOK = S // 4
    W = TOK + HALO
    nc = bass.Bass("TRN2", target_bir_lowering=False)
    k = K(nc)
    io1 = l1_io(k, S)
    io2 = l2_io(k, W)
    io3 = l3_io(k, W)
    NCH = max(1, S // OCS)
    osrc = [k.dint("osrc%d" % j, [256, min(S, OCS)], BF16) for j in range(NCH)]
    og = [k.dint("og%d" % j, [1024, min(S, OCS)], BF16) for j in range(NCH)]
    h1 = k.dint("h1s", [1024, W])
    gg = k.dint("ggs", [1024, W], BF16)
    hl = k.dint("hls", [1024, W])
    pl = k.dint("pls", [1024, W])
    exs = k.dint("exs", [1024, 2])
    exg = k.dint("exg", [4096, 2])
    out = k.dout("outT", [1024, TOK])
    groups = [[0, 1, 2, 3], [4, 5, 6, 7]]
    upto = int(os.environ.get("FUSE_UPTO", "3"))
    tile_bufs = {}

    def o_out(g, otile, b_ot):
        j, off = (g * 512) // OCS, (g * 512) % OCS
        ov = osrc[j].rearrange("(r p) s -> p r s", p=64)
        bb = Buf()
        tile_bufs.setdefault(j, []).append(bb)
        k.dma(ov[:, :, off:off + 512], otile[:], [b_ot], [bb], eng="pool")
        if off + 512 == min(S, OCS):
            k.P.dma((lambda j=j: nc.gpsimd.collective_compute("AllGather", ALU.bypass, replica_groups=groups,
                                                               ins=[osrc[j][:, :]], outs=[og[j][:, :]])),
                    tile_bufs[j], (), eng="pool", inc=1)

    emit_l1(k, S, io1, o_out)
    k.end_phase()
    emit_l2(k, W, S, io2, og, h1, gg, hl, pl, exs)
    k.end_phase()
    if upto == 2:
        return nc, k.P.finish()
    if os.environ.get("NOCC2"):
        k.dma(exg[0:1024, :], exs[:, :], (), ())
    else:
        k.P.dma(lambda: nc.gpsimd.collective_compute("AllGather", ALU.bypass, replica_groups=groups, ins=[exs[:, :]], outs=[exg[:, :]]),
                (), (), eng="pool", inc=1)
    k.P.barrier()
    emit_l3(k, W, io3, io2["m"], h1, gg, hl, pl, exg, out)
    cnt = k.P.finish()
    return nc, cnt


def _pk(v):
    return np.ascontiguousarray(v.reshape(-1, 128).T)


def _pkw(w):
    t, C = w.shape
    return np.ascontiguousarray(w.T.reshape(C // 128, 128, t).transpose(1, 0, 2))


def kernel(x, mix_norm_g, ffn_norm_g, final_norm_g,
           ev_w_in, ev_w_gk2, ev_b_gk2, ev_gla_norm_g, ev_w_out,
           od_w_in, od_conv_w, od_conv_b, od_w_a, od_b_a, od_w_x, od_b_x, od_lambda, od_w_out,
           ffn_w_up, ffn_conv_w, ffn_conv_b, ffn_w_down):
    f = lambda a: np.asarray(a, dtype=np.float32)
    x = f(x)
    Bsz, S, D = x.shape
    TOK = S // 4
    W = TOK + HALO
    if S not in _F:
        _F[S] = build_fused(S)
    nc, _ = _F[S]
    consts = l1_consts(S)
    w = f(ev_w_in)[0]
    gn = _pk(f(mix_norm_g)[0])
    perm = []
    for g in range(4):
        for j in range(4):
            head = 2 * g + (j % 2)
            base = head * 64 if j < 2 else 512 + head * 64
            perm += list(range(base, base + 64))
    w_out0 = np.ascontiguousarray(f(ev_w_out)[0][np.asarray(perm)])
    shared = {
        "gn": gn, "w_out0": w_out0, "fg0": _pk(f(ffn_norm_g)[0]), "w_up0": f(ffn_w_up)[0],
        "cw0": _pkw(f(ffn_conv_w)[0]), "cb0": _pk(f(ffn_conv_b)[0]), "w_down0": f(ffn_w_down)[0],
        "mg": _pk(f(mix_norm_g)[1]), "w_in": f(od_w_in)[0], "rcw": _pkw(f(od_conv_w)[0]), "rcb": _pk(f(od_conv_b)[0]),
        "wa": np.ascontiguousarray(f(od_w_a)[0].reshape(1024, 256)), "ba": _pk(f(od_b_a)[0]),
        "wx": np.ascontiguousarray(f(od_w_x)[0].reshape(1024, 256)), "bx": _pk(f(od_b_x)[0]),
        "lam": _pk(f(od_lambda)[0]),
        "w_out1": f(od_w_out)[0], "fg1": _pk(f(ffn_norm_g)[1]), "w_up1": f(ffn_w_up)[1],
        "cw1": _pkw(f(ffn_conv_w)[1]), "cb1": _pk(f(ffn_conv_b)[1]), "w_down1": f(ffn_w_down)[1],
        "fin": _pk(f(final_norm_g)),
    }
    shared.update(consts)
    in_maps = []
    for b in range(Bsz):
        xTb = np.ascontiguousarray(x[b].T)
        for c in range(4):
            h0, h1 = 2 * c, 2 * c + 1
            cols = []
            for base in (0, 512, 1024):
                cols += [np.arange(base + h0 * 64, base + h0 * 64 + 64), np.arange(base + h1 * 64, base + h1 * 64 + 64)]
            for base in (1536, 1792):
                cols += [np.arange(base + h0 * 32, base + h0 * 32 + 32), np.arange(base + h1 * 32, base + h1 * 32 + 32)]
            cols += [np.arange(2048 + h0 * 64, 2048 + h0 * 64 + 64), np.arange(2048 + h1 * 64, 2048 + h1 * 64 + 64)]
            cols += [np.arange(2560, 2576)]
            cols += [np.arange(2576 + h0 * 64, 2576 + h0 * 64 + 64), np.arange(2576 + h1 * 64, 2576 + h1 * 64 + 64)]
            cols = np.concatenate(cols)
            t0 = c * TOK
            xw = np.zeros((1024, W), np.float32)
            if c == 0:
                xw[:, HALO:] = xTb[:, 0:TOK]
            else:
                xw[:] = xTb[:, t0 - HALO:t0 + TOK]
            sel = np.zeros((128, 4), np.float32)
            sel[:, c] = 1.0
            sr = np.zeros((128, 4), np.float32)
            sr[:, :c] = 1.0
            m = dict(shared)
            m.update({
                "xT": xTb, "w1": np.ascontiguousarray(w[:, cols]),
                "wgk2": np.ascontiguousarray(f(ev_w_gk2)[0][:, h0 * 32:h0 * 32 + 64]),
                "bgk2": np.ascontiguousarray(f(ev_b_gk2)[0][h0 * 32:h0 * 32 + 64].reshape(64, 1)),
                "glag": np.ascontiguousarray(f(ev_gla_norm_g)[0][h0:h0 + 2].T),
                "xTw": xw, "m": np.full((128, 1), 0.0 if c == 0 else 1.0, np.float32),
                "sel": sel, "srank": sr, "oms": 1.0 - sr,
            })
            in_maps.append(m)
    res = run_bass_kernel_spmd(nc, in_maps, core_ids=list(range(len(in_maps)))).results
    out = np.zeros((Bsz, S, D), np.float32)
    for b in range(Bsz):
        for c in range(4):
            out[b, c * TOK:(c + 1) * TOK, :] = np.asarray(res[b * 4 + c]["outT"]).T
    return out
```
